# Optimizing a Trainium2 kernel written in Bass

```python
import math
import jax, jax.numpy as jnp
from jax import lax
import numpy as np

D_MODEL = 1024
BATCH = 2
SEQ = 16384
DEPTH = 2

ATT_HEADS = 8
ATT_KV_GROUPS = 2
ATT_HPG = ATT_HEADS // ATT_KV_GROUPS
HEAD_DIM = 64
D_ATT = ATT_HEADS * HEAD_DIM
D_KV = ATT_KV_GROUPS * HEAD_DIM
N_BRANCH = 3
CMP_LEN = 32
CMP_STRIDE = 16
CMP_HIDDEN = 256
SLC_BLOCK = 64
SLC_TOPN = 16
WINDOW = 512
Q_BLOCK = 128
SEL_FORCE = 1.0e4
REL_BUCKETS = 32
REL_MAX_DIST = 128
GMLP_GROUPS = 4
GMLP_GROUP_DIM = 64
D_GMLP = GMLP_GROUPS * GMLP_GROUP_DIM
GMLP_CHUNK = 128
SSM_HEADS = 4
SSM_HEAD_DIM = 64
D_SSM = SSM_HEADS * SSM_HEAD_DIM
SSM_GROUPS = 2
SSM_STATE = 128
SSM_CONV = 4
SSM_CHUNK = 256
D_XBC = D_SSM + 2 * SSM_GROUPS * SSM_STATE
D_MIX = D_ATT + D_GMLP + D_SSM
D_ATT_IN = D_ATT + 6 * D_KV + ATT_HEADS * N_BRANCH
D_GMLP_IN = 2 * D_GMLP
D_SSM_IN = D_SSM + D_XBC + SSM_HEADS
D_IN = D_ATT_IN + D_GMLP_IN + D_SSM_IN
D_FF = 2816
N_SUB = 3
ALPHA = (2 * DEPTH) ** 0.25
BETA = (8 * DEPTH) ** -0.25
LN_EPS = 1e-5

kernel_name = "hybrid_nsa_gmlp_ssd_macaron_deepnorm"


def layer_norm(x, g, b):
    xf = x.astype(jnp.float32)
    mu = xf.mean(-1, keepdims=True)
    var = jnp.square(xf - mu).mean(-1, keepdims=True)
    return ((xf - mu) * lax.rsqrt(var + LN_EPS)).astype(x.dtype) * g + b


def swiglu(h, w1, w3, w2):
    return (jax.nn.silu(h @ w1) * (h @ w3)) @ w2


def masked_softmax(s, mask):
    s = jnp.where(mask, s.astype(jnp.float32), -jnp.inf)
    m = jnp.max(s, axis=-1, keepdims=True)
    m = jnp.where(jnp.isfinite(m), m, 0.0)
    p = jnp.exp(s - m)
    return p / jnp.maximum(p.sum(-1, keepdims=True), 1e-30)


def rel_bucket(dist):
    n = jnp.maximum(dist, 0)
    max_exact = REL_BUCKETS // 2
    nf = jnp.maximum(n, 1).astype(jnp.float32)
    large = max_exact + (jnp.log(nf / max_exact) / math.log(REL_MAX_DIST / max_exact)
                         * (REL_BUCKETS - max_exact)).astype(jnp.int32)
    large = jnp.minimum(large, REL_BUCKETS - 1)
    return jnp.where(n < max_exact, n, large)


def nsa_attention(q, k_cmp, v_cmp, k_slc, v_slc, k_win, v_win, gates, rel_bias,
                  cmp_pe, cmp_w1, cmp_b1, cmp_w2):
    Bn, T = q.shape[:2]
    G, I, Dh = ATT_KV_GROUPS, ATT_HPG, HEAD_DIM
    n_cmp = (T - CMP_LEN) // CMP_STRIDE + 1
    n_slc = T // SLC_BLOCK
    n_qb = T // Q_BLOCK
    top_n = min(SLC_TOPN, n_slc)
    kw_len = Q_BLOCK + WINDOW

    cmp_start = np.arange(n_cmp) * CMP_STRIDE
    blk_idx = cmp_start[:, None] + np.arange(CMP_LEN)[None, :]

    def compress(kv, j):
        blocks = kv[:, blk_idx] + cmp_pe[j][None, None, :, None, :]
        blocks = blocks.transpose(0, 3, 1, 2, 4).reshape(Bn, G, n_cmp, CMP_LEN * Dh)
        return jax.nn.gelu(blocks @ cmp_w1[j] + cmp_b1[j]) @ cmp_w2[j]

    kc, vc = compress(k_cmp, 0), compress(v_cmp, 1)
    cmp_end = jnp.asarray(cmp_start + CMP_LEN - 1, jnp.int32)
    slc_start = np.arange(n_slc) * SLC_BLOCK
    overlap = np.clip(np.minimum(cmp_start[:, None] + CMP_LEN, slc_start[None, :] + SLC_BLOCK)
                      - np.maximum(cmp_start[:, None], slc_start[None, :]), 0, None)
    overlap = jnp.asarray(overlap / CMP_LEN, jnp.float32)

    ks = k_slc.transpose(0, 2, 1, 3).reshape(Bn, G, n_slc, SLC_BLOCK, Dh)
    vs = v_slc.transpose(0, 2, 1, 3).reshape(Bn, G, n_slc, SLC_BLOCK, Dh)
    pad_w = ((0, 0), (0, 0), (WINDOW, 0), (0, 0))
    kw = jnp.pad(k_win.transpose(0, 2, 1, 3), pad_w)
    vw = jnp.pad(v_win.transpose(0, 2, 1, 3), pad_w)

    win_dist = np.arange(Q_BLOCK)[:, None] + WINDOW - np.arange(kw_len)[None, :]
    win_mask = jnp.asarray((win_dist >= 0) & (win_dist < WINDOW))
    win_bias = rel_bias[rel_bucket(jnp.asarray(win_dist, jnp.int32))]
    win_bias = win_bias.reshape(Q_BLOCK, kw_len, G, I).transpose(2, 3, 0, 1)
    rb = rel_bias.reshape(REL_BUCKETS, G, I)
    gather_blocks = jax.vmap(jax.vmap(lambda kb, s: kb[s]))
    bias_by_group = jax.vmap(lambda tb, bk: tb[bk], in_axes=(1, 1), out_axes=1)

    qb_all = q.reshape(Bn, n_qb, Q_BLOCK, G, I, Dh).transpose(1, 0, 3, 4, 2, 5) * (Dh ** -0.5)
    gb_all = gates.reshape(Bn, n_qb, Q_BLOCK, G, I, N_BRANCH).transpose(1, 0, 3, 4, 2, 5)

    def block(args):
        b_idx, qb, gb = args
        start = b_idx * Q_BLOCK
        t = start + jnp.arange(Q_BLOCK, dtype=jnp.int32)
        dist_c = t[:, None] - cmp_end[None, :]
        bias_c = rel_bias[rel_bucket(dist_c)].reshape(Q_BLOCK, n_cmp, G, I).transpose(2, 3, 0, 1)
        p_c = masked_softmax(jnp.einsum('bgiqd,bgkd->bgiqk', qb, kc) + bias_c, dist_c >= 0)
        o_c = jnp.einsum('bgiqk,bgkd->bgiqd', p_c.astype(vc.dtype), vc)
        imp = jnp.einsum('bgiqk,kn->bgqn', p_c, overlap)
        cur = (t // SLC_BLOCK)[:, None]
        blk = jnp.arange(n_slc, dtype=jnp.int32)[None, :]
        forced = (blk == 0) | (blk == cur) | (blk == cur - 1)
        imp = jnp.where(forced, SEL_FORCE, jnp.where(blk <= cur, imp, -SEL_FORCE))
        _, sel = lax.top_k(imp, top_n)
        k_sel = gather_blocks(ks, sel).reshape(Bn, G, Q_BLOCK, top_n * SLC_BLOCK, Dh)
        v_sel = gather_blocks(vs, sel).reshape(Bn, G, Q_BLOCK, top_n * SLC_BLOCK, Dh)
        pos_sel = (sel[..., None] * SLC_BLOCK + jnp.arange(SLC_BLOCK, dtype=jnp.int32)
                   ).reshape(Bn, G, Q_BLOCK, top_n * SLC_BLOCK)
        dist_s = t[None, None, :, None] - pos_sel
        bias_s = bias_by_group(rb, rel_bucket(dist_s)).transpose(0, 1, 4, 2, 3)
        p_s = masked_softmax(jnp.einsum('bgiqd,bgqkd->bgiqk', qb, k_sel) + bias_s,
                             (dist_s >= 0)[:, :, None])
        o_s = jnp.einsum('bgiqk,bgqkd->bgiqd', p_s.astype(v_sel.dtype), v_sel)
        kb = lax.dynamic_slice_in_dim(kw, start, kw_len, axis=2)
        vb = lax.dynamic_slice_in_dim(vw, start, kw_len, axis=2)
        pos_w = start - WINDOW + jnp.arange(kw_len, dtype=jnp.int32)
        p_w = masked_softmax(jnp.einsum('bgiqd,bgkd->bgiqk', qb, kb) + win_bias,
                             win_mask & (pos_w >= 0)[None, :])
        o_w = jnp.einsum('bgiqk,bgkd->bgiqd', p_w.astype(vb.dtype), vb)
        return gb[..., 0:1] * o_c + gb[..., 1:2] * o_s + gb[..., 2:3] * o_w

    o = lax.map(block, (jnp.arange(n_qb, dtype=jnp.int32), qb_all, gb_all))
    return o.transpose(1, 0, 4, 2, 3, 5).reshape(Bn, T, ATT_HEADS * Dh)


def gmlp_spatial_gating(uv, ln_g, ln_b, ws, bs):
    Bn, T, _ = uv.shape
    u, v = jnp.split(jax.nn.gelu(uv), 2, axis=-1)
    v = layer_norm(v, ln_g, ln_b)
    v = v.reshape(Bn, T // GMLP_CHUNK, GMLP_CHUNK, GMLP_GROUPS, GMLP_GROUP_DIM)
    causal = jnp.tril(jnp.ones((GMLP_CHUNK, GMLP_CHUNK), dtype=bool))
    w = jnp.where(causal[None], ws, 0.0)
    sv = jnp.einsum('gts,bcsgd->bctgd', w, v) + bs.T[None, None, :, :, None]
    return u * sv.reshape(Bn, T, D_GMLP)


def segsum(a):
    L = a.shape[-1]
    cs = jnp.cumsum(a, axis=-1)
    mask = jnp.tril(jnp.ones((L, L), dtype=bool))
    return jnp.where(mask, cs[..., :, None] - cs[..., None, :], -jnp.inf)


def ssd_chunked(X, A, Bh, Ch):
    Bn, T, H, P = X.shape
    nc = -(-T // SSM_CHUNK)
    pad = nc * SSM_CHUNK - T
    def chunk(a):
        a = jnp.pad(a, [(0, 0), (0, pad)] + [(0, 0)] * (a.ndim - 2))
        return a.reshape(Bn, nc, SSM_CHUNK, *a.shape[2:])
    X, A, Bh, Ch = chunk(X), chunk(A), chunk(Bh), chunk(Ch)
    A = A.transpose(0, 3, 1, 2)
    A_cs = jnp.cumsum(A, axis=-1)
    Lmat = jnp.exp(segsum(A))
    scores = jnp.einsum('bclhn,bcshn->bhcls', Ch, Bh) * Lmat
    y_diag = jnp.einsum('bhcls,bcshp->bclhp', scores, X)
    decay = jnp.exp(A_cs[..., -1:] - A_cs)
    states = jnp.einsum('bclhn,bhcl,bclhp->bchpn', Bh, decay, X)
    chunk_decay = jnp.exp(A_cs[..., -1])

    def step(h, inp):
        s_c, dec = inp
        return dec[:, :, None, None] * h + s_c, h

    h0 = jnp.zeros((Bn, H, P, Bh.shape[-1]), X.dtype)
    _, prev = lax.scan(step, h0, (states.transpose(1, 0, 2, 3, 4), chunk_decay.transpose(2, 0, 1)))
    prev = prev.transpose(1, 0, 2, 3, 4)
    y_off = jnp.einsum('bclhn,bchpn,bhcl->bclhp', Ch, prev, jnp.exp(A_cs))
    return (y_diag + y_off).reshape(Bn, nc * SSM_CHUNK, H, P)[:, :T]


def mamba2_ssd(zxbcdt, conv_w, conv_b, dt_bias, a_log, d_skip, norm_g):
    Bn, T, _ = zxbcdt.shape
    z, xbc, dt = jnp.split(zxbcdt, [D_SSM, D_SSM + D_XBC], axis=-1)
    xbc = lax.conv_general_dilated(xbc, conv_w[:, None, :], window_strides=(1,),
                                   padding=[(SSM_CONV - 1, 0)],
                                   dimension_numbers=('NWC', 'WIO', 'NWC'),
                                   feature_group_count=D_XBC) + conv_b
    xbc = jax.nn.silu(xbc)
    xs, Bm, Cm = jnp.split(xbc, [D_SSM, D_SSM + SSM_GROUPS * SSM_STATE], axis=-1)
    dt = jax.nn.softplus((dt + dt_bias).astype(jnp.float32))
    A = -jnp.exp(a_log.astype(jnp.float32))
    X = xs.reshape(Bn, T, SSM_HEADS, SSM_HEAD_DIM)
    rep = SSM_HEADS // SSM_GROUPS
    Bh = jnp.repeat(Bm.reshape(Bn, T, SSM_GROUPS, SSM_STATE), rep, axis=2).astype(jnp.float32)
    Ch = jnp.repeat(Cm.reshape(Bn, T, SSM_GROUPS, SSM_STATE), rep, axis=2).astype(jnp.float32)
    y = ssd_chunked(X.astype(jnp.float32) * dt[..., None], A * dt, Bh, Ch)
    y = y + X.astype(jnp.float32) * d_skip.astype(jnp.float32)[:, None]
    g = (y.reshape(Bn, T, D_SSM) * jax.nn.silu(z.astype(jnp.float32))).reshape(Bn, T, SSM_GROUPS, -1)
    g = g * lax.rsqrt(jnp.mean(jnp.square(g), axis=-1, keepdims=True) + LN_EPS)
    return g.reshape(Bn, T, D_SSM).astype(zxbcdt.dtype) * norm_g


def hybrid_mixer(h, rel_bias, w_in, w_out, cmp_pe, cmp_w1, cmp_b1, cmp_w2,
                 gmlp_ln_g, gmlp_ln_b, gmlp_ws, gmlp_bs,
                 conv_w, conv_b, dt_bias, a_log, d_skip, norm_g):
    Bn, T, _ = h.shape
    proj = h @ w_in
    att_in, gmlp_in, ssm_in = jnp.split(proj, [D_ATT_IN, D_ATT_IN + D_GMLP_IN], axis=-1)
    cuts = [D_ATT + i * D_KV for i in range(7)]
    q, kc, vc, ks, vs, kw, vw, g = jnp.split(att_in, cuts, axis=-1)
    kvr = lambda a: a.reshape(Bn, T, ATT_KV_GROUPS, HEAD_DIM)
    gates = jax.nn.sigmoid(g).reshape(Bn, T, ATT_HEADS, N_BRANCH)
    o_att = nsa_attention(q.reshape(Bn, T, ATT_HEADS, HEAD_DIM), kvr(kc), kvr(vc), kvr(ks), kvr(vs),
                          kvr(kw), kvr(vw), gates, rel_bias, cmp_pe, cmp_w1, cmp_b1, cmp_w2)
    o_gmlp = gmlp_spatial_gating(gmlp_in, gmlp_ln_g, gmlp_ln_b, gmlp_ws, gmlp_bs)
    o_ssm = mamba2_ssd(ssm_in, conv_w, conv_b, dt_bias, a_log, d_skip, norm_g)
    return jnp.concatenate([o_att, o_gmlp, o_ssm], axis=-1) @ w_out


def post_norm_residual(x, y, gate, weight, g, b):
    return layer_norm(ALPHA * x + weight * (1.0 + gate) * y, g, b)


def setup_inputs(seed: int = 0) -> dict:
    key = jax.random.key(seed)
    k = jax.random.split(key, 30)
    nrm = lambda kk, shape, s: jax.random.normal(kk, shape, jnp.float32) * s
    L = DEPTH
    dt_init = jnp.exp(jax.random.uniform(k[25], (L, SSM_HEADS), jnp.float32,
                                         math.log(1e-3), math.log(1e-1)))
    return {
        "x": nrm(k[0], (BATCH, SEQ, D_MODEL), 1.0),
        "c": nrm(k[1], (BATCH, D_MODEL), 1.0),
        "rel_bias": nrm(k[2], (REL_BUCKETS, ATT_HEADS), 0.5),
        "ada_w": nrm(k[3], (L, D_MODEL, N_SUB * 3 * D_MODEL), 0.5 * D_MODEL ** -0.5),
        "ada_b": nrm(k[4], (L, N_SUB * 3 * D_MODEL), 0.02),
        "ln_g": 1.0 + nrm(k[5], (L, N_SUB, D_MODEL), 0.02),
        "ln_b": nrm(k[6], (L, N_SUB, D_MODEL), 0.02),
        "ffn_w1": nrm(k[7], (L, 2, D_MODEL, D_FF), BETA * D_MODEL ** -0.5),
        "ffn_w3": nrm(k[8], (L, 2, D_MODEL, D_FF), BETA * D_MODEL ** -0.5),
        "ffn_w2": nrm(k[9], (L, 2, D_FF, D_MODEL), BETA * D_FF ** -0.5),
        "w_in": nrm(k[10], (L, D_MODEL, D_IN), D_MODEL ** -0.5),
        "w_out": nrm(k[11], (L, D_MIX, D_MODEL), BETA * D_MIX ** -0.5),
        "cmp_pe": nrm(k[12], (L, 2, CMP_LEN, HEAD_DIM), 0.1),
        "cmp_w1": nrm(k[13], (L, 2, CMP_LEN * HEAD_DIM, CMP_HIDDEN), (CMP_LEN * HEAD_DIM) ** -0.5),
        "cmp_b1": nrm(k[14], (L, 2, CMP_HIDDEN), 0.02),
        "cmp_w2": nrm(k[15], (L, 2, CMP_HIDDEN, HEAD_DIM), 1.5 * CMP_HIDDEN ** -0.5),
        "gmlp_ln_g": 1.0 + nrm(k[16], (L, D_GMLP), 0.02),
        "gmlp_ln_b": nrm(k[17], (L, D_GMLP), 0.02),
        "gmlp_ws": nrm(k[18], (L, GMLP_GROUPS, GMLP_CHUNK, GMLP_CHUNK), GMLP_CHUNK ** -0.5),
        "gmlp_bs": 1.0 + nrm(k[19], (L, GMLP_GROUPS, GMLP_CHUNK), 0.02),
        "ssm_conv_w": nrm(k[20], (L, SSM_CONV, D_XBC), SSM_CONV ** -0.5),
        "ssm_conv_b": nrm(k[21], (L, D_XBC), 0.02),
        "ssm_dt_bias": dt_init + jnp.log(-jnp.expm1(-dt_init)),
        "ssm_a_log": jnp.log(jax.random.uniform(k[22], (L, SSM_HEADS), jnp.float32, 1.0, 16.0)),
        "ssm_d": 1.0 + nrm(k[23], (L, SSM_HEADS), 0.02),
        "ssm_norm_g": 1.0 + nrm(k[24], (L, D_SSM), 0.02),
    }


def reference(x, c, rel_bias, ada_w, ada_b, ln_g, ln_b, ffn_w1, ffn_w3, ffn_w2, w_in, w_out,
              cmp_pe, cmp_w1, cmp_b1, cmp_w2, gmlp_ln_g, gmlp_ln_b, gmlp_ws, gmlp_bs,
              ssm_conv_w, ssm_conv_b, ssm_dt_bias, ssm_a_log, ssm_d, ssm_norm_g):
    Bn = x.shape[0]
    for l in range(DEPTH):
        mod = (jax.nn.silu(c) @ ada_w[l] + ada_b[l]).reshape(Bn, N_SUB, 3, D_MODEL)
        shift, scale, gate = mod[:, :, 0, None], mod[:, :, 1, None], mod[:, :, 2, None]
        h = x * (1.0 + scale[:, 0]) + shift[:, 0]
        y = swiglu(h, ffn_w1[l, 0], ffn_w3[l, 0], ffn_w2[l, 0])
        x = post_norm_residual(x, y, gate[:, 0], 0.5, ln_g[l, 0], ln_b[l, 0])
        h = x * (1.0 + scale[:, 1]) + shift[:, 1]
        y = hybrid_mixer(h, rel_bias, w_in[l], w_out[l], cmp_pe[l], cmp_w1[l], cmp_b1[l], cmp_w2[l],
                         gmlp_ln_g[l], gmlp_ln_b[l], gmlp_ws[l], gmlp_bs[l],
                         ssm_conv_w[l], ssm_conv_b[l], ssm_dt_bias[l], ssm_a_log[l], ssm_d[l],
                         ssm_norm_g[l])
        x = post_norm_residual(x, y, gate[:, 1], 1.0, ln_g[l, 1], ln_b[l, 1])
        h = x * (1.0 + scale[:, 2]) + shift[:, 2]
        y = swiglu(h, ffn_w1[l, 1], ffn_w3[l, 1], ffn_w2[l, 1])
        x = post_norm_residual(x, y, gate[:, 2], 0.5, ln_g[l, 2], ln_b[l, 2])
    return x
```

```python
import numpy as np
import ml_dtypes
from contextlib import ExitStack, contextmanager
import concourse.bass as bass
import concourse.mybir as mybir
from concourse.bass_utils import run_bass_kernel_spmd

F32 = mybir.dt.float32
BF16 = mybir.dt.bfloat16
AF = mybir.ActivationFunctionType
ALU = mybir.AluOpType
NPBF = ml_dtypes.bfloat16

D = 1024
DEPTH = 2
SEQ = 16384
BATCH = 2
DFF = 2816
NFT = DFF // 128
DIN = 2844
ALPHA = (2 * DEPTH) ** 0.25
LN_EPS = 1e-5
NCORE = 8
TOKC = BATCH * SEQ // NCORE
NTT = TOKC // 128

ENGS = ("pe", "dve", "act", "pool", "sp")


_PSUM_KEYS = ("bank", "ptb", "sbank", "nS", "nO", "nI", "ncmp")


def _is_psum_key(k):
    return k == "nPT" or (isinstance(k, tuple) and k[0] in _PSUM_KEYS)


class Prog:
    def __init__(self, nc, n_dma_slots=8):
        self.nc = nc
        self.stack = ExitStack()
        self.ops = {e: [] for e in ENGS}
        self.sem = {e: self.stack.enter_context(nc.semaphore("s_" + e)) for e in ENGS}
        self.cnt = {e: 0 for e in ENGS}
        self.seen = {e: {} for e in ENGS}
        self.last_w = {}
        self.readers = {}
        self.nslot = n_dma_slots
        self.dma_sems, self.dma_vals, self.dma_next = {}, {}, {}
        self.semobj = dict(self.sem)
        for q in ("sp", "act", "pool"):
            self.dma_sems[q] = [self.stack.enter_context(nc.semaphore(f"d_{q}{i}")) for i in range(n_dma_slots)]
            self.dma_vals[q] = [0] * n_dma_slots
            self.dma_next[q] = 0
            for i, s in enumerate(self.dma_sems[q]):
                self.semobj[(q, i)] = s
        self.n_inst = 0
        self._rr = 0

    def sb(self, name, shape, dtype, stack=None):
        return (stack or self.stack).enter_context(self.nc.sbuf_tensor(name, list(shape), dtype))

    def ps(self, name, shape, dtype=F32, stack=None):
        return (stack or self.stack).enter_context(self.nc.psum_tensor(name, list(shape), dtype))

    def _need(self, eng, tok, waits):
        semkey, val, _ = tok
        if self.seen[eng].get(semkey, 0) >= val:
            return
        self.seen[eng][semkey] = val
        waits.append((semkey, val))

    def _deps(self, eng, engid, reads, writes):
        waits = []
        for r in reads:
            t = self.last_w.get(r)
            if t is not None:
                self._need(eng, t, waits)
        for w in writes:
            t = self.last_w.get(w)
            if t is not None and (t[2] != engid or eng != "pe"):
                self._need(eng, t, waits)
            for sk, (v, eid) in self.readers.get(w, {}).items():
                if eid != engid or eng != "pe":
                    self._need(eng, (sk, v, eid), waits)
        return waits

    def _commit(self, tok, reads, writes):
        for r in reads:
            self.readers.setdefault(r, {})[tok[0]] = (tok[1], tok[2])
        for w in writes:
            self.last_w[w] = tok
            self.readers[w] = {}

    def op(self, eng, fn, reads=(), writes=()):
        ps_reads = [k for k in reads if _is_psum_key(k)]
        if ps_reads:
            writes = list(writes) + [k for k in ps_reads if k not in writes]
        waits = self._deps(eng, eng, reads, writes)
        self.cnt[eng] += 1
        tok = (eng, self.cnt[eng], eng)
        self._commit(tok, reads, writes)
        self.ops[eng].append((waits, fn, self.sem[eng], 1))
        self.n_inst += 1
        return tok

    def dma(self, q, out, in_, reads=(), writes=(), **kw):
        slot = self.dma_next[q]
        self.dma_next[q] = (slot + 1) % self.nslot
        semkey = (q, slot)
        engid = ("dma", q, slot)
        waits = self._deps(q, engid, reads, writes)
        prev = self.dma_vals[q][slot]
        if prev > 0:
            self._need(q, (semkey, prev, engid), waits)
        val = prev + 16
        self.dma_vals[q][slot] = val
        tok = (semkey, val, engid)
        self._commit(tok, reads, writes)
        self.ops[q].append((waits, lambda e: e.dma_start(out=out, in_=in_, **kw), self.dma_sems[q][slot], 16))
        self.n_inst += 1
        return tok

    def barrier(self):
        for e in ENGS:
            waits = []
            for o in ENGS:
                if o != e and self.cnt[o] > 0:
                    self._need(e, (o, self.cnt[o], o), waits)
            for q in self.dma_sems:
                for i in range(self.nslot):
                    v = self.dma_vals[q][i]
                    if v > 0:
                        self._need(e, ((q, i), v, None), waits)
            if waits:
                self.ops[e].append((waits, None, None, 0))

    @contextmanager
    def scope(self):
        st = ExitStack()
        try:
            yield st
        finally:
            self.barrier()
            st.close()

    def finish(self):
        self.barrier()
        engmap = {"pe": "tensor", "dve": "vector", "act": "scalar", "pool": "gpsimd", "sp": "sync"}
        semobj = self.semobj
        with self.nc.Block() as block:
            for e in ENGS:
                ops = self.ops[e]
                if not ops:
                    continue

                def body(engine, ops=ops):
                    for waits, fn, sem, inc in ops:
                        for sk, v in waits:
                            engine.wait_ge(semobj[sk], v)
                        if fn is not None:
                            fn(engine).then_inc(sem, inc)

                getattr(block, engmap[e])(body)
        self.stack.close()


class RowCtx:
    def __init__(self, p, nc, consts):
        self.p, self.nc = p, nc
        self.identb = p.sb("identb", [128, 128], BF16)
        p.dma("pool", self.identb[:], consts["ident"], writes=["identb"])
        self.eps = p.sb("epsc", [128, 1], F32)
        p.op("dve", lambda e: e.memset(self.eps[:], LN_EPS), writes=["epsc"])
        self.bank = [p.ps(f"bank{i}", [128, 512], F32) for i in range(6)]
        self.ptb = [p.ps(f"ptb{i}", [128, 1024], BF16) for i in range(2)]


def emit_mod(p, rc, st, cT, ada_w, ada_b, ncols, dst, posts):
    nck = ncols // 512
    with p.scope() as s2:
        ct = p.sb("mod_ct", [128, 8], F32, s2)
        sc = p.sb("mod_sc", [128, 8], F32, s2)
        scb = p.sb("mod_scb", [128, 8, 128], F32, s2)
        bb = p.sb("mod_bb", [128, ncols], F32, s2)
        wch = [p.sb(f"mod_w{i}", [128, 8, 512], F32, s2) for i in range(2)]
        p.dma("sp", ct[:], cT, writes=["mod_ct"])
        p.dma("act", bb[:], ada_b.partition_broadcast(128), writes=["mod_bb"])
        p.op("act", lambda e: e.activation(out=sc[:], in_=ct[:], func=AF.Silu), reads=["mod_ct"], writes=["mod_sc"])
        for dt in range(8):
            p.op("dve", lambda e, dt=dt: e.tensor_copy(out=scb[:, dt, :], in_=sc[:, dt:dt + 1].to_broadcast([128, 128])),
                 reads=["mod_sc"], writes=[("mod_scb", dt)])
        for ci in range(nck):
            w = wch[ci % 2]
            wk = ("mod_w", ci % 2)
            p.dma("sp" if ci % 2 == 0 else "act", w[:],
                  ada_w[:, ci * 512:(ci + 1) * 512].rearrange("(dt p) n -> p dt n", p=128), writes=[wk])
            bk = rc.bank[ci % 2]
            for dt in range(8):
                p.op("pe", lambda e, dt=dt, w=w, bk=bk: e.matmul(bk[:], lhsT=scb[:, dt, :], rhs=w[:, dt, :],
                                                                  start=(dt == 0), stop=(dt == 7)),
                     reads=[wk, ("mod_scb", dt)], writes=[("bank", ci % 2)])
            p.op("dve", lambda e, ci=ci, bk=bk: e.tensor_tensor(out=dst[:, ci * 512:(ci + 1) * 512], in0=bk[:],
                                                                 in1=bb[:, ci * 512:(ci + 1) * 512], op=ALU.add),
                 reads=[("bank", ci % 2), "mod_bb"], writes=["modtile"])
        for (c0, c1, add, mul) in posts:
            p.op("dve", lambda e, c0=c0, c1=c1, add=add, mul=mul: e.tensor_scalar(
                out=dst[:, c0:c1], in0=dst[:, c0:c1], scalar1=float(add), scalar2=float(mul), op0=ALU.add, op1=ALU.mult),
                reads=["modtile"], writes=["modtile"])


def emit_modulate_T(p, rc, xt, xkey, SC, SH, hf, hb, hT, tag, ptb_i=0):
    p.op("dve", lambda e: e.tensor_tensor(out=hf[:], in0=xt, in1=SC, op=ALU.mult),
         reads=[xkey, "modtile"], writes=[tag + "hf"])
    p.op("pool", lambda e: e.tensor_tensor(out=hb[:], in0=hf[:], in1=SH, op=ALU.add),
         reads=[tag + "hf", "modtile"], writes=[tag + "hb"])
    pt = rc.ptb[ptb_i]
    for dt in range(8):
        p.op("pe", lambda e, dt=dt: e.transpose(out=pt[:, dt * 128:(dt + 1) * 128], in_=hb[:, dt * 128:(dt + 1) * 128],
                                                 identity=rc.identb[:]),
             reads=[tag + "hb", "identb"], writes=[("ptb", ptb_i)])
    p.op("act", lambda e: e.activation(out=hT[:].rearrange("p a b -> p (a b)"), in_=pt[:], func=AF.Copy),
         reads=[("ptb", ptb_i)], writes=[tag + "hT"])


def emit_res_ln(p, rc, ybanks, xres, xkey, GF, lng, lnb, r, stats, mv, rstd, xo, tag):
    for half in range(2):
        sl = slice(half * 512, (half + 1) * 512)
        p.op("dve", lambda e, half=half, sl=sl: e.tensor_tensor(out=r[:, sl], in0=rc.bank[ybanks[half]][:], in1=GF[:, sl],
                                                                op=ALU.mult),
             reads=[("bank", ybanks[half]), "modtile"], writes=[(tag + "r", half)])
        p.op("dve", lambda e, sl=sl: e.scalar_tensor_tensor(out=r[:, sl], in0=xres[:, sl], scalar=float(ALPHA), in1=r[:, sl],
                                                            op0=ALU.mult, op1=ALU.add),
             reads=[(tag + "r", half), xkey], writes=[(tag + "r", half)])
        p.op("dve", lambda e, half=half, sl=sl: e.bn_stats(out=stats[:, half * 6:(half + 1) * 6], in_=r[:, sl]),
             reads=[(tag + "r", half)], writes=[(tag + "st", half)])
    p.op("dve", lambda e: e.bn_aggr(out=mv[:], in_=stats[:]), reads=[(tag + "st", 0), (tag + "st", 1)], writes=[tag + "mv"])
    p.op("act", lambda e: e.activation(out=rstd[:], in_=mv[:, 1:2], func=AF.Sqrt, bias=rc.eps[:], scale=1.0),
         reads=[tag + "mv", "epsc"], writes=[tag + "rstd"])
    p.op("dve", lambda e: e.reciprocal(out=rstd[:], in_=rstd[:]), reads=[tag + "rstd"], writes=[tag + "rstd"])
    p.op("dve", lambda e: e.tensor_scalar(out=r[:], in0=r[:], scalar1=mv[:, 0:1], scalar2=rstd[:, 0:1],
                                          op0=ALU.subtract, op1=ALU.mult),
         reads=[(tag + "r", 0), (tag + "r", 1), tag + "mv", tag + "rstd"], writes=[(tag + "r", 0), (tag + "r", 1)])
    p.op("pool", lambda e: e.tensor_tensor(out=r[:], in0=r[:], in1=lng[:], op=ALU.mult),
         reads=[(tag + "r", 0), (tag + "r", 1), "lngb"], writes=[(tag + "r", 0), (tag + "r", 1)])
    p.op("pool", lambda e: e.tensor_tensor(out=xo[:], in0=r[:], in1=lnb[:], op=ALU.add),
         reads=[(tag + "r", 0), (tag + "r", 1), "lngb"], writes=[tag + "xo"])


def emit_ffn(p, rc, x_in, x_out, w1, w3, w2, SC, SH, GF, lng, lnb):
    with p.scope() as st:
        W1 = p.sb("W1", [128, 8, DFF], BF16, st)
        W3 = p.sb("W3", [128, 8, DFF], BF16, st)
        W2 = p.sb("W2", [128, NFT, D], BF16, st)
        for dt in range(8):
            p.dma("pool", W1[:, dt, :], w1[dt * 128:(dt + 1) * 128, :], writes=["W1"])
            p.dma("pool", W3[:, dt, :], w3[dt * 128:(dt + 1) * 128, :], writes=["W3"])
        for f0 in range(0, NFT, 2):
            p.dma("pool", W2[:, f0:f0 + 2, :], w2[f0 * 128:(f0 + 2) * 128, :].rearrange("(f p) n -> p f n", p=128), writes=["W2"])
        xt = [p.sb(f"f_x{i}", [128, D], F32, st) for i in range(2)]
        hf = p.sb("f_hf", [128, D], F32, st)
        hb = p.sb("f_hb", [128, D], BF16, st)
        hT = p.sb("f_hT", [128, 8, 128], BF16, st)
        aT = p.sb("f_aT", [128, NFT, 128], BF16, st)
        sg = [p.sb(f"f_sg{i}", [128, 128], F32, st) for i in range(2)]
        r = p.sb("f_r", [128, D], F32, st)
        xo = [p.sb(f"f_xo{i}", [128, D], F32, st) for i in range(2)]
        stats = p.sb("f_stats", [128, 12], F32, st)
        mv = p.sb("f_mv", [128, 2], F32, st)
        rstd = p.sb("f_rstd", [128, 1], F32, st)
        for tt in range(NTT):
            x = xt[tt % 2]
            xk = ("f_x", tt % 2)
            p.dma("sp", x[:], x_in[tt * 128:(tt + 1) * 128, :], writes=[xk])
            emit_modulate_T(p, rc, x[:], xk, SC, SH, hf, hb, hT, "f_", ptb_i=tt % 2)
            for ft in range(NFT):
                b = ft % 2
                bk = rc.bank[b]
                for dt in range(8):
                    p.op("pe", lambda e, dt=dt, ft=ft, bk=bk: e.matmul(bk[:, 0:128], lhsT=W1[:, dt, ft * 128:(ft + 1) * 128],
                                                                        rhs=hT[:, dt, :], start=(dt == 0), stop=(dt == 7)),
                         reads=["W1", "f_hT"], writes=[("bank", b)])
                for dt in range(8):
                    p.op("pe", lambda e, dt=dt, ft=ft, bk=bk: e.matmul(bk[:, 128:256], lhsT=W3[:, dt, ft * 128:(ft + 1) * 128],
                                                                        rhs=hT[:, dt, :], start=(dt == 0), stop=(dt == 7)),
                         reads=["W3", "f_hT"], writes=[("bank", b)])
                p.op("act", lambda e, b=b, bk=bk: e.activation(out=sg[b][:], in_=bk[:, 0:128], func=AF.Silu),
                     reads=[("bank", b)], writes=[("f_sg", b)])
                p.op("dve", lambda e, b=b, bk=bk, ft=ft: e.tensor_tensor(out=aT[:, ft, :], in0=sg[b][:], in1=bk[:, 128:256],
                                                                          op=ALU.mult),
                     reads=[("bank", b), ("f_sg", b)], writes=[("f_aT", ft)])
            for half in range(2):
                bk = rc.bank[2 + half]
                for ft in range(NFT):
                    p.op("pe", lambda e, ft=ft, bk=bk, half=half: e.matmul(bk[:], lhsT=aT[:, ft, :],
                                                                            rhs=W2[:, ft, half * 512:(half + 1) * 512],
                                                                            start=(ft == 0), stop=(ft == NFT - 1)),
                         reads=[("f_aT", ft), "W2"], writes=[("bank", 2 + half)])
            o = xo[tt % 2]
            emit_res_ln(p, rc, (2, 3), x, xk, GF, lng, lnb, r, stats, mv, rstd, o, "f_")
            p.dma("act", x_out[tt * 128:(tt + 1) * 128, :], o[:], reads=["f_xo"], writes=[])


def build_A():
    nc = bass.Bass("TRN2", target_bir_lowering=False)
    di = lambda n, s, d=F32: nc.dram_tensor(n, list(s), d, kind="ExternalInput").ap()
    do = lambda n, s, d=F32: nc.dram_tensor(n, list(s), d, kind="ExternalOutput").ap()
    x_in = di("x", [TOKC, D])
    cT = di("cT", [128, 8])
    ada_w = di("ada_w", [D, 5120])
    ada_b = di("ada_b", [5120])
    lng_d, lnb_d = di("lng", [D]), di("lnb", [D])
    w1, w3, w2 = di("w1", [D, DFF]), di("w3", [D, DFF]), di("w2", [DFF, D])
    w_in = di("w_in", [D, DIN])
    glng_d, glnb_d = di("glng", [256]), di("glnb", [256])
    wsT_d = di("wsT", [4, 128, 128])
    bsT_d = di("bsT", [128, 4])
    consts = {"ident": di("ident", [128, 128]), "tri": di("tri", [128, 128])}
    x1_d = do("x1", [TOKC, D])
    QT_d = do("QT", [512, TOKC], BF16)
    KcT_d, VcT_d = do("KcT", [128, TOKC], BF16), do("VcT", [128, TOKC], BF16)
    KsT_d, KwT_d = do("KsT", [128, TOKC], BF16), do("KwT", [128, TOKC], BF16)
    Vs_d, Vw_d = do("Vs", [TOKC, 128], BF16), do("Vw", [TOKC, 128], BF16)
    tmf_d = do("tmf", [TOKC, 1052])
    ogm_d = do("ogm", [TOKC, 256], BF16)

    p = Prog(nc)
    rc = RowCtx(p, nc, consts)
    MOD = p.sb("MOD", [128, 5120], F32)
    lng = p.sb("lng_t", [128, D], F32)
    lnb = p.sb("lnb_t", [128, D], F32)
    p.dma("sp", lng[:], lng_d.partition_broadcast(128), writes=["lngb"])
    p.dma("sp", lnb[:], lnb_d.partition_broadcast(128), writes=["lngb"])
    emit_mod(p, rc, None, cT, ada_w, ada_b, 5120, MOD,
             [(1024, 2048, 1.0, 1.0), (2048, 3072, 1.0, 0.5), (4096, 5120, 1.0, 1.0)])
    import os
    STOP = os.environ.get("KSTOP", "")
    if STOP == "mod":
        p.finish(); return nc
    emit_ffn(p, rc, x_in, x1_d, w1, w3, w2, MOD[:, 1024:2048], MOD[:, 0:1024], MOD[:, 2048:3072], lng, lnb)
    if STOP == "ffn":
        p.finish(); return nc

    with p.scope() as st:
        WIN = p.sb("WIN", [128, 8, DIN], BF16, st)
        for dt in range(8):
            p.dma("pool", WIN[:, dt, :], w_in[dt * 128:(dt + 1) * 128, :], writes=["WIN"])
        tri = p.sb("tri_t", [128, 128], F32, st)
        p.dma("sp", tri[:], consts["tri"], writes=["tri_t"])
        wsf = p.sb("wsf", [128, 4, 128], F32, st)
        p.dma("sp", wsf[:], wsT_d.rearrange("g s t -> s g t"), writes=["wsf"])
        WS = p.sb("WS", [128, 4, 128], BF16, st)
        for g in range(4):
            p.op("dve", lambda e, g=g: e.tensor_tensor(out=WS[:, g, :], in0=wsf[:, g, :], in1=tri[:], op=ALU.mult),
                 reads=["wsf", "tri_t"], writes=["WS"])
        BS = p.sb("BS", [128, 4], F32, st)
        p.dma("sp", BS[:], bsT_d, writes=["BS"])
        glng = p.sb("glng_t", [128, 256], F32, st)
        glnb = p.sb("glnb_t", [128, 256], F32, st)
        p.dma("sp", glng[:], glng_d.partition_broadcast(128), writes=["glngb"])
        p.dma("sp", glnb[:], glnb_d.partition_broadcast(128), writes=["glngb"])
        xt = [p.sb(f"a_x{i}", [128, D], F32, st) for i in range(2)]
        hf = p.sb("a_hf", [128, D], F32, st)
        hb = p.sb("a_hb", [128, D], BF16, st)
        hT = p.sb("a_hT", [128, 8, 128], BF16, st)
        fm = [p.sb(f"a_fm{i}", [128, 8, 128], BF16, st) for i in range(2)]
        vsw = [p.sb(f"a_vsw{i}", [128, 256], BF16, st) for i in range(2)]
        gu = p.sb("a_gu", [128, 512], F32, st)
        vn = p.sb("a_vn", [128, 256], F32, st)
        vnb = p.sb("a_vnb", [128, 256], BF16, st)
        gst = p.sb("a_gst", [128, 6], F32, st)
        gmv = p.sb("a_gmv", [128, 2], F32, st)
        grs = p.sb("a_grs", [128, 1], F32, st)
        ogm = [p.sb(f"a_ogm{i}", [128, 256], BF16, st) for i in range(2)]
        ssm = [p.sb(f"a_ssm{i}", [128, 1052], F32, st) for i in range(2)]
        SC1, SH1 = MOD[:, 4096:5120], MOD[:, 3072:4096]
        fm_cols = [0, 128, 256, 384, 512, 640, 768, 1024]
        for tt in range(NTT):
            i2 = tt % 2
            tok = slice(tt * 128, (tt + 1) * 128)
            x = xt[i2]
            xk = ("a_x", i2)
            p.dma("sp", x[:], x1_d[tok, :], reads=[], writes=[xk])
            emit_modulate_T(p, rc, x[:], xk, SC1, SH1, hf, hb, hT, "a_", ptb_i=0)
            for ci, c0 in enumerate(fm_cols):
                b = ci // 4
                bk = rc.bank[b]
                for dt in range(8):
                    p.op("pe", lambda e, dt=dt, c0=c0, bk=bk, ci=ci: e.matmul(
                        bk[:, (ci % 4) * 128:(ci % 4 + 1) * 128], lhsT=WIN[:, dt, c0:c0 + 128], rhs=hT[:, dt, :],
                        start=(dt == 0), stop=(dt == 7)), reads=["WIN", "a_hT"], writes=[("bank", b)])
            f = fm[i2]
            p.op("act", lambda e, f=f: e.activation(out=f[:, 0:4, :].rearrange("p a b -> p (a b)"), in_=rc.bank[0][:],
                                                    func=AF.Identity, scale=0.125),
                 reads=[("bank", 0)], writes=[("a_fm", i2)])
            p.op("dve", lambda e, f=f: e.tensor_copy(out=f[:, 4:8, :].rearrange("p a b -> p (a b)"), in_=rc.bank[1][:]),
                 reads=[("bank", 1)], writes=[("a_fm", i2)])
            p.dma("act", QT_d[:, tok].rearrange("(c p) t -> p c t", p=128), f[:, 0:4, :], reads=[("a_fm", i2)])
            for k, dd in enumerate((KcT_d, VcT_d, KsT_d, KwT_d)):
                p.dma("act", dd[:, tok], f[:, 4 + k, :], reads=[("a_fm", i2)])
            def tm(bank_i, col_off, c0, c1):
                bk = rc.bank[bank_i]
                for dt in range(8):
                    p.op("pe", lambda e, dt=dt: e.matmul(bk[:, col_off:col_off + (c1 - c0)], lhsT=hT[:, dt, :],
                                                         rhs=WIN[:, dt, c0:c1], start=(dt == 0), stop=(dt == 7)),
                         reads=["WIN", "a_hT"], writes=[("bank", bank_i)])
            tm(2, 0, 896, 1024)
            tm(2, 128, 1152, 1304)
            tm(2, 280, 2716, 2844)
            tm(3, 0, 1304, 1816)
            tm(4, 0, 1816, 2328)
            tm(5, 0, 2328, 2840)
            vv = vsw[i2]
            p.op("dve", lambda e, vv=vv: e.tensor_copy(out=vv[:], in_=rc.bank[2][:, 0:256]),
                 reads=[("bank", 2)], writes=[("a_vsw", i2)])
            p.dma("act", Vs_d[tok, :], vv[:, 0:128], reads=[("a_vsw", i2)])
            p.dma("act", Vw_d[tok, :], vv[:, 128:256], reads=[("a_vsw", i2)])
            sm = ssm[i2]
            p.op("act", lambda e, sm=sm: e.activation(out=sm[:, 1028:1052], in_=rc.bank[2][:, 256:280], func=AF.Sigmoid),
                 reads=[("bank", 2)], writes=[("a_ssm", i2)])
            p.op("dve", lambda e, sm=sm: e.tensor_copy(out=sm[:, 1024:1028], in_=rc.bank[2][:, 280 + 124:280 + 128]),
                 reads=[("bank", 2)], writes=[("a_ssm", i2)])
            p.op("act", lambda e, sm=sm: e.activation(out=sm[:, 0:512], in_=rc.bank[4][:], func=AF.Copy),
                 reads=[("bank", 4)], writes=[("a_ssm", i2)])
            p.op("dve", lambda e, sm=sm: e.tensor_copy(out=sm[:, 512:1024], in_=rc.bank[5][:]),
                 reads=[("bank", 5)], writes=[("a_ssm", i2)])
            p.dma("sp", tmf_d[tok, :], sm[:], reads=[("a_ssm", i2)])
            p.op("act", lambda e: e.activation(out=gu[:], in_=rc.bank[3][:], func=AF.Gelu_apprx_tanh),
                 reads=[("bank", 3)], writes=["a_gu"])
            p.op("dve", lambda e: e.bn_stats(out=gst[:], in_=gu[:, 256:512]), reads=["a_gu"], writes=["a_gst"])
            p.op("dve", lambda e: e.bn_aggr(out=gmv[:], in_=gst[:]), reads=["a_gst"], writes=["a_gmv"])
            p.op("act", lambda e: e.activation(out=grs[:], in_=gmv[:, 1:2], func=AF.Sqrt, bias=rc.eps[:], scale=1.0),
                 reads=["a_gmv", "epsc"], writes=["a_grs"])
            p.op("dve", lambda e: e.reciprocal(out=grs[:], in_=grs[:]), reads=["a_grs"], writes=["a_grs"])
            p.op("dve", lambda e: e.tensor_scalar(out=vn[:], in0=gu[:, 256:512], scalar1=gmv[:, 0:1], scalar2=grs[:, 0:1],
                                                  op0=ALU.subtract, op1=ALU.mult),
                 reads=["a_gu", "a_gmv", "a_grs"], writes=["a_vn"])
            p.op("pool", lambda e: e.tensor_tensor(out=vn[:], in0=vn[:], in1=glng[:], op=ALU.mult),
                 reads=["a_vn", "glngb"], writes=["a_vn"])
            p.op("pool", lambda e: e.tensor_tensor(out=vnb[:], in0=vn[:], in1=glnb[:], op=ALU.add),
                 reads=["a_vn", "glngb"], writes=["a_vnb"])
            for g in range(4):
                p.op("pe", lambda e, g=g: e.matmul(rc.bank[3][:, g * 64:(g + 1) * 64], lhsT=WS[:, g, :],
                                                   rhs=vnb[:, g * 64:(g + 1) * 64], start=True, stop=True),
                     reads=["WS", "a_vnb"], writes=[("bank", 3)])
            og = ogm[i2]
            for g in range(4):
                p.op("dve", lambda e, g=g, og=og: e.scalar_tensor_tensor(
                    out=og[:, g * 64:(g + 1) * 64], in0=rc.bank[3][:, g * 64:(g + 1) * 64], scalar=BS[:, g:g + 1],
                    in1=gu[:, g * 64:(g + 1) * 64], op0=ALU.add, op1=ALU.mult),
                    reads=[("bank", 3), "a_gu", "BS"], writes=[("a_ogm", i2)])
            p.dma("act", ogm_d[tok, :], og[:], reads=[("a_ogm", i2)])
    p.finish()
    return nc


def build_C():
    nc = bass.Bass("TRN2", target_bir_lowering=False)
    di = lambda n, s, d=F32: nc.dram_tensor(n, list(s), d, kind="ExternalInput").ap()
    do = lambda n, s, d=F32: nc.dram_tensor(n, list(s), d, kind="ExternalOutput").ap()
    x1_d = di("x1", [TOKC, D])
    cT = di("cT", [128, 8])
    ada_w = di("ada_w", [D, 4096])
    ada_b = di("ada_b", [4096])
    lng1_d, lnb1_d = di("lng1", [D]), di("lnb1", [D])
    lng2_d, lnb2_d = di("lng2", [D]), di("lnb2", [D])
    w1, w3, w2 = di("w1", [D, DFF]), di("w3", [D, DFF]), di("w2", [DFF, D])
    w_out = di("w_out", [D, D])
    oatt_d = di("oatt", [TOKC, 512])
    ogm_d = di("ogm", [TOKC, 256], BF16)
    yssm_d = di("yssm", [TOKC, 256])
    zz_d = di("zz", [TOKC, 256])
    ng_d = di("normg", [256])
    consts = {"ident": di("ident", [128, 128])}
    x2_d = do("x2", [TOKC, D])
    x3_d = do("x3", [TOKC, D])

    p = Prog(nc)
    rc = RowCtx(p, nc, consts)
    MOD = p.sb("MOD", [128, 4096], F32)
    lng = p.sb("lng_t", [128, D], F32)
    lnb = p.sb("lnb_t", [128, D], F32)
    emit_mod(p, rc, None, cT, ada_w, ada_b, 4096, MOD,
             [(0, 1024, 1.0, 1.0), (2048, 3072, 1.0, 1.0), (3072, 4096, 1.0, 0.5)])
    p.dma("sp", lng[:], lng1_d.partition_broadcast(128), writes=["lngb"])
    p.dma("sp", lnb[:], lnb1_d.partition_broadcast(128), writes=["lngb"])
    with p.scope() as st:
        WO = p.sb("WO", [128, 8, D], BF16, st)
        for dt in range(8):
            p.dma("pool", WO[:, dt, :], w_out[dt * 128:(dt + 1) * 128, :], writes=["WO"])
        ng = p.sb("ng_t", [128, 256], F32, st)
        p.dma("sp", ng[:], ng_d.partition_broadcast(128), writes=["ng_t"])
        xt = [p.sb(f"c_x{i}", [128, D], F32, st) for i in range(2)]
        oa = [p.sb(f"c_oa{i}", [128, 512], F32, st) for i in range(2)]
        ys = [p.sb(f"c_ys{i}", [128, 512], F32, st) for i in range(2)]
        om = p.sb("c_om", [128, D], BF16, st)
        omT = p.sb("c_omT", [128, 8, 128], BF16, st)
        sz = p.sb("c_sz", [128, 256], F32, st)
        gg = p.sb("c_gg", [128, 256], F32, st)
        junk = p.sb("c_junk", [128, 128], F32, st)
        ss = p.sb("c_ss", [128, 2], F32, st)
        r = p.sb("c_r", [128, D], F32, st)
        xo = [p.sb(f"c_xo{i}", [128, D], F32, st) for i in range(2)]
        stats = p.sb("c_stats", [128, 12], F32, st)
        mv = p.sb("c_mv", [128, 2], F32, st)
        rstd = p.sb("c_rstd", [128, 1], F32, st)
        for tt in range(NTT):
            i2 = tt % 2
            tok = slice(tt * 128, (tt + 1) * 128)
            x, xk = xt[i2], ("c_x", i2)
            p.dma("sp", x[:], x1_d[tok, :], writes=[xk])
            p.dma("sp", oa[i2][:], oatt_d[tok, :], writes=[("c_oa", i2)])
            p.dma("pool", om[:, 512:768], ogm_d[tok, :], writes=["c_om_g"])
            p.dma("act", ys[i2][:, 0:256], yssm_d[tok, :], writes=[("c_ys", i2)])
            p.dma("act", ys[i2][:, 256:512], zz_d[tok, :], writes=[("c_ys", i2)])
            p.op("pool", lambda e, i2=i2: e.tensor_copy(out=om[:, 0:512], in_=oa[i2][:]), reads=[("c_oa", i2)], writes=["c_om_a"])
            p.op("act", lambda e, i2=i2: e.activation(out=sz[:], in_=ys[i2][:, 256:512], func=AF.Silu),
                 reads=[("c_ys", i2)], writes=["c_sz"])
            p.op("dve", lambda e, i2=i2: e.tensor_tensor(out=gg[:], in0=ys[i2][:, 0:256], in1=sz[:], op=ALU.mult),
                 reads=[("c_ys", i2), "c_sz"], writes=["c_gg"])
            p.op("dve", lambda e: e.tensor_tensor(out=sz[:], in0=gg[:], in1=gg[:], op=ALU.mult), reads=["c_gg"], writes=["c_sz"])
            for k in range(2):
                p.op("dve", lambda e, k=k: e.reduce_sum(out=ss[:, k:k + 1], in_=sz[:, k * 128:(k + 1) * 128],
                                                        axis=mybir.AxisListType.X), reads=["c_sz"], writes=[("c_ss", k)])
            p.op("act", lambda e: e.activation(out=ss[:], in_=ss[:], func=AF.Sqrt, bias=rc.eps[:], scale=1.0 / 128.0),
                 reads=[("c_ss", 0), ("c_ss", 1), "epsc"], writes=[("c_ss", 0), ("c_ss", 1)])
            p.op("dve", lambda e: e.reciprocal(out=ss[:], in_=ss[:]), reads=[("c_ss", 0), ("c_ss", 1)],
                 writes=[("c_ss", 0), ("c_ss", 1)])
            for k in range(2):
                p.op("dve", lambda e, k=k: e.scalar_tensor_tensor(out=om[:, 768 + k * 128:768 + (k + 1) * 128],
                                                                  in0=gg[:, k * 128:(k + 1) * 128], scalar=ss[:, k:k + 1],
                                                                  in1=ng[:, k * 128:(k + 1) * 128], op0=ALU.mult, op1=ALU.mult),
                     reads=["c_gg", ("c_ss", 0), ("c_ss", 1), "ng_t"], writes=[("c_om_s", k)])
            pt = rc.ptb[i2]
            for dt in range(8):
                p.op("pe", lambda e, dt=dt, pt=pt: e.transpose(out=pt[:, dt * 128:(dt + 1) * 128],
                                                               in_=om[:, dt * 128:(dt + 1) * 128], identity=rc.identb[:]),
                     reads=["c_om_a", "c_om_g", ("c_om_s", 0), ("c_om_s", 1), "identb"], writes=[("ptb", i2)])
            p.op("act", lambda e, pt=pt: e.activation(out=omT[:].rearrange("p a b -> p (a b)"), in_=pt[:], func=AF.Copy),
                 reads=[("ptb", i2)], writes=["c_omT"])
            for half in range(2):
                bk = rc.bank[2 + half]
                for dt in range(8):
                    p.op("pe", lambda e, dt=dt, bk=bk, half=half: e.matmul(bk[:], lhsT=omT[:, dt, :],
                                                                            rhs=WO[:, dt, half * 512:(half + 1) * 512],
                                                                            start=(dt == 0), stop=(dt == 7)),
                         reads=["c_omT", "WO"], writes=[("bank", 2 + half)])
            o = xo[i2]
            emit_res_ln(p, rc, (2, 3), x, xk, MOD[:, 0:1024], lng, lnb, r, stats, mv, rstd, o, "c_")
            p.dma("act", x2_d[tok, :], o[:], reads=["c_xo"])
    p.dma("sp", lng[:], lng2_d.partition_broadcast(128), writes=["lngb"])
    p.dma("sp", lnb[:], lnb2_d.partition_broadcast(128), writes=["lngb"])
    emit_ffn(p, rc, x2_d, x3_d, w1, w3, w2, MOD[:, 2048:3072], MOD[:, 1024:2048], MOD[:, 3072:4096], lng, lnb)
    p.finish()
    return nc


NCH = SEQ // 256


def build_S():
    nc = bass.Bass("TRN2", target_bir_lowering=False)
    di = lambda n, s, d=F32: nc.dram_tensor(n, list(s), d, kind="ExternalInput").ap()
    do = lambda n, s, d=F32: nc.dram_tensor(n, list(s), d, kind="ExternalOutput").ap()
    xc4_d = di("xc4", [SEQ, 4, 320])
    cw_d = di("cw", [4 * 320])
    cb_d = di("cb", [320])
    dtr_d = di("dtr", [SEQ])
    sc_d = di("scal", [4])
    tri_d = di("tri", [128, 128])
    triw_d = di("triw", [2, 128, 256])
    ntriw_d = di("ntriw", [2, 128, 256])
    ident_d = di("identf", [128, 128])
    y_d = do("y", [SEQ, 64])
    p = Prog(nc)
    bank = [p.ps(f"sbank{i}", [128, 512], F32) for i in range(5)]
    CW = p.sb("CW", [128, 4, 320], F32)
    CB = p.sb("CB", [128, 320], F32)
    DTR = p.sb("DTR", [128, SEQ // 128], F32)
    SCL = p.sb("SCL", [128, 4], F32)
    TRI = p.sb("TRI", [128, 128], F32)
    ONES = p.sb("ONES", [128, 128], F32)
    TRIW = p.sb("TRIW", [128, 2, 256], F32)
    NTRIW = p.sb("NTRIW", [128, 2, 256], F32)
    IDF = p.sb("IDF", [128, 128], F32)
    ANEG = p.sb("ANEG", [128, 1], F32)
    ONE1 = p.sb("ONE1", [128, 1], F32)
    S = p.sb("Sst", [128, 64], F32)
    p.dma("sp", CW[:].rearrange("p k n -> p (k n)"), cw_d.partition_broadcast(128), writes=["CW"])
    p.dma("sp", CB[:], cb_d.partition_broadcast(128), writes=["CB"])
    p.dma("sp", DTR[:], dtr_d.rearrange("(n p) -> p n", p=128), writes=["DTR"], allow_slow_non_contiguous=True)
    p.dma("sp", SCL[:], sc_d.partition_broadcast(128), writes=["SCL"])
    p.dma("act", TRI[:], tri_d, writes=["TRI"])
    p.dma("act", TRIW[:], triw_d.rearrange("i p n -> p i n"), writes=["TRIW"])
    p.dma("act", NTRIW[:], ntriw_d.rearrange("i p n -> p i n"), writes=["NTRIW"])
    p.dma("act", IDF[:], ident_d, writes=["IDF"])
    p.op("dve", lambda e: e.memset(ONES[:], 1.0), writes=["ONES"])
    p.op("dve", lambda e: e.memset(ONE1[:], 1.0), writes=["ONE1"])
    p.op("dve", lambda e: e.memset(S[:], 0.0), writes=["S"])
    p.op("act", lambda e: e.activation(out=ANEG[:], in_=SCL[:, 1:2], func=AF.Exp), reads=["SCL"], writes=["ANEG"])
    p.op("dve", lambda e: e.tensor_scalar(out=ANEG[:], in0=ANEG[:], scalar1=-1.0, scalar2=None, op0=ALU.mult),
         reads=["ANEG"], writes=["ANEG"])
    X4 = [p.sb(f"s_x4{i}", [128, 2, 4, 320], F32) for i in range(2)]
    acc = p.sb("s_acc", [128, 2, 320], F32)
    tmp = p.sb("s_tmp", [128, 2, 320], F32)
    XA = p.sb("s_XA", [128, 2, 320], F32)
    dx = p.sb("s_dx", [128, 2], F32)
    dax = p.sb("s_dax", [128, 2], F32)
    dl = p.sb("s_dl", [128, 2], F32)
    dtt = p.sb("s_dt", [128, 2], F32)
    aa = p.sb("s_a", [128, 2], F32)
    abc = p.sb("s_abc", [128, 2, 128], F32)
    cscol = p.sb("s_cscol", [128, 2], F32)
    cl = p.sb("s_cl", [128, 1], F32)
    ecl = p.sb("s_ecl", [128, 1], F32)
    dec = p.sb("s_dec", [128, 2], F32)
    lt = p.sb("s_lt", [128, 2, 256], F32)
    ecs = p.sb("s_ecs", [128, 256], F32)
    BCT = p.sb("s_BCT", [128, 512], F32)
    scT = p.sb("s_scT", [128, 2, 256], F32)
    CsT = p.sb("s_CsT", [128, 256], F32)
    Xd = p.sb("s_Xd", [128, 2, 64], F32)
    Xdd = p.sb("s_Xdd", [128, 2, 64], F32)
    yo = [p.sb(f"s_yo{i}", [128, 2, 64], F32) for i in range(2)]
    for c in range(NCH):
        i2 = c % 2
        x4 = X4[i2]
        p.dma("sp" if i2 == 0 else "act", x4[:].rearrange("p j k n -> p j (k n)"),
              xc4_d[c * 256:(c + 1) * 256].rearrange("(j p) k n -> p j (k n)", p=128), writes=[("s_x4", i2)])
        for j in range(2):
            eng = "dve" if j == 0 else "pool"
            p.op(eng, lambda e, j=j, x4=x4: e.tensor_tensor(out=acc[:, j, :], in0=x4[:, j, 0, :], in1=CW[:, 0, :], op=ALU.mult),
                 reads=[("s_x4", i2), "CW"], writes=[("s_acc", j)])
            for k in range(1, 4):
                p.op(eng, lambda e, j=j, k=k, x4=x4: e.tensor_tensor(out=tmp[:, j, :], in0=x4[:, j, k, :], in1=CW[:, k, :], op=ALU.mult),
                     reads=[("s_x4", i2), "CW"], writes=[("s_tmp", j)])
                p.op(eng, lambda e, j=j: e.tensor_tensor(out=acc[:, j, :], in0=acc[:, j, :], in1=tmp[:, j, :], op=ALU.add),
                     reads=[("s_acc", j), ("s_tmp", j)], writes=[("s_acc", j)])
            p.op(eng, lambda e, j=j: e.tensor_tensor(out=acc[:, j, :], in0=acc[:, j, :], in1=CB[:], op=ALU.add),
                 reads=[("s_acc", j), "CB"], writes=[("s_acc", j)])
        p.op("act", lambda e: e.activation(out=XA[:].rearrange("p j n -> p (j n)"), in_=acc[:].rearrange("p j n -> p (j n)"),
                                           func=AF.Silu), reads=[("s_acc", 0), ("s_acc", 1)], writes=["s_XA"])
        p.op("dve", lambda e, c=c: e.tensor_scalar(out=dx[:], in0=DTR[:, 2 * c:2 * c + 2], scalar1=SCL[:, 0:1], scalar2=None,
                                                    op0=ALU.add), reads=["DTR", "SCL"], writes=["s_dx"])
        p.op("act", lambda e: e.activation(out=dax[:], in_=dx[:], func=AF.Abs), reads=["s_dx"], writes=["s_dax"])
        p.op("act", lambda e: e.activation(out=dl[:], in_=dax[:], func=AF.Exp, scale=-1.0), reads=["s_dax"], writes=["s_dl"])
        p.op("act", lambda e: e.activation(out=dl[:], in_=dl[:], func=AF.Ln, bias=ONE1[:], scale=1.0),
             reads=["s_dl", "ONE1"], writes=["s_dl"])
        p.op("dve", lambda e: e.scalar_tensor_tensor(out=dtt[:], in0=dx[:], scalar=0.0, in1=dl[:], op0=ALU.max, op1=ALU.add),
             reads=["s_dx", "s_dl"], writes=["s_dt"])
        p.op("dve", lambda e: e.tensor_scalar(out=aa[:], in0=dtt[:], scalar1=ANEG[:, 0:1], scalar2=None, op0=ALU.mult),
             reads=["s_dt", "ANEG"], writes=["s_a"])
        for i in range(2):
            p.op("dve", lambda e, i=i: e.tensor_scalar(out=abc[:, i, :], in0=ONES[:], scalar1=aa[:, i:i + 1], scalar2=None,
                                                       op0=ALU.mult), reads=["s_a", "ONES"], writes=[("s_abc", i)])
        for i in range(2):
            p.op("pe", lambda e, i=i: e.matmul(bank[0][:, 0:256], lhsT=abc[:, i, :], rhs=TRIW[:, i, :], start=(i == 0), stop=(i == 1)),
                 reads=[("s_abc", i), "TRIW"], writes=[("sbank", 0)])
        p.op("pe", lambda e: e.matmul(bank[0][:, 256:257], lhsT=TRI[:], rhs=aa[:, 0:1], start=True, stop=True),
             reads=["TRI", "s_a"], writes=[("sbank", 0)])
        p.op("pe", lambda e: e.matmul(bank[0][:, 257:258], lhsT=ONES[:], rhs=aa[:, 0:1], start=True, stop=False),
             reads=["ONES", "s_a"], writes=[("sbank", 0)])
        p.op("pe", lambda e: e.matmul(bank[0][:, 257:258], lhsT=TRI[:], rhs=aa[:, 1:2], start=False, stop=True),
             reads=["TRI", "s_a"], writes=[("sbank", 0)])
        p.op("dve", lambda e: e.tensor_copy(out=cscol[:], in_=bank[0][:, 256:258]), reads=[("sbank", 0)], writes=["s_cscol"])
        p.op("dve", lambda e: e.tensor_copy(out=cl[:], in_=bank[0][:, 255:256]), reads=[("sbank", 0)], writes=["s_cl"])
        for i in range(2):
            p.op("dve", lambda e, i=i: e.scalar_tensor_tensor(out=lt[:, i, :], in0=bank[0][:, 0:256], scalar=cscol[:, i:i + 1],
                                                              in1=NTRIW[:, i, :], op0=ALU.subtract, op1=ALU.add),
                 reads=[("sbank", 0), "s_cscol", "NTRIW"], writes=[("s_lt", i)])
            p.op("act", lambda e, i=i: e.activation(out=lt[:, i, :], in_=lt[:, i, :], func=AF.Exp),
                 reads=[("s_lt", i)], writes=[("s_lt", i)])
        p.op("act", lambda e: e.activation(out=ecs[:], in_=bank[0][:, 0:256], func=AF.Exp), reads=[("sbank", 0)], writes=["s_ecs"])
        for j in range(2):
            p.op("pe", lambda e, j=j: e.transpose(out=bank[1][:, j * 128:(j + 1) * 128], in_=XA[:, j, 64:192], identity=IDF[:]),
                 reads=["s_XA", "IDF"], writes=[("sbank", 1)])
            p.op("pe", lambda e, j=j: e.transpose(out=bank[1][:, 256 + j * 128:256 + (j + 1) * 128], in_=XA[:, j, 192:320],
                                                  identity=IDF[:]), reads=["s_XA", "IDF"], writes=[("sbank", 1)])
        p.op("act", lambda e: e.activation(out=BCT[:], in_=bank[1][:], func=AF.Copy), reads=[("sbank", 1)], writes=["s_BCT"])
        for i in range(2):
            p.op("pe", lambda e, i=i: e.matmul(bank[2][:, i * 256:(i + 1) * 256], lhsT=BCT[:, i * 128:(i + 1) * 128],
                                               rhs=BCT[:, 256:512], start=True, stop=True),
                 reads=["s_BCT"], writes=[("sbank", 2)])
        for i in range(2):
            p.op("dve", lambda e, i=i: e.tensor_tensor(out=scT[:, i, :], in0=bank[2][:, i * 256:(i + 1) * 256], in1=lt[:, i, :],
                                                       op=ALU.mult), reads=[("sbank", 2), ("s_lt", i)], writes=[("s_scT", i)])
        p.op("pool", lambda e: e.tensor_tensor(out=CsT[:], in0=BCT[:, 256:512], in1=ecs[:], op=ALU.mult),
             reads=["s_BCT", "s_ecs"], writes=["s_CsT"])
        for i in range(2):
            p.op("dve", lambda e, i=i: e.tensor_scalar(out=Xd[:, i, :], in0=XA[:, i, 0:64], scalar1=dtt[:, i:i + 1], scalar2=None,
                                                       op0=ALU.mult), reads=["s_XA", "s_dt"], writes=[("s_Xd", i)])
        for j in range(2):
            ops = [(scT[:, i, j * 128:(j + 1) * 128], Xd[:, i, :], [("s_scT", i), ("s_Xd", i)]) for i in range(j + 1)]
            ops.append((CsT[:, j * 128:(j + 1) * 128], S[:], ["s_CsT", "S"]))
            for n, (lh, rh, rd) in enumerate(ops):
                p.op("pe", lambda e, lh=lh, rh=rh, n=n, j=j, last=(n == len(ops) - 1): e.matmul(
                    bank[3][:, j * 64:(j + 1) * 64], lhsT=lh, rhs=rh, start=(n == 0), stop=last),
                    reads=rd, writes=[("sbank", 3)])
        y2 = yo[i2]
        for j in range(2):
            p.op("dve", lambda e, j=j, y2=y2: e.scalar_tensor_tensor(out=y2[:, j, :], in0=XA[:, j, 0:64], scalar=SCL[:, 2:3],
                                                                      in1=bank[3][:, j * 64:(j + 1) * 64], op0=ALU.mult, op1=ALU.add),
                 reads=["s_XA", "SCL", ("sbank", 3)], writes=[("s_yo", i2)])
        p.dma("act", y_d[c * 256:(c + 1) * 256, :].rearrange("(j p) n -> p j n", p=128), y2[:], reads=[("s_yo", i2)])
        p.op("act", lambda e: e.activation(out=dec[:], in_=cscol[:], func=AF.Exp, bias=cl[:], scale=-1.0),
             reads=["s_cscol", "s_cl"], writes=["s_dec"])
        for i in range(2):
            p.op("dve", lambda e, i=i: e.tensor_scalar(out=Xdd[:, i, :], in0=Xd[:, i, :], scalar1=dec[:, i:i + 1], scalar2=None,
                                                       op0=ALU.mult), reads=[("s_Xd", i), "s_dec"], writes=[("s_Xdd", i)])
        for i in range(2):
            p.op("pe", lambda e, i=i: e.matmul(bank[4][:, 0:64], lhsT=XA[:, i, 64:192], rhs=Xdd[:, i, :], start=(i == 0), stop=(i == 1)),
                 reads=["s_XA", ("s_Xdd", i)], writes=[("sbank", 4)])
        p.op("act", lambda e: e.activation(out=ecl[:], in_=cl[:], func=AF.Exp), reads=["s_cl"], writes=["s_ecl"])
        p.op("dve", lambda e: e.scalar_tensor_tensor(out=S[:], in0=S[:], scalar=ecl[:, 0:1], in1=bank[4][:, 0:64],
                                                     op0=ALU.mult, op1=ALU.add), reads=["S", "s_ecl", ("sbank", 4)], writes=["S"])
    p.finish()
    return nc


NQI = SEQ // 256
NQF = SEQ // 256


def build_N():
    nc = bass.Bass("TRN2", target_bir_lowering=False)
    di = lambda n, s, d=F32: nc.dram_tensor(n, list(s), d, kind="ExternalInput").ap()
    do = lambda n, s, d=F32: nc.dram_tensor(n, list(s), d, kind="ExternalOutput").ap()
    QT_d = di("QT", [64, 4, NQF, 128], BF16)
    blk_d = di("blk", [2, 16, 128, 1024], BF16)
    w1_d = di("cw1", [2, 2048, 256])
    b1_d = di("cb1", [2, 128, 2])
    pe_d = di("cpe", [2, 128, 16])
    w2_d = di("cw2", [2, 256, 64])
    KsT_d = di("KsT", [64, SEQ], BF16)
    VsA_d = di("VsA", [SEQ, 65], BF16)
    KwT_d = di("KwT", [64, SEQ + 512], BF16)
    VwA_d = di("VwA", [SEQ + 512, 65], BF16)
    gates_d = di("gates", [NQF, 128, 12])
    TC_d = di("TC", [10, 128, 512])
    TS_d = di("TS", [4, 128, 512])
    TW_d = di("TW", [5, 128, 512])
    OV_d = di("OV", [8, 128, 256])
    FB_d = di("FB", [NQF, 128, 256])
    EX_d = di("EX", [128, 64 * 128])
    ident_d = di("ident", [128, 128])
    o_d = do("o", [NQF, 128, 256])
    p = Prog(nc)
    SB = [p.ps(f"nS{i}", [128, 512], F32) for i in range(2)]
    OB = [p.ps(f"nO{i}", [128, 512], F32) for i in range(3)]
    IMPB = [p.ps(f"nI{i}", [128, 512], F32) for i in range(2)]
    PT = p.ps("nPT", [128, 1024], BF16)
    identb = p.sb("identb", [128, 128], BF16)
    p.dma("pool", identb[:], ident_d, writes=["identb"])
    KC = p.sb("KC", [64, 1024], BF16)
    VC = p.sb("VC", [128, 8, 65], BF16)
    p.op("dve", lambda e: e.memset(VC[:], 1.0), writes=["VC"])
    import os
    NSK = os.environ.get("NSKIP", "")
    with p.scope() as st:
        W1 = p.sb("n_W1", [128, 2, 16, 256], BF16, st)
        W2 = p.sb("n_W2", [128, 2, 2, 64], BF16, st)
        PE2 = p.sb("n_PE", [128, 2, 16], BF16, st)
        B1 = p.sb("n_B1", [128, 2, 2], F32, st)
        hid = p.sb("n_hid", [128, 2, 2, 1024], BF16, st)
        blk = [p.sb(f"n_blk{i}", [128, 1024], BF16, st) for i in range(2)]
        for j in range(2):
            p.dma("pool", W1[:, j, :, :], w1_d[j].rearrange("(lt p) n -> p lt n", p=128), writes=["n_W1"])
            p.dma("pool", W2[:, j, :, :], w2_d[j].rearrange("(ht p) n -> p ht n", p=128), writes=["n_W2"])
            p.dma("pool", PE2[:, j, :], pe_d[j], writes=["n_PE"])
            p.dma("sp", B1[:, j, :], b1_d[j], writes=["n_B1"])
        for j in range(2):
            for ht in range(2):
                for lt in range(16):
                    p.op("pe", lambda e, j=j, ht=ht, lt=lt: e.matmul(SB[0][:, 0:1], lhsT=W1[:, j, lt, ht * 128:(ht + 1) * 128],
                                                                      rhs=PE2[:, j, lt:lt + 1], start=(lt == 0), stop=(lt == 15)),
                         reads=["n_W1", "n_PE"], writes=[("nS", 0)])
                p.op("dve", lambda e, j=j, ht=ht: e.tensor_tensor(out=B1[:, j, ht:ht + 1], in0=B1[:, j, ht:ht + 1], in1=SB[0][:, 0:1],
                                                                  op=ALU.add), reads=[("nS", 0), "n_B1"], writes=["n_B1"])
            for lt in range(16):
                bb = blk[lt % 2]
                p.dma("sp" if lt % 2 == 0 else "act", bb[:], blk_d[j, lt], writes=[("n_blk", lt % 2)])
                for ht in range(2):
                    for ic in range(2):
                        bank = (SB + OB + IMPB)[ht * 2 + ic + 2]
                        p.op("pe", lambda e, j=j, ht=ht, ic=ic, lt=lt, bb=bb, bank=bank: e.matmul(
                            bank[:], lhsT=W1[:, j, lt, ht * 128:(ht + 1) * 128], rhs=bb[:, ic * 512:(ic + 1) * 512],
                            start=(lt == 0), stop=(lt == 15)), reads=["n_W1", ("n_blk", lt % 2)], writes=[("ncmp", ht, ic)])
            for ht in range(2):
                for ic in range(2):
                    bank = (SB + OB + IMPB)[ht * 2 + ic + 2]
                    p.op("act", lambda e, j=j, ht=ht, ic=ic, bank=bank: e.activation(
                        out=hid[:, j, ht, ic * 512:(ic + 1) * 512], in_=bank[:], func=AF.Gelu_apprx_tanh, bias=B1[:, j, ht:ht + 1], scale=1.0),
                        reads=[("ncmp", ht, ic), "n_B1"], writes=[("n_hid", j)])
        for ic in range(2):
            for ht in range(2):
                p.op("pe", lambda e, ic=ic, ht=ht: e.matmul(SB[0][0:64, :], lhsT=W2[:, 0, ht, :], rhs=hid[:, 0, ht, ic * 512:(ic + 1) * 512],
                                                            start=(ht == 0), stop=(ht == 1)), reads=["n_W2", ("n_hid", 0)], writes=[("nS", 0)])
            p.op("act", lambda e, ic=ic: e.activation(out=KC[:, ic * 512:(ic + 1) * 512], in_=SB[0][0:64, :], func=AF.Copy),
                 reads=[("nS", 0)], writes=["KC"])
        for it in range(8):
            for ht in range(2):
                p.op("pe", lambda e, it=it, ht=ht: e.matmul(SB[1][:, it * 64:(it + 1) * 64], lhsT=hid[:, 1, ht, it * 128:(it + 1) * 128],
                                                            rhs=W2[:, 1, ht, :], start=(ht == 0), stop=(ht == 1)),
                     reads=["n_W2", ("n_hid", 1)], writes=[("nS", 1)])
        p.op("act", lambda e: e.activation(out=VC[:, :, 0:64], in_=SB[1][:].rearrange("p (a b) -> p a b", b=64), func=AF.Copy),
             reads=[("nS", 1)], writes=["VC"])
    KsT = p.sb("KsT_sb", [64, SEQ], BF16)
    VsA = p.sb("VsA_sb", [128, SEQ // 128, 65], BF16)
    for c4 in range(4):
        p.dma("sp", KsT[:, c4 * 4096:(c4 + 1) * 4096], KsT_d[:, c4 * 4096:(c4 + 1) * 4096], writes=["KsT"])
        p.dma("act", VsA[:, c4 * 32:(c4 + 1) * 32, :], VsA_d[c4 * 4096:(c4 + 1) * 4096, :].rearrange("(n p) c -> p n c", p=128),
              writes=["VsA"])
    TC = p.sb("TC_sb", [128, 10, 512], F32)
    TS = p.sb("TS_sb", [128, 4, 512], F32)
    TW = p.sb("TW_sb", [128, 5, 512], F32)
    OV = p.sb("OV_sb", [128, 8, 256], BF16)
    EX = p.sb("EX_sb", [128, 64, 128], BF16)
    p.dma("sp", TC[:], TC_d.rearrange("n p c -> p n c"), writes=["TC"])
    p.dma("sp", TS[:], TS_d.rearrange("n p c -> p n c"), writes=["TS"])
    p.dma("sp", TW[:], TW_d.rearrange("n p c -> p n c"), writes=["TW"])
    p.dma("pool", OV[:], OV_d.rearrange("n p c -> p n c"), writes=["OV"])
    p.dma("pool", EX[:].rearrange("p a b -> p (a b)"), EX_d, writes=["EX"])
    Qt = [p.sb(f"n_q{i}", [64, 4, 128], BF16) for i in range(2)]
    GT = [p.sb(f"n_g{i}", [128, 12], F32) for i in range(2)]
    FBt = [p.sb(f"n_fb{i}", [128, 256], F32) for i in range(2)]
    Kw = [p.sb(f"n_kw{i}", [64, 640], BF16) for i in range(2)]
    Vw = [p.sb(f"n_vw{i}", [128, 5, 65], BF16) for i in range(2)]
    sbt = [p.sb(f"n_sb{i}", [128, 512], F32) for i in range(2)]
    Et = [p.sb(f"n_E{i}", [128, 512], BF16) for i in range(2)]
    rsc = p.sb("n_rsc", [128, 4], F32)
    imp = p.sb("n_imp", [128, 256], F32)
    imp2 = p.sb("n_imp2", [128, 256], F32)
    m8 = p.sb("n_m8", [128, 8], F32)
    m8b = p.sb("n_m8b", [128, 8], F32)
    thr = p.sb("n_thr", [128, 1], F32)
    negm = p.sb("n_negm", [128, 256], BF16)
    NegMT = p.sb("n_negmT", [128, 2, 128], BF16)
    wbr = p.sb("n_wbr", [128, 3, 4], F32)
    ot = [p.sb(f"n_o{i}", [128, 256], F32) for i in range(2)]
    cnt = [0]

    def att_tile(kT, kkeys, v, vkeys, table, tkeys, qt_ap, qkey, acc_i, mask=None):
        b = cnt[0] % 2
        cnt[0] += 1
        S, sbb, E = SB[b], sbt[b], Et[b]
        p.op("pe", lambda e: e.matmul(S[:], lhsT=kT, rhs=qt_ap, start=True, stop=(mask is None)),
             reads=kkeys + [qkey], writes=[("nS", b)])
        if mask is not None:
            ex, nm = mask
            p.op("pe", lambda e: e.matmul(S[:].rearrange("p (h q) -> p h q", h=4), lhsT=ex,
                                          rhs=nm.unsqueeze(1).to_broadcast([128, 4, 128]), start=False, stop=True),
                 reads=["EX", "n_negmT"], writes=[("nS", b)])
        p.op("dve", lambda e: e.tensor_tensor(out=sbb[:], in0=S[:], in1=table, op=ALU.add),
             reads=[("nS", b)] + tkeys, writes=[("n_sb", b)])
        p.op("act", lambda e: e.activation(out=E[:], in_=sbb[:], func=AF.Exp), reads=[("n_sb", b)], writes=[("n_E", b)])
        for h in range(4):
            p.op("pe", lambda e, h=h: e.matmul(OB[acc_i][:, h * 65:(h + 1) * 65], lhsT=E[:, h * 128:(h + 1) * 128], rhs=v,
                                               start=False, stop=False, skip_group_check=True),
                 reads=[("n_E", b)] + vkeys, writes=[("nO", acc_i)])
        return E, b

    for i in range(NQI):
        i2 = i % 2
        q, g_, fb, kw, vw = Qt[i2], GT[i2], FBt[i2], Kw[i2], Vw[i2]
        p.dma("sp", q[:], QT_d[:, :, i, :], writes=[("n_q", i2)])
        p.dma("sp", g_[:], gates_d[i], writes=[("n_g", i2)])
        p.dma("sp", fb[:], FB_d[i], writes=[("n_fb", i2)])
        p.dma("act", kw[:], KwT_d[:, 2 * i * 128:(2 * i + 5) * 128], writes=[("n_kw", i2)])
        p.dma("act", vw[:], VwA_d[2 * i * 128:(2 * i + 5) * 128, :].rearrange("(n p) c -> p n c", p=128), writes=[("n_vw", i2)])
        for a in range(3):
            p.op("dve", lambda e, a=a: e.memset(OB[a][:], 0.0), writes=[("nO", a)])
        for a in range(2):
            p.op("dve", lambda e, a=a: e.memset(IMPB[a][:], 0.0), writes=[("nI", a)])
        qap = q[:].rearrange("p h q -> p (h q)")
        qk = ("n_q", i2)
        for kc in range((2 * i + 1) // 16 + 1):
            e_ = 2 * i - 16 * kc
            tidx = e_ // 2 if e_ <= 16 else 9
            E, b = att_tile(KC[:, kc * 128:(kc + 1) * 128], ["KC"], VC[:, kc, :], ["VC"], TC[:, tidx, :], ["TC"], qap, qk, 0)
            for h in range(4):
                p.op("pe", lambda e, h=h, E=E, kc=kc: e.matmul(IMPB[h // 2][:, (h % 2) * 256:(h % 2 + 1) * 256],
                                                               lhsT=E[:, h * 128:(h + 1) * 128], rhs=OV[:, kc, :],
                                                               start=False, stop=False, skip_group_check=True),
                     reads=[("n_E", b), "OV"], writes=[("nI", h // 2)])
        sums = lambda a: OB[a][:, 0:260].rearrange("p (h c) -> p h c", c=65)[:, :, 64]
        p.op("dve", lambda e: e.tensor_scalar(out=rsc[:], in0=sums(0), scalar1=1e-30, scalar2=None, op0=ALU.max),
             reads=[("nO", 0)], writes=["n_rsc"])
        p.op("dve", lambda e: e.reciprocal(out=rsc[:], in_=rsc[:]), reads=["n_rsc"], writes=["n_rsc"])
        for h in range(4):
            p.op("dve", lambda e, h=h, fb=fb: e.scalar_tensor_tensor(
                out=imp[:], in0=IMPB[h // 2][:, (h % 2) * 256:(h % 2 + 1) * 256], scalar=rsc[:, h:h + 1],
                in1=(fb[:] if h == 0 else imp[:]), op0=ALU.mult, op1=ALU.add),
                reads=[("nI", h // 2), "n_rsc", ("n_fb", i2), "n_imp"], writes=["n_imp"])
        p.op("dve", lambda e: e.max(out=m8[:], in_=imp[:]), reads=["n_imp"], writes=["n_m8"])
        p.op("dve", lambda e: e.match_replace(out=imp2[:], in_to_replace=m8[:], in_values=imp[:], imm_value=-1e9),
             reads=["n_imp", "n_m8"], writes=["n_imp2"])
        p.op("dve", lambda e: e.max(out=m8b[:], in_=imp2[:]), reads=["n_imp2"], writes=["n_m8b"])
        p.op("dve", lambda e: e.tensor_scalar(out=thr[:], in0=m8b[:, 7:8], scalar1=-5000.0, scalar2=None, op0=ALU.max),
             reads=["n_m8b"], writes=["n_thr"])
        p.op("dve", lambda e: e.tensor_scalar(out=imp2[:], in0=imp[:], scalar1=thr[:, 0:1], scalar2=None, op0=ALU.is_ge),
             reads=["n_imp", "n_thr"], writes=["n_imp2"])
        p.op("dve", lambda e: e.tensor_scalar(out=negm[:], in0=imp2[:], scalar1=-1.0, scalar2=30000.0, op0=ALU.add, op1=ALU.mult),
             reads=["n_imp2"], writes=["n_negm"])
        for jc in range(2):
            p.op("pe", lambda e, jc=jc: e.transpose(out=PT[:, jc * 128:(jc + 1) * 128], in_=negm[:, jc * 128:(jc + 1) * 128],
                                                    identity=identb[:]), reads=["n_negm", "identb"], writes=["nPT"])
        p.op("act", lambda e: e.activation(out=NegMT[:].rearrange("p a b -> p (a b)"), in_=PT[:, 0:256], func=AF.Copy),
             reads=["nPT"], writes=["n_negmT"])
        for kt in range(2 * i + 2):
            d3 = kt - (2 * i - 1)
            tb = TS[:, d3, :] if d3 >= 0 else TS[:, 3, :]
            att_tile(KsT[:, kt * 128:(kt + 1) * 128], ["KsT"], VsA[:, kt, :], ["VsA"], tb, ["TS"], qap, qk, 1,
                     mask=(EX[:, kt % 64, :], NegMT[:, kt // 64, :]))
        for d in range(5):
            att_tile(kw[:, d * 128:(d + 1) * 128], [("n_kw", i2)], vw[:, d, :], [("n_vw", i2)], TW[:, d, :], ["TW"], qap, qk, 2)
        o = ot[i2]
        for a in range(3):
            p.op("dve", lambda e, a=a: e.tensor_scalar(out=wbr[:, a, :], in0=sums(a), scalar1=1e-30, scalar2=None, op0=ALU.max),
                 reads=[("nO", a)], writes=[("n_wbr", a)])
            p.op("dve", lambda e, a=a: e.reciprocal(out=wbr[:, a, :], in_=wbr[:, a, :]), reads=[("n_wbr", a)], writes=[("n_wbr", a)])
            p.op("dve", lambda e, a=a, g_=g_: e.tensor_tensor(out=wbr[:, a, :], in0=wbr[:, a, :],
                                                               in1=g_[:].rearrange("p (h c) -> p h c", c=3)[:, :, a], op=ALU.mult),
                 reads=[("n_wbr", a), ("n_g", i2)], writes=[("n_wbr", a)])
        for h in range(4):
            for a in range(3):
                src = OB[a][:, h * 65:h * 65 + 64]
                if a == 0:
                    p.op("dve", lambda e, h=h, src=src, o=o: e.tensor_scalar(out=o[:, h * 64:(h + 1) * 64], in0=src,
                                                                            scalar1=wbr[:, 0, h:h + 1], scalar2=None, op0=ALU.mult),
                         reads=[("nO", 0), ("n_wbr", 0)], writes=[("n_o", i2)])
                else:
                    p.op("dve", lambda e, h=h, a=a, src=src, o=o: e.scalar_tensor_tensor(
                        out=o[:, h * 64:(h + 1) * 64], in0=src, scalar=wbr[:, a, h:h + 1], in1=o[:, h * 64:(h + 1) * 64],
                        op0=ALU.mult, op1=ALU.add), reads=[("nO", a), ("n_wbr", a), ("n_o", i2)], writes=[("n_o", i2)])
        p.dma("act", o_d[i], o[:], reads=[("n_o", i2)])
    p.finish()
    return nc


_CONST = {}


def _consts():
    if not _CONST:
        _CONST["ident"] = np.eye(128, dtype=np.float32)
        s = np.arange(128)
        _CONST["tri"] = (s[:, None] <= s[None, :]).astype(np.float32)
    return _CONST


_NC_CACHE = {}


def _get_nc(name, builder):
    return builder()


def run_A(inp, l, x_cur):
    cst = _consts()
    c = inp["c"]
    maps = []
    cols = np.r_[0:3072, 3072:5120]
    ada_w = np.ascontiguousarray(inp["ada_w"][l][:, cols])
    ada_b = np.ascontiguousarray(inp["ada_b"][l][cols])
    wsT = np.ascontiguousarray(inp["gmlp_ws"][l].transpose(0, 2, 1))
    bsT = np.ascontiguousarray(inp["gmlp_bs"][l].T)
    for r in range(NCORE):
        b = r // 4
        maps.append({
            "x": np.ascontiguousarray(x_cur[r * TOKC:(r + 1) * TOKC]),
            "cT": np.ascontiguousarray(c[b].reshape(8, 128).T),
            "ada_w": ada_w, "ada_b": ada_b,
            "lng": inp["ln_g"][l, 0], "lnb": inp["ln_b"][l, 0],
            "w1": inp["ffn_w1"][l, 0], "w3": inp["ffn_w3"][l, 0], "w2": inp["ffn_w2"][l, 0],
            "w_in": inp["w_in"][l],
            "glng": inp["gmlp_ln_g"][l], "glnb": inp["gmlp_ln_b"][l],
            "wsT": wsT, "bsT": bsT,
            "ident": cst["ident"], "tri": cst["tri"],
        })
    nc = build_A()
    res = run_bass_kernel_spmd(nc, maps, core_ids=list(range(NCORE)))
    return res.results


def run_C(inp, l, x1, oatt, ogm, yssm, zz):
    cst = _consts()
    c = inp["c"]
    cols = np.r_[5120:9216]
    ada_w = np.ascontiguousarray(inp["ada_w"][l][:, cols])
    ada_b = np.ascontiguousarray(inp["ada_b"][l][cols])
    maps = []
    for r in range(NCORE):
        b = r // 4
        sl = slice(r * TOKC, (r + 1) * TOKC)
        maps.append({
            "x1": np.ascontiguousarray(x1[sl]), "cT": np.ascontiguousarray(c[b].reshape(8, 128).T),
            "ada_w": ada_w, "ada_b": ada_b,
            "lng1": inp["ln_g"][l, 1], "lnb1": inp["ln_b"][l, 1], "lng2": inp["ln_g"][l, 2], "lnb2": inp["ln_b"][l, 2],
            "w1": inp["ffn_w1"][l, 1], "w3": inp["ffn_w3"][l, 1], "w2": inp["ffn_w2"][l, 1],
            "w_out": inp["w_out"][l],
            "oatt": np.ascontiguousarray(oatt[sl]), "ogm": np.ascontiguousarray(ogm[sl]),
            "yssm": np.ascontiguousarray(yssm[sl]), "zz": np.ascontiguousarray(zz[sl]),
            "normg": inp["ssm_norm_g"][l], "ident": cst["ident"],
        })
    nc = build_C()
    res = run_bass_kernel_spmd(nc, maps, core_ids=list(range(NCORE)))
    return res.results


def run_S(inp, l, xbc, dtr):
    cst = _consts()
    s_ = np.arange(128)
    l_ = np.arange(256)
    triw = np.stack([(s_[:, None] + 128 * i <= l_[None, :]).astype(np.float32) for i in range(2)])
    ntriw = ((triw - 1.0) * 30000.0).astype(np.float32)
    maps = []
    cwl, cbl = inp["ssm_conv_w"][l], inp["ssm_conv_b"][l]
    for r in range(NCORE):
        b, h = r // 4, r % 4
        g = h // 2
        cols = np.r_[64 * h:64 * h + 64, 256 + 128 * g:256 + 128 * g + 128, 512 + 128 * g:512 + 128 * g + 128]
        xcat = xbc[b][:, cols]
        xpad = np.concatenate([np.zeros((3, 320), np.float32), xcat], 0)
        xc4 = np.ascontiguousarray(np.stack([xpad[k:k + SEQ] for k in range(4)], axis=1))
        maps.append({
            "xc4": xc4, "cw": np.ascontiguousarray(cwl[:, cols]).reshape(-1), "cb": np.ascontiguousarray(cbl[cols]),
            "dtr": np.ascontiguousarray(dtr[b][:, h]),
            "scal": np.array([inp["ssm_dt_bias"][l, h], inp["ssm_a_log"][l, h], inp["ssm_d"][l, h], 0.0], np.float32),
            "tri": cst["tri"], "triw": triw, "ntriw": ntriw, "identf": cst["ident"],
        })
    nc = build_S()
    res = run_bass_kernel_spmd(nc, maps, core_ids=list(range(NCORE)))
    y = np.zeros((BATCH, SEQ, 256), np.float32)
    for r in range(NCORE):
        y[r // 4][:, 64 * (r % 4):64 * (r % 4) + 64] = res.results[r]["y"]
    return y


def _bucket(n):
    n = np.maximum(n, 0)
    nf = np.maximum(n, 1).astype(np.float32)
    large = 16 + (np.log(nf / np.float32(16)) / np.float32(np.log(8.0)) * np.float32(16)).astype(np.int32)
    large = np.minimum(large, 31)
    return np.where(n < 16, n, large)


def _table(rb_aug, dist, valid, g):
    idx = np.where(valid, _bucket(dist), 32)
    t = rb_aug[idx][:, :, 4 * g:4 * g + 4]
    return np.ascontiguousarray(t.transpose(0, 2, 1).reshape(128, 512))


def run_N(inp, l, QT_all, KcT_all, VcT_all, KsT_all, KwT_all, Vs_all, Vw_all, gates_all):
    cst = _consts()
    T = SEQ
    rb_aug = np.concatenate([inp["rel_bias"], np.full((1, 8), -30000.0, np.float32)], 0)
    k = np.arange(128)[:, None]
    q = np.arange(128)[None, :]
    allv = np.ones((128, 128), bool)
    big = np.full((128, 128), 100000)
    kg = np.arange(1024)
    cs = 16 * kg
    j = np.arange(256)
    ov = np.clip(np.minimum(cs[:, None] + 32, 64 * j[None, :] + 64) - np.maximum(cs[:, None], 64 * j[None, :]), 0, None) / 32.0
    ov[1023] = 0.0
    OV = ov.reshape(8, 128, 256).astype(np.float32)
    jj = np.arange(128)[:, None, None]
    mm = np.arange(64)[None, :, None]
    kk = np.arange(128)[None, None, :]
    EX = (jj == 2 * mm + kk // 64).astype(np.float32).reshape(128, 64 * 128)
    maps = []
    for r in range(NCORE):
        b, g, par = r // 4, (r // 2) % 2, r % 2
        TC = [_table(rb_aug, 128 * (2 * e2 + par) + q - 16 * k - 31, (128 * (2 * e2 + par) + q - 16 * k - 31) >= 0, g) for e2 in range(9)]
        TC.append(_table(rb_aug, big, allv, g))
        prev = _table(rb_aug, 128 + q - k, allv, g)
        diag = _table(rb_aug, q - k, (q - k) >= 0, g)
        far = _table(rb_aug, big, allv, g)
        none = _table(rb_aug, big, ~allv, g)
        TS = [prev, diag, none, far] if par == 0 else [far, prev, diag, far]
        TW = [_table(rb_aug, q + 128 * (4 - d) - k, ((q + 128 * (4 - d) - k) >= 0) & ((q + 128 * (4 - d) - k) < 512), g) for d in range(5)]
        qt = 2 * np.arange(NQF) + par
        t = (128 * qt[:, None] + np.arange(128)[None, :])
        cur = (t // 64)[:, :, None]
        jb = np.arange(256)[None, None, :]
        FB = np.where((jb == 0) | (jb == cur) | (jb == cur - 1), 1e4, np.where(jb <= cur, 0.0, -1e4)).astype(np.float32)
        tok = slice(b * T, (b + 1) * T)
        QTc = QT_all[:, tok].reshape(8, 64, 128, 128)[4 * g:4 * g + 4, :, par::2, :].transpose(1, 0, 2, 3)
        blks = []
        for src in (KcT_all, VcT_all):
            kcg = src[g * 64:(g + 1) * 64, tok]
            bl = np.zeros((16, 2, 64, 1024), kcg.dtype)
            ii = np.arange(1023)
            for lt in range(16):
                for lb in range(2):
                    bl[lt, lb, :, :1023] = kcg[:, 16 * ii + 2 * lt + lb]
            blks.append(bl.reshape(16, 128, 1024))
        KsT = KsT_all[g * 64:(g + 1) * 64, tok]
        ones = np.ones((T, 1), Vs_all.dtype)
        VsA = np.concatenate([Vs_all[tok, g * 64:(g + 1) * 64], ones], 1)
        KwP = np.zeros((64, 512 + T + 128), KwT_all.dtype)
        KwP[:, 512:512 + T] = KwT_all[g * 64:(g + 1) * 64, tok]
        VwP = np.zeros((512 + T + 128, 65), Vw_all.dtype)
        VwP[512:512 + T] = np.concatenate([Vw_all[tok, g * 64:(g + 1) * 64], ones], 1)
        gt = gates_all[tok, 12 * g:12 * g + 12].reshape(128, 128, 12)[par::2]
        maps.append({
            "QT": np.ascontiguousarray(QTc), "blk": np.ascontiguousarray(np.stack(blks)),
            "cw1": inp["cmp_w1"][l], "cb1": np.ascontiguousarray(inp["cmp_b1"][l].reshape(2, 2, 128).transpose(0, 2, 1)),
            "cpe": np.ascontiguousarray(inp["cmp_pe"][l].reshape(2, 16, 2, 64).transpose(0, 2, 3, 1).reshape(2, 128, 16)),
            "cw2": inp["cmp_w2"][l],
            "KsT": np.ascontiguousarray(KsT), "VsA": np.ascontiguousarray(VsA),
            "KwT": np.ascontiguousarray(KwP[:, par * 128:par * 128 + T + 512]),
            "VwA": np.ascontiguousarray(VwP[par * 128:par * 128 + T + 512]),
            "gates": np.ascontiguousarray(gt), "TC": np.stack(TC), "TS": np.stack(TS), "TW": np.stack(TW),
            "OV": OV, "FB": FB, "EX": EX, "ident": cst["ident"],
        })
    nc = build_N()
    res = run_bass_kernel_spmd(nc, maps, core_ids=list(range(NCORE)))
    o_att = np.zeros((BATCH, SEQ, 512), np.float32)
    for r in range(NCORE):
        b, g, par = r // 4, (r // 2) % 2, r % 2
        o = res.results[r]["o"]
        o_att[b].reshape(128, 128, 512)[par::2, :, g * 256:(g + 1) * 256] = o
    return o_att


def kernel(**inp):
    inp = {k: np.asarray(v) for k, v in inp.items()}
    x = np.ascontiguousarray(inp["x"].reshape(-1, D))
    cat = lambda res, name, ax: np.concatenate([r[name] for r in res], ax)
    for l in range(DEPTH):
        ra = run_A(inp, l, x)
        x1 = cat(ra, "x1", 0)
        tmf = cat(ra, "tmf", 0)
        o_att = run_N(inp, l, cat(ra, "QT", 1), cat(ra, "KcT", 1), cat(ra, "VcT", 1), cat(ra, "KsT", 1), cat(ra, "KwT", 1),
                      cat(ra, "Vs", 0), cat(ra, "Vw", 0), tmf[:, 1028:1052])
        ypre = run_S(inp, l, tmf[:, 256:1024].reshape(BATCH, SEQ, 768), tmf[:, 1024:1028].reshape(BATCH, SEQ, 4))
        rcz = run_C(inp, l, x1, o_att.reshape(-1, 512), cat(ra, "ogm", 0), ypre.reshape(-1, 256), np.ascontiguousarray(tmf[:, 0:256]))
        x = cat(rcz, "x3", 0)
    return x.reshape(BATCH, SEQ, D).astype(np.float32)
```

```python
import numpy as np
import ml_dtypes
from contextlib import ExitStack, contextmanager
import concourse.bass as bass
import concourse.mybir as mybir
from concourse.bass_utils import run_bass_kernel_spmd

F32 = mybir.dt.float32
BF16 = mybir.dt.bfloat16
AF = mybir.ActivationFunctionType
ALU = mybir.AluOpType
NPBF = ml_dtypes.bfloat16

D = 1024
DEPTH = 2
SEQ = 16384
BATCH = 2
DFF = 2816
NFT = DFF // 128
DIN = 2844
ALPHA = (2 * DEPTH) ** 0.25
LN_EPS = 1e-5
NCORE = 8
TOKC = BATCH * SEQ // NCORE
NTT = TOKC // 128

ENGS = ("pe", "dve", "act", "pool", "sp")


_PSUM_KEYS = ("bank", "ptb", "sbank", "nS", "nO", "nI", "ncmp")


def _is_psum_key(k):
    return k == "nPT" or (isinstance(k, tuple) and k[0] in _PSUM_KEYS)


class Prog:
    def __init__(self, nc, n_dma_slots=8):
        self.nc = nc
        self.stack = ExitStack()
        self.ops = {e: [] for e in ENGS}
        self.sem = {e: self.stack.enter_context(nc.semaphore("s_" + e)) for e in ENGS}
        self.cnt = {e: 0 for e in ENGS}
        self.seen = {e: {} for e in ENGS}
        self.last_w = {}
        self.readers = {}
        self.nslot = n_dma_slots
        self.dma_sems, self.dma_vals, self.dma_next = {}, {}, {}
        self.semobj = dict(self.sem)
        for q in ("sp", "act", "pool"):
            self.dma_sems[q] = [self.stack.enter_context(nc.semaphore(f"d_{q}{i}")) for i in range(n_dma_slots)]
            self.dma_vals[q] = [0] * n_dma_slots
            self.dma_next[q] = 0
            for i, s in enumerate(self.dma_sems[q]):
                self.semobj[(q, i)] = s
        self.n_inst = 0
        self._rr = 0

    def sb(self, name, shape, dtype, stack=None):
        return (stack or self.stack).enter_context(self.nc.sbuf_tensor(name, list(shape), dtype))

    def ps(self, name, shape, dtype=F32, stack=None):
        return (stack or self.stack).enter_context(self.nc.psum_tensor(name, list(shape), dtype))

    def _need(self, eng, tok, waits):
        semkey, val, _ = tok
        if self.seen[eng].get(semkey, 0) >= val:
            return
        self.seen[eng][semkey] = val
        waits.append((semkey, val))

    def _deps(self, eng, engid, reads, writes):
        waits = []
        for r in reads:
            t = self.last_w.get(r)
            if t is not None:
                self._need(eng, t, waits)
        for w in writes:
            t = self.last_w.get(w)
            if t is not None and (t[2] != engid or eng != "pe"):
                self._need(eng, t, waits)
            for sk, (v, eid) in self.readers.get(w, {}).items():
                if eid != engid or eng != "pe":
                    self._need(eng, (sk, v, eid), waits)
        return waits

    def _commit(self, tok, reads, writes):
        for r in reads:
            self.readers.setdefault(r, {})[tok[0]] = (tok[1], tok[2])
        for w in writes:
            self.last_w[w] = tok
            self.readers[w] = {}

    def op(self, eng, fn, reads=(), writes=()):
        ps_reads = [k for k in reads if _is_psum_key(k)]
        if ps_reads:
            writes = list(writes) + [k for k in ps_reads if k not in writes]
        waits = self._deps(eng, eng, reads, writes)
        self.cnt[eng] += 1
        tok = (eng, self.cnt[eng], eng)
        self._commit(tok, reads, writes)
        self.ops[eng].append((waits, fn, self.sem[eng], 1))
        self.n_inst += 1
        return tok

    def dma(self, q, out, in_, reads=(), writes=(), **kw):
        slot = self.dma_next[q]
        self.dma_next[q] = (slot + 1) % self.nslot
        semkey = (q, slot)
        engid = ("dma", q, slot)
        waits = self._deps(q, engid, reads, writes)
        prev = self.dma_vals[q][slot]
        if prev > 0:
            self._need(q, (semkey, prev, engid), waits)
        val = prev + 16
        self.dma_vals[q][slot] = val
        tok = (semkey, val, engid)
        self._commit(tok, reads, writes)
        self.ops[q].append((waits, lambda e: e.dma_start(out=out, in_=in_, **kw), self.dma_sems[q][slot], 16))
        self.n_inst += 1
        return tok

    def barrier(self):
        for e in ENGS:
            waits = []
            for o in ENGS:
                if o != e and self.cnt[o] > 0:
                    self._need(e, (o, self.cnt[o], o), waits)
            for q in self.dma_sems:
                for i in range(self.nslot):
                    v = self.dma_vals[q][i]
                    if v > 0:
                        self._need(e, ((q, i), v, None), waits)
            if waits:
                self.ops[e].append((waits, None, None, 0))

    @contextmanager
    def scope(self):
        st = ExitStack()
        try:
            yield st
        finally:
            self.barrier()
            st.close()

    def finish(self):
        self.barrier()
        engmap = {"pe": "tensor", "dve": "vector", "act": "scalar", "pool": "gpsimd", "sp": "sync"}
        semobj = self.semobj
        with self.nc.Block() as block:
            for e in ENGS:
                ops = self.ops[e]
                if not ops:
                    continue

                def body(engine, ops=ops):
                    for waits, fn, sem, inc in ops:
                        for sk, v in waits:
                            engine.wait_ge(semobj[sk], v)
                        if fn is not None:
                            fn(engine).then_inc(sem, inc)

                getattr(block, engmap[e])(body)
        self.stack.close()


class RowCtx:
    def __init__(self, p, nc, consts):
        self.p, self.nc = p, nc
        self.identb = p.sb("identb", [128, 128], BF16)
        p.dma("pool", self.identb[:], consts["ident"], writes=["identb"])
        self.eps = p.sb("epsc", [128, 1], F32)
        p.op("dve", lambda e: e.memset(self.eps[:], LN_EPS), writes=["epsc"])
        self.bank = [p.ps(f"bank{i}", [128, 512], F32) for i in range(6)]
        self.ptb = [p.ps(f"ptb{i}", [128, 1024], BF16) for i in range(2)]


def emit_mod(p, rc, st, cT, ada_w, ada_b, ncols, dst, posts):
    nck = ncols // 512
    with p.scope() as s2:
        ct = p.sb("mod_ct", [128, 8], F32, s2)
        sc = p.sb("mod_sc", [128, 8], F32, s2)
        scb = p.sb("mod_scb", [128, 8, 128], F32, s2)
        bb = p.sb("mod_bb", [128, ncols], F32, s2)
        wch = [p.sb(f"mod_w{i}", [128, 8, 512], F32, s2) for i in range(2)]
        p.dma("sp", ct[:], cT, writes=["mod_ct"])
        p.dma("act", bb[:], ada_b.partition_broadcast(128), writes=["mod_bb"])
        p.op("act", lambda e: e.activation(out=sc[:], in_=ct[:], func=AF.Silu), reads=["mod_ct"], writes=["mod_sc"])
        for dt in range(8):
            p.op("dve", lambda e, dt=dt: e.tensor_copy(out=scb[:, dt, :], in_=sc[:, dt:dt + 1].to_broadcast([128, 128])),
                 reads=["mod_sc"], writes=[("mod_scb", dt)])
        for ci in range(nck):
            w = wch[ci % 2]
            wk = ("mod_w", ci % 2)
            p.dma("sp" if ci % 2 == 0 else "act", w[:],
                  ada_w[:, ci * 512:(ci + 1) * 512].rearrange("(dt p) n -> p dt n", p=128), writes=[wk])
            bk = rc.bank[ci % 2]
            for dt in range(8):
                p.op("pe", lambda e, dt=dt, w=w, bk=bk: e.matmul(bk[:], lhsT=scb[:, dt, :], rhs=w[:, dt, :],
                                                                  start=(dt == 0), stop=(dt == 7)),
                     reads=[wk, ("mod_scb", dt)], writes=[("bank", ci % 2)])
            p.op("dve", lambda e, ci=ci, bk=bk: e.tensor_tensor(out=dst[:, ci * 512:(ci + 1) * 512], in0=bk[:],
                                                                 in1=bb[:, ci * 512:(ci + 1) * 512], op=ALU.add),
                 reads=[("bank", ci % 2), "mod_bb"], writes=["modtile"])
        for (c0, c1, add, mul) in posts:
            p.op("dve", lambda e, c0=c0, c1=c1, add=add, mul=mul: e.tensor_scalar(
                out=dst[:, c0:c1], in0=dst[:, c0:c1], scalar1=float(add), scalar2=float(mul), op0=ALU.add, op1=ALU.mult),
                reads=["modtile"], writes=["modtile"])


def emit_modulate_T(p, rc, xt, xkey, SC, SH, hf, hb, hT, tag, ptb_i=0):
    p.op("dve", lambda e: e.tensor_tensor(out=hf[:], in0=xt, in1=SC, op=ALU.mult),
         reads=[xkey, "modtile"], writes=[tag + "hf"])
    p.op("pool", lambda e: e.tensor_tensor(out=hb[:], in0=hf[:], in1=SH, op=ALU.add),
         reads=[tag + "hf", "modtile"], writes=[tag + "hb"])
    pt = rc.ptb[ptb_i]
    for dt in range(8):
        p.op("pe", lambda e, dt=dt: e.transpose(out=pt[:, dt * 128:(dt + 1) * 128], in_=hb[:, dt * 128:(dt + 1) * 128],
                                                 identity=rc.identb[:]),
             reads=[tag + "hb", "identb"], writes=[("ptb", ptb_i)])
    p.op("act", lambda e: e.activation(out=hT[:].rearrange("p a b -> p (a b)"), in_=pt[:], func=AF.Copy),
         reads=[("ptb", ptb_i)], writes=[tag + "hT"])


def emit_res_ln(p, rc, ybanks, xres, xkey, GF, lng, lnb, r, stats, mv, rstd, xo, tag):
    for half in range(2):
        sl = slice(half * 512, (half + 1) * 512)
        p.op("dve", lambda e, half=half, sl=sl: e.tensor_tensor(out=r[:, sl], in0=rc.bank[ybanks[half]][:], in1=GF[:, sl],
                                                                op=ALU.mult),
             reads=[("bank", ybanks[half]), "modtile"], writes=[(tag + "r", half)])
        p.op("dve", lambda e, sl=sl: e.scalar_tensor_tensor(out=r[:, sl], in0=xres[:, sl], scalar=float(ALPHA), in1=r[:, sl],
                                                            op0=ALU.mult, op1=ALU.add),
             reads=[(tag + "r", half), xkey], writes=[(tag + "r", half)])
        p.op("dve", lambda e, half=half, sl=sl: e.bn_stats(out=stats[:, half * 6:(half + 1) * 6], in_=r[:, sl]),
             reads=[(tag + "r", half)], writes=[(tag + "st", half)])
    p.op("dve", lambda e: e.bn_aggr(out=mv[:], in_=stats[:]), reads=[(tag + "st", 0), (tag + "st", 1)], writes=[tag + "mv"])
    p.op("act", lambda e: e.activation(out=rstd[:], in_=mv[:, 1:2], func=AF.Sqrt, bias=rc.eps[:], scale=1.0),
         reads=[tag + "mv", "epsc"], writes=[tag + "rstd"])
    p.op("dve", lambda e: e.reciprocal(out=rstd[:], in_=rstd[:]), reads=[tag + "rstd"], writes=[tag + "rstd"])
    p.op("dve", lambda e: e.tensor_scalar(out=r[:], in0=r[:], scalar1=mv[:, 0:1], scalar2=rstd[:, 0:1],
                                          op0=ALU.subtract, op1=ALU.mult),
         reads=[(tag + "r", 0), (tag + "r", 1), tag + "mv", tag + "rstd"], writes=[(tag + "r", 0), (tag + "r", 1)])
    p.op("pool", lambda e: e.tensor_tensor(out=r[:], in0=r[:], in1=lng[:], op=ALU.mult),
         reads=[(tag + "r", 0), (tag + "r", 1), "lngb"], writes=[(tag + "r", 0), (tag + "r", 1)])
    p.op("pool", lambda e: e.tensor_tensor(out=xo[:], in0=r[:], in1=lnb[:], op=ALU.add),
         reads=[(tag + "r", 0), (tag + "r", 1), "lngb"], writes=[tag + "xo"])


def emit_ffn(p, rc, x_in, x_out, w1, w3, w2, SC, SH, GF, lng, lnb):
    with p.scope() as st:
        W1 = p.sb("W1", [128, 8, DFF], BF16, st)
        W3 = p.sb("W3", [128, 8, DFF], BF16, st)
        W2 = p.sb("W2", [128, NFT, D], BF16, st)
        for dt in range(8):
            p.dma("pool", W1[:, dt, :], w1[dt * 128:(dt + 1) * 128, :], writes=["W1"])
            p.dma("pool", W3[:, dt, :], w3[dt * 128:(dt + 1) * 128, :], writes=["W3"])
        for f0 in range(0, NFT, 2):
            p.dma("pool", W2[:, f0:f0 + 2, :], w2[f0 * 128:(f0 + 2) * 128, :].rearrange("(f p) n -> p f n", p=128), writes=["W2"])
        xt = [p.sb(f"f_x{i}", [128, D], F32, st) for i in range(2)]
        hf = p.sb("f_hf", [128, D], F32, st)
        hb = p.sb("f_hb", [128, D], BF16, st)
        hT = p.sb("f_hT", [128, 8, 128], BF16, st)
        aT = p.sb("f_aT", [128, NFT, 128], BF16, st)
        sg = [p.sb(f"f_sg{i}", [128, 128], F32, st) for i in range(2)]
        r = p.sb("f_r", [128, D], F32, st)
        xo = [p.sb(f"f_xo{i}", [128, D], F32, st) for i in range(2)]
        stats = p.sb("f_stats", [128, 12], F32, st)
        mv = p.sb("f_mv", [128, 2], F32, st)
        rstd = p.sb("f_rstd", [128, 1], F32, st)
        for tt in range(NTT):
            x = xt[tt % 2]
            xk = ("f_x", tt % 2)
            p.dma("sp", x[:], x_in[tt * 128:(tt + 1) * 128, :], writes=[xk])
            emit_modulate_T(p, rc, x[:], xk, SC, SH, hf, hb, hT, "f_", ptb_i=tt % 2)
            for ft in range(NFT):
                b = ft % 2
                bk = rc.bank[b]
                for dt in range(8):
                    p.op("pe", lambda e, dt=dt, ft=ft, bk=bk: e.matmul(bk[:, 0:128], lhsT=W1[:, dt, ft * 128:(ft + 1) * 128],
                                                                        rhs=hT[:, dt, :], start=(dt == 0), stop=(dt == 7)),
                         reads=["W1", "f_hT"], writes=[("bank", b)])
                for dt in range(8):
                    p.op("pe", lambda e, dt=dt, ft=ft, bk=bk: e.matmul(bk[:, 128:256], lhsT=W3[:, dt, ft * 128:(ft + 1) * 128],
                                                                        rhs=hT[:, dt, :], start=(dt == 0), stop=(dt == 7)),
                         reads=["W3", "f_hT"], writes=[("bank", b)])
                p.op("act", lambda e, b=b, bk=bk: e.activation(out=sg[b][:], in_=bk[:, 0:128], func=AF.Silu),
                     reads=[("bank", b)], writes=[("f_sg", b)])
                p.op("dve", lambda e, b=b, bk=bk, ft=ft: e.tensor_tensor(out=aT[:, ft, :], in0=sg[b][:], in1=bk[:, 128:256],
                                                                          op=ALU.mult),
                     reads=[("bank", b), ("f_sg", b)], writes=[("f_aT", ft)])
            for half in range(2):
                bk = rc.bank[2 + half]
                for ft in range(NFT):
                    p.op("pe", lambda e, ft=ft, bk=bk, half=half: e.matmul(bk[:], lhsT=aT[:, ft, :],
                                                                            rhs=W2[:, ft, half * 512:(half + 1) * 512],
                                                                            start=(ft == 0), stop=(ft == NFT - 1)),
                         reads=[("f_aT", ft), "W2"], writes=[("bank", 2 + half)])
            o = xo[tt % 2]
            emit_res_ln(p, rc, (2, 3), x, xk, GF, lng, lnb, r, stats, mv, rstd, o, "f_")
            p.dma("act", x_out[tt * 128:(tt + 1) * 128, :], o[:], reads=["f_xo"], writes=[])


def build_A():
    nc = bass.Bass("TRN2", target_bir_lowering=False)
    di = lambda n, s, d=F32: nc.dram_tensor(n, list(s), d, kind="ExternalInput").ap()
    do = lambda n, s, d=F32: nc.dram_tensor(n, list(s), d, kind="ExternalOutput").ap()
    x_in = di("x", [TOKC, D])
    cT = di("cT", [128, 8])
    ada_w = di("ada_w", [D, 5120])
    ada_b = di("ada_b", [5120])
    lng_d, lnb_d = di("lng", [D]), di("lnb", [D])
    w1, w3, w2 = di("w1", [D, DFF]), di("w3", [D, DFF]), di("w2", [DFF, D])
    w_in = di("w_in", [D, DIN])
    glng_d, glnb_d = di("glng", [256]), di("glnb", [256])
    wsT_d = di("wsT", [4, 128, 128])
    bsT_d = di("bsT", [128, 4])
    consts = {"ident": di("ident", [128, 128]), "tri": di("tri", [128, 128])}
    x1_d = do("x1", [TOKC, D])
    QT_d = do("QT", [512, TOKC], BF16)
    KcT_d, VcT_d = do("KcT", [128, TOKC], BF16), do("VcT", [128, TOKC], BF16)
    KsT_d, KwT_d = do("KsT", [128, TOKC], BF16), do("KwT", [128, TOKC], BF16)
    Vs_d, Vw_d = do("Vs", [TOKC, 128], BF16), do("Vw", [TOKC, 128], BF16)
    tmf_d = do("tmf", [TOKC, 1052])
    ogm_d = do("ogm", [TOKC, 256], BF16)

    p = Prog(nc)
    rc = RowCtx(p, nc, consts)
    MOD = p.sb("MOD", [128, 5120], F32)
    lng = p.sb("lng_t", [128, D], F32)
    lnb = p.sb("lnb_t", [128, D], F32)
    p.dma("sp", lng[:], lng_d.partition_broadcast(128), writes=["lngb"])
    p.dma("sp", lnb[:], lnb_d.partition_broadcast(128), writes=["lngb"])
    emit_mod(p, rc, None, cT, ada_w, ada_b, 5120, MOD,
             [(1024, 2048, 1.0, 1.0), (2048, 3072, 1.0, 0.5), (4096, 5120, 1.0, 1.0)])
    import os
    STOP = os.environ.get("KSTOP", "")
    if STOP == "mod":
        p.finish(); return nc
    emit_ffn(p, rc, x_in, x1_d, w1, w3, w2, MOD[:, 1024:2048], MOD[:, 0:1024], MOD[:, 2048:3072], lng, lnb)
    if STOP == "ffn":
        p.finish(); return nc

    with p.scope() as st:
        WIN = p.sb("WIN", [128, 8, DIN], BF16, st)
        for dt in range(8):
            p.dma("pool", WIN[:, dt, :], w_in[dt * 128:(dt + 1) * 128, :], writes=["WIN"])
        tri = p.sb("tri_t", [128, 128], F32, st)
        p.dma("sp", tri[:], consts["tri"], writes=["tri_t"])
        wsf = p.sb("wsf", [128, 4, 128], F32, st)
        p.dma("sp", wsf[:], wsT_d.rearrange("g s t -> s g t"), writes=["wsf"])
        WS = p.sb("WS", [128, 4, 128], BF16, st)
        for g in range(4):
            p.op("dve", lambda e, g=g: e.tensor_tensor(out=WS[:, g, :], in0=wsf[:, g, :], in1=tri[:], op=ALU.mult),
                 reads=["wsf", "tri_t"], writes=["WS"])
        BS = p.sb("BS", [128, 4], F32, st)
        p.dma("sp", BS[:], bsT_d, writes=["BS"])
        glng = p.sb("glng_t", [128, 256], F32, st)
        glnb = p.sb("glnb_t", [128, 256], F32, st)
        p.dma("sp", glng[:], glng_d.partition_broadcast(128), writes=["glngb"])
        p.dma("sp", glnb[:], glnb_d.partition_broadcast(128), writes=["glngb"])
        xt = [p.sb(f"a_x{i}", [128, D], F32, st) for i in range(2)]
        hf = p.sb("a_hf", [128, D], F32, st)
        hb = p.sb("a_hb", [128, D], BF16, st)
        hT = p.sb("a_hT", [128, 8, 128], BF16, st)
        fm = [p.sb(f"a_fm{i}", [128, 8, 128], BF16, st) for i in range(2)]
        vsw = [p.sb(f"a_vsw{i}", [128, 256], BF16, st) for i in range(2)]
        gu = p.sb("a_gu", [128, 512], F32, st)
        vn = p.sb("a_vn", [128, 256], F32, st)
        vnb = p.sb("a_vnb", [128, 256], BF16, st)
        gst = p.sb("a_gst", [128, 6], F32, st)
        gmv = p.sb("a_gmv", [128, 2], F32, st)
        grs = p.sb("a_grs", [128, 1], F32, st)
        ogm = [p.sb(f"a_ogm{i}", [128, 256], BF16, st) for i in range(2)]
        ssm = [p.sb(f"a_ssm{i}", [128, 1052], F32, st) for i in range(2)]
        SC1, SH1 = MOD[:, 4096:5120], MOD[:, 3072:4096]
        fm_cols = [0, 128, 256, 384, 512, 640, 768, 1024]
        for tt in range(NTT):
            i2 = tt % 2
            tok = slice(tt * 128, (tt + 1) * 128)
            x = xt[i2]
            xk = ("a_x", i2)
            p.dma("sp", x[:], x1_d[tok, :], reads=[], writes=[xk])
            emit_modulate_T(p, rc, x[:], xk, SC1, SH1, hf, hb, hT, "a_", ptb_i=0)
            for ci, c0 in enumerate(fm_cols):
                b = ci // 4
                bk = rc.bank[b]
                for dt in range(8):
                    p.op("pe", lambda e, dt=dt, c0=c0, bk=bk, ci=ci: e.matmul(
                        bk[:, (ci % 4) * 128:(ci % 4 + 1) * 128], lhsT=WIN[:, dt, c0:c0 + 128], rhs=hT[:, dt, :],
                        start=(dt == 0), stop=(dt == 7)), reads=["WIN", "a_hT"], writes=[("bank", b)])
            f = fm[i2]
            p.op("act", lambda e, f=f: e.activation(out=f[:, 0:4, :].rearrange("p a b -> p (a b)"), in_=rc.bank[0][:],
                                                    func=AF.Identity, scale=0.125),
                 reads=[("bank", 0)], writes=[("a_fm", i2)])
            p.op("dve", lambda e, f=f: e.tensor_copy(out=f[:, 4:8, :].rearrange("p a b -> p (a b)"), in_=rc.bank[1][:]),
                 reads=[("bank", 1)], writes=[("a_fm", i2)])
            p.dma("act", QT_d[:, tok].rearrange("(c p) t -> p c t", p=128), f[:, 0:4, :], reads=[("a_fm", i2)])
            for k, dd in enumerate((KcT_d, VcT_d, KsT_d, KwT_d)):
                p.dma("act", dd[:, tok], f[:, 4 + k, :], reads=[("a_fm", i2)])
            def tm(bank_i, col_off, c0, c1):
                bk = rc.bank[bank_i]
                for dt in range(8):
                    p.op("pe", lambda e, dt=dt: e.matmul(bk[:, col_off:col_off + (c1 - c0)], lhsT=hT[:, dt, :],
                                                         rhs=WIN[:, dt, c0:c1], start=(dt == 0), stop=(dt == 7)),
                         reads=["WIN", "a_hT"], writes=[("bank", bank_i)])
            tm(2, 0, 896, 1024)
            tm(2, 128, 1152, 1304)
            tm(2, 280, 2716, 2844)
            tm(3, 0, 1304, 1816)
            tm(4, 0, 1816, 2328)
            tm(5, 0, 2328, 2840)
            vv = vsw[i2]
            p.op("dve", lambda e, vv=vv: e.tensor_copy(out=vv[:], in_=rc.bank[2][:, 0:256]),
                 reads=[("bank", 2)], writes=[("a_vsw", i2)])
            p.dma("act", Vs_d[tok, :], vv[:, 0:128], reads=[("a_vsw", i2)])
            p.dma("act", Vw_d[tok, :], vv[:, 128:256], reads=[("a_vsw", i2)])
            sm = ssm[i2]
            p.op("act", lambda e, sm=sm: e.activation(out=sm[:, 1028:1052], in_=rc.bank[2][:, 256:280], func=AF.Sigmoid),
                 reads=[("bank", 2)], writes=[("a_ssm", i2)])
            p.op("dve", lambda e, sm=sm: e.tensor_copy(out=sm[:, 1024:1028], in_=rc.bank[2][:, 280 + 124:280 + 128]),
                 reads=[("bank", 2)], writes=[("a_ssm", i2)])
            p.op("act", lambda e, sm=sm: e.activation(out=sm[:, 0:512], in_=rc.bank[4][:], func=AF.Copy),
                 reads=[("bank", 4)], writes=[("a_ssm", i2)])
            p.op("dve", lambda e, sm=sm: e.tensor_copy(out=sm[:, 512:1024], in_=rc.bank[5][:]),
                 reads=[("bank", 5)], writes=[("a_ssm", i2)])
            p.dma("sp", tmf_d[tok, :], sm[:], reads=[("a_ssm", i2)])
            p.op("act", lambda e: e.activation(out=gu[:], in_=rc.bank[3][:], func=AF.Gelu_apprx_tanh),
                 reads=[("bank", 3)], writes=["a_gu"])
            p.op("dve", lambda e: e.bn_stats(out=gst[:], in_=gu[:, 256:512]), reads=["a_gu"], writes=["a_gst"])
            p.op("dve", lambda e: e.bn_aggr(out=gmv[:], in_=gst[:]), reads=["a_gst"], writes=["a_gmv"])
            p.op("act", lambda e: e.activation(out=grs[:], in_=gmv[:, 1:2], func=AF.Sqrt, bias=rc.eps[:], scale=1.0),
                 reads=["a_gmv", "epsc"], writes=["a_grs"])
            p.op("dve", lambda e: e.reciprocal(out=grs[:], in_=grs[:]), reads=["a_grs"], writes=["a_grs"])
            p.op("dve", lambda e: e.tensor_scalar(out=vn[:], in0=gu[:, 256:512], scalar1=gmv[:, 0:1], scalar2=grs[:, 0:1],
                                                  op0=ALU.subtract, op1=ALU.mult),
                 reads=["a_gu", "a_gmv", "a_grs"], writes=["a_vn"])
            p.op("pool", lambda e: e.tensor_tensor(out=vn[:], in0=vn[:], in1=glng[:], op=ALU.mult),
                 reads=["a_vn", "glngb"], writes=["a_vn"])
            p.op("pool", lambda e: e.tensor_tensor(out=vnb[:], in0=vn[:], in1=glnb[:], op=ALU.add),
                 reads=["a_vn", "glngb"], writes=["a_vnb"])
            for g in range(4):
                p.op("pe", lambda e, g=g: e.matmul(rc.bank[3][:, g * 64:(g + 1) * 64], lhsT=WS[:, g, :],
                                                   rhs=vnb[:, g * 64:(g + 1) * 64], start=True, stop=True),
                     reads=["WS", "a_vnb"], writes=[("bank", 3)])
            og = ogm[i2]
            for g in range(4):
                p.op("dve", lambda e, g=g, og=og: e.scalar_tensor_tensor(
                    out=og[:, g * 64:(g + 1) * 64], in0=rc.bank[3][:, g * 64:(g + 1) * 64], scalar=BS[:, g:g + 1],
                    in1=gu[:, g * 64:(g + 1) * 64], op0=ALU.add, op1=ALU.mult),
                    reads=[("bank", 3), "a_gu", "BS"], writes=[("a_ogm", i2)])
            p.dma("act", ogm_d[tok, :], og[:], reads=[("a_ogm", i2)])
    p.finish()
    return nc


def build_C():
    nc = bass.Bass("TRN2", target_bir_lowering=False)
    di = lambda n, s, d=F32: nc.dram_tensor(n, list(s), d, kind="ExternalInput").ap()
    do = lambda n, s, d=F32: nc.dram_tensor(n, list(s), d, kind="ExternalOutput").ap()
    x1_d = di("x1", [TOKC, D])
    cT = di("cT", [128, 8])
    ada_w = di("ada_w", [D, 4096])
    ada_b = di("ada_b", [4096])
    lng1_d, lnb1_d = di("lng1", [D]), di("lnb1", [D])
    lng2_d, lnb2_d = di("lng2", [D]), di("lnb2", [D])
    w1, w3, w2 = di("w1", [D, DFF]), di("w3", [D, DFF]), di("w2", [DFF, D])
    w_out = di("w_out", [D, D])
    oatt_d = di("oatt", [TOKC, 512])
    ogm_d = di("ogm", [TOKC, 256], BF16)
    yssm_d = di("yssm", [TOKC, 256])
    zz_d = di("zz", [TOKC, 256])
    ng_d = di("normg", [256])
    consts = {"ident": di("ident", [128, 128])}
    x2_d = do("x2", [TOKC, D])
    x3_d = do("x3", [TOKC, D])

    p = Prog(nc)
    rc = RowCtx(p, nc, consts)
    MOD = p.sb("MOD", [128, 4096], F32)
    lng = p.sb("lng_t", [128, D], F32)
    lnb = p.sb("lnb_t", [128, D], F32)
    emit_mod(p, rc, None, cT, ada_w, ada_b, 4096, MOD,
             [(0, 1024, 1.0, 1.0), (2048, 3072, 1.0, 1.0), (3072, 4096, 1.0, 0.5)])
    p.dma("sp", lng[:], lng1_d.partition_broadcast(128), writes=["lngb"])
    p.dma("sp", lnb[:], lnb1_d.partition_broadcast(128), writes=["lngb"])
    with p.scope() as st:
        WO = p.sb("WO", [128, 8, D], BF16, st)
        for dt in range(8):
            p.dma("pool", WO[:, dt, :], w_out[dt * 128:(dt + 1) * 128, :], writes=["WO"])
        ng = p.sb("ng_t", [128, 256], F32, st)
        p.dma("sp", ng[:], ng_d.partition_broadcast(128), writes=["ng_t"])
        xt = [p.sb(f"c_x{i}", [128, D], F32, st) for i in range(2)]
        oa = [p.sb(f"c_oa{i}", [128, 512], F32, st) for i in range(2)]
        ys = [p.sb(f"c_ys{i}", [128, 512], F32, st) for i in range(2)]
        om = p.sb("c_om", [128, D], BF16, st)
        omT = p.sb("c_omT", [128, 8, 128], BF16, st)
        sz = p.sb("c_sz", [128, 256], F32, st)
        gg = p.sb("c_gg", [128, 256], F32, st)
        junk = p.sb("c_junk", [128, 128], F32, st)
        ss = p.sb("c_ss", [128, 2], F32, st)
        r = p.sb("c_r", [128, D], F32, st)
        xo = [p.sb(f"c_xo{i}", [128, D], F32, st) for i in range(2)]
        stats = p.sb("c_stats", [128, 12], F32, st)
        mv = p.sb("c_mv", [128, 2], F32, st)
        rstd = p.sb("c_rstd", [128, 1], F32, st)
        for tt in range(NTT):
            i2 = tt % 2
            tok = slice(tt * 128, (tt + 1) * 128)
            x, xk = xt[i2], ("c_x", i2)
            p.dma("sp", x[:], x1_d[tok, :], writes=[xk])
            p.dma("sp", oa[i2][:], oatt_d[tok, :], writes=[("c_oa", i2)])
            p.dma("pool", om[:, 512:768], ogm_d[tok, :], writes=["c_om_g"])
            p.dma("act", ys[i2][:, 0:256], yssm_d[tok, :], writes=[("c_ys", i2)])
            p.dma("act", ys[i2][:, 256:512], zz_d[tok, :], writes=[("c_ys", i2)])
            p.op("pool", lambda e, i2=i2: e.tensor_copy(out=om[:, 0:512], in_=oa[i2][:]), reads=[("c_oa", i2)], writes=["c_om_a"])
            p.op("act", lambda e, i2=i2: e.activation(out=sz[:], in_=ys[i2][:, 256:512], func=AF.Silu),
                 reads=[("c_ys", i2)], writes=["c_sz"])
            p.op("dve", lambda e, i2=i2: e.tensor_tensor(out=gg[:], in0=ys[i2][:, 0:256], in1=sz[:], op=ALU.mult),
                 reads=[("c_ys", i2), "c_sz"], writes=["c_gg"])
            p.op("dve", lambda e: e.tensor_tensor(out=sz[:], in0=gg[:], in1=gg[:], op=ALU.mult), reads=["c_gg"], writes=["c_sz"])
            for k in range(2):
                p.op("dve", lambda e, k=k: e.reduce_sum(out=ss[:, k:k + 1], in_=sz[:, k * 128:(k + 1) * 128],
                                                        axis=mybir.AxisListType.X), reads=["c_sz"], writes=[("c_ss", k)])
            p.op("act", lambda e: e.activation(out=ss[:], in_=ss[:], func=AF.Sqrt, bias=rc.eps[:], scale=1.0 / 128.0),
                 reads=[("c_ss", 0), ("c_ss", 1), "epsc"], writes=[("c_ss", 0), ("c_ss", 1)])
            p.op("dve", lambda e: e.reciprocal(out=ss[:], in_=ss[:]), reads=[("c_ss", 0), ("c_ss", 1)],
                 writes=[("c_ss", 0), ("c_ss", 1)])
            for k in range(2):
                p.op("dve", lambda e, k=k: e.scalar_tensor_tensor(out=om[:, 768 + k * 128:768 + (k + 1) * 128],
                                                                  in0=gg[:, k * 128:(k + 1) * 128], scalar=ss[:, k:k + 1],
                                                                  in1=ng[:, k * 128:(k + 1) * 128], op0=ALU.mult, op1=ALU.mult),
                     reads=["c_gg", ("c_ss", 0), ("c_ss", 1), "ng_t"], writes=[("c_om_s", k)])
            pt = rc.ptb[i2]
            for dt in range(8):
                p.op("pe", lambda e, dt=dt, pt=pt: e.transpose(out=pt[:, dt * 128:(dt + 1) * 128],
                                                               in_=om[:, dt * 128:(dt + 1) * 128], identity=rc.identb[:]),
                     reads=["c_om_a", "c_om_g", ("c_om_s", 0), ("c_om_s", 1), "identb"], writes=[("ptb", i2)])
            p.op("act", lambda e, pt=pt: e.activation(out=omT[:].rearrange("p a b -> p (a b)"), in_=pt[:], func=AF.Copy),
                 reads=[("ptb", i2)], writes=["c_omT"])
            for half in range(2):
                bk = rc.bank[2 + half]
                for dt in range(8):
                    p.op("pe", lambda e, dt=dt, bk=bk, half=half: e.matmul(bk[:], lhsT=omT[:, dt, :],
                                                                            rhs=WO[:, dt, half * 512:(half + 1) * 512],
                                                                            start=(dt == 0), stop=(dt == 7)),
                         reads=["c_omT", "WO"], writes=[("bank", 2 + half)])
            o = xo[i2]
            emit_res_ln(p, rc, (2, 3), x, xk, MOD[:, 0:1024], lng, lnb, r, stats, mv, rstd, o, "c_")
            p.dma("act", x2_d[tok, :], o[:], reads=["c_xo"])
    p.dma("sp", lng[:], lng2_d.partition_broadcast(128), writes=["lngb"])
    p.dma("sp", lnb[:], lnb2_d.partition_broadcast(128), writes=["lngb"])
    emit_ffn(p, rc, x2_d, x3_d, w1, w3, w2, MOD[:, 2048:3072], MOD[:, 1024:2048], MOD[:, 3072:4096], lng, lnb)
    p.finish()
    return nc


NCH = SEQ // 256


def build_S():
    nc = bass.Bass("TRN2", target_bir_lowering=False)
    di = lambda n, s, d=F32: nc.dram_tensor(n, list(s), d, kind="ExternalInput").ap()
    do = lambda n, s, d=F32: nc.dram_tensor(n, list(s), d, kind="ExternalOutput").ap()
    xc4_d = di("xc4", [SEQ, 4, 320])
    cw_d = di("cw", [4 * 320])
    cb_d = di("cb", [320])
    dtr_d = di("dtr", [SEQ])
    sc_d = di("scal", [4])
    tri_d = di("tri", [128, 128])
    triw_d = di("triw", [2, 128, 256])
    ntriw_d = di("ntriw", [2, 128, 256])
    ident_d = di("identf", [128, 128])
    y_d = do("y", [SEQ, 64])
    p = Prog(nc)
    bank = [p.ps(f"sbank{i}", [128, 512], F32) for i in range(5)]
    CW = p.sb("CW", [128, 4, 320], F32)
    CB = p.sb("CB", [128, 320], F32)
    DTR = p.sb("DTR", [128, SEQ // 128], F32)
    SCL = p.sb("SCL", [128, 4], F32)
    TRI = p.sb("TRI", [128, 128], F32)
    ONES = p.sb("ONES", [128, 128], F32)
    TRIW = p.sb("TRIW", [128, 2, 256], F32)
    NTRIW = p.sb("NTRIW", [128, 2, 256], F32)
    IDF = p.sb("IDF", [128, 128], F32)
    ANEG = p.sb("ANEG", [128, 1], F32)
    ONE1 = p.sb("ONE1", [128, 1], F32)
    S = p.sb("Sst", [128, 64], F32)
    p.dma("sp", CW[:].rearrange("p k n -> p (k n)"), cw_d.partition_broadcast(128), writes=["CW"])
    p.dma("sp", CB[:], cb_d.partition_broadcast(128), writes=["CB"])
    p.dma("sp", DTR[:], dtr_d.rearrange("(n p) -> p n", p=128), writes=["DTR"], allow_slow_non_contiguous=True)
    p.dma("sp", SCL[:], sc_d.partition_broadcast(128), writes=["SCL"])
    p.dma("act", TRI[:], tri_d, writes=["TRI"])
    p.dma("act", TRIW[:], triw_d.rearrange("i p n -> p i n"), writes=["TRIW"])
    p.dma("act", NTRIW[:], ntriw_d.rearrange("i p n -> p i n"), writes=["NTRIW"])
    p.dma("act", IDF[:], ident_d, writes=["IDF"])
    p.op("dve", lambda e: e.memset(ONES[:], 1.0), writes=["ONES"])
    p.op("dve", lambda e: e.memset(ONE1[:], 1.0), writes=["ONE1"])
    p.op("dve", lambda e: e.memset(S[:], 0.0), writes=["S"])
    p.op("act", lambda e: e.activation(out=ANEG[:], in_=SCL[:, 1:2], func=AF.Exp), reads=["SCL"], writes=["ANEG"])
    p.op("dve", lambda e: e.tensor_scalar(out=ANEG[:], in0=ANEG[:], scalar1=-1.0, scalar2=None, op0=ALU.mult),
         reads=["ANEG"], writes=["ANEG"])
    X4 = [p.sb(f"s_x4{i}", [128, 2, 4, 320], F32) for i in range(2)]
    acc = p.sb("s_acc", [128, 2, 320], F32)
    tmp = p.sb("s_tmp", [128, 2, 320], F32)
    XA = p.sb("s_XA", [128, 2, 320], F32)
    dx = p.sb("s_dx", [128, 2], F32)
    dax = p.sb("s_dax", [128, 2], F32)
    dl = p.sb("s_dl", [128, 2], F32)
    dtt = p.sb("s_dt", [128, 2], F32)
    aa = p.sb("s_a", [128, 2], F32)
    abc = p.sb("s_abc", [128, 2, 128], F32)
    cscol = p.sb("s_cscol", [128, 2], F32)
    cl = p.sb("s_cl", [128, 1], F32)
    ecl = p.sb("s_ecl", [128, 1], F32)
    dec = p.sb("s_dec", [128, 2], F32)
    lt = p.sb("s_lt", [128, 2, 256], F32)
    ecs = p.sb("s_ecs", [128, 256], F32)
    BCT = p.sb("s_BCT", [128, 512], F32)
    scT = p.sb("s_scT", [128, 2, 256], F32)
    CsT = p.sb("s_CsT", [128, 256], F32)
    Xd = p.sb("s_Xd", [128, 2, 64], F32)
    Xdd = p.sb("s_Xdd", [128, 2, 64], F32)
    yo = [p.sb(f"s_yo{i}", [128, 2, 64], F32) for i in range(2)]
    for c in range(NCH):
        i2 = c % 2
        x4 = X4[i2]
        p.dma("sp" if i2 == 0 else "act", x4[:].rearrange("p j k n -> p j (k n)"),
              xc4_d[c * 256:(c + 1) * 256].rearrange("(j p) k n -> p j (k n)", p=128), writes=[("s_x4", i2)])
        for j in range(2):
            eng = "dve" if j == 0 else "pool"
            p.op(eng, lambda e, j=j, x4=x4: e.tensor_tensor(out=acc[:, j, :], in0=x4[:, j, 0, :], in1=CW[:, 0, :], op=ALU.mult),
                 reads=[("s_x4", i2), "CW"], writes=[("s_acc", j)])
            for k in range(1, 4):
                p.op(eng, lambda e, j=j, k=k, x4=x4: e.tensor_tensor(out=tmp[:, j, :], in0=x4[:, j, k, :], in1=CW[:, k, :], op=ALU.mult),
                     reads=[("s_x4", i2), "CW"], writes=[("s_tmp", j)])
                p.op(eng, lambda e, j=j: e.tensor_tensor(out=acc[:, j, :], in0=acc[:, j, :], in1=tmp[:, j, :], op=ALU.add),
                     reads=[("s_acc", j), ("s_tmp", j)], writes=[("s_acc", j)])
            p.op(eng, lambda e, j=j: e.tensor_tensor(out=acc[:, j, :], in0=acc[:, j, :], in1=CB[:], op=ALU.add),
                 reads=[("s_acc", j), "CB"], writes=[("s_acc", j)])
        p.op("act", lambda e: e.activation(out=XA[:].rearrange("p j n -> p (j n)"), in_=acc[:].rearrange("p j n -> p (j n)"),
                                           func=AF.Silu), reads=[("s_acc", 0), ("s_acc", 1)], writes=["s_XA"])
        p.op("dve", lambda e, c=c: e.tensor_scalar(out=dx[:], in0=DTR[:, 2 * c:2 * c + 2], scalar1=SCL[:, 0:1], scalar2=None,
                                                    op0=ALU.add), reads=["DTR", "SCL"], writes=["s_dx"])
        p.op("act", lambda e: e.activation(out=dax[:], in_=dx[:], func=AF.Abs), reads=["s_dx"], writes=["s_dax"])
        p.op("act", lambda e: e.activation(out=dl[:], in_=dax[:], func=AF.Exp, scale=-1.0), reads=["s_dax"], writes=["s_dl"])
        p.op("act", lambda e: e.activation(out=dl[:], in_=dl[:], func=AF.Ln, bias=ONE1[:], scale=1.0),
             reads=["s_dl", "ONE1"], writes=["s_dl"])
        p.op("dve", lambda e: e.scalar_tensor_tensor(out=dtt[:], in0=dx[:], scalar=0.0, in1=dl[:], op0=ALU.max, op1=ALU.add),
             reads=["s_dx", "s_dl"], writes=["s_dt"])
        p.op("dve", lambda e: e.tensor_scalar(out=aa[:], in0=dtt[:], scalar1=ANEG[:, 0:1], scalar2=None, op0=ALU.mult),
             reads=["s_dt", "ANEG"], writes=["s_a"])
        for i in range(2):
            p.op("dve", lambda e, i=i: e.tensor_scalar(out=abc[:, i, :], in0=ONES[:], scalar1=aa[:, i:i + 1], scalar2=None,
                                                       op0=ALU.mult), reads=["s_a", "ONES"], writes=[("s_abc", i)])
        for i in range(2):
            p.op("pe", lambda e, i=i: e.matmul(bank[0][:, 0:256], lhsT=abc[:, i, :], rhs=TRIW[:, i, :], start=(i == 0), stop=(i == 1)),
                 reads=[("s_abc", i), "TRIW"], writes=[("sbank", 0)])
        p.op("pe", lambda e: e.matmul(bank[0][:, 256:257], lhsT=TRI[:], rhs=aa[:, 0:1], start=True, stop=True),
             reads=["TRI", "s_a"], writes=[("sbank", 0)])
        p.op("pe", lambda e: e.matmul(bank[0][:, 257:258], lhsT=ONES[:], rhs=aa[:, 0:1], start=True, stop=False),
             reads=["ONES", "s_a"], writes=[("sbank", 0)])
        p.op("pe", lambda e: e.matmul(bank[0][:, 257:258], lhsT=TRI[:], rhs=aa[:, 1:2], start=False, stop=True),
             reads=["TRI", "s_a"], writes=[("sbank", 0)])
        p.op("dve", lambda e: e.tensor_copy(out=cscol[:], in_=bank[0][:, 256:258]), reads=[("sbank", 0)], writes=["s_cscol"])
        p.op("dve", lambda e: e.tensor_copy(out=cl[:], in_=bank[0][:, 255:256]), reads=[("sbank", 0)], writes=["s_cl"])
        for i in range(2):
            p.op("dve", lambda e, i=i: e.scalar_tensor_tensor(out=lt[:, i, :], in0=bank[0][:, 0:256], scalar=cscol[:, i:i + 1],
                                                              in1=NTRIW[:, i, :], op0=ALU.subtract, op1=ALU.add),
                 reads=[("sbank", 0), "s_cscol", "NTRIW"], writes=[("s_lt", i)])
            p.op("act", lambda e, i=i: e.activation(out=lt[:, i, :], in_=lt[:, i, :], func=AF.Exp),
                 reads=[("s_lt", i)], writes=[("s_lt", i)])
        p.op("act", lambda e: e.activation(out=ecs[:], in_=bank[0][:, 0:256], func=AF.Exp), reads=[("sbank", 0)], writes=["s_ecs"])
        for j in range(2):
            p.op("pe", lambda e, j=j: e.transpose(out=bank[1][:, j * 128:(j + 1) * 128], in_=XA[:, j, 64:192], identity=IDF[:]),
                 reads=["s_XA", "IDF"], writes=[("sbank", 1)])
            p.op("pe", lambda e, j=j: e.transpose(out=bank[1][:, 256 + j * 128:256 + (j + 1) * 128], in_=XA[:, j, 192:320],
                                                  identity=IDF[:]), reads=["s_XA", "IDF"], writes=[("sbank", 1)])
        p.op("act", lambda e: e.activation(out=BCT[:], in_=bank[1][:], func=AF.Copy), reads=[("sbank", 1)], writes=["s_BCT"])
        for i in range(2):
            p.op("pe", lambda e, i=i: e.matmul(bank[2][:, i * 256:(i + 1) * 256], lhsT=BCT[:, i * 128:(i + 1) * 128],
                                               rhs=BCT[:, 256:512], start=True, stop=True),
                 reads=["s_BCT"], writes=[("sbank", 2)])
        for i in range(2):
            p.op("dve", lambda e, i=i: e.tensor_tensor(out=scT[:, i, :], in0=bank[2][:, i * 256:(i + 1) * 256], in1=lt[:, i, :],
                                                       op=ALU.mult), reads=[("sbank", 2), ("s_lt", i)], writes=[("s_scT", i)])
        p.op("pool", lambda e: e.tensor_tensor(out=CsT[:], in0=BCT[:, 256:512], in1=ecs[:], op=ALU.mult),
             reads=["s_BCT", "s_ecs"], writes=["s_CsT"])
        for i in range(2):
            p.op("dve", lambda e, i=i: e.tensor_scalar(out=Xd[:, i, :], in0=XA[:, i, 0:64], scalar1=dtt[:, i:i + 1], scalar2=None,
                                                       op0=ALU.mult), reads=["s_XA", "s_dt"], writes=[("s_Xd", i)])
        for j in range(2):
            ops = [(scT[:, i, j * 128:(j + 1) * 128], Xd[:, i, :], [("s_scT", i), ("s_Xd", i)]) for i in range(j + 1)]
            ops.append((CsT[:, j * 128:(j + 1) * 128], S[:], ["s_CsT", "S"]))
            for n, (lh, rh, rd) in enumerate(ops):
                p.op("pe", lambda e, lh=lh, rh=rh, n=n, j=j, last=(n == len(ops) - 1): e.matmul(
                    bank[3][:, j * 64:(j + 1) * 64], lhsT=lh, rhs=rh, start=(n == 0), stop=last),
                    reads=rd, writes=[("sbank", 3)])
        y2 = yo[i2]
        for j in range(2):
            p.op("dve", lambda e, j=j, y2=y2: e.scalar_tensor_tensor(out=y2[:, j, :], in0=XA[:, j, 0:64], scalar=SCL[:, 2:3],
                                                                      in1=bank[3][:, j * 64:(j + 1) * 64], op0=ALU.mult, op1=ALU.add),
                 reads=["s_XA", "SCL", ("sbank", 3)], writes=[("s_yo", i2)])
        p.dma("act", y_d[c * 256:(c + 1) * 256, :].rearrange("(j p) n -> p j n", p=128), y2[:], reads=[("s_yo", i2)])
        p.op("act", lambda e: e.activation(out=dec[:], in_=cscol[:], func=AF.Exp, bias=cl[:], scale=-1.0),
             reads=["s_cscol", "s_cl"], writes=["s_dec"])
        for i in range(2):
            p.op("dve", lambda e, i=i: e.tensor_scalar(out=Xdd[:, i, :], in0=Xd[:, i, :], scalar1=dec[:, i:i + 1], scalar2=None,
                                                       op0=ALU.mult), reads=[("s_Xd", i), "s_dec"], writes=[("s_Xdd", i)])
        for i in range(2):
            p.op("pe", lambda e, i=i: e.matmul(bank[4][:, 0:64], lhsT=XA[:, i, 64:192], rhs=Xdd[:, i, :], start=(i == 0), stop=(i == 1)),
                 reads=["s_XA", ("s_Xdd", i)], writes=[("sbank", 4)])
        p.op("act", lambda e: e.activation(out=ecl[:], in_=cl[:], func=AF.Exp), reads=["s_cl"], writes=["s_ecl"])
        p.op("dve", lambda e: e.scalar_tensor_tensor(out=S[:], in0=S[:], scalar=ecl[:, 0:1], in1=bank[4][:, 0:64],
                                                     op0=ALU.mult, op1=ALU.add), reads=["S", "s_ecl", ("sbank", 4)], writes=["S"])
    p.finish()
    return nc


NQI = SEQ // 256
NQF = SEQ // 256


def build_N():
    nc = bass.Bass("TRN2", target_bir_lowering=False)
    di = lambda n, s, d=F32: nc.dram_tensor(n, list(s), d, kind="ExternalInput").ap()
    do = lambda n, s, d=F32: nc.dram_tensor(n, list(s), d, kind="ExternalOutput").ap()
    QT_d = di("QT", [64, 4, NQF, 128], BF16)
    blk_d = di("blk", [2, 16, 128, 1024], BF16)
    w1_d = di("cw1", [2, 2048, 256])
    b1_d = di("cb1", [2, 128, 2])
    pe_d = di("cpe", [2, 128, 16])
    w2_d = di("cw2", [2, 256, 64])
    KsT_d = di("KsT", [64, SEQ], BF16)
    VsA_d = di("VsA", [SEQ, 65], BF16)
    KwT_d = di("KwT", [64, SEQ + 512], BF16)
    VwA_d = di("VwA", [SEQ + 512, 65], BF16)
    gates_d = di("gates", [NQF, 128, 12])
    TC_d = di("TC", [10, 128, 512])
    TS_d = di("TS", [4, 128, 512])
    TW_d = di("TW", [5, 128, 512])
    OV_d = di("OV", [8, 128, 256])
    FB_d = di("FB", [NQF, 128, 256])
    EX_d = di("EX", [128, 64 * 128])
    ident_d = di("ident", [128, 128])
    o_d = do("o", [NQF, 128, 256])
    p = Prog(nc)
    SB = [p.ps(f"nS{i}", [128, 512], F32) for i in range(2)]
    OB = [p.ps(f"nO{i}", [128, 512], F32) for i in range(3)]
    IMPB = [p.ps(f"nI{i}", [128, 512], F32) for i in range(2)]
    PT = p.ps("nPT", [128, 1024], BF16)
    identb = p.sb("identb", [128, 128], BF16)
    p.dma("pool", identb[:], ident_d, writes=["identb"])
    KC = p.sb("KC", [64, 1024], BF16)
    VC = p.sb("VC", [128, 8, 65], BF16)
    p.op("dve", lambda e: e.memset(VC[:], 1.0), writes=["VC"])
    import os
    NSK = os.environ.get("NSKIP", "")
    with p.scope() as st:
        W1 = p.sb("n_W1", [128, 2, 16, 256], BF16, st)
        W2 = p.sb("n_W2", [128, 2, 2, 64], BF16, st)
        PE2 = p.sb("n_PE", [128, 2, 16], BF16, st)
        B1 = p.sb("n_B1", [128, 2, 2], F32, st)
        hid = p.sb("n_hid", [128, 2, 2, 1024], BF16, st)
        blk = [p.sb(f"n_blk{i}", [128, 1024], BF16, st) for i in range(2)]
        for j in range(2):
            p.dma("pool", W1[:, j, :, :], w1_d[j].rearrange("(lt p) n -> p lt n", p=128), writes=["n_W1"])
            p.dma("pool", W2[:, j, :, :], w2_d[j].rearrange("(ht p) n -> p ht n", p=128), writes=["n_W2"])
            p.dma("pool", PE2[:, j, :], pe_d[j], writes=["n_PE"])
            p.dma("sp", B1[:, j, :], b1_d[j], writes=["n_B1"])
        for j in range(2):
            for ht in range(2):
                for lt in range(16):
                    p.op("pe", lambda e, j=j, ht=ht, lt=lt: e.matmul(SB[0][:, 0:1], lhsT=W1[:, j, lt, ht * 128:(ht + 1) * 128],
                                                                      rhs=PE2[:, j, lt:lt + 1], start=(lt == 0), stop=(lt == 15)),
                         reads=["n_W1", "n_PE"], writes=[("nS", 0)])
                p.op("dve", lambda e, j=j, ht=ht: e.tensor_tensor(out=B1[:, j, ht:ht + 1], in0=B1[:, j, ht:ht + 1], in1=SB[0][:, 0:1],
                                                                  op=ALU.add), reads=[("nS", 0), "n_B1"], writes=["n_B1"])
            for lt in range(16):
                bb = blk[lt % 2]
                p.dma("sp" if lt % 2 == 0 else "act", bb[:], blk_d[j, lt], writes=[("n_blk", lt % 2)])
                for ht in range(2):
                    for ic in range(2):
                        bank = (SB + OB + IMPB)[ht * 2 + ic + 2]
                        p.op("pe", lambda e, j=j, ht=ht, ic=ic, lt=lt, bb=bb, bank=bank: e.matmul(
                            bank[:], lhsT=W1[:, j, lt, ht * 128:(ht + 1) * 128], rhs=bb[:, ic * 512:(ic + 1) * 512],
                            start=(lt == 0), stop=(lt == 15)), reads=["n_W1", ("n_blk", lt % 2)], writes=[("ncmp", ht, ic)])
            for ht in range(2):
                for ic in range(2):
                    bank = (SB + OB + IMPB)[ht * 2 + ic + 2]
                    p.op("act", lambda e, j=j, ht=ht, ic=ic, bank=bank: e.activation(
                        out=hid[:, j, ht, ic * 512:(ic + 1) * 512], in_=bank[:], func=AF.Gelu_apprx_tanh, bias=B1[:, j, ht:ht + 1], scale=1.0),
                        reads=[("ncmp", ht, ic), "n_B1"], writes=[("n_hid", j)])
        for ic in range(2):
            for ht in range(2):
                p.op("pe", lambda e, ic=ic, ht=ht: e.matmul(SB[0][0:64, :], lhsT=W2[:, 0, ht, :], rhs=hid[:, 0, ht, ic * 512:(ic + 1) * 512],
                                                            start=(ht == 0), stop=(ht == 1)), reads=["n_W2", ("n_hid", 0)], writes=[("nS", 0)])
            p.op("act", lambda e, ic=ic: e.activation(out=KC[:, ic * 512:(ic + 1) * 512], in_=SB[0][0:64, :], func=AF.Copy),
                 reads=[("nS", 0)], writes=["KC"])
        for it in range(8):
            for ht in range(2):
                p.op("pe", lambda e, it=it, ht=ht: e.matmul(SB[1][:, it * 64:(it + 1) * 64], lhsT=hid[:, 1, ht, it * 128:(it + 1) * 128],
                                                            rhs=W2[:, 1, ht, :], start=(ht == 0), stop=(ht == 1)),
                     reads=["n_W2", ("n_hid", 1)], writes=[("nS", 1)])
        p.op("act", lambda e: e.activation(out=VC[:, :, 0:64], in_=SB[1][:].rearrange("p (a b) -> p a b", b=64), func=AF.Copy),
             reads=[("nS", 1)], writes=["VC"])
    KsT = p.sb("KsT_sb", [64, SEQ], BF16)
    VsA = p.sb("VsA_sb", [128, SEQ // 128, 65], BF16)
    for c4 in range(4):
        p.dma("sp", KsT[:, c4 * 4096:(c4 + 1) * 4096], KsT_d[:, c4 * 4096:(c4 + 1) * 4096], writes=["KsT"])
        p.dma("act", VsA[:, c4 * 32:(c4 + 1) * 32, :], VsA_d[c4 * 4096:(c4 + 1) * 4096, :].rearrange("(n p) c -> p n c", p=128),
              writes=["VsA"])
    TC = p.sb("TC_sb", [128, 10, 512], F32)
    TS = p.sb("TS_sb", [128, 4, 512], F32)
    TW = p.sb("TW_sb", [128, 5, 512], F32)
    OV = p.sb("OV_sb", [128, 8, 256], BF16)
    EX = p.sb("EX_sb", [128, 64, 128], BF16)
    p.dma("sp", TC[:], TC_d.rearrange("n p c -> p n c"), writes=["TC"])
    p.dma("sp", TS[:], TS_d.rearrange("n p c -> p n c"), writes=["TS"])
    p.dma("sp", TW[:], TW_d.rearrange("n p c -> p n c"), writes=["TW"])
    p.dma("pool", OV[:], OV_d.rearrange("n p c -> p n c"), writes=["OV"])
    p.dma("pool", EX[:].rearrange("p a b -> p (a b)"), EX_d, writes=["EX"])
    Qt = [p.sb(f"n_q{i}", [64, 4, 128], BF16) for i in range(2)]
    GT = [p.sb(f"n_g{i}", [128, 12], F32) for i in range(2)]
    FBt = [p.sb(f"n_fb{i}", [128, 256], F32) for i in range(2)]
    Kw = [p.sb(f"n_kw{i}", [64, 640], BF16) for i in range(2)]
    Vw = [p.sb(f"n_vw{i}", [128, 5, 65], BF16) for i in range(2)]
    sbt = [p.sb(f"n_sb{i}", [128, 512], F32) for i in range(2)]
    Et = [p.sb(f"n_E{i}", [128, 512], BF16) for i in range(2)]
    rsc = p.sb("n_rsc", [128, 4], F32)
    imp = p.sb("n_imp", [128, 256], F32)
    imp2 = p.sb("n_imp2", [128, 256], F32)
    m8 = p.sb("n_m8", [128, 8], F32)
    m8b = p.sb("n_m8b", [128, 8], F32)
    thr = p.sb("n_thr", [128, 1], F32)
    negm = p.sb("n_negm", [128, 256], BF16)
    NegMT = p.sb("n_negmT", [128, 2, 128], BF16)
    wbr = p.sb("n_wbr", [128, 3, 4], F32)
    ot = [p.sb(f"n_o{i}", [128, 256], F32) for i in range(2)]
    cnt = [0]

    def att_s(t, qt_ap, qkey):
        b = cnt[0] % 2
        cnt[0] += 1
        S, sbb, E = SB[b], sbt[b], Et[b]
        mask = t.get("mask")
        p.op("pe", lambda e: e.matmul(S[:], lhsT=t["kT"], rhs=qt_ap, start=True, stop=(mask is None)),
             reads=t["kkeys"] + [qkey], writes=[("nS", b)])
        if mask is not None:
            ex, nm = mask
            p.op("pe", lambda e: e.matmul(S[:].rearrange("p (h q) -> p h q", h=4), lhsT=ex,
                                          rhs=nm.unsqueeze(1).to_broadcast([128, 4, 128]), start=False, stop=True),
                 reads=["EX", "n_negmT"], writes=[("nS", b)])
        p.op("dve", lambda e: e.tensor_tensor(out=sbb[:], in0=S[:], in1=t["table"], op=ALU.add),
             reads=[("nS", b)] + t["tkeys"], writes=[("n_sb", b)])
        p.op("act", lambda e: e.activation(out=E[:], in_=sbb[:], func=AF.Exp), reads=[("n_sb", b)], writes=[("n_E", b)])
        return E, b

    def att_pv(t, E, b):
        acc_i = t["acc"]
        for h in range(4):
            p.op("pe", lambda e, h=h: e.matmul(OB[acc_i][:, h * 65:(h + 1) * 65], lhsT=E[:, h * 128:(h + 1) * 128], rhs=t["v"],
                                               start=False, stop=False, skip_group_check=True),
                 reads=[("n_E", b)] + t["vkeys"], writes=[("nO", acc_i)])
        kc = t.get("imp_kc")
        if kc is not None:
            for h in range(4):
                p.op("pe", lambda e, h=h: e.matmul(IMPB[h // 2][:, (h % 2) * 256:(h % 2 + 1) * 256],
                                                   lhsT=E[:, h * 128:(h + 1) * 128], rhs=OV[:, kc, :],
                                                   start=False, stop=False, skip_group_check=True),
                     reads=[("n_E", b), "OV"], writes=[("nI", h // 2)])

    def run_tiles(tiles, qt_ap, qkey):
        prev = None
        for t in tiles:
            cur = att_s(t, qt_ap, qkey)
            if prev is not None:
                att_pv(*prev)
            prev = (t, cur[0], cur[1])
        if prev is not None:
            att_pv(*prev)

    for i in range(NQI):
        i2 = i % 2
        q, g_, fb, kw, vw = Qt[i2], GT[i2], FBt[i2], Kw[i2], Vw[i2]
        p.dma("sp", q[:], QT_d[:, :, i, :], writes=[("n_q", i2)])
        p.dma("sp", g_[:], gates_d[i], writes=[("n_g", i2)])
        p.dma("sp", fb[:], FB_d[i], writes=[("n_fb", i2)])
        p.dma("act", kw[:], KwT_d[:, 2 * i * 128:(2 * i + 5) * 128], writes=[("n_kw", i2)])
        p.dma("act", vw[:], VwA_d[2 * i * 128:(2 * i + 5) * 128, :].rearrange("(n p) c -> p n c", p=128), writes=[("n_vw", i2)])
        for a in range(3):
            p.op("dve", lambda e, a=a: e.memset(OB[a][:], 0.0), writes=[("nO", a)])
        for a in range(2):
            p.op("dve", lambda e, a=a: e.memset(IMPB[a][:], 0.0), writes=[("nI", a)])
        qap = q[:].rearrange("p h q -> p (h q)")
        qk = ("n_q", i2)
        tiles = []
        for kc in range((2 * i + 1) // 16 + 1):
            e_ = 2 * i - 16 * kc
            tidx = e_ // 2 if e_ <= 16 else 9
            tiles.append(dict(kT=KC[:, kc * 128:(kc + 1) * 128], kkeys=["KC"], v=VC[:, kc, :], vkeys=["VC"],
                              table=TC[:, tidx, :], tkeys=["TC"], acc=0, imp_kc=kc))
        for d in range(5):
            tiles.append(dict(kT=kw[:, d * 128:(d + 1) * 128], kkeys=[("n_kw", i2)], v=vw[:, d, :], vkeys=[("n_vw", i2)],
                              table=TW[:, d, :], tkeys=["TW"], acc=2))
        run_tiles(tiles, qap, qk)
        sums = lambda a: OB[a][:, 0:260].rearrange("p (h c) -> p h c", c=65)[:, :, 64]
        p.op("dve", lambda e: e.tensor_scalar(out=rsc[:], in0=sums(0), scalar1=1e-30, scalar2=None, op0=ALU.max),
             reads=[("nO", 0)], writes=["n_rsc"])
        p.op("dve", lambda e: e.reciprocal(out=rsc[:], in_=rsc[:]), reads=["n_rsc"], writes=["n_rsc"])
        for h in range(4):
            p.op("dve", lambda e, h=h, fb=fb: e.scalar_tensor_tensor(
                out=imp[:], in0=IMPB[h // 2][:, (h % 2) * 256:(h % 2 + 1) * 256], scalar=rsc[:, h:h + 1],
                in1=(fb[:] if h == 0 else imp[:]), op0=ALU.mult, op1=ALU.add),
                reads=[("nI", h // 2), "n_rsc", ("n_fb", i2), "n_imp"], writes=["n_imp"])
        p.op("dve", lambda e: e.max(out=m8[:], in_=imp[:]), reads=["n_imp"], writes=["n_m8"])
        p.op("dve", lambda e: e.match_replace(out=imp2[:], in_to_replace=m8[:], in_values=imp[:], imm_value=-1e9),
             reads=["n_imp", "n_m8"], writes=["n_imp2"])
        p.op("dve", lambda e: e.max(out=m8b[:], in_=imp2[:]), reads=["n_imp2"], writes=["n_m8b"])
        p.op("dve", lambda e: e.tensor_scalar(out=thr[:], in0=m8b[:, 7:8], scalar1=-5000.0, scalar2=None, op0=ALU.max),
             reads=["n_m8b"], writes=["n_thr"])
        p.op("dve", lambda e: e.tensor_scalar(out=imp2[:], in0=imp[:], scalar1=thr[:, 0:1], scalar2=None, op0=ALU.is_ge),
             reads=["n_imp", "n_thr"], writes=["n_imp2"])
        p.op("dve", lambda e: e.tensor_scalar(out=negm[:], in0=imp2[:], scalar1=-1.0, scalar2=30000.0, op0=ALU.add, op1=ALU.mult),
             reads=["n_imp2"], writes=["n_negm"])
        for jc in range(2):
            p.op("pe", lambda e, jc=jc: e.transpose(out=PT[:, jc * 128:(jc + 1) * 128], in_=negm[:, jc * 128:(jc + 1) * 128],
                                                    identity=identb[:]), reads=["n_negm", "identb"], writes=["nPT"])
        p.op("act", lambda e: e.activation(out=NegMT[:].rearrange("p a b -> p (a b)"), in_=PT[:, 0:256], func=AF.Copy),
             reads=["nPT"], writes=["n_negmT"])
        tiles = []
        for kt in range(2 * i + 2):
            d3 = kt - (2 * i - 1)
            tb = TS[:, d3, :] if d3 >= 0 else TS[:, 3, :]
            tiles.append(dict(kT=KsT[:, kt * 128:(kt + 1) * 128], kkeys=["KsT"], v=VsA[:, kt, :], vkeys=["VsA"],
                              table=tb, tkeys=["TS"], acc=1, mask=(EX[:, kt % 64, :], NegMT[:, kt // 64, :])))
        run_tiles(tiles, qap, qk)
        o = ot[i2]
        for a in range(3):
            p.op("dve", lambda e, a=a: e.tensor_scalar(out=wbr[:, a, :], in0=sums(a), scalar1=1e-30, scalar2=None, op0=ALU.max),
                 reads=[("nO", a)], writes=[("n_wbr", a)])
            p.op("dve", lambda e, a=a: e.reciprocal(out=wbr[:, a, :], in_=wbr[:, a, :]), reads=[("n_wbr", a)], writes=[("n_wbr", a)])
            p.op("dve", lambda e, a=a, g_=g_: e.tensor_tensor(out=wbr[:, a, :], in0=wbr[:, a, :],
                                                               in1=g_[:].rearrange("p (h c) -> p h c", c=3)[:, :, a], op=ALU.mult),
                 reads=[("n_wbr", a), ("n_g", i2)], writes=[("n_wbr", a)])
        for h in range(4):
            for a in range(3):
                src = OB[a][:, h * 65:h * 65 + 64]
                if a == 0:
                    p.op("dve", lambda e, h=h, src=src, o=o: e.tensor_scalar(out=o[:, h * 64:(h + 1) * 64], in0=src,
                                                                            scalar1=wbr[:, 0, h:h + 1], scalar2=None, op0=ALU.mult),
                         reads=[("nO", 0), ("n_wbr", 0)], writes=[("n_o", i2)])
                else:
                    p.op("dve", lambda e, h=h, a=a, src=src, o=o: e.scalar_tensor_tensor(
                        out=o[:, h * 64:(h + 1) * 64], in0=src, scalar=wbr[:, a, h:h + 1], in1=o[:, h * 64:(h + 1) * 64],
                        op0=ALU.mult, op1=ALU.add), reads=[("nO", a), ("n_wbr", a), ("n_o", i2)], writes=[("n_o", i2)])
        p.dma("act", o_d[i], o[:], reads=[("n_o", i2)])
    p.finish()
    return nc


_CONST = {}


def _consts():
    if not _CONST:
        _CONST["ident"] = np.eye(128, dtype=np.float32)
        s = np.arange(128)
        _CONST["tri"] = (s[:, None] <= s[None, :]).astype(np.float32)
    return _CONST


_NC_CACHE = {}


def _get_nc(name, builder):
    return builder()


def run_A(inp, l, x_cur):
    cst = _consts()
    c = inp["c"]
    maps = []
    cols = np.r_[0:3072, 3072:5120]
    ada_w = np.ascontiguousarray(inp["ada_w"][l][:, cols])
    ada_b = np.ascontiguousarray(inp["ada_b"][l][cols])
    wsT = np.ascontiguousarray(inp["gmlp_ws"][l].transpose(0, 2, 1))
    bsT = np.ascontiguousarray(inp["gmlp_bs"][l].T)
    for r in range(NCORE):
        b = r // 4
        maps.append({
            "x": np.ascontiguousarray(x_cur[r * TOKC:(r + 1) * TOKC]),
            "cT": np.ascontiguousarray(c[b].reshape(8, 128).T),
            "ada_w": ada_w, "ada_b": ada_b,
            "lng": inp["ln_g"][l, 0], "lnb": inp["ln_b"][l, 0],
            "w1": inp["ffn_w1"][l, 0], "w3": inp["ffn_w3"][l, 0], "w2": inp["ffn_w2"][l, 0],
            "w_in": inp["w_in"][l],
            "glng": inp["gmlp_ln_g"][l], "glnb": inp["gmlp_ln_b"][l],
            "wsT": wsT, "bsT": bsT,
            "ident": cst["ident"], "tri": cst["tri"],
        })
    nc = build_A()
    res = run_bass_kernel_spmd(nc, maps, core_ids=list(range(NCORE)))
    return res.results


def run_C(inp, l, x1, oatt, ogm, yssm, zz):
    cst = _consts()
    c = inp["c"]
    cols = np.r_[5120:9216]
    ada_w = np.ascontiguousarray(inp["ada_w"][l][:, cols])
    ada_b = np.ascontiguousarray(inp["ada_b"][l][cols])
    maps = []
    for r in range(NCORE):
        b = r // 4
        sl = slice(r * TOKC, (r + 1) * TOKC)
        maps.append({
            "x1": np.ascontiguousarray(x1[sl]), "cT": np.ascontiguousarray(c[b].reshape(8, 128).T),
            "ada_w": ada_w, "ada_b": ada_b,
            "lng1": inp["ln_g"][l, 1], "lnb1": inp["ln_b"][l, 1], "lng2": inp["ln_g"][l, 2], "lnb2": inp["ln_b"][l, 2],
            "w1": inp["ffn_w1"][l, 1], "w3": inp["ffn_w3"][l, 1], "w2": inp["ffn_w2"][l, 1],
            "w_out": inp["w_out"][l],
            "oatt": np.ascontiguousarray(oatt[sl]), "ogm": np.ascontiguousarray(ogm[sl]),
            "yssm": np.ascontiguousarray(yssm[sl]), "zz": np.ascontiguousarray(zz[sl]),
            "normg": inp["ssm_norm_g"][l], "ident": cst["ident"],
        })
    nc = build_C()
    res = run_bass_kernel_spmd(nc, maps, core_ids=list(range(NCORE)))
    return res.results


def run_S(inp, l, xbc, dtr):
    cst = _consts()
    s_ = np.arange(128)
    l_ = np.arange(256)
    triw = np.stack([(s_[:, None] + 128 * i <= l_[None, :]).astype(np.float32) for i in range(2)])
    ntriw = ((triw - 1.0) * 30000.0).astype(np.float32)
    maps = []
    cwl, cbl = inp["ssm_conv_w"][l], inp["ssm_conv_b"][l]
    for r in range(NCORE):
        b, h = r // 4, r % 4
        g = h // 2
        cols = np.r_[64 * h:64 * h + 64, 256 + 128 * g:256 + 128 * g + 128, 512 + 128 * g:512 + 128 * g + 128]
        xcat = xbc[b][:, cols]
        xpad = np.concatenate([np.zeros((3, 320), np.float32), xcat], 0)
        xc4 = np.ascontiguousarray(np.stack([xpad[k:k + SEQ] for k in range(4)], axis=1))
        maps.append({
            "xc4": xc4, "cw": np.ascontiguousarray(cwl[:, cols]).reshape(-1), "cb": np.ascontiguousarray(cbl[cols]),
            "dtr": np.ascontiguousarray(dtr[b][:, h]),
            "scal": np.array([inp["ssm_dt_bias"][l, h], inp["ssm_a_log"][l, h], inp["ssm_d"][l, h], 0.0], np.float32),
            "tri": cst["tri"], "triw": triw, "ntriw": ntriw, "identf": cst["ident"],
        })
    nc = build_S()
    res = run_bass_kernel_spmd(nc, maps, core_ids=list(range(NCORE)))
    y = np.zeros((BATCH, SEQ, 256), np.float32)
    for r in range(NCORE):
        y[r // 4][:, 64 * (r % 4):64 * (r % 4) + 64] = res.results[r]["y"]
    return y


def _bucket(n):
    n = np.maximum(n, 0)
    nf = np.maximum(n, 1).astype(np.float32)
    large = 16 + (np.log(nf / np.float32(16)) / np.float32(np.log(8.0)) * np.float32(16)).astype(np.int32)
    large = np.minimum(large, 31)
    return np.where(n < 16, n, large)


def _table(rb_aug, dist, valid, g):
    idx = np.where(valid, _bucket(dist), 32)
    t = rb_aug[idx][:, :, 4 * g:4 * g + 4]
    return np.ascontiguousarray(t.transpose(0, 2, 1).reshape(128, 512))


def run_N(inp, l, QT_all, KcT_all, VcT_all, KsT_all, KwT_all, Vs_all, Vw_all, gates_all):
    cst = _consts()
    T = SEQ
    rb_aug = np.concatenate([inp["rel_bias"], np.full((1, 8), -30000.0, np.float32)], 0)
    k = np.arange(128)[:, None]
    q = np.arange(128)[None, :]
    allv = np.ones((128, 128), bool)
    big = np.full((128, 128), 100000)
    kg = np.arange(1024)
    cs = 16 * kg
    j = np.arange(256)
    ov = np.clip(np.minimum(cs[:, None] + 32, 64 * j[None, :] + 64) - np.maximum(cs[:, None], 64 * j[None, :]), 0, None) / 32.0
    ov[1023] = 0.0
    OV = ov.reshape(8, 128, 256).astype(np.float32)
    jj = np.arange(128)[:, None, None]
    mm = np.arange(64)[None, :, None]
    kk = np.arange(128)[None, None, :]
    EX = (jj == 2 * mm + kk // 64).astype(np.float32).reshape(128, 64 * 128)
    maps = []
    for r in range(NCORE):
        b, g, par = r // 4, (r // 2) % 2, r % 2
        TC = [_table(rb_aug, 128 * (2 * e2 + par) + q - 16 * k - 31, (128 * (2 * e2 + par) + q - 16 * k - 31) >= 0, g) for e2 in range(9)]
        TC.append(_table(rb_aug, big, allv, g))
        prev = _table(rb_aug, 128 + q - k, allv, g)
        diag = _table(rb_aug, q - k, (q - k) >= 0, g)
        far = _table(rb_aug, big, allv, g)
        none = _table(rb_aug, big, ~allv, g)
        TS = [prev, diag, none, far] if par == 0 else [far, prev, diag, far]
        TW = [_table(rb_aug, q + 128 * (4 - d) - k, ((q + 128 * (4 - d) - k) >= 0) & ((q + 128 * (4 - d) - k) < 512), g) for d in range(5)]
        qt = 2 * np.arange(NQF) + par
        t = (128 * qt[:, None] + np.arange(128)[None, :])
        cur = (t // 64)[:, :, None]
        jb = np.arange(256)[None, None, :]
        FB = np.where((jb == 0) | (jb == cur) | (jb == cur - 1), 1e4, np.where(jb <= cur, 0.0, -1e4)).astype(np.float32)
        tok = slice(b * T, (b + 1) * T)
        QTc = QT_all[:, tok].reshape(8, 64, 128, 128)[4 * g:4 * g + 4, :, par::2, :].transpose(1, 0, 2, 3)
        blks = []
        for src in (KcT_all, VcT_all):
            kcg = src[g * 64:(g + 1) * 64, tok]
            bl = np.zeros((16, 2, 64, 1024), kcg.dtype)
            ii = np.arange(1023)
            for lt in range(16):
                for lb in range(2):
                    bl[lt, lb, :, :1023] = kcg[:, 16 * ii + 2 * lt + lb]
            blks.append(bl.reshape(16, 128, 1024))
        KsT = KsT_all[g * 64:(g + 1) * 64, tok]
        ones = np.ones((T, 1), Vs_all.dtype)
        VsA = np.concatenate([Vs_all[tok, g * 64:(g + 1) * 64], ones], 1)
        KwP = np.zeros((64, 512 + T + 128), KwT_all.dtype)
        KwP[:, 512:512 + T] = KwT_all[g * 64:(g + 1) * 64, tok]
        VwP = np.zeros((512 + T + 128, 65), Vw_all.dtype)
        VwP[512:512 + T] = np.concatenate([Vw_all[tok, g * 64:(g + 1) * 64], ones], 1)
        gt = gates_all[tok, 12 * g:12 * g + 12].reshape(128, 128, 12)[par::2]
        maps.append({
            "QT": np.ascontiguousarray(QTc), "blk": np.ascontiguousarray(np.stack(blks)),
            "cw1": inp["cmp_w1"][l], "cb1": np.ascontiguousarray(inp["cmp_b1"][l].reshape(2, 2, 128).transpose(0, 2, 1)),
            "cpe": np.ascontiguousarray(inp["cmp_pe"][l].reshape(2, 16, 2, 64).transpose(0, 2, 3, 1).reshape(2, 128, 16)),
            "cw2": inp["cmp_w2"][l],
            "KsT": np.ascontiguousarray(KsT), "VsA": np.ascontiguousarray(VsA),
            "KwT": np.ascontiguousarray(KwP[:, par * 128:par * 128 + T + 512]),
            "VwA": np.ascontiguousarray(VwP[par * 128:par * 128 + T + 512]),
            "gates": np.ascontiguousarray(gt), "TC": np.stack(TC), "TS": np.stack(TS), "TW": np.stack(TW),
            "OV": OV, "FB": FB, "EX": EX, "ident": cst["ident"],
        })
    nc = build_N()
    res = run_bass_kernel_spmd(nc, maps, core_ids=list(range(NCORE)))
    o_att = np.zeros((BATCH, SEQ, 512), np.float32)
    for r in range(NCORE):
        b, g, par = r // 4, (r // 2) % 2, r % 2
        o = res.results[r]["o"]
        o_att[b].reshape(128, 128, 512)[par::2, :, g * 256:(g + 1) * 256] = o
    return o_att


def kernel(**inp):
    inp = {k: np.asarray(v) for k, v in inp.items()}
    x = np.ascontiguousarray(inp["x"].reshape(-1, D))
    cat = lambda res, name, ax: np.concatenate([r[name] for r in res], ax)
    for l in range(DEPTH):
        ra = run_A(inp, l, x)
        x1 = cat(ra, "x1", 0)
        tmf = cat(ra, "tmf", 0)
        o_att = run_N(inp, l, cat(ra, "QT", 1), cat(ra, "KcT", 1), cat(ra, "VcT", 1), cat(ra, "KsT", 1), cat(ra, "KwT", 1),
                      cat(ra, "Vs", 0), cat(ra, "Vw", 0), tmf[:, 1028:1052])
        ypre = run_S(inp, l, tmf[:, 256:1024].reshape(BATCH, SEQ, 768), tmf[:, 1024:1028].reshape(BATCH, SEQ, 4))
        rcz = run_C(inp, l, x1, o_att.reshape(-1, 512), cat(ra, "ogm", 0), ypre.reshape(-1, 256), np.ascontiguousarray(tmf[:, 0:256]))
        x = cat(rcz, "x3", 0)
    return x.reshape(BATCH, SEQ, D).astype(np.float32)
```

```python
import numpy as np
import ml_dtypes
from contextlib import ExitStack, contextmanager
import concourse.bass as bass
import concourse.mybir as mybir
from concourse.bass_utils import run_bass_kernel_spmd

F32 = mybir.dt.float32
BF16 = mybir.dt.bfloat16
AF = mybir.ActivationFunctionType
ALU = mybir.AluOpType
NPBF = ml_dtypes.bfloat16

D = 1024
DEPTH = 2
SEQ = 16384
BATCH = 2
DFF = 2816
NFT = DFF // 128
DIN = 2844
ALPHA = (2 * DEPTH) ** 0.25
LN_EPS = 1e-5
NCORE = 8
TOKC = BATCH * SEQ // NCORE
NTT = TOKC // 128

ENGS = ("pe", "dve", "act", "pool", "sp")


_PSUM_KEYS = ("bank", "ptb", "sbank", "nS", "nO", "nI", "ncmp")


def _is_psum_key(k):
    return k == "nPT" or (isinstance(k, tuple) and k[0] in _PSUM_KEYS)


class Prog:
    def __init__(self, nc, n_dma_slots=8):
        self.nc = nc
        self.stack = ExitStack()
        self.ops = {e: [] for e in ENGS}
        self.sem = {e: self.stack.enter_context(nc.semaphore("s_" + e)) for e in ENGS}
        self.cnt = {e: 0 for e in ENGS}
        self.seen = {e: {} for e in ENGS}
        self.last_w = {}
        self.readers = {}
        self.nslot = n_dma_slots
        self.dma_sems, self.dma_vals, self.dma_next = {}, {}, {}
        self.semobj = dict(self.sem)
        for q in ("sp", "act", "pool"):
            self.dma_sems[q] = [self.stack.enter_context(nc.semaphore(f"d_{q}{i}")) for i in range(n_dma_slots)]
            self.dma_vals[q] = [0] * n_dma_slots
            self.dma_next[q] = 0
            for i, s in enumerate(self.dma_sems[q]):
                self.semobj[(q, i)] = s
        self.n_inst = 0
        self._rr = 0

    def sb(self, name, shape, dtype, stack=None):
        return (stack or self.stack).enter_context(self.nc.sbuf_tensor(name, list(shape), dtype))

    def ps(self, name, shape, dtype=F32, stack=None):
        return (stack or self.stack).enter_context(self.nc.psum_tensor(name, list(shape), dtype))

    def _need(self, eng, tok, waits):
        semkey, val, _ = tok
        if self.seen[eng].get(semkey, 0) >= val:
            return
        self.seen[eng][semkey] = val
        waits.append((semkey, val))

    def _deps(self, eng, engid, reads, writes):
        waits = []
        for r in reads:
            t = self.last_w.get(r)
            if t is not None:
                self._need(eng, t, waits)
        for w in writes:
            t = self.last_w.get(w)
            if t is not None and (t[2] != engid or eng != "pe"):
                self._need(eng, t, waits)
            for sk, (v, eid) in self.readers.get(w, {}).items():
                if eid != engid or eng != "pe":
                    self._need(eng, (sk, v, eid), waits)
        return waits

    def _commit(self, tok, reads, writes):
        for r in reads:
            self.readers.setdefault(r, {})[tok[0]] = (tok[1], tok[2])
        for w in writes:
            self.last_w[w] = tok
            self.readers[w] = {}

    def op(self, eng, fn, reads=(), writes=()):
        ps_reads = [k for k in reads if _is_psum_key(k)]
        if ps_reads:
            writes = list(writes) + [k for k in ps_reads if k not in writes]
        waits = self._deps(eng, eng, reads, writes)
        self.cnt[eng] += 1
        tok = (eng, self.cnt[eng], eng)
        self._commit(tok, reads, writes)
        self.ops[eng].append((waits, fn, self.sem[eng], 1))
        self.n_inst += 1
        return tok

    def dma(self, q, out, in_, reads=(), writes=(), **kw):
        slot = self.dma_next[q]
        self.dma_next[q] = (slot + 1) % self.nslot
        semkey = (q, slot)
        engid = ("dma", q, slot)
        waits = self._deps(q, engid, reads, writes)
        prev = self.dma_vals[q][slot]
        if prev > 0:
            self._need(q, (semkey, prev, engid), waits)
        val = prev + 16
        self.dma_vals[q][slot] = val
        tok = (semkey, val, engid)
        self._commit(tok, reads, writes)
        self.ops[q].append((waits, lambda e: e.dma_start(out=out, in_=in_, **kw), self.dma_sems[q][slot], 16))
        self.n_inst += 1
        return tok

    def barrier(self):
        for e in ENGS:
            waits = []
            for o in ENGS:
                if o != e and self.cnt[o] > 0:
                    self._need(e, (o, self.cnt[o], o), waits)
            for q in self.dma_sems:
                for i in range(self.nslot):
                    v = self.dma_vals[q][i]
                    if v > 0:
                        self._need(e, ((q, i), v, None), waits)
            if waits:
                self.ops[e].append((waits, None, None, 0))

    @contextmanager
    def scope(self):
        st = ExitStack()
        try:
            yield st
        finally:
            self.barrier()
            st.close()

    def finish(self):
        self.barrier()
        engmap = {"pe": "tensor", "dve": "vector", "act": "scalar", "pool": "gpsimd", "sp": "sync"}
        semobj = self.semobj
        with self.nc.Block() as block:
            for e in ENGS:
                ops = self.ops[e]
                if not ops:
                    continue

                def body(engine, ops=ops):
                    for waits, fn, sem, inc in ops:
                        for sk, v in waits:
                            engine.wait_ge(semobj[sk], v)
                        if fn is not None:
                            fn(engine).then_inc(sem, inc)

                getattr(block, engmap[e])(body)
        self.stack.close()


class RowCtx:
    def __init__(self, p, nc, consts):
        self.p, self.nc = p, nc
        self.identb = p.sb("identb", [128, 128], BF16)
        p.dma("pool", self.identb[:], consts["ident"], writes=["identb"])
        self.eps = p.sb("epsc", [128, 1], F32)
        p.op("dve", lambda e: e.memset(self.eps[:], LN_EPS), writes=["epsc"])
        self.bank = [p.ps(f"bank{i}", [128, 512], F32) for i in range(6)]
        self.ptb = [p.ps(f"ptb{i}", [128, 1024], BF16) for i in range(2)]


def emit_mod(p, rc, st, cT, ada_w, ada_b, ncols, dst, posts):
    nck = ncols // 512
    with p.scope() as s2:
        ct = p.sb("mod_ct", [128, 8], F32, s2)
        sc = p.sb("mod_sc", [128, 8], F32, s2)
        scb = p.sb("mod_scb", [128, 8, 128], F32, s2)
        bb = p.sb("mod_bb", [128, ncols], F32, s2)
        wch = [p.sb(f"mod_w{i}", [128, 8, 512], F32, s2) for i in range(2)]
        p.dma("sp", ct[:], cT, writes=["mod_ct"])
        p.dma("act", bb[:], ada_b.partition_broadcast(128), writes=["mod_bb"])
        p.op("act", lambda e: e.activation(out=sc[:], in_=ct[:], func=AF.Silu), reads=["mod_ct"], writes=["mod_sc"])
        for dt in range(8):
            p.op("dve", lambda e, dt=dt: e.tensor_copy(out=scb[:, dt, :], in_=sc[:, dt:dt + 1].to_broadcast([128, 128])),
                 reads=["mod_sc"], writes=[("mod_scb", dt)])
        for ci in range(nck):
            w = wch[ci % 2]
            wk = ("mod_w", ci % 2)
            p.dma("sp" if ci % 2 == 0 else "act", w[:],
                  ada_w[:, ci * 512:(ci + 1) * 512].rearrange("(dt p) n -> p dt n", p=128), writes=[wk])
            bk = rc.bank[ci % 2]
            for dt in range(8):
                p.op("pe", lambda e, dt=dt, w=w, bk=bk: e.matmul(bk[:], lhsT=scb[:, dt, :], rhs=w[:, dt, :],
                                                                  start=(dt == 0), stop=(dt == 7)),
                     reads=[wk, ("mod_scb", dt)], writes=[("bank", ci % 2)])
            p.op("dve", lambda e, ci=ci, bk=bk: e.tensor_tensor(out=dst[:, ci * 512:(ci + 1) * 512], in0=bk[:],
                                                                 in1=bb[:, ci * 512:(ci + 1) * 512], op=ALU.add),
                 reads=[("bank", ci % 2), "mod_bb"], writes=["modtile"])
        for (c0, c1, add, mul) in posts:
            p.op("dve", lambda e, c0=c0, c1=c1, add=add, mul=mul: e.tensor_scalar(
                out=dst[:, c0:c1], in0=dst[:, c0:c1], scalar1=float(add), scalar2=float(mul), op0=ALU.add, op1=ALU.mult),
                reads=["modtile"], writes=["modtile"])


def emit_modulate_T(p, rc, xt, xkey, SC, SH, hf, hb, hT, tag, ptb_i=0):
    p.op("dve", lambda e: e.tensor_tensor(out=hf[:], in0=xt, in1=SC, op=ALU.mult),
         reads=[xkey, "modtile"], writes=[tag + "hf"])
    p.op("pool", lambda e: e.tensor_tensor(out=hb[:], in0=hf[:], in1=SH, op=ALU.add),
         reads=[tag + "hf", "modtile"], writes=[tag + "hb"])
    pt = rc.ptb[ptb_i]
    for dt in range(8):
        p.op("pe", lambda e, dt=dt: e.transpose(out=pt[:, dt * 128:(dt + 1) * 128], in_=hb[:, dt * 128:(dt + 1) * 128],
                                                 identity=rc.identb[:]),
             reads=[tag + "hb", "identb"], writes=[("ptb", ptb_i)])
    p.op("act", lambda e: e.activation(out=hT[:].rearrange("p a b -> p (a b)"), in_=pt[:], func=AF.Copy),
         reads=[("ptb", ptb_i)], writes=[tag + "hT"])


def emit_res_ln(p, rc, ybanks, xres, xkey, GF, lng, lnb, r, stats, mv, rstd, xo, tag):
    for half in range(2):
        sl = slice(half * 512, (half + 1) * 512)
        p.op("dve", lambda e, half=half, sl=sl: e.tensor_tensor(out=r[:, sl], in0=rc.bank[ybanks[half]][:], in1=GF[:, sl],
                                                                op=ALU.mult),
             reads=[("bank", ybanks[half]), "modtile"], writes=[(tag + "r", half)])
        p.op("dve", lambda e, sl=sl: e.scalar_tensor_tensor(out=r[:, sl], in0=xres[:, sl], scalar=float(ALPHA), in1=r[:, sl],
                                                            op0=ALU.mult, op1=ALU.add),
             reads=[(tag + "r", half), xkey], writes=[(tag + "r", half)])
        p.op("dve", lambda e, half=half, sl=sl: e.bn_stats(out=stats[:, half * 6:(half + 1) * 6], in_=r[:, sl]),
             reads=[(tag + "r", half)], writes=[(tag + "st", half)])
    p.op("dve", lambda e: e.bn_aggr(out=mv[:], in_=stats[:]), reads=[(tag + "st", 0), (tag + "st", 1)], writes=[tag + "mv"])
    p.op("act", lambda e: e.activation(out=rstd[:], in_=mv[:, 1:2], func=AF.Sqrt, bias=rc.eps[:], scale=1.0),
         reads=[tag + "mv", "epsc"], writes=[tag + "rstd"])
    p.op("dve", lambda e: e.reciprocal(out=rstd[:], in_=rstd[:]), reads=[tag + "rstd"], writes=[tag + "rstd"])
    p.op("dve", lambda e: e.tensor_scalar(out=r[:], in0=r[:], scalar1=mv[:, 0:1], scalar2=rstd[:, 0:1],
                                          op0=ALU.subtract, op1=ALU.mult),
         reads=[(tag + "r", 0), (tag + "r", 1), tag + "mv", tag + "rstd"], writes=[(tag + "r", 0), (tag + "r", 1)])
    p.op("pool", lambda e: e.tensor_tensor(out=r[:], in0=r[:], in1=lng[:], op=ALU.mult),
         reads=[(tag + "r", 0), (tag + "r", 1), "lngb"], writes=[(tag + "r", 0), (tag + "r", 1)])
    p.op("pool", lambda e: e.tensor_tensor(out=xo[:], in0=r[:], in1=lnb[:], op=ALU.add),
         reads=[(tag + "r", 0), (tag + "r", 1), "lngb"], writes=[tag + "xo"])


def emit_ffn(p, rc, x_in, x_out, w1, w3, w2, SC, SH, GF, lng, lnb):
    with p.scope() as st:
        W1 = p.sb("W1", [128, 8, DFF], BF16, st)
        W3 = p.sb("W3", [128, 8, DFF], BF16, st)
        W2 = p.sb("W2", [128, NFT, D], BF16, st)
        for dt in range(8):
            p.dma("pool", W1[:, dt, :], w1[dt * 128:(dt + 1) * 128, :], writes=["W1"])
            p.dma("pool", W3[:, dt, :], w3[dt * 128:(dt + 1) * 128, :], writes=["W3"])
        for f0 in range(0, NFT, 2):
            p.dma("pool", W2[:, f0:f0 + 2, :], w2[f0 * 128:(f0 + 2) * 128, :].rearrange("(f p) n -> p f n", p=128), writes=["W2"])
        xt = [p.sb(f"f_x{i}", [128, D], F32, st) for i in range(2)]
        hf = p.sb("f_hf", [128, D], F32, st)
        hb = p.sb("f_hb", [128, D], BF16, st)
        hT = p.sb("f_hT", [128, 8, 128], BF16, st)
        aT = p.sb("f_aT", [128, NFT, 128], BF16, st)
        sg = [p.sb(f"f_sg{i}", [128, 128], F32, st) for i in range(2)]
        r = p.sb("f_r", [128, D], F32, st)
        xo = [p.sb(f"f_xo{i}", [128, D], F32, st) for i in range(2)]
        stats = p.sb("f_stats", [128, 12], F32, st)
        mv = p.sb("f_mv", [128, 2], F32, st)
        rstd = p.sb("f_rstd", [128, 1], F32, st)
        for tt in range(NTT):
            x = xt[tt % 2]
            xk = ("f_x", tt % 2)
            p.dma("sp", x[:], x_in[tt * 128:(tt + 1) * 128, :], writes=[xk])
            emit_modulate_T(p, rc, x[:], xk, SC, SH, hf, hb, hT, "f_", ptb_i=tt % 2)
            for ft in range(NFT):
                b = ft % 2
                bk = rc.bank[b]
                for dt in range(8):
                    p.op("pe", lambda e, dt=dt, ft=ft, bk=bk: e.matmul(bk[:, 0:128], lhsT=W1[:, dt, ft * 128:(ft + 1) * 128],
                                                                        rhs=hT[:, dt, :], start=(dt == 0), stop=(dt == 7)),
                         reads=["W1", "f_hT"], writes=[("bank", b)])
                for dt in range(8):
                    p.op("pe", lambda e, dt=dt, ft=ft, bk=bk: e.matmul(bk[:, 128:256], lhsT=W3[:, dt, ft * 128:(ft + 1) * 128],
                                                                        rhs=hT[:, dt, :], start=(dt == 0), stop=(dt == 7)),
                         reads=["W3", "f_hT"], writes=[("bank", b)])
                p.op("act", lambda e, b=b, bk=bk: e.activation(out=sg[b][:], in_=bk[:, 0:128], func=AF.Silu),
                     reads=[("bank", b)], writes=[("f_sg", b)])
                p.op("dve", lambda e, b=b, bk=bk, ft=ft: e.tensor_tensor(out=aT[:, ft, :], in0=sg[b][:], in1=bk[:, 128:256],
                                                                          op=ALU.mult),
                     reads=[("bank", b), ("f_sg", b)], writes=[("f_aT", ft)])
            for half in range(2):
                bk = rc.bank[2 + half]
                for ft in range(NFT):
                    p.op("pe", lambda e, ft=ft, bk=bk, half=half: e.matmul(bk[:], lhsT=aT[:, ft, :],
                                                                            rhs=W2[:, ft, half * 512:(half + 1) * 512],
                                                                            start=(ft == 0), stop=(ft == NFT - 1)),
                         reads=[("f_aT", ft), "W2"], writes=[("bank", 2 + half)])
            o = xo[tt % 2]
            emit_res_ln(p, rc, (2, 3), x, xk, GF, lng, lnb, r, stats, mv, rstd, o, "f_")
            p.dma("act", x_out[tt * 128:(tt + 1) * 128, :], o[:], reads=["f_xo"], writes=[])


def build_A():
    nc = bass.Bass("TRN2", target_bir_lowering=False)
    di = lambda n, s, d=F32: nc.dram_tensor(n, list(s), d, kind="ExternalInput").ap()
    do = lambda n, s, d=F32: nc.dram_tensor(n, list(s), d, kind="ExternalOutput").ap()
    x_in = di("x", [TOKC, D])
    cT = di("cT", [128, 8])
    ada_w = di("ada_w", [D, 5120])
    ada_b = di("ada_b", [5120])
    lng_d, lnb_d = di("lng", [D]), di("lnb", [D])
    w1, w3, w2 = di("w1", [D, DFF]), di("w3", [D, DFF]), di("w2", [DFF, D])
    w_in = di("w_in", [D, DIN])
    glng_d, glnb_d = di("glng", [256]), di("glnb", [256])
    wsT_d = di("wsT", [4, 128, 128])
    bsT_d = di("bsT", [128, 4])
    consts = {"ident": di("ident", [128, 128]), "tri": di("tri", [128, 128])}
    x1_d = do("x1", [TOKC, D])
    QT_d = do("QT", [512, TOKC], BF16)
    KcT_d, VcT_d = do("KcT", [128, TOKC], BF16), do("VcT", [128, TOKC], BF16)
    KsT_d, KwT_d = do("KsT", [128, TOKC], BF16), do("KwT", [128, TOKC], BF16)
    Vs_d, Vw_d = do("Vs", [TOKC, 128], BF16), do("Vw", [TOKC, 128], BF16)
    tmf_d = do("tmf", [TOKC, 1052])
    ogm_d = do("ogm", [TOKC, 256], BF16)

    p = Prog(nc)
    rc = RowCtx(p, nc, consts)
    MOD = p.sb("MOD", [128, 5120], F32)
    lng = p.sb("lng_t", [128, D], F32)
    lnb = p.sb("lnb_t", [128, D], F32)
    p.dma("sp", lng[:], lng_d.partition_broadcast(128), writes=["lngb"])
    p.dma("sp", lnb[:], lnb_d.partition_broadcast(128), writes=["lngb"])
    emit_mod(p, rc, None, cT, ada_w, ada_b, 5120, MOD,
             [(1024, 2048, 1.0, 1.0), (2048, 3072, 1.0, 0.5), (4096, 5120, 1.0, 1.0)])
    import os
    STOP = os.environ.get("KSTOP", "")
    if STOP == "mod":
        p.finish(); return nc
    emit_ffn(p, rc, x_in, x1_d, w1, w3, w2, MOD[:, 1024:2048], MOD[:, 0:1024], MOD[:, 2048:3072], lng, lnb)
    if STOP == "ffn":
        p.finish(); return nc

    with p.scope() as st:
        WIN = p.sb("WIN", [128, 8, DIN], BF16, st)
        for dt in range(8):
            p.dma("pool", WIN[:, dt, :], w_in[dt * 128:(dt + 1) * 128, :], writes=["WIN"])
        tri = p.sb("tri_t", [128, 128], F32, st)
        p.dma("sp", tri[:], consts["tri"], writes=["tri_t"])
        wsf = p.sb("wsf", [128, 4, 128], F32, st)
        p.dma("sp", wsf[:], wsT_d.rearrange("g s t -> s g t"), writes=["wsf"])
        WS = p.sb("WS", [128, 4, 128], BF16, st)
        for g in range(4):
            p.op("dve", lambda e, g=g: e.tensor_tensor(out=WS[:, g, :], in0=wsf[:, g, :], in1=tri[:], op=ALU.mult),
                 reads=["wsf", "tri_t"], writes=["WS"])
        BS = p.sb("BS", [128, 4], F32, st)
        p.dma("sp", BS[:], bsT_d, writes=["BS"])
        glng = p.sb("glng_t", [128, 256], F32, st)
        glnb = p.sb("glnb_t", [128, 256], F32, st)
        p.dma("sp", glng[:], glng_d.partition_broadcast(128), writes=["glngb"])
        p.dma("sp", glnb[:], glnb_d.partition_broadcast(128), writes=["glngb"])
        xt = [p.sb(f"a_x{i}", [128, D], F32, st) for i in range(2)]
        hf = p.sb("a_hf", [128, D], F32, st)
        hb = p.sb("a_hb", [128, D], BF16, st)
        hT = p.sb("a_hT", [128, 8, 128], BF16, st)
        fm = [p.sb(f"a_fm{i}", [128, 8, 128], BF16, st) for i in range(2)]
        vsw = [p.sb(f"a_vsw{i}", [128, 256], BF16, st) for i in range(2)]
        gu = p.sb("a_gu", [128, 512], F32, st)
        vn = p.sb("a_vn", [128, 256], F32, st)
        vnb = p.sb("a_vnb", [128, 256], BF16, st)
        gst = p.sb("a_gst", [128, 6], F32, st)
        gmv = p.sb("a_gmv", [128, 2], F32, st)
        grs = p.sb("a_grs", [128, 1], F32, st)
        ogm = [p.sb(f"a_ogm{i}", [128, 256], BF16, st) for i in range(2)]
        ssm = [p.sb(f"a_ssm{i}", [128, 1052], F32, st) for i in range(2)]
        SC1, SH1 = MOD[:, 4096:5120], MOD[:, 3072:4096]
        fm_cols = [0, 128, 256, 384, 512, 640, 768, 1024]
        for tt in range(NTT):
            i2 = tt % 2
            tok = slice(tt * 128, (tt + 1) * 128)
            x = xt[i2]
            xk = ("a_x", i2)
            p.dma("sp", x[:], x1_d[tok, :], reads=[], writes=[xk])
            emit_modulate_T(p, rc, x[:], xk, SC1, SH1, hf, hb, hT, "a_", ptb_i=0)
            for ci, c0 in enumerate(fm_cols):
                b = ci // 4
                bk = rc.bank[b]
                for dt in range(8):
                    p.op("pe", lambda e, dt=dt, c0=c0, bk=bk, ci=ci: e.matmul(
                        bk[:, (ci % 4) * 128:(ci % 4 + 1) * 128], lhsT=WIN[:, dt, c0:c0 + 128], rhs=hT[:, dt, :],
                        start=(dt == 0), stop=(dt == 7)), reads=["WIN", "a_hT"], writes=[("bank", b)])
            f = fm[i2]
            p.op("act", lambda e, f=f: e.activation(out=f[:, 0:4, :].rearrange("p a b -> p (a b)"), in_=rc.bank[0][:],
                                                    func=AF.Identity, scale=0.125),
                 reads=[("bank", 0)], writes=[("a_fm", i2)])
            p.op("dve", lambda e, f=f: e.tensor_copy(out=f[:, 4:8, :].rearrange("p a b -> p (a b)"), in_=rc.bank[1][:]),
                 reads=[("bank", 1)], writes=[("a_fm", i2)])
            p.dma("act", QT_d[:, tok].rearrange("(c p) t -> p c t", p=128), f[:, 0:4, :], reads=[("a_fm", i2)])
            for k, dd in enumerate((KcT_d, VcT_d, KsT_d, KwT_d)):
                p.dma("act", dd[:, tok], f[:, 4 + k, :], reads=[("a_fm", i2)])
            def tm(bank_i, col_off, c0, c1):
                bk = rc.bank[bank_i]
                for dt in range(8):
                    p.op("pe", lambda e, dt=dt: e.matmul(bk[:, col_off:col_off + (c1 - c0)], lhsT=hT[:, dt, :],
                                                         rhs=WIN[:, dt, c0:c1], start=(dt == 0), stop=(dt == 7)),
                         reads=["WIN", "a_hT"], writes=[("bank", bank_i)])
            tm(2, 0, 896, 1024)
            tm(2, 128, 1152, 1304)
            tm(2, 280, 2716, 2844)
            tm(3, 0, 1304, 1816)
            tm(4, 0, 1816, 2328)
            tm(5, 0, 2328, 2840)
            vv = vsw[i2]
            p.op("dve", lambda e, vv=vv: e.tensor_copy(out=vv[:], in_=rc.bank[2][:, 0:256]),
                 reads=[("bank", 2)], writes=[("a_vsw", i2)])
            p.dma("act", Vs_d[tok, :], vv[:, 0:128], reads=[("a_vsw", i2)])
            p.dma("act", Vw_d[tok, :], vv[:, 128:256], reads=[("a_vsw", i2)])
            sm = ssm[i2]
            p.op("act", lambda e, sm=sm: e.activation(out=sm[:, 1028:1052], in_=rc.bank[2][:, 256:280], func=AF.Sigmoid),
                 reads=[("bank", 2)], writes=[("a_ssm", i2)])
            p.op("dve", lambda e, sm=sm: e.tensor_copy(out=sm[:, 1024:1028], in_=rc.bank[2][:, 280 + 124:280 + 128]),
                 reads=[("bank", 2)], writes=[("a_ssm", i2)])
            p.op("act", lambda e, sm=sm: e.activation(out=sm[:, 0:512], in_=rc.bank[4][:], func=AF.Copy),
                 reads=[("bank", 4)], writes=[("a_ssm", i2)])
            p.op("dve", lambda e, sm=sm: e.tensor_copy(out=sm[:, 512:1024], in_=rc.bank[5][:]),
                 reads=[("bank", 5)], writes=[("a_ssm", i2)])
            p.dma("sp", tmf_d[tok, :], sm[:], reads=[("a_ssm", i2)])
            p.op("act", lambda e: e.activation(out=gu[:], in_=rc.bank[3][:], func=AF.Gelu_apprx_tanh),
                 reads=[("bank", 3)], writes=["a_gu"])
            p.op("dve", lambda e: e.bn_stats(out=gst[:], in_=gu[:, 256:512]), reads=["a_gu"], writes=["a_gst"])
            p.op("dve", lambda e: e.bn_aggr(out=gmv[:], in_=gst[:]), reads=["a_gst"], writes=["a_gmv"])
            p.op("act", lambda e: e.activation(out=grs[:], in_=gmv[:, 1:2], func=AF.Sqrt, bias=rc.eps[:], scale=1.0),
                 reads=["a_gmv", "epsc"], writes=["a_grs"])
            p.op("dve", lambda e: e.reciprocal(out=grs[:], in_=grs[:]), reads=["a_grs"], writes=["a_grs"])
            p.op("dve", lambda e: e.tensor_scalar(out=vn[:], in0=gu[:, 256:512], scalar1=gmv[:, 0:1], scalar2=grs[:, 0:1],
                                                  op0=ALU.subtract, op1=ALU.mult),
                 reads=["a_gu", "a_gmv", "a_grs"], writes=["a_vn"])
            p.op("pool", lambda e: e.tensor_tensor(out=vn[:], in0=vn[:], in1=glng[:], op=ALU.mult),
                 reads=["a_vn", "glngb"], writes=["a_vn"])
            p.op("pool", lambda e: e.tensor_tensor(out=vnb[:], in0=vn[:], in1=glnb[:], op=ALU.add),
                 reads=["a_vn", "glngb"], writes=["a_vnb"])
            for g in range(4):
                p.op("pe", lambda e, g=g: e.matmul(rc.bank[3][:, g * 64:(g + 1) * 64], lhsT=WS[:, g, :],
                                                   rhs=vnb[:, g * 64:(g + 1) * 64], start=True, stop=True),
                     reads=["WS", "a_vnb"], writes=[("bank", 3)])
            og = ogm[i2]
            for g in range(4):
                p.op("dve", lambda e, g=g, og=og: e.scalar_tensor_tensor(
                    out=og[:, g * 64:(g + 1) * 64], in0=rc.bank[3][:, g * 64:(g + 1) * 64], scalar=BS[:, g:g + 1],
                    in1=gu[:, g * 64:(g + 1) * 64], op0=ALU.add, op1=ALU.mult),
                    reads=[("bank", 3), "a_gu", "BS"], writes=[("a_ogm", i2)])
            p.dma("act", ogm_d[tok, :], og[:], reads=[("a_ogm", i2)])
    p.finish()
    return nc


def build_C():
    nc = bass.Bass("TRN2", target_bir_lowering=False)
    di = lambda n, s, d=F32: nc.dram_tensor(n, list(s), d, kind="ExternalInput").ap()
    do = lambda n, s, d=F32: nc.dram_tensor(n, list(s), d, kind="ExternalOutput").ap()
    x1_d = di("x1", [TOKC, D])
    cT = di("cT", [128, 8])
    ada_w = di("ada_w", [D, 4096])
    ada_b = di("ada_b", [4096])
    lng1_d, lnb1_d = di("lng1", [D]), di("lnb1", [D])
    lng2_d, lnb2_d = di("lng2", [D]), di("lnb2", [D])
    w1, w3, w2 = di("w1", [D, DFF]), di("w3", [D, DFF]), di("w2", [DFF, D])
    w_out = di("w_out", [D, D])
    oatt_d = di("oatt", [TOKC, 512])
    ogm_d = di("ogm", [TOKC, 256], BF16)
    yssm_d = di("yssm", [TOKC, 256])
    zz_d = di("zz", [TOKC, 256])
    ng_d = di("normg", [256])
    consts = {"ident": di("ident", [128, 128])}
    x2_d = do("x2", [TOKC, D])
    x3_d = do("x3", [TOKC, D])

    p = Prog(nc)
    rc = RowCtx(p, nc, consts)
    MOD = p.sb("MOD", [128, 4096], F32)
    lng = p.sb("lng_t", [128, D], F32)
    lnb = p.sb("lnb_t", [128, D], F32)
    emit_mod(p, rc, None, cT, ada_w, ada_b, 4096, MOD,
             [(0, 1024, 1.0, 1.0), (2048, 3072, 1.0, 1.0), (3072, 4096, 1.0, 0.5)])
    p.dma("sp", lng[:], lng1_d.partition_broadcast(128), writes=["lngb"])
    p.dma("sp", lnb[:], lnb1_d.partition_broadcast(128), writes=["lngb"])
    with p.scope() as st:
        WO = p.sb("WO", [128, 8, D], BF16, st)
        for dt in range(8):
            p.dma("pool", WO[:, dt, :], w_out[dt * 128:(dt + 1) * 128, :], writes=["WO"])
        ng = p.sb("ng_t", [128, 256], F32, st)
        p.dma("sp", ng[:], ng_d.partition_broadcast(128), writes=["ng_t"])
        xt = [p.sb(f"c_x{i}", [128, D], F32, st) for i in range(2)]
        oa = [p.sb(f"c_oa{i}", [128, 512], F32, st) for i in range(2)]
        ys = [p.sb(f"c_ys{i}", [128, 512], F32, st) for i in range(2)]
        om = p.sb("c_om", [128, D], BF16, st)
        omT = p.sb("c_omT", [128, 8, 128], BF16, st)
        sz = p.sb("c_sz", [128, 256], F32, st)
        gg = p.sb("c_gg", [128, 256], F32, st)
        junk = p.sb("c_junk", [128, 128], F32, st)
        ss = p.sb("c_ss", [128, 2], F32, st)
        r = p.sb("c_r", [128, D], F32, st)
        xo = [p.sb(f"c_xo{i}", [128, D], F32, st) for i in range(2)]
        stats = p.sb("c_stats", [128, 12], F32, st)
        mv = p.sb("c_mv", [128, 2], F32, st)
        rstd = p.sb("c_rstd", [128, 1], F32, st)
        for tt in range(NTT):
            i2 = tt % 2
            tok = slice(tt * 128, (tt + 1) * 128)
            x, xk = xt[i2], ("c_x", i2)
            p.dma("sp", x[:], x1_d[tok, :], writes=[xk])
            p.dma("sp", oa[i2][:], oatt_d[tok, :], writes=[("c_oa", i2)])
            p.dma("pool", om[:, 512:768], ogm_d[tok, :], writes=["c_om_g"])
            p.dma("act", ys[i2][:, 0:256], yssm_d[tok, :], writes=[("c_ys", i2)])
            p.dma("act", ys[i2][:, 256:512], zz_d[tok, :], writes=[("c_ys", i2)])
            p.op("pool", lambda e, i2=i2: e.tensor_copy(out=om[:, 0:512], in_=oa[i2][:]), reads=[("c_oa", i2)], writes=["c_om_a"])
            p.op("act", lambda e, i2=i2: e.activation(out=sz[:], in_=ys[i2][:, 256:512], func=AF.Silu),
                 reads=[("c_ys", i2)], writes=["c_sz"])
            p.op("dve", lambda e, i2=i2: e.tensor_tensor(out=gg[:], in0=ys[i2][:, 0:256], in1=sz[:], op=ALU.mult),
                 reads=[("c_ys", i2), "c_sz"], writes=["c_gg"])
            p.op("dve", lambda e: e.tensor_tensor(out=sz[:], in0=gg[:], in1=gg[:], op=ALU.mult), reads=["c_gg"], writes=["c_sz"])
            for k in range(2):
                p.op("dve", lambda e, k=k: e.reduce_sum(out=ss[:, k:k + 1], in_=sz[:, k * 128:(k + 1) * 128],
                                                        axis=mybir.AxisListType.X), reads=["c_sz"], writes=[("c_ss", k)])
            p.op("act", lambda e: e.activation(out=ss[:], in_=ss[:], func=AF.Sqrt, bias=rc.eps[:], scale=1.0 / 128.0),
                 reads=[("c_ss", 0), ("c_ss", 1), "epsc"], writes=[("c_ss", 0), ("c_ss", 1)])
            p.op("dve", lambda e: e.reciprocal(out=ss[:], in_=ss[:]), reads=[("c_ss", 0), ("c_ss", 1)],
                 writes=[("c_ss", 0), ("c_ss", 1)])
            for k in range(2):
                p.op("dve", lambda e, k=k: e.scalar_tensor_tensor(out=om[:, 768 + k * 128:768 + (k + 1) * 128],
                                                                  in0=gg[:, k * 128:(k + 1) * 128], scalar=ss[:, k:k + 1],
                                                                  in1=ng[:, k * 128:(k + 1) * 128], op0=ALU.mult, op1=ALU.mult),
                     reads=["c_gg", ("c_ss", 0), ("c_ss", 1), "ng_t"], writes=[("c_om_s", k)])
            pt = rc.ptb[i2]
            for dt in range(8):
                p.op("pe", lambda e, dt=dt, pt=pt: e.transpose(out=pt[:, dt * 128:(dt + 1) * 128],
                                                               in_=om[:, dt * 128:(dt + 1) * 128], identity=rc.identb[:]),
                     reads=["c_om_a", "c_om_g", ("c_om_s", 0), ("c_om_s", 1), "identb"], writes=[("ptb", i2)])
            p.op("act", lambda e, pt=pt: e.activation(out=omT[:].rearrange("p a b -> p (a b)"), in_=pt[:], func=AF.Copy),
                 reads=[("ptb", i2)], writes=["c_omT"])
            for half in range(2):
                bk = rc.bank[2 + half]
                for dt in range(8):
                    p.op("pe", lambda e, dt=dt, bk=bk, half=half: e.matmul(bk[:], lhsT=omT[:, dt, :],
                                                                            rhs=WO[:, dt, half * 512:(half + 1) * 512],
                                                                            start=(dt == 0), stop=(dt == 7)),
                         reads=["c_omT", "WO"], writes=[("bank", 2 + half)])
            o = xo[i2]
            emit_res_ln(p, rc, (2, 3), x, xk, MOD[:, 0:1024], lng, lnb, r, stats, mv, rstd, o, "c_")
            p.dma("act", x2_d[tok, :], o[:], reads=["c_xo"])
    p.dma("sp", lng[:], lng2_d.partition_broadcast(128), writes=["lngb"])
    p.dma("sp", lnb[:], lnb2_d.partition_broadcast(128), writes=["lngb"])
    emit_ffn(p, rc, x2_d, x3_d, w1, w3, w2, MOD[:, 2048:3072], MOD[:, 1024:2048], MOD[:, 3072:4096], lng, lnb)
    p.finish()
    return nc


NCH = SEQ // 256


def build_S():
    nc = bass.Bass("TRN2", target_bir_lowering=False)
    di = lambda n, s, d=F32: nc.dram_tensor(n, list(s), d, kind="ExternalInput").ap()
    do = lambda n, s, d=F32: nc.dram_tensor(n, list(s), d, kind="ExternalOutput").ap()
    xc4_d = di("xc4", [SEQ, 4, 320])
    cw_d = di("cw", [4 * 320])
    cb_d = di("cb", [320])
    dtr_d = di("dtr", [SEQ])
    sc_d = di("scal", [4])
    tri_d = di("tri", [128, 128])
    triw_d = di("triw", [2, 128, 256])
    ntriw_d = di("ntriw", [2, 128, 256])
    ident_d = di("identf", [128, 128])
    y_d = do("y", [SEQ, 64])
    p = Prog(nc)
    bank = [p.ps(f"sbank{i}", [128, 512], F32) for i in range(5)]
    CW = p.sb("CW", [128, 4, 320], F32)
    CB = p.sb("CB", [128, 320], F32)
    DTR = p.sb("DTR", [128, SEQ // 128], F32)
    SCL = p.sb("SCL", [128, 4], F32)
    TRI = p.sb("TRI", [128, 128], F32)
    ONES = p.sb("ONES", [128, 128], F32)
    TRIW = p.sb("TRIW", [128, 2, 256], F32)
    NTRIW = p.sb("NTRIW", [128, 2, 256], F32)
    IDF = p.sb("IDF", [128, 128], F32)
    ANEG = p.sb("ANEG", [128, 1], F32)
    ONE1 = p.sb("ONE1", [128, 1], F32)
    S = p.sb("Sst", [128, 64], F32)
    p.dma("sp", CW[:].rearrange("p k n -> p (k n)"), cw_d.partition_broadcast(128), writes=["CW"])
    p.dma("sp", CB[:], cb_d.partition_broadcast(128), writes=["CB"])
    p.dma("sp", DTR[:], dtr_d.rearrange("(n p) -> p n", p=128), writes=["DTR"], allow_slow_non_contiguous=True)
    p.dma("sp", SCL[:], sc_d.partition_broadcast(128), writes=["SCL"])
    p.dma("act", TRI[:], tri_d, writes=["TRI"])
    p.dma("act", TRIW[:], triw_d.rearrange("i p n -> p i n"), writes=["TRIW"])
    p.dma("act", NTRIW[:], ntriw_d.rearrange("i p n -> p i n"), writes=["NTRIW"])
    p.dma("act", IDF[:], ident_d, writes=["IDF"])
    p.op("dve", lambda e: e.memset(ONES[:], 1.0), writes=["ONES"])
    p.op("dve", lambda e: e.memset(ONE1[:], 1.0), writes=["ONE1"])
    p.op("dve", lambda e: e.memset(S[:], 0.0), writes=["S"])
    p.op("act", lambda e: e.activation(out=ANEG[:], in_=SCL[:, 1:2], func=AF.Exp), reads=["SCL"], writes=["ANEG"])
    p.op("dve", lambda e: e.tensor_scalar(out=ANEG[:], in0=ANEG[:], scalar1=-1.0, scalar2=None, op0=ALU.mult),
         reads=["ANEG"], writes=["ANEG"])
    X4 = [p.sb(f"s_x4{i}", [128, 2, 4, 320], F32) for i in range(2)]
    acc = p.sb("s_acc", [128, 2, 320], F32)
    tmp = p.sb("s_tmp", [128, 2, 320], F32)
    XA = p.sb("s_XA", [128, 2, 320], F32)
    dx = p.sb("s_dx", [128, 2], F32)
    dax = p.sb("s_dax", [128, 2], F32)
    dl = p.sb("s_dl", [128, 2], F32)
    dtt = p.sb("s_dt", [128, 2], F32)
    aa = p.sb("s_a", [128, 2], F32)
    abc = p.sb("s_abc", [128, 2, 128], F32)
    cscol = p.sb("s_cscol", [128, 2], F32)
    cl = p.sb("s_cl", [128, 1], F32)
    ecl = p.sb("s_ecl", [128, 1], F32)
    dec = p.sb("s_dec", [128, 2], F32)
    lt = p.sb("s_lt", [128, 2, 256], F32)
    ecs = p.sb("s_ecs", [128, 256], F32)
    BCT = p.sb("s_BCT", [128, 512], F32)
    scT = p.sb("s_scT", [128, 2, 256], F32)
    CsT = p.sb("s_CsT", [128, 256], F32)
    Xd = p.sb("s_Xd", [128, 2, 64], F32)
    Xdd = p.sb("s_Xdd", [128, 2, 64], F32)
    yo = [p.sb(f"s_yo{i}", [128, 2, 64], F32) for i in range(2)]
    for c in range(NCH):
        i2 = c % 2
        x4 = X4[i2]
        p.dma("sp" if i2 == 0 else "act", x4[:].rearrange("p j k n -> p j (k n)"),
              xc4_d[c * 256:(c + 1) * 256].rearrange("(j p) k n -> p j (k n)", p=128), writes=[("s_x4", i2)])
        for j in range(2):
            eng = "dve" if j == 0 else "pool"
            p.op(eng, lambda e, j=j, x4=x4: e.tensor_tensor(out=acc[:, j, :], in0=x4[:, j, 0, :], in1=CW[:, 0, :], op=ALU.mult),
                 reads=[("s_x4", i2), "CW"], writes=[("s_acc", j)])
            for k in range(1, 4):
                p.op(eng, lambda e, j=j, k=k, x4=x4: e.tensor_tensor(out=tmp[:, j, :], in0=x4[:, j, k, :], in1=CW[:, k, :], op=ALU.mult),
                     reads=[("s_x4", i2), "CW"], writes=[("s_tmp", j)])
                p.op(eng, lambda e, j=j: e.tensor_tensor(out=acc[:, j, :], in0=acc[:, j, :], in1=tmp[:, j, :], op=ALU.add),
                     reads=[("s_acc", j), ("s_tmp", j)], writes=[("s_acc", j)])
            p.op(eng, lambda e, j=j: e.tensor_tensor(out=acc[:, j, :], in0=acc[:, j, :], in1=CB[:], op=ALU.add),
                 reads=[("s_acc", j), "CB"], writes=[("s_acc", j)])
        p.op("act", lambda e: e.activation(out=XA[:].rearrange("p j n -> p (j n)"), in_=acc[:].rearrange("p j n -> p (j n)"),
                                           func=AF.Silu), reads=[("s_acc", 0), ("s_acc", 1)], writes=["s_XA"])
        p.op("dve", lambda e, c=c: e.tensor_scalar(out=dx[:], in0=DTR[:, 2 * c:2 * c + 2], scalar1=SCL[:, 0:1], scalar2=None,
                                                    op0=ALU.add), reads=["DTR", "SCL"], writes=["s_dx"])
        p.op("act", lambda e: e.activation(out=dax[:], in_=dx[:], func=AF.Abs), reads=["s_dx"], writes=["s_dax"])
        p.op("act", lambda e: e.activation(out=dl[:], in_=dax[:], func=AF.Exp, scale=-1.0), reads=["s_dax"], writes=["s_dl"])
        p.op("act", lambda e: e.activation(out=dl[:], in_=dl[:], func=AF.Ln, bias=ONE1[:], scale=1.0),
             reads=["s_dl", "ONE1"], writes=["s_dl"])
        p.op("dve", lambda e: e.scalar_tensor_tensor(out=dtt[:], in0=dx[:], scalar=0.0, in1=dl[:], op0=ALU.max, op1=ALU.add),
             reads=["s_dx", "s_dl"], writes=["s_dt"])
        p.op("dve", lambda e: e.tensor_scalar(out=aa[:], in0=dtt[:], scalar1=ANEG[:, 0:1], scalar2=None, op0=ALU.mult),
             reads=["s_dt", "ANEG"], writes=["s_a"])
        for i in range(2):
            p.op("dve", lambda e, i=i: e.tensor_scalar(out=abc[:, i, :], in0=ONES[:], scalar1=aa[:, i:i + 1], scalar2=None,
                                                       op0=ALU.mult), reads=["s_a", "ONES"], writes=[("s_abc", i)])
        for i in range(2):
            p.op("pe", lambda e, i=i: e.matmul(bank[0][:, 0:256], lhsT=abc[:, i, :], rhs=TRIW[:, i, :], start=(i == 0), stop=(i == 1)),
                 reads=[("s_abc", i), "TRIW"], writes=[("sbank", 0)])
        p.op("pe", lambda e: e.matmul(bank[0][:, 256:257], lhsT=TRI[:], rhs=aa[:, 0:1], start=True, stop=True),
             reads=["TRI", "s_a"], writes=[("sbank", 0)])
        p.op("pe", lambda e: e.matmul(bank[0][:, 257:258], lhsT=ONES[:], rhs=aa[:, 0:1], start=True, stop=False),
             reads=["ONES", "s_a"], writes=[("sbank", 0)])
        p.op("pe", lambda e: e.matmul(bank[0][:, 257:258], lhsT=TRI[:], rhs=aa[:, 1:2], start=False, stop=True),
             reads=["TRI", "s_a"], writes=[("sbank", 0)])
        p.op("dve", lambda e: e.tensor_copy(out=cscol[:], in_=bank[0][:, 256:258]), reads=[("sbank", 0)], writes=["s_cscol"])
        p.op("dve", lambda e: e.tensor_copy(out=cl[:], in_=bank[0][:, 255:256]), reads=[("sbank", 0)], writes=["s_cl"])
        for i in range(2):
            p.op("dve", lambda e, i=i: e.scalar_tensor_tensor(out=lt[:, i, :], in0=bank[0][:, 0:256], scalar=cscol[:, i:i + 1],
                                                              in1=NTRIW[:, i, :], op0=ALU.subtract, op1=ALU.add),
                 reads=[("sbank", 0), "s_cscol", "NTRIW"], writes=[("s_lt", i)])
            p.op("act", lambda e, i=i: e.activation(out=lt[:, i, :], in_=lt[:, i, :], func=AF.Exp),
                 reads=[("s_lt", i)], writes=[("s_lt", i)])
        p.op("act", lambda e: e.activation(out=ecs[:], in_=bank[0][:, 0:256], func=AF.Exp), reads=[("sbank", 0)], writes=["s_ecs"])
        for j in range(2):
            p.op("pe", lambda e, j=j: e.transpose(out=bank[1][:, j * 128:(j + 1) * 128], in_=XA[:, j, 64:192], identity=IDF[:]),
                 reads=["s_XA", "IDF"], writes=[("sbank", 1)])
            p.op("pe", lambda e, j=j: e.transpose(out=bank[1][:, 256 + j * 128:256 + (j + 1) * 128], in_=XA[:, j, 192:320],
                                                  identity=IDF[:]), reads=["s_XA", "IDF"], writes=[("sbank", 1)])
        p.op("act", lambda e: e.activation(out=BCT[:], in_=bank[1][:], func=AF.Copy), reads=[("sbank", 1)], writes=["s_BCT"])
        for i in range(2):
            p.op("pe", lambda e, i=i: e.matmul(bank[2][:, i * 256:(i + 1) * 256], lhsT=BCT[:, i * 128:(i + 1) * 128],
                                               rhs=BCT[:, 256:512], start=True, stop=True),
                 reads=["s_BCT"], writes=[("sbank", 2)])
        for i in range(2):
            p.op("dve", lambda e, i=i: e.tensor_tensor(out=scT[:, i, :], in0=bank[2][:, i * 256:(i + 1) * 256], in1=lt[:, i, :],
                                                       op=ALU.mult), reads=[("sbank", 2), ("s_lt", i)], writes=[("s_scT", i)])
        p.op("pool", lambda e: e.tensor_tensor(out=CsT[:], in0=BCT[:, 256:512], in1=ecs[:], op=ALU.mult),
             reads=["s_BCT", "s_ecs"], writes=["s_CsT"])
        for i in range(2):
            p.op("dve", lambda e, i=i: e.tensor_scalar(out=Xd[:, i, :], in0=XA[:, i, 0:64], scalar1=dtt[:, i:i + 1], scalar2=None,
                                                       op0=ALU.mult), reads=["s_XA", "s_dt"], writes=[("s_Xd", i)])
        for j in range(2):
            ops = [(scT[:, i, j * 128:(j + 1) * 128], Xd[:, i, :], [("s_scT", i), ("s_Xd", i)]) for i in range(j + 1)]
            ops.append((CsT[:, j * 128:(j + 1) * 128], S[:], ["s_CsT", "S"]))
            for n, (lh, rh, rd) in enumerate(ops):
                p.op("pe", lambda e, lh=lh, rh=rh, n=n, j=j, last=(n == len(ops) - 1): e.matmul(
                    bank[3][:, j * 64:(j + 1) * 64], lhsT=lh, rhs=rh, start=(n == 0), stop=last),
                    reads=rd, writes=[("sbank", 3)])
        y2 = yo[i2]
        for j in range(2):
            p.op("dve", lambda e, j=j, y2=y2: e.scalar_tensor_tensor(out=y2[:, j, :], in0=XA[:, j, 0:64], scalar=SCL[:, 2:3],
                                                                      in1=bank[3][:, j * 64:(j + 1) * 64], op0=ALU.mult, op1=ALU.add),
                 reads=["s_XA", "SCL", ("sbank", 3)], writes=[("s_yo", i2)])
        p.dma("act", y_d[c * 256:(c + 1) * 256, :].rearrange("(j p) n -> p j n", p=128), y2[:], reads=[("s_yo", i2)])
        p.op("act", lambda e: e.activation(out=dec[:], in_=cscol[:], func=AF.Exp, bias=cl[:], scale=-1.0),
             reads=["s_cscol", "s_cl"], writes=["s_dec"])
        for i in range(2):
            p.op("dve", lambda e, i=i: e.tensor_scalar(out=Xdd[:, i, :], in0=Xd[:, i, :], scalar1=dec[:, i:i + 1], scalar2=None,
                                                       op0=ALU.mult), reads=[("s_Xd", i), "s_dec"], writes=[("s_Xdd", i)])
        for i in range(2):
            p.op("pe", lambda e, i=i: e.matmul(bank[4][:, 0:64], lhsT=XA[:, i, 64:192], rhs=Xdd[:, i, :], start=(i == 0), stop=(i == 1)),
                 reads=["s_XA", ("s_Xdd", i)], writes=[("sbank", 4)])
        p.op("act", lambda e: e.activation(out=ecl[:], in_=cl[:], func=AF.Exp), reads=["s_cl"], writes=["s_ecl"])
        p.op("dve", lambda e: e.scalar_tensor_tensor(out=S[:], in0=S[:], scalar=ecl[:, 0:1], in1=bank[4][:, 0:64],
                                                     op0=ALU.mult, op1=ALU.add), reads=["S", "s_ecl", ("sbank", 4)], writes=["S"])
    p.finish()
    return nc


NQI = SEQ // 256
NQF = SEQ // 256


def build_N():
    nc = bass.Bass("TRN2", target_bir_lowering=False)
    di = lambda n, s, d=F32: nc.dram_tensor(n, list(s), d, kind="ExternalInput").ap()
    do = lambda n, s, d=F32: nc.dram_tensor(n, list(s), d, kind="ExternalOutput").ap()
    QT_d = di("QT", [64, 4, NQF, 128], BF16)
    blk_d = di("blk", [2, 16, 128, 1024], BF16)
    w1_d = di("cw1", [2, 2048, 256])
    b1_d = di("cb1", [2, 128, 2])
    pe_d = di("cpe", [2, 128, 16])
    w2_d = di("cw2", [2, 256, 64])
    KsT_d = di("KsT", [64, SEQ], BF16)
    VsA_d = di("VsA", [SEQ, 65], BF16)
    KwT_d = di("KwT", [64, SEQ + 512], BF16)
    VwA_d = di("VwA", [SEQ + 512, 65], BF16)
    gates_d = di("gates", [NQF, 128, 12])
    TC_d = di("TC", [10, 128, 512])
    TS_d = di("TS", [4, 128, 512])
    TW_d = di("TW", [5, 128, 512])
    OV_d = di("OV", [8, 128, 256])
    FB_d = di("FB", [NQF, 128, 256])
    EX_d = di("EX", [128, 64 * 128])
    ident_d = di("ident", [128, 128])
    o_d = do("o", [NQF, 128, 256])
    p = Prog(nc)
    SB = [p.ps(f"nS{i}", [128, 512], F32) for i in range(2)]
    OB = [p.ps(f"nO{i}", [128, 512], F32) for i in range(3)]
    IMPB = [p.ps(f"nI{i}", [128, 512], F32) for i in range(2)]
    PT = p.ps("nPT", [128, 1024], BF16)
    identb = p.sb("identb", [128, 128], BF16)
    p.dma("pool", identb[:], ident_d, writes=["identb"])
    KC = p.sb("KC", [64, 1024], BF16)
    VC = p.sb("VC", [128, 8, 65], BF16)
    p.op("dve", lambda e: e.memset(VC[:], 1.0), writes=["VC"])
    import os
    NSK = os.environ.get("NSKIP", "")
    with p.scope() as st:
        W1 = p.sb("n_W1", [128, 2, 16, 256], BF16, st)
        W2 = p.sb("n_W2", [128, 2, 2, 64], BF16, st)
        PE2 = p.sb("n_PE", [128, 2, 16], BF16, st)
        B1 = p.sb("n_B1", [128, 2, 2], F32, st)
        hid = p.sb("n_hid", [128, 2, 2, 1024], BF16, st)
        blk = [p.sb(f"n_blk{i}", [128, 1024], BF16, st) for i in range(2)]
        for j in range(2):
            p.dma("pool", W1[:, j, :, :], w1_d[j].rearrange("(lt p) n -> p lt n", p=128), writes=["n_W1"])
            p.dma("pool", W2[:, j, :, :], w2_d[j].rearrange("(ht p) n -> p ht n", p=128), writes=["n_W2"])
            p.dma("pool", PE2[:, j, :], pe_d[j], writes=["n_PE"])
            p.dma("sp", B1[:, j, :], b1_d[j], writes=["n_B1"])
        for j in range(2):
            for ht in range(2):
                for lt in range(16):
                    p.op("pe", lambda e, j=j, ht=ht, lt=lt: e.matmul(SB[0][:, 0:1], lhsT=W1[:, j, lt, ht * 128:(ht + 1) * 128],
                                                                      rhs=PE2[:, j, lt:lt + 1], start=(lt == 0), stop=(lt == 15)),
                         reads=["n_W1", "n_PE"], writes=[("nS", 0)])
                p.op("dve", lambda e, j=j, ht=ht: e.tensor_tensor(out=B1[:, j, ht:ht + 1], in0=B1[:, j, ht:ht + 1], in1=SB[0][:, 0:1],
                                                                  op=ALU.add), reads=[("nS", 0), "n_B1"], writes=["n_B1"])
            for lt in range(16):
                bb = blk[lt % 2]
                p.dma("sp" if lt % 2 == 0 else "act", bb[:], blk_d[j, lt], writes=[("n_blk", lt % 2)])
                for ht in range(2):
                    for ic in range(2):
                        bank = (SB + OB + IMPB)[ht * 2 + ic + 2]
                        p.op("pe", lambda e, j=j, ht=ht, ic=ic, lt=lt, bb=bb, bank=bank: e.matmul(
                            bank[:], lhsT=W1[:, j, lt, ht * 128:(ht + 1) * 128], rhs=bb[:, ic * 512:(ic + 1) * 512],
                            start=(lt == 0), stop=(lt == 15)), reads=["n_W1", ("n_blk", lt % 2)], writes=[("ncmp", ht, ic)])
            for ht in range(2):
                for ic in range(2):
                    bank = (SB + OB + IMPB)[ht * 2 + ic + 2]
                    p.op("act", lambda e, j=j, ht=ht, ic=ic, bank=bank: e.activation(
                        out=hid[:, j, ht, ic * 512:(ic + 1) * 512], in_=bank[:], func=AF.Gelu_apprx_tanh, bias=B1[:, j, ht:ht + 1], scale=1.0),
                        reads=[("ncmp", ht, ic), "n_B1"], writes=[("n_hid", j)])
        for ic in range(2):
            for ht in range(2):
                p.op("pe", lambda e, ic=ic, ht=ht: e.matmul(SB[0][0:64, :], lhsT=W2[:, 0, ht, :], rhs=hid[:, 0, ht, ic * 512:(ic + 1) * 512],
                                                            start=(ht == 0), stop=(ht == 1)), reads=["n_W2", ("n_hid", 0)], writes=[("nS", 0)])
            p.op("act", lambda e, ic=ic: e.activation(out=KC[:, ic * 512:(ic + 1) * 512], in_=SB[0][0:64, :], func=AF.Copy),
                 reads=[("nS", 0)], writes=["KC"])
        for it in range(8):
            for ht in range(2):
                p.op("pe", lambda e, it=it, ht=ht: e.matmul(SB[1][:, it * 64:(it + 1) * 64], lhsT=hid[:, 1, ht, it * 128:(it + 1) * 128],
                                                            rhs=W2[:, 1, ht, :], start=(ht == 0), stop=(ht == 1)),
                     reads=["n_W2", ("n_hid", 1)], writes=[("nS", 1)])
        p.op("act", lambda e: e.activation(out=VC[:, :, 0:64], in_=SB[1][:].rearrange("p (a b) -> p a b", b=64), func=AF.Copy),
             reads=[("nS", 1)], writes=["VC"])
    KsT = p.sb("KsT_sb", [64, SEQ], BF16)
    VsA = p.sb("VsA_sb", [128, SEQ // 128, 65], BF16)
    for c4 in range(4):
        p.dma("sp", KsT[:, c4 * 4096:(c4 + 1) * 4096], KsT_d[:, c4 * 4096:(c4 + 1) * 4096], writes=["KsT"])
        p.dma("act", VsA[:, c4 * 32:(c4 + 1) * 32, :], VsA_d[c4 * 4096:(c4 + 1) * 4096, :].rearrange("(n p) c -> p n c", p=128),
              writes=["VsA"])
    TC = p.sb("TC_sb", [128, 10, 512], F32)
    TS = p.sb("TS_sb", [128, 4, 512], F32)
    TW = p.sb("TW_sb", [128, 5, 512], F32)
    OV = p.sb("OV_sb", [128, 8, 256], BF16)
    EX = p.sb("EX_sb", [128, 64, 128], BF16)
    p.dma("sp", TC[:], TC_d.rearrange("n p c -> p n c"), writes=["TC"])
    p.dma("sp", TS[:], TS_d.rearrange("n p c -> p n c"), writes=["TS"])
    p.dma("sp", TW[:], TW_d.rearrange("n p c -> p n c"), writes=["TW"])
    p.dma("pool", OV[:], OV_d.rearrange("n p c -> p n c"), writes=["OV"])
    p.dma("pool", EX[:].rearrange("p a b -> p (a b)"), EX_d, writes=["EX"])
    Qt = [p.sb(f"n_q{i}", [64, 4, 128], BF16) for i in range(2)]
    GT = [p.sb(f"n_g{i}", [128, 12], F32) for i in range(2)]
    FBt = [p.sb(f"n_fb{i}", [128, 256], F32) for i in range(2)]
    Kw = [p.sb(f"n_kw{i}", [64, 640], BF16) for i in range(2)]
    Vw = [p.sb(f"n_vw{i}", [128, 5, 65], BF16) for i in range(2)]
    sbt = [p.sb(f"n_sb{i}", [128, 512], F32) for i in range(4)]
    Et = [p.sb(f"n_E{i}", [128, 512], BF16) for i in range(4)]
    rsc = p.sb("n_rsc", [128, 4], F32)
    imp = p.sb("n_imp", [128, 256], F32)
    imp2 = p.sb("n_imp2", [128, 256], F32)
    m8 = p.sb("n_m8", [128, 8], F32)
    m8b = p.sb("n_m8b", [128, 8], F32)
    thr = p.sb("n_thr", [128, 1], F32)
    negm = p.sb("n_negm", [128, 256], BF16)
    NegMT = p.sb("n_negmT", [128, 2, 128], BF16)
    wbr = p.sb("n_wbr", [128, 3, 4], F32)
    ot = [p.sb(f"n_o{i}", [128, 256], F32) for i in range(2)]
    cnt = [0]

    SBANKS = [(SB[0], ("nS", 0)), (SB[1], ("nS", 1)), (IMPB[0], ("nI", 0)), (IMPB[1], ("nI", 1))]

    def att_s(t, qt_ap, qkey, nb):
        b = cnt[0] % nb
        cnt[0] += 1
        (S, skey), sbb, E = SBANKS[b], sbt[b], Et[b]
        mask = t.get("mask")
        p.op("pe", lambda e: e.matmul(S[:], lhsT=t["kT"], rhs=qt_ap, start=True, stop=(mask is None)),
             reads=t["kkeys"] + [qkey], writes=[skey])
        if mask is not None:
            ex, nm = mask
            p.op("pe", lambda e: e.matmul(S[:].rearrange("p (h q) -> p h q", h=4), lhsT=ex,
                                          rhs=nm.unsqueeze(1).to_broadcast([128, 4, 128]), start=False, stop=True),
                 reads=["EX", "n_negmT"], writes=[skey])
        p.op("dve", lambda e: e.tensor_tensor(out=sbb[:], in0=S[:], in1=t["table"], op=ALU.add),
             reads=[skey] + t["tkeys"], writes=[("n_sb", b)])
        p.op("act", lambda e: e.activation(out=E[:], in_=sbb[:], func=AF.Exp), reads=[("n_sb", b)], writes=[("n_E", b)])
        return E, b

    def att_pv(t, E, b):
        acc_i = t["acc"]
        for h in range(4):
            p.op("pe", lambda e, h=h: e.matmul(OB[acc_i][:, h * 65:(h + 1) * 65], lhsT=E[:, h * 128:(h + 1) * 128], rhs=t["v"],
                                               start=False, stop=False, skip_group_check=True),
                 reads=[("n_E", b)] + t["vkeys"], writes=[("nO", acc_i)])
        kc = t.get("imp_kc")
        if kc is not None:
            for h in range(4):
                p.op("pe", lambda e, h=h: e.matmul(IMPB[h // 2][:, (h % 2) * 256:(h % 2 + 1) * 256],
                                                   lhsT=E[:, h * 128:(h + 1) * 128], rhs=OV[:, kc, :],
                                                   start=False, stop=False, skip_group_check=True),
                     reads=[("n_E", b), "OV"], writes=[("nI", h // 2)])

    def run_tiles(tiles, qt_ap, qkey, nb):
        pend = []
        for t in tiles:
            cur = att_s(t, qt_ap, qkey, nb)
            pend.append((t, cur[0], cur[1]))
            if len(pend) > nb - 1:
                att_pv(*pend.pop(0))
        for x in pend:
            att_pv(*x)

    for i in range(NQI):
        i2 = i % 2
        q, g_, fb, kw, vw = Qt[i2], GT[i2], FBt[i2], Kw[i2], Vw[i2]
        p.dma("sp", q[:], QT_d[:, :, i, :], writes=[("n_q", i2)])
        p.dma("sp", g_[:], gates_d[i], writes=[("n_g", i2)])
        p.dma("sp", fb[:], FB_d[i], writes=[("n_fb", i2)])
        p.dma("act", kw[:], KwT_d[:, 2 * i * 128:(2 * i + 5) * 128], writes=[("n_kw", i2)])
        p.dma("act", vw[:], VwA_d[2 * i * 128:(2 * i + 5) * 128, :].rearrange("(n p) c -> p n c", p=128), writes=[("n_vw", i2)])
        for a in range(3):
            p.op("dve", lambda e, a=a: e.memset(OB[a][:], 0.0), writes=[("nO", a)])
        for a in range(2):
            p.op("dve", lambda e, a=a: e.memset(IMPB[a][:], 0.0), writes=[("nI", a)])
        qap = q[:].rearrange("p h q -> p (h q)")
        qk = ("n_q", i2)
        tiles = []
        for kc in range((2 * i + 1) // 16 + 1):
            e_ = 2 * i - 16 * kc
            tidx = e_ // 2 if e_ <= 16 else 9
            tiles.append(dict(kT=KC[:, kc * 128:(kc + 1) * 128], kkeys=["KC"], v=VC[:, kc, :], vkeys=["VC"],
                              table=TC[:, tidx, :], tkeys=["TC"], acc=0, imp_kc=kc))
        for d in range(5):
            tiles.append(dict(kT=kw[:, d * 128:(d + 1) * 128], kkeys=[("n_kw", i2)], v=vw[:, d, :], vkeys=[("n_vw", i2)],
                              table=TW[:, d, :], tkeys=["TW"], acc=2))
        cnt[0] = 0
        run_tiles(tiles, qap, qk, 2)
        sums = lambda a: OB[a][:, 0:260].rearrange("p (h c) -> p h c", c=65)[:, :, 64]
        p.op("dve", lambda e: e.tensor_scalar(out=rsc[:], in0=sums(0), scalar1=1e-30, scalar2=None, op0=ALU.max),
             reads=[("nO", 0)], writes=["n_rsc"])
        p.op("dve", lambda e: e.reciprocal(out=rsc[:], in_=rsc[:]), reads=["n_rsc"], writes=["n_rsc"])
        for h in range(4):
            p.op("dve", lambda e, h=h, fb=fb: e.scalar_tensor_tensor(
                out=imp[:], in0=IMPB[h // 2][:, (h % 2) * 256:(h % 2 + 1) * 256], scalar=rsc[:, h:h + 1],
                in1=(fb[:] if h == 0 else imp[:]), op0=ALU.mult, op1=ALU.add),
                reads=[("nI", h // 2), "n_rsc", ("n_fb", i2), "n_imp"], writes=["n_imp"])
        p.op("dve", lambda e: e.max(out=m8[:], in_=imp[:]), reads=["n_imp"], writes=["n_m8"])
        p.op("dve", lambda e: e.match_replace(out=imp2[:], in_to_replace=m8[:], in_values=imp[:], imm_value=-1e9),
             reads=["n_imp", "n_m8"], writes=["n_imp2"])
        p.op("dve", lambda e: e.max(out=m8b[:], in_=imp2[:]), reads=["n_imp2"], writes=["n_m8b"])
        p.op("dve", lambda e: e.tensor_scalar(out=thr[:], in0=m8b[:, 7:8], scalar1=-5000.0, scalar2=None, op0=ALU.max),
             reads=["n_m8b"], writes=["n_thr"])
        p.op("dve", lambda e: e.tensor_scalar(out=imp2[:], in0=imp[:], scalar1=thr[:, 0:1], scalar2=None, op0=ALU.is_ge),
             reads=["n_imp", "n_thr"], writes=["n_imp2"])
        p.op("dve", lambda e: e.tensor_scalar(out=negm[:], in0=imp2[:], scalar1=-1.0, scalar2=30000.0, op0=ALU.add, op1=ALU.mult),
             reads=["n_imp2"], writes=["n_negm"])
        for jc in range(2):
            p.op("pe", lambda e, jc=jc: e.transpose(out=PT[:, jc * 128:(jc + 1) * 128], in_=negm[:, jc * 128:(jc + 1) * 128],
                                                    identity=identb[:]), reads=["n_negm", "identb"], writes=["nPT"])
        p.op("act", lambda e: e.activation(out=NegMT[:].rearrange("p a b -> p (a b)"), in_=PT[:, 0:256], func=AF.Copy),
             reads=["nPT"], writes=["n_negmT"])
        tiles = []
        for kt in range(2 * i + 2):
            d3 = kt - (2 * i - 1)
            tb = TS[:, d3, :] if d3 >= 0 else TS[:, 3, :]
            tiles.append(dict(kT=KsT[:, kt * 128:(kt + 1) * 128], kkeys=["KsT"], v=VsA[:, kt, :], vkeys=["VsA"],
                              table=tb, tkeys=["TS"], acc=1, mask=(EX[:, kt % 64, :], NegMT[:, kt // 64, :])))
        cnt[0] = 0
        run_tiles(tiles, qap, qk, 4)
        o = ot[i2]
        for a in range(3):
            p.op("dve", lambda e, a=a: e.tensor_scalar(out=wbr[:, a, :], in0=sums(a), scalar1=1e-30, scalar2=None, op0=ALU.max),
                 reads=[("nO", a)], writes=[("n_wbr", a)])
            p.op("dve", lambda e, a=a: e.reciprocal(out=wbr[:, a, :], in_=wbr[:, a, :]), reads=[("n_wbr", a)], writes=[("n_wbr", a)])
            p.op("dve", lambda e, a=a, g_=g_: e.tensor_tensor(out=wbr[:, a, :], in0=wbr[:, a, :],
                                                               in1=g_[:].rearrange("p (h c) -> p h c", c=3)[:, :, a], op=ALU.mult),
                 reads=[("n_wbr", a), ("n_g", i2)], writes=[("n_wbr", a)])
        for h in range(4):
            for a in range(3):
                src = OB[a][:, h * 65:h * 65 + 64]
                if a == 0:
                    p.op("dve", lambda e, h=h, src=src, o=o: e.tensor_scalar(out=o[:, h * 64:(h + 1) * 64], in0=src,
                                                                            scalar1=wbr[:, 0, h:h + 1], scalar2=None, op0=ALU.mult),
                         reads=[("nO", 0), ("n_wbr", 0)], writes=[("n_o", i2)])
                else:
                    p.op("dve", lambda e, h=h, a=a, src=src, o=o: e.scalar_tensor_tensor(
                        out=o[:, h * 64:(h + 1) * 64], in0=src, scalar=wbr[:, a, h:h + 1], in1=o[:, h * 64:(h + 1) * 64],
                        op0=ALU.mult, op1=ALU.add), reads=[("nO", a), ("n_wbr", a), ("n_o", i2)], writes=[("n_o", i2)])
        p.dma("act", o_d[i], o[:], reads=[("n_o", i2)])
    p.finish()
    return nc


_CONST = {}


def _consts():
    if not _CONST:
        _CONST["ident"] = np.eye(128, dtype=np.float32)
        s = np.arange(128)
        _CONST["tri"] = (s[:, None] <= s[None, :]).astype(np.float32)
    return _CONST


_NC_CACHE = {}


def _get_nc(name, builder):
    return builder()


def run_A(inp, l, x_cur):
    cst = _consts()
    c = inp["c"]
    maps = []
    cols = np.r_[0:3072, 3072:5120]
    ada_w = np.ascontiguousarray(inp["ada_w"][l][:, cols])
    ada_b = np.ascontiguousarray(inp["ada_b"][l][cols])
    wsT = np.ascontiguousarray(inp["gmlp_ws"][l].transpose(0, 2, 1))
    bsT = np.ascontiguousarray(inp["gmlp_bs"][l].T)
    for r in range(NCORE):
        b = r // 4
        maps.append({
            "x": np.ascontiguousarray(x_cur[r * TOKC:(r + 1) * TOKC]),
            "cT": np.ascontiguousarray(c[b].reshape(8, 128).T),
            "ada_w": ada_w, "ada_b": ada_b,
            "lng": inp["ln_g"][l, 0], "lnb": inp["ln_b"][l, 0],
            "w1": inp["ffn_w1"][l, 0], "w3": inp["ffn_w3"][l, 0], "w2": inp["ffn_w2"][l, 0],
            "w_in": inp["w_in"][l],
            "glng": inp["gmlp_ln_g"][l], "glnb": inp["gmlp_ln_b"][l],
            "wsT": wsT, "bsT": bsT,
            "ident": cst["ident"], "tri": cst["tri"],
        })
    nc = build_A()
    res = run_bass_kernel_spmd(nc, maps, core_ids=list(range(NCORE)))
    return res.results


def run_C(inp, l, x1, oatt, ogm, yssm, zz):
    cst = _consts()
    c = inp["c"]
    cols = np.r_[5120:9216]
    ada_w = np.ascontiguousarray(inp["ada_w"][l][:, cols])
    ada_b = np.ascontiguousarray(inp["ada_b"][l][cols])
    maps = []
    for r in range(NCORE):
        b = r // 4
        sl = slice(r * TOKC, (r + 1) * TOKC)
        maps.append({
            "x1": np.ascontiguousarray(x1[sl]), "cT": np.ascontiguousarray(c[b].reshape(8, 128).T),
            "ada_w": ada_w, "ada_b": ada_b,
            "lng1": inp["ln_g"][l, 1], "lnb1": inp["ln_b"][l, 1], "lng2": inp["ln_g"][l, 2], "lnb2": inp["ln_b"][l, 2],
            "w1": inp["ffn_w1"][l, 1], "w3": inp["ffn_w3"][l, 1], "w2": inp["ffn_w2"][l, 1],
            "w_out": inp["w_out"][l],
            "oatt": np.ascontiguousarray(oatt[sl]), "ogm": np.ascontiguousarray(ogm[sl]),
            "yssm": np.ascontiguousarray(yssm[sl]), "zz": np.ascontiguousarray(zz[sl]),
            "normg": inp["ssm_norm_g"][l], "ident": cst["ident"],
        })
    nc = build_C()
    res = run_bass_kernel_spmd(nc, maps, core_ids=list(range(NCORE)))
    return res.results


def run_S(inp, l, xbc, dtr):
    cst = _consts()
    s_ = np.arange(128)
    l_ = np.arange(256)
    triw = np.stack([(s_[:, None] + 128 * i <= l_[None, :]).astype(np.float32) for i in range(2)])
    ntriw = ((triw - 1.0) * 30000.0).astype(np.float32)
    maps = []
    cwl, cbl = inp["ssm_conv_w"][l], inp["ssm_conv_b"][l]
    for r in range(NCORE):
        b, h = r // 4, r % 4
        g = h // 2
        cols = np.r_[64 * h:64 * h + 64, 256 + 128 * g:256 + 128 * g + 128, 512 + 128 * g:512 + 128 * g + 128]
        xcat = xbc[b][:, cols]
        xpad = np.concatenate([np.zeros((3, 320), np.float32), xcat], 0)
        xc4 = np.ascontiguousarray(np.stack([xpad[k:k + SEQ] for k in range(4)], axis=1))
        maps.append({
            "xc4": xc4, "cw": np.ascontiguousarray(cwl[:, cols]).reshape(-1), "cb": np.ascontiguousarray(cbl[cols]),
            "dtr": np.ascontiguousarray(dtr[b][:, h]),
            "scal": np.array([inp["ssm_dt_bias"][l, h], inp["ssm_a_log"][l, h], inp["ssm_d"][l, h], 0.0], np.float32),
            "tri": cst["tri"], "triw": triw, "ntriw": ntriw, "identf": cst["ident"],
        })
    nc = build_S()
    res = run_bass_kernel_spmd(nc, maps, core_ids=list(range(NCORE)))
    y = np.zeros((BATCH, SEQ, 256), np.float32)
    for r in range(NCORE):
        y[r // 4][:, 64 * (r % 4):64 * (r % 4) + 64] = res.results[r]["y"]
    return y


def _bucket(n):
    n = np.maximum(n, 0)
    nf = np.maximum(n, 1).astype(np.float32)
    large = 16 + (np.log(nf / np.float32(16)) / np.float32(np.log(8.0)) * np.float32(16)).astype(np.int32)
    large = np.minimum(large, 31)
    return np.where(n < 16, n, large)


def _table(rb_aug, dist, valid, g):
    idx = np.where(valid, _bucket(dist), 32)
    t = rb_aug[idx][:, :, 4 * g:4 * g + 4]
    return np.ascontiguousarray(t.transpose(0, 2, 1).reshape(128, 512))


def run_N(inp, l, QT_all, KcT_all, VcT_all, KsT_all, KwT_all, Vs_all, Vw_all, gates_all):
    cst = _consts()
    T = SEQ
    rb_aug = np.concatenate([inp["rel_bias"], np.full((1, 8), -30000.0, np.float32)], 0)
    k = np.arange(128)[:, None]
    q = np.arange(128)[None, :]
    allv = np.ones((128, 128), bool)
    big = np.full((128, 128), 100000)
    kg = np.arange(1024)
    cs = 16 * kg
    j = np.arange(256)
    ov = np.clip(np.minimum(cs[:, None] + 32, 64 * j[None, :] + 64) - np.maximum(cs[:, None], 64 * j[None, :]), 0, None) / 32.0
    ov[1023] = 0.0
    OV = ov.reshape(8, 128, 256).astype(np.float32)
    jj = np.arange(128)[:, None, None]
    mm = np.arange(64)[None, :, None]
    kk = np.arange(128)[None, None, :]
    EX = (jj == 2 * mm + kk // 64).astype(np.float32).reshape(128, 64 * 128)
    maps = []
    for r in range(NCORE):
        b, g, par = r // 4, (r // 2) % 2, r % 2
        TC = [_table(rb_aug, 128 * (2 * e2 + par) + q - 16 * k - 31, (128 * (2 * e2 + par) + q - 16 * k - 31) >= 0, g) for e2 in range(9)]
        TC.append(_table(rb_aug, big, allv, g))
        prev = _table(rb_aug, 128 + q - k, allv, g)
        diag = _table(rb_aug, q - k, (q - k) >= 0, g)
        far = _table(rb_aug, big, allv, g)
        none = _table(rb_aug, big, ~allv, g)
        TS = [prev, diag, none, far] if par == 0 else [far, prev, diag, far]
        TW = [_table(rb_aug, q + 128 * (4 - d) - k, ((q + 128 * (4 - d) - k) >= 0) & ((q + 128 * (4 - d) - k) < 512), g) for d in range(5)]
        qt = 2 * np.arange(NQF) + par
        t = (128 * qt[:, None] + np.arange(128)[None, :])
        cur = (t // 64)[:, :, None]
        jb = np.arange(256)[None, None, :]
        FB = np.where((jb == 0) | (jb == cur) | (jb == cur - 1), 1e4, np.where(jb <= cur, 0.0, -1e4)).astype(np.float32)
        tok = slice(b * T, (b + 1) * T)
        QTc = QT_all[:, tok].reshape(8, 64, 128, 128)[4 * g:4 * g + 4, :, par::2, :].transpose(1, 0, 2, 3)
        blks = []
        for src in (KcT_all, VcT_all):
            kcg = src[g * 64:(g + 1) * 64, tok]
            bl = np.zeros((16, 2, 64, 1024), kcg.dtype)
            ii = np.arange(1023)
            for lt in range(16):
                for lb in range(2):
                    bl[lt, lb, :, :1023] = kcg[:, 16 * ii + 2 * lt + lb]
            blks.append(bl.reshape(16, 128, 1024))
        KsT = KsT_all[g * 64:(g + 1) * 64, tok]
        ones = np.ones((T, 1), Vs_all.dtype)
        VsA = np.concatenate([Vs_all[tok, g * 64:(g + 1) * 64], ones], 1)
        KwP = np.zeros((64, 512 + T + 128), KwT_all.dtype)
        KwP[:, 512:512 + T] = KwT_all[g * 64:(g + 1) * 64, tok]
        VwP = np.zeros((512 + T + 128, 65), Vw_all.dtype)
        VwP[512:512 + T] = np.concatenate([Vw_all[tok, g * 64:(g + 1) * 64], ones], 1)
        gt = gates_all[tok, 12 * g:12 * g + 12].reshape(128, 128, 12)[par::2]
        maps.append({
            "QT": np.ascontiguousarray(QTc), "blk": np.ascontiguousarray(np.stack(blks)),
            "cw1": inp["cmp_w1"][l], "cb1": np.ascontiguousarray(inp["cmp_b1"][l].reshape(2, 2, 128).transpose(0, 2, 1)),
            "cpe": np.ascontiguousarray(inp["cmp_pe"][l].reshape(2, 16, 2, 64).transpose(0, 2, 3, 1).reshape(2, 128, 16)),
            "cw2": inp["cmp_w2"][l],
            "KsT": np.ascontiguousarray(KsT), "VsA": np.ascontiguousarray(VsA),
            "KwT": np.ascontiguousarray(KwP[:, par * 128:par * 128 + T + 512]),
            "VwA": np.ascontiguousarray(VwP[par * 128:par * 128 + T + 512]),
            "gates": np.ascontiguousarray(gt), "TC": np.stack(TC), "TS": np.stack(TS), "TW": np.stack(TW),
            "OV": OV, "FB": FB, "EX": EX, "ident": cst["ident"],
        })
    nc = build_N()
    res = run_bass_kernel_spmd(nc, maps, core_ids=list(range(NCORE)))
    o_att = np.zeros((BATCH, SEQ, 512), np.float32)
    for r in range(NCORE):
        b, g, par = r // 4, (r // 2) % 2, r % 2
        o = res.results[r]["o"]
        o_att[b].reshape(128, 128, 512)[par::2, :, g * 256:(g + 1) * 256] = o
    return o_att


def kernel(**inp):
    inp = {k: np.asarray(v) for k, v in inp.items()}
    x = np.ascontiguousarray(inp["x"].reshape(-1, D))
    cat = lambda res, name, ax: np.concatenate([r[name] for r in res], ax)
    for l in range(DEPTH):
        ra = run_A(inp, l, x)
        x1 = cat(ra, "x1", 0)
        tmf = cat(ra, "tmf", 0)
        o_att = run_N(inp, l, cat(ra, "QT", 1), cat(ra, "KcT", 1), cat(ra, "VcT", 1), cat(ra, "KsT", 1), cat(ra, "KwT", 1),
                      cat(ra, "Vs", 0), cat(ra, "Vw", 0), tmf[:, 1028:1052])
        ypre = run_S(inp, l, tmf[:, 256:1024].reshape(BATCH, SEQ, 768), tmf[:, 1024:1028].reshape(BATCH, SEQ, 4))
        rcz = run_C(inp, l, x1, o_att.reshape(-1, 512), cat(ra, "ogm", 0), ypre.reshape(-1, 256), np.ascontiguousarray(tmf[:, 0:256]))
        x = cat(rcz, "x3", 0)
    return x.reshape(BATCH, SEQ, D).astype(np.float32)
```

```python
import numpy as np
import ml_dtypes
from contextlib import ExitStack, contextmanager
import concourse.bass as bass
import concourse.mybir as mybir
from concourse.bass_utils import run_bass_kernel_spmd

F32 = mybir.dt.float32
BF16 = mybir.dt.bfloat16
AF = mybir.ActivationFunctionType
ALU = mybir.AluOpType
NPBF = ml_dtypes.bfloat16

D = 1024
DEPTH = 2
SEQ = 16384
BATCH = 2
DFF = 2816
NFT = DFF // 128
DIN = 2844
ALPHA = (2 * DEPTH) ** 0.25
LN_EPS = 1e-5
NCORE = 8
TOKC = BATCH * SEQ // NCORE
NTT = TOKC // 128

ENGS = ("pe", "dve", "act", "pool", "sp")


_PSUM_KEYS = ("bank", "ptb", "sbank", "nS", "nO", "nI", "ncmp")


def _is_psum_key(k):
    return k == "nPT" or (isinstance(k, tuple) and k[0] in _PSUM_KEYS)


class Prog:
    def __init__(self, nc, n_dma_slots=8):
        self.nc = nc
        self.stack = ExitStack()
        self.ops = {e: [] for e in ENGS}
        self.sem = {e: self.stack.enter_context(nc.semaphore("s_" + e)) for e in ENGS}
        self.cnt = {e: 0 for e in ENGS}
        self.seen = {e: {} for e in ENGS}
        self.last_w = {}
        self.readers = {}
        self.nslot = n_dma_slots
        self.dma_sems, self.dma_vals, self.dma_next = {}, {}, {}
        self.semobj = dict(self.sem)
        for q in ("sp", "act", "pool"):
            self.dma_sems[q] = [self.stack.enter_context(nc.semaphore(f"d_{q}{i}")) for i in range(n_dma_slots)]
            self.dma_vals[q] = [0] * n_dma_slots
            self.dma_next[q] = 0
            for i, s in enumerate(self.dma_sems[q]):
                self.semobj[(q, i)] = s
        self.n_inst = 0
        self._rr = 0

    def sb(self, name, shape, dtype, stack=None):
        return (stack or self.stack).enter_context(self.nc.sbuf_tensor(name, list(shape), dtype))

    def ps(self, name, shape, dtype=F32, stack=None):
        return (stack or self.stack).enter_context(self.nc.psum_tensor(name, list(shape), dtype))

    def _need(self, eng, tok, waits):
        semkey, val, _ = tok
        if self.seen[eng].get(semkey, 0) >= val:
            return
        self.seen[eng][semkey] = val
        waits.append((semkey, val))

    def _deps(self, eng, engid, reads, writes):
        waits = []
        for r in reads:
            t = self.last_w.get(r)
            if t is not None:
                self._need(eng, t, waits)
        for w in writes:
            t = self.last_w.get(w)
            if t is not None and (t[2] != engid or eng != "pe"):
                self._need(eng, t, waits)
            for sk, (v, eid) in self.readers.get(w, {}).items():
                if eid != engid or eng != "pe":
                    self._need(eng, (sk, v, eid), waits)
        return waits

    def _commit(self, tok, reads, writes):
        for r in reads:
            self.readers.setdefault(r, {})[tok[0]] = (tok[1], tok[2])
        for w in writes:
            self.last_w[w] = tok
            self.readers[w] = {}

    def op(self, eng, fn, reads=(), writes=()):
        ps_reads = [k for k in reads if _is_psum_key(k)]
        if ps_reads:
            writes = list(writes) + [k for k in ps_reads if k not in writes]
        waits = self._deps(eng, eng, reads, writes)
        self.cnt[eng] += 1
        tok = (eng, self.cnt[eng], eng)
        self._commit(tok, reads, writes)
        self.ops[eng].append((waits, fn, self.sem[eng], 1))
        self.n_inst += 1
        return tok

    def dma(self, q, out, in_, reads=(), writes=(), **kw):
        slot = self.dma_next[q]
        self.dma_next[q] = (slot + 1) % self.nslot
        semkey = (q, slot)
        engid = ("dma", q, slot)
        waits = self._deps(q, engid, reads, writes)
        prev = self.dma_vals[q][slot]
        if prev > 0:
            self._need(q, (semkey, prev, engid), waits)
        val = prev + 16
        self.dma_vals[q][slot] = val
        tok = (semkey, val, engid)
        self._commit(tok, reads, writes)
        self.ops[q].append((waits, lambda e: e.dma_start(out=out, in_=in_, **kw), self.dma_sems[q][slot], 16))
        self.n_inst += 1
        return tok

    def barrier(self):
        for e in ENGS:
            waits = []
            for o in ENGS:
                if o != e and self.cnt[o] > 0:
                    self._need(e, (o, self.cnt[o], o), waits)
            for q in self.dma_sems:
                for i in range(self.nslot):
                    v = self.dma_vals[q][i]
                    if v > 0:
                        self._need(e, ((q, i), v, None), waits)
            if waits:
                self.ops[e].append((waits, None, None, 0))

    @contextmanager
    def scope(self):
        st = ExitStack()
        try:
            yield st
        finally:
            self.barrier()
            st.close()

    def finish(self):
        self.barrier()
        engmap = {"pe": "tensor", "dve": "vector", "act": "scalar", "pool": "gpsimd", "sp": "sync"}
        semobj = self.semobj
        with self.nc.Block() as block:
            for e in ENGS:
                ops = self.ops[e]
                if not ops:
                    continue

                def body(engine, ops=ops):
                    for waits, fn, sem, inc in ops:
                        for sk, v in waits:
                            engine.wait_ge(semobj[sk], v)
                        if fn is not None:
                            fn(engine).then_inc(sem, inc)

                getattr(block, engmap[e])(body)
        self.stack.close()


class RowCtx:
    def __init__(self, p, nc, consts):
        self.p, self.nc = p, nc
        self.identb = p.sb("identb", [128, 128], BF16)
        p.dma("pool", self.identb[:], consts["ident"], writes=["identb"])
        self.eps = p.sb("epsc", [128, 1], F32)
        p.op("dve", lambda e: e.memset(self.eps[:], LN_EPS), writes=["epsc"])
        self.bank = [p.ps(f"bank{i}", [128, 512], F32) for i in range(6)]
        self.ptb = [p.ps(f"ptb{i}", [128, 1024], BF16) for i in range(2)]


def emit_mod(p, rc, st, cT, ada_w, ada_b, ncols, dst, posts):
    nck = ncols // 512
    with p.scope() as s2:
        ct = p.sb("mod_ct", [128, 8], F32, s2)
        sc = p.sb("mod_sc", [128, 8], F32, s2)
        scb = p.sb("mod_scb", [128, 8, 128], F32, s2)
        bb = p.sb("mod_bb", [128, ncols], F32, s2)
        wch = [p.sb(f"mod_w{i}", [128, 8, 512], F32, s2) for i in range(2)]
        p.dma("sp", ct[:], cT, writes=["mod_ct"])
        p.dma("act", bb[:], ada_b.partition_broadcast(128), writes=["mod_bb"])
        p.op("act", lambda e: e.activation(out=sc[:], in_=ct[:], func=AF.Silu), reads=["mod_ct"], writes=["mod_sc"])
        for dt in range(8):
            p.op("dve", lambda e, dt=dt: e.tensor_copy(out=scb[:, dt, :], in_=sc[:, dt:dt + 1].to_broadcast([128, 128])),
                 reads=["mod_sc"], writes=[("mod_scb", dt)])
        for ci in range(nck):
            w = wch[ci % 2]
            wk = ("mod_w", ci % 2)
            p.dma("sp" if ci % 2 == 0 else "act", w[:],
                  ada_w[:, ci * 512:(ci + 1) * 512].rearrange("(dt p) n -> p dt n", p=128), writes=[wk])
            bk = rc.bank[ci % 2]
            for dt in range(8):
                p.op("pe", lambda e, dt=dt, w=w, bk=bk: e.matmul(bk[:], lhsT=scb[:, dt, :], rhs=w[:, dt, :],
                                                                  start=(dt == 0), stop=(dt == 7)),
                     reads=[wk, ("mod_scb", dt)], writes=[("bank", ci % 2)])
            p.op("dve", lambda e, ci=ci, bk=bk: e.tensor_tensor(out=dst[:, ci * 512:(ci + 1) * 512], in0=bk[:],
                                                                 in1=bb[:, ci * 512:(ci + 1) * 512], op=ALU.add),
                 reads=[("bank", ci % 2), "mod_bb"], writes=["modtile"])
        for (c0, c1, add, mul) in posts:
            p.op("dve", lambda e, c0=c0, c1=c1, add=add, mul=mul: e.tensor_scalar(
                out=dst[:, c0:c1], in0=dst[:, c0:c1], scalar1=float(add), scalar2=float(mul), op0=ALU.add, op1=ALU.mult),
                reads=["modtile"], writes=["modtile"])


def emit_modulate_T(p, rc, xt, xkey, SC, SH, hf, hb, hT, tag, ptb_i=0):
    p.op("dve", lambda e: e.tensor_tensor(out=hf[:], in0=xt, in1=SC, op=ALU.mult),
         reads=[xkey, "modtile"], writes=[tag + "hf"])
    p.op("pool", lambda e: e.tensor_tensor(out=hb[:], in0=hf[:], in1=SH, op=ALU.add),
         reads=[tag + "hf", "modtile"], writes=[tag + "hb"])
    pt = rc.ptb[ptb_i]
    for dt in range(8):
        p.op("pe", lambda e, dt=dt: e.transpose(out=pt[:, dt * 128:(dt + 1) * 128], in_=hb[:, dt * 128:(dt + 1) * 128],
                                                 identity=rc.identb[:]),
             reads=[tag + "hb", "identb"], writes=[("ptb", ptb_i)])
    p.op("act", lambda e: e.activation(out=hT[:].rearrange("p a b -> p (a b)"), in_=pt[:], func=AF.Copy),
         reads=[("ptb", ptb_i)], writes=[tag + "hT"])


def emit_res_ln(p, rc, ybanks, xres, xkey, GF, lng, lnb, r, stats, mv, rstd, xo, tag):
    for half in range(2):
        sl = slice(half * 512, (half + 1) * 512)
        p.op("dve", lambda e, half=half, sl=sl: e.tensor_tensor(out=r[:, sl], in0=rc.bank[ybanks[half]][:], in1=GF[:, sl],
                                                                op=ALU.mult),
             reads=[("bank", ybanks[half]), "modtile"], writes=[(tag + "r", half)])
        p.op("dve", lambda e, sl=sl: e.scalar_tensor_tensor(out=r[:, sl], in0=xres[:, sl], scalar=float(ALPHA), in1=r[:, sl],
                                                            op0=ALU.mult, op1=ALU.add),
             reads=[(tag + "r", half), xkey], writes=[(tag + "r", half)])
        p.op("dve", lambda e, half=half, sl=sl: e.bn_stats(out=stats[:, half * 6:(half + 1) * 6], in_=r[:, sl]),
             reads=[(tag + "r", half)], writes=[(tag + "st", half)])
    p.op("dve", lambda e: e.bn_aggr(out=mv[:], in_=stats[:]), reads=[(tag + "st", 0), (tag + "st", 1)], writes=[tag + "mv"])
    p.op("act", lambda e: e.activation(out=rstd[:], in_=mv[:, 1:2], func=AF.Sqrt, bias=rc.eps[:], scale=1.0),
         reads=[tag + "mv", "epsc"], writes=[tag + "rstd"])
    p.op("dve", lambda e: e.reciprocal(out=rstd[:], in_=rstd[:]), reads=[tag + "rstd"], writes=[tag + "rstd"])
    p.op("dve", lambda e: e.tensor_scalar(out=r[:], in0=r[:], scalar1=mv[:, 0:1], scalar2=rstd[:, 0:1],
                                          op0=ALU.subtract, op1=ALU.mult),
         reads=[(tag + "r", 0), (tag + "r", 1), tag + "mv", tag + "rstd"], writes=[(tag + "r", 0), (tag + "r", 1)])
    p.op("pool", lambda e: e.tensor_tensor(out=r[:], in0=r[:], in1=lng[:], op=ALU.mult),
         reads=[(tag + "r", 0), (tag + "r", 1), "lngb"], writes=[(tag + "r", 0), (tag + "r", 1)])
    p.op("pool", lambda e: e.tensor_tensor(out=xo[:], in0=r[:], in1=lnb[:], op=ALU.add),
         reads=[(tag + "r", 0), (tag + "r", 1), "lngb"], writes=[tag + "xo"])


def emit_ffn(p, rc, x_in, x_out, w1, w3, w2, SC, SH, GF, lng, lnb):
    with p.scope() as st:
        W1 = p.sb("W1", [128, 8, DFF], BF16, st)
        W3 = p.sb("W3", [128, 8, DFF], BF16, st)
        W2 = p.sb("W2", [128, NFT, D], BF16, st)
        for dt in range(8):
            p.dma("pool", W1[:, dt, :], w1[dt * 128:(dt + 1) * 128, :], writes=["W1"])
            p.dma("pool", W3[:, dt, :], w3[dt * 128:(dt + 1) * 128, :], writes=["W3"])
        for f0 in range(0, NFT, 2):
            p.dma("pool", W2[:, f0:f0 + 2, :], w2[f0 * 128:(f0 + 2) * 128, :].rearrange("(f p) n -> p f n", p=128), writes=["W2"])
        xt = [p.sb(f"f_x{i}", [128, D], F32, st) for i in range(2)]
        hf = p.sb("f_hf", [128, D], F32, st)
        hb = p.sb("f_hb", [128, D], BF16, st)
        hT = p.sb("f_hT", [128, 8, 128], BF16, st)
        aT = p.sb("f_aT", [128, NFT, 128], BF16, st)
        sg = [p.sb(f"f_sg{i}", [128, 128], F32, st) for i in range(2)]
        r = p.sb("f_r", [128, D], F32, st)
        xo = [p.sb(f"f_xo{i}", [128, D], F32, st) for i in range(2)]
        stats = p.sb("f_stats", [128, 12], F32, st)
        mv = p.sb("f_mv", [128, 2], F32, st)
        rstd = p.sb("f_rstd", [128, 1], F32, st)
        for tt in range(NTT):
            x = xt[tt % 2]
            xk = ("f_x", tt % 2)
            p.dma("sp", x[:], x_in[tt * 128:(tt + 1) * 128, :], writes=[xk])
            emit_modulate_T(p, rc, x[:], xk, SC, SH, hf, hb, hT, "f_", ptb_i=tt % 2)
            for ft in range(NFT):
                b = ft % 2
                bk = rc.bank[b]
                for dt in range(8):
                    p.op("pe", lambda e, dt=dt, ft=ft, bk=bk: e.matmul(bk[:, 0:128], lhsT=W1[:, dt, ft * 128:(ft + 1) * 128],
                                                                        rhs=hT[:, dt, :], start=(dt == 0), stop=(dt == 7)),
                         reads=["W1", "f_hT"], writes=[("bank", b)])
                for dt in range(8):
                    p.op("pe", lambda e, dt=dt, ft=ft, bk=bk: e.matmul(bk[:, 128:256], lhsT=W3[:, dt, ft * 128:(ft + 1) * 128],
                                                                        rhs=hT[:, dt, :], start=(dt == 0), stop=(dt == 7)),
                         reads=["W3", "f_hT"], writes=[("bank", b)])
                p.op("act", lambda e, b=b, bk=bk: e.activation(out=sg[b][:], in_=bk[:, 0:128], func=AF.Silu),
                     reads=[("bank", b)], writes=[("f_sg", b)])
                p.op("dve", lambda e, b=b, bk=bk, ft=ft: e.tensor_tensor(out=aT[:, ft, :], in0=sg[b][:], in1=bk[:, 128:256],
                                                                          op=ALU.mult),
                     reads=[("bank", b), ("f_sg", b)], writes=[("f_aT", ft)])
            for half in range(2):
                bk = rc.bank[2 + half]
                for ft in range(NFT):
                    p.op("pe", lambda e, ft=ft, bk=bk, half=half: e.matmul(bk[:], lhsT=aT[:, ft, :],
                                                                            rhs=W2[:, ft, half * 512:(half + 1) * 512],
                                                                            start=(ft == 0), stop=(ft == NFT - 1)),
                         reads=[("f_aT", ft), "W2"], writes=[("bank", 2 + half)])
            o = xo[tt % 2]
            emit_res_ln(p, rc, (2, 3), x, xk, GF, lng, lnb, r, stats, mv, rstd, o, "f_")
            p.dma("act", x_out[tt * 128:(tt + 1) * 128, :], o[:], reads=["f_xo"], writes=[])


def build_A():
    nc = bass.Bass("TRN2", target_bir_lowering=False)
    di = lambda n, s, d=F32: nc.dram_tensor(n, list(s), d, kind="ExternalInput").ap()
    do = lambda n, s, d=F32: nc.dram_tensor(n, list(s), d, kind="ExternalOutput").ap()
    x_in = di("x", [TOKC, D])
    cT = di("cT", [128, 8])
    ada_w = di("ada_w", [D, 5120])
    ada_b = di("ada_b", [5120])
    lng_d, lnb_d = di("lng", [D]), di("lnb", [D])
    w1, w3, w2 = di("w1", [D, DFF]), di("w3", [D, DFF]), di("w2", [DFF, D])
    w_in = di("w_in", [D, DIN])
    glng_d, glnb_d = di("glng", [256]), di("glnb", [256])
    wsT_d = di("wsT", [4, 128, 128])
    bsT_d = di("bsT", [128, 4])
    consts = {"ident": di("ident", [128, 128]), "tri": di("tri", [128, 128])}
    x1_d = do("x1", [TOKC, D])
    QT_d = do("QT", [512, TOKC], BF16)
    KcT_d, VcT_d = do("KcT", [128, TOKC], BF16), do("VcT", [128, TOKC], BF16)
    KsT_d, KwT_d = do("KsT", [128, TOKC], BF16), do("KwT", [128, TOKC], BF16)
    Vs_d, Vw_d = do("Vs", [TOKC, 128], BF16), do("Vw", [TOKC, 128], BF16)
    tmf_d = do("tmf", [TOKC, 1052])
    ogm_d = do("ogm", [TOKC, 256], BF16)

    p = Prog(nc)
    rc = RowCtx(p, nc, consts)
    MOD = p.sb("MOD", [128, 5120], F32)
    lng = p.sb("lng_t", [128, D], F32)
    lnb = p.sb("lnb_t", [128, D], F32)
    p.dma("sp", lng[:], lng_d.partition_broadcast(128), writes=["lngb"])
    p.dma("sp", lnb[:], lnb_d.partition_broadcast(128), writes=["lngb"])
    emit_mod(p, rc, None, cT, ada_w, ada_b, 5120, MOD,
             [(1024, 2048, 1.0, 1.0), (2048, 3072, 1.0, 0.5), (4096, 5120, 1.0, 1.0)])
    import os
    STOP = os.environ.get("KSTOP", "")
    if STOP == "mod":
        p.finish(); return nc
    emit_ffn(p, rc, x_in, x1_d, w1, w3, w2, MOD[:, 1024:2048], MOD[:, 0:1024], MOD[:, 2048:3072], lng, lnb)
    if STOP == "ffn":
        p.finish(); return nc

    with p.scope() as st:
        WIN = p.sb("WIN", [128, 8, DIN], BF16, st)
        for dt in range(8):
            p.dma("pool", WIN[:, dt, :], w_in[dt * 128:(dt + 1) * 128, :], writes=["WIN"])
        tri = p.sb("tri_t", [128, 128], F32, st)
        p.dma("sp", tri[:], consts["tri"], writes=["tri_t"])
        wsf = p.sb("wsf", [128, 4, 128], F32, st)
        p.dma("sp", wsf[:], wsT_d.rearrange("g s t -> s g t"), writes=["wsf"])
        WS = p.sb("WS", [128, 4, 128], BF16, st)
        for g in range(4):
            p.op("dve", lambda e, g=g: e.tensor_tensor(out=WS[:, g, :], in0=wsf[:, g, :], in1=tri[:], op=ALU.mult),
                 reads=["wsf", "tri_t"], writes=["WS"])
        BS = p.sb("BS", [128, 4], F32, st)
        p.dma("sp", BS[:], bsT_d, writes=["BS"])
        glng = p.sb("glng_t", [128, 256], F32, st)
        glnb = p.sb("glnb_t", [128, 256], F32, st)
        p.dma("sp", glng[:], glng_d.partition_broadcast(128), writes=["glngb"])
        p.dma("sp", glnb[:], glnb_d.partition_broadcast(128), writes=["glngb"])
        xt = [p.sb(f"a_x{i}", [128, D], F32, st) for i in range(2)]
        hf = p.sb("a_hf", [128, D], F32, st)
        hb = p.sb("a_hb", [128, D], BF16, st)
        hT = p.sb("a_hT", [128, 8, 128], BF16, st)
        fm = [p.sb(f"a_fm{i}", [128, 8, 128], BF16, st) for i in range(2)]
        vsw = [p.sb(f"a_vsw{i}", [128, 256], BF16, st) for i in range(2)]
        gu = p.sb("a_gu", [128, 512], F32, st)
        vn = p.sb("a_vn", [128, 256], F32, st)
        vnb = p.sb("a_vnb", [128, 256], BF16, st)
        gst = p.sb("a_gst", [128, 6], F32, st)
        gmv = p.sb("a_gmv", [128, 2], F32, st)
        grs = p.sb("a_grs", [128, 1], F32, st)
        ogm = [p.sb(f"a_ogm{i}", [128, 256], BF16, st) for i in range(2)]
        ssm = [p.sb(f"a_ssm{i}", [128, 1052], F32, st) for i in range(2)]
        SC1, SH1 = MOD[:, 4096:5120], MOD[:, 3072:4096]
        fm_cols = [0, 128, 256, 384, 512, 640, 768, 1024]
        for tt in range(NTT):
            i2 = tt % 2
            tok = slice(tt * 128, (tt + 1) * 128)
            x = xt[i2]
            xk = ("a_x", i2)
            p.dma("sp", x[:], x1_d[tok, :], reads=[], writes=[xk])
            emit_modulate_T(p, rc, x[:], xk, SC1, SH1, hf, hb, hT, "a_", ptb_i=0)
            for ci, c0 in enumerate(fm_cols):
                b = ci // 4
                bk = rc.bank[b]
                for dt in range(8):
                    p.op("pe", lambda e, dt=dt, c0=c0, bk=bk, ci=ci: e.matmul(
                        bk[:, (ci % 4) * 128:(ci % 4 + 1) * 128], lhsT=WIN[:, dt, c0:c0 + 128], rhs=hT[:, dt, :],
                        start=(dt == 0), stop=(dt == 7)), reads=["WIN", "a_hT"], writes=[("bank", b)])
            f = fm[i2]
            p.op("act", lambda e, f=f: e.activation(out=f[:, 0:4, :].rearrange("p a b -> p (a b)"), in_=rc.bank[0][:],
                                                    func=AF.Identity, scale=0.125),
                 reads=[("bank", 0)], writes=[("a_fm", i2)])
            p.op("dve", lambda e, f=f: e.tensor_copy(out=f[:, 4:8, :].rearrange("p a b -> p (a b)"), in_=rc.bank[1][:]),
                 reads=[("bank", 1)], writes=[("a_fm", i2)])
            p.dma("act", QT_d[:, tok].rearrange("(c p) t -> p c t", p=128), f[:, 0:4, :], reads=[("a_fm", i2)])
            for k, dd in enumerate((KcT_d, VcT_d, KsT_d, KwT_d)):
                p.dma("act", dd[:, tok], f[:, 4 + k, :], reads=[("a_fm", i2)])
            def tm(bank_i, col_off, c0, c1):
                bk = rc.bank[bank_i]
                for dt in range(8):
                    p.op("pe", lambda e, dt=dt: e.matmul(bk[:, col_off:col_off + (c1 - c0)], lhsT=hT[:, dt, :],
                                                         rhs=WIN[:, dt, c0:c1], start=(dt == 0), stop=(dt == 7)),
                         reads=["WIN", "a_hT"], writes=[("bank", bank_i)])
            tm(2, 0, 896, 1024)
            tm(2, 128, 1152, 1304)
            tm(2, 280, 2716, 2844)
            tm(3, 0, 1304, 1816)
            tm(4, 0, 1816, 2328)
            tm(5, 0, 2328, 2840)
            vv = vsw[i2]
            p.op("dve", lambda e, vv=vv: e.tensor_copy(out=vv[:], in_=rc.bank[2][:, 0:256]),
                 reads=[("bank", 2)], writes=[("a_vsw", i2)])
            p.dma("act", Vs_d[tok, :], vv[:, 0:128], reads=[("a_vsw", i2)])
            p.dma("act", Vw_d[tok, :], vv[:, 128:256], reads=[("a_vsw", i2)])
            sm = ssm[i2]
            p.op("act", lambda e, sm=sm: e.activation(out=sm[:, 1028:1052], in_=rc.bank[2][:, 256:280], func=AF.Sigmoid),
                 reads=[("bank", 2)], writes=[("a_ssm", i2)])
            p.op("dve", lambda e, sm=sm: e.tensor_copy(out=sm[:, 1024:1028], in_=rc.bank[2][:, 280 + 124:280 + 128]),
                 reads=[("bank", 2)], writes=[("a_ssm", i2)])
            p.op("act", lambda e, sm=sm: e.activation(out=sm[:, 0:512], in_=rc.bank[4][:], func=AF.Copy),
                 reads=[("bank", 4)], writes=[("a_ssm", i2)])
            p.op("dve", lambda e, sm=sm: e.tensor_copy(out=sm[:, 512:1024], in_=rc.bank[5][:]),
                 reads=[("bank", 5)], writes=[("a_ssm", i2)])
            p.dma("sp", tmf_d[tok, :], sm[:], reads=[("a_ssm", i2)])
            p.op("act", lambda e: e.activation(out=gu[:], in_=rc.bank[3][:], func=AF.Gelu_apprx_tanh),
                 reads=[("bank", 3)], writes=["a_gu"])
            p.op("dve", lambda e: e.bn_stats(out=gst[:], in_=gu[:, 256:512]), reads=["a_gu"], writes=["a_gst"])
            p.op("dve", lambda e: e.bn_aggr(out=gmv[:], in_=gst[:]), reads=["a_gst"], writes=["a_gmv"])
            p.op("act", lambda e: e.activation(out=grs[:], in_=gmv[:, 1:2], func=AF.Sqrt, bias=rc.eps[:], scale=1.0),
                 reads=["a_gmv", "epsc"], writes=["a_grs"])
            p.op("dve", lambda e: e.reciprocal(out=grs[:], in_=grs[:]), reads=["a_grs"], writes=["a_grs"])
            p.op("dve", lambda e: e.tensor_scalar(out=vn[:], in0=gu[:, 256:512], scalar1=gmv[:, 0:1], scalar2=grs[:, 0:1],
                                                  op0=ALU.subtract, op1=ALU.mult),
                 reads=["a_gu", "a_gmv", "a_grs"], writes=["a_vn"])
            p.op("pool", lambda e: e.tensor_tensor(out=vn[:], in0=vn[:], in1=glng[:], op=ALU.mult),
                 reads=["a_vn", "glngb"], writes=["a_vn"])
            p.op("pool", lambda e: e.tensor_tensor(out=vnb[:], in0=vn[:], in1=glnb[:], op=ALU.add),
                 reads=["a_vn", "glngb"], writes=["a_vnb"])
            for g in range(4):
                p.op("pe", lambda e, g=g: e.matmul(rc.bank[3][:, g * 64:(g + 1) * 64], lhsT=WS[:, g, :],
                                                   rhs=vnb[:, g * 64:(g + 1) * 64], start=True, stop=True),
                     reads=["WS", "a_vnb"], writes=[("bank", 3)])
            og = ogm[i2]
            for g in range(4):
                p.op("dve", lambda e, g=g, og=og: e.scalar_tensor_tensor(
                    out=og[:, g * 64:(g + 1) * 64], in0=rc.bank[3][:, g * 64:(g + 1) * 64], scalar=BS[:, g:g + 1],
                    in1=gu[:, g * 64:(g + 1) * 64], op0=ALU.add, op1=ALU.mult),
                    reads=[("bank", 3), "a_gu", "BS"], writes=[("a_ogm", i2)])
            p.dma("act", ogm_d[tok, :], og[:], reads=[("a_ogm", i2)])
    p.finish()
    return nc


def build_C():
    nc = bass.Bass("TRN2", target_bir_lowering=False)
    di = lambda n, s, d=F32: nc.dram_tensor(n, list(s), d, kind="ExternalInput").ap()
    do = lambda n, s, d=F32: nc.dram_tensor(n, list(s), d, kind="ExternalOutput").ap()
    x1_d = di("x1", [TOKC, D])
    cT = di("cT", [128, 8])
    ada_w = di("ada_w", [D, 4096])
    ada_b = di("ada_b", [4096])
    lng1_d, lnb1_d = di("lng1", [D]), di("lnb1", [D])
    lng2_d, lnb2_d = di("lng2", [D]), di("lnb2", [D])
    w1, w3, w2 = di("w1", [D, DFF]), di("w3", [D, DFF]), di("w2", [DFF, D])
    w_out = di("w_out", [D, D])
    oatt_d = di("oatt", [TOKC, 512])
    ogm_d = di("ogm", [TOKC, 256], BF16)
    yssm_d = di("yssm", [TOKC, 256])
    zz_d = di("zz", [TOKC, 256])
    ng_d = di("normg", [256])
    consts = {"ident": di("ident", [128, 128])}
    x2_d = do("x2", [TOKC, D])
    x3_d = do("x3", [TOKC, D])

    p = Prog(nc)
    rc = RowCtx(p, nc, consts)
    MOD = p.sb("MOD", [128, 4096], F32)
    lng = p.sb("lng_t", [128, D], F32)
    lnb = p.sb("lnb_t", [128, D], F32)
    emit_mod(p, rc, None, cT, ada_w, ada_b, 4096, MOD,
             [(0, 1024, 1.0, 1.0), (2048, 3072, 1.0, 1.0), (3072, 4096, 1.0, 0.5)])
    p.dma("sp", lng[:], lng1_d.partition_broadcast(128), writes=["lngb"])
    p.dma("sp", lnb[:], lnb1_d.partition_broadcast(128), writes=["lngb"])
    with p.scope() as st:
        WO = p.sb("WO", [128, 8, D], BF16, st)
        for dt in range(8):
            p.dma("pool", WO[:, dt, :], w_out[dt * 128:(dt + 1) * 128, :], writes=["WO"])
        ng = p.sb("ng_t", [128, 256], F32, st)
        p.dma("sp", ng[:], ng_d.partition_broadcast(128), writes=["ng_t"])
        xt = [p.sb(f"c_x{i}", [128, D], F32, st) for i in range(2)]
        oa = [p.sb(f"c_oa{i}", [128, 512], F32, st) for i in range(2)]
        ys = [p.sb(f"c_ys{i}", [128, 512], F32, st) for i in range(2)]
        om = p.sb("c_om", [128, D], BF16, st)
        omT = p.sb("c_omT", [128, 8, 128], BF16, st)
        sz = p.sb("c_sz", [128, 256], F32, st)
        gg = p.sb("c_gg", [128, 256], F32, st)
        junk = p.sb("c_junk", [128, 128], F32, st)
        ss = p.sb("c_ss", [128, 2], F32, st)
        r = p.sb("c_r", [128, D], F32, st)
        xo = [p.sb(f"c_xo{i}", [128, D], F32, st) for i in range(2)]
        stats = p.sb("c_stats", [128, 12], F32, st)
        mv = p.sb("c_mv", [128, 2], F32, st)
        rstd = p.sb("c_rstd", [128, 1], F32, st)
        for tt in range(NTT):
            i2 = tt % 2
            tok = slice(tt * 128, (tt + 1) * 128)
            x, xk = xt[i2], ("c_x", i2)
            p.dma("sp", x[:], x1_d[tok, :], writes=[xk])
            p.dma("sp", oa[i2][:], oatt_d[tok, :], writes=[("c_oa", i2)])
            p.dma("pool", om[:, 512:768], ogm_d[tok, :], writes=["c_om_g"])
            p.dma("act", ys[i2][:, 0:256], yssm_d[tok, :], writes=[("c_ys", i2)])
            p.dma("act", ys[i2][:, 256:512], zz_d[tok, :], writes=[("c_ys", i2)])
            p.op("pool", lambda e, i2=i2: e.tensor_copy(out=om[:, 0:512], in_=oa[i2][:]), reads=[("c_oa", i2)], writes=["c_om_a"])
            p.op("act", lambda e, i2=i2: e.activation(out=sz[:], in_=ys[i2][:, 256:512], func=AF.Silu),
                 reads=[("c_ys", i2)], writes=["c_sz"])
            p.op("dve", lambda e, i2=i2: e.tensor_tensor(out=gg[:], in0=ys[i2][:, 0:256], in1=sz[:], op=ALU.mult),
                 reads=[("c_ys", i2), "c_sz"], writes=["c_gg"])
            p.op("dve", lambda e: e.tensor_tensor(out=sz[:], in0=gg[:], in1=gg[:], op=ALU.mult), reads=["c_gg"], writes=["c_sz"])
            for k in range(2):
                p.op("dve", lambda e, k=k: e.reduce_sum(out=ss[:, k:k + 1], in_=sz[:, k * 128:(k + 1) * 128],
                                                        axis=mybir.AxisListType.X), reads=["c_sz"], writes=[("c_ss", k)])
            p.op("act", lambda e: e.activation(out=ss[:], in_=ss[:], func=AF.Sqrt, bias=rc.eps[:], scale=1.0 / 128.0),
                 reads=[("c_ss", 0), ("c_ss", 1), "epsc"], writes=[("c_ss", 0), ("c_ss", 1)])
            p.op("dve", lambda e: e.reciprocal(out=ss[:], in_=ss[:]), reads=[("c_ss", 0), ("c_ss", 1)],
                 writes=[("c_ss", 0), ("c_ss", 1)])
            for k in range(2):
                p.op("dve", lambda e, k=k: e.scalar_tensor_tensor(out=om[:, 768 + k * 128:768 + (k + 1) * 128],
                                                                  in0=gg[:, k * 128:(k + 1) * 128], scalar=ss[:, k:k + 1],
                                                                  in1=ng[:, k * 128:(k + 1) * 128], op0=ALU.mult, op1=ALU.mult),
                     reads=["c_gg", ("c_ss", 0), ("c_ss", 1), "ng_t"], writes=[("c_om_s", k)])
            pt = rc.ptb[i2]
            for dt in range(8):
                p.op("pe", lambda e, dt=dt, pt=pt: e.transpose(out=pt[:, dt * 128:(dt + 1) * 128],
                                                               in_=om[:, dt * 128:(dt + 1) * 128], identity=rc.identb[:]),
                     reads=["c_om_a", "c_om_g", ("c_om_s", 0), ("c_om_s", 1), "identb"], writes=[("ptb", i2)])
            p.op("act", lambda e, pt=pt: e.activation(out=omT[:].rearrange("p a b -> p (a b)"), in_=pt[:], func=AF.Copy),
                 reads=[("ptb", i2)], writes=["c_omT"])
            for half in range(2):
                bk = rc.bank[2 + half]
                for dt in range(8):
                    p.op("pe", lambda e, dt=dt, bk=bk, half=half: e.matmul(bk[:], lhsT=omT[:, dt, :],
                                                                            rhs=WO[:, dt, half * 512:(half + 1) * 512],
                                                                            start=(dt == 0), stop=(dt == 7)),
                         reads=["c_omT", "WO"], writes=[("bank", 2 + half)])
            o = xo[i2]
            emit_res_ln(p, rc, (2, 3), x, xk, MOD[:, 0:1024], lng, lnb, r, stats, mv, rstd, o, "c_")
            p.dma("act", x2_d[tok, :], o[:], reads=["c_xo"])
    p.dma("sp", lng[:], lng2_d.partition_broadcast(128), writes=["lngb"])
    p.dma("sp", lnb[:], lnb2_d.partition_broadcast(128), writes=["lngb"])
    emit_ffn(p, rc, x2_d, x3_d, w1, w3, w2, MOD[:, 2048:3072], MOD[:, 1024:2048], MOD[:, 3072:4096], lng, lnb)
    p.finish()
    return nc


NCH = SEQ // 256


def build_S():
    nc = bass.Bass("TRN2", target_bir_lowering=False)
    di = lambda n, s, d=F32: nc.dram_tensor(n, list(s), d, kind="ExternalInput").ap()
    do = lambda n, s, d=F32: nc.dram_tensor(n, list(s), d, kind="ExternalOutput").ap()
    xc4_d = di("xc4", [SEQ, 4, 320])
    cw_d = di("cw", [4 * 320])
    cb_d = di("cb", [320])
    dtr_d = di("dtr", [SEQ])
    sc_d = di("scal", [4])
    tri_d = di("tri", [128, 128])
    triw_d = di("triw", [2, 128, 256])
    ntriw_d = di("ntriw", [2, 128, 256])
    ident_d = di("identf", [128, 128])
    y_d = do("y", [SEQ, 64])
    p = Prog(nc)
    bank = [p.ps(f"sbank{i}", [128, 512], F32) for i in range(5)]
    CW = p.sb("CW", [128, 4, 320], F32)
    CB = p.sb("CB", [128, 320], F32)
    DTR = p.sb("DTR", [128, SEQ // 128], F32)
    SCL = p.sb("SCL", [128, 4], F32)
    TRI = p.sb("TRI", [128, 128], F32)
    ONES = p.sb("ONES", [128, 128], F32)
    TRIW = p.sb("TRIW", [128, 2, 256], F32)
    NTRIW = p.sb("NTRIW", [128, 2, 256], F32)
    IDF = p.sb("IDF", [128, 128], F32)
    ANEG = p.sb("ANEG", [128, 1], F32)
    ONE1 = p.sb("ONE1", [128, 1], F32)
    S = p.sb("Sst", [128, 64], F32)
    p.dma("sp", CW[:].rearrange("p k n -> p (k n)"), cw_d.partition_broadcast(128), writes=["CW"])
    p.dma("sp", CB[:], cb_d.partition_broadcast(128), writes=["CB"])
    p.dma("sp", DTR[:], dtr_d.rearrange("(n p) -> p n", p=128), writes=["DTR"], allow_slow_non_contiguous=True)
    p.dma("sp", SCL[:], sc_d.partition_broadcast(128), writes=["SCL"])
    p.dma("act", TRI[:], tri_d, writes=["TRI"])
    p.dma("act", TRIW[:], triw_d.rearrange("i p n -> p i n"), writes=["TRIW"])
    p.dma("act", NTRIW[:], ntriw_d.rearrange("i p n -> p i n"), writes=["NTRIW"])
    p.dma("act", IDF[:], ident_d, writes=["IDF"])
    p.op("dve", lambda e: e.memset(ONES[:], 1.0), writes=["ONES"])
    p.op("dve", lambda e: e.memset(ONE1[:], 1.0), writes=["ONE1"])
    p.op("dve", lambda e: e.memset(S[:], 0.0), writes=["S"])
    p.op("act", lambda e: e.activation(out=ANEG[:], in_=SCL[:, 1:2], func=AF.Exp), reads=["SCL"], writes=["ANEG"])
    p.op("dve", lambda e: e.tensor_scalar(out=ANEG[:], in0=ANEG[:], scalar1=-1.0, scalar2=None, op0=ALU.mult),
         reads=["ANEG"], writes=["ANEG"])
    X4 = [p.sb(f"s_x4{i}", [128, 2, 4, 320], F32) for i in range(2)]
    acc = p.sb("s_acc", [128, 2, 320], F32)
    tmp = p.sb("s_tmp", [128, 2, 320], F32)
    XA = p.sb("s_XA", [128, 2, 320], F32)
    dx = p.sb("s_dx", [128, 2], F32)
    dax = p.sb("s_dax", [128, 2], F32)
    dl = p.sb("s_dl", [128, 2], F32)
    dtt = p.sb("s_dt", [128, 2], F32)
    aa = p.sb("s_a", [128, 2], F32)
    abc = p.sb("s_abc", [128, 2, 128], F32)
    cscol = p.sb("s_cscol", [128, 2], F32)
    cl = p.sb("s_cl", [128, 1], F32)
    ecl = p.sb("s_ecl", [128, 1], F32)
    dec = p.sb("s_dec", [128, 2], F32)
    lt = p.sb("s_lt", [128, 2, 256], F32)
    ecs = p.sb("s_ecs", [128, 256], F32)
    BCT = p.sb("s_BCT", [128, 512], F32)
    scT = p.sb("s_scT", [128, 2, 256], F32)
    CsT = p.sb("s_CsT", [128, 256], F32)
    Xd = p.sb("s_Xd", [128, 2, 64], F32)
    Xdd = p.sb("s_Xdd", [128, 2, 64], F32)
    yo = [p.sb(f"s_yo{i}", [128, 2, 64], F32) for i in range(2)]
    for c in range(NCH):
        i2 = c % 2
        x4 = X4[i2]
        p.dma("sp" if i2 == 0 else "act", x4[:].rearrange("p j k n -> p j (k n)"),
              xc4_d[c * 256:(c + 1) * 256].rearrange("(j p) k n -> p j (k n)", p=128), writes=[("s_x4", i2)])
        for j in range(2):
            eng = "dve" if j == 0 else "pool"
            p.op(eng, lambda e, j=j, x4=x4: e.tensor_tensor(out=acc[:, j, :], in0=x4[:, j, 0, :], in1=CW[:, 0, :], op=ALU.mult),
                 reads=[("s_x4", i2), "CW"], writes=[("s_acc", j)])
            for k in range(1, 4):
                p.op(eng, lambda e, j=j, k=k, x4=x4: e.tensor_tensor(out=tmp[:, j, :], in0=x4[:, j, k, :], in1=CW[:, k, :], op=ALU.mult),
                     reads=[("s_x4", i2), "CW"], writes=[("s_tmp", j)])
                p.op(eng, lambda e, j=j: e.tensor_tensor(out=acc[:, j, :], in0=acc[:, j, :], in1=tmp[:, j, :], op=ALU.add),
                     reads=[("s_acc", j), ("s_tmp", j)], writes=[("s_acc", j)])
            p.op(eng, lambda e, j=j: e.tensor_tensor(out=acc[:, j, :], in0=acc[:, j, :], in1=CB[:], op=ALU.add),
                 reads=[("s_acc", j), "CB"], writes=[("s_acc", j)])
        p.op("act", lambda e: e.activation(out=XA[:].rearrange("p j n -> p (j n)"), in_=acc[:].rearrange("p j n -> p (j n)"),
                                           func=AF.Silu), reads=[("s_acc", 0), ("s_acc", 1)], writes=["s_XA"])
        p.op("dve", lambda e, c=c: e.tensor_scalar(out=dx[:], in0=DTR[:, 2 * c:2 * c + 2], scalar1=SCL[:, 0:1], scalar2=None,
                                                    op0=ALU.add), reads=["DTR", "SCL"], writes=["s_dx"])
        p.op("act", lambda e: e.activation(out=dax[:], in_=dx[:], func=AF.Abs), reads=["s_dx"], writes=["s_dax"])
        p.op("act", lambda e: e.activation(out=dl[:], in_=dax[:], func=AF.Exp, scale=-1.0), reads=["s_dax"], writes=["s_dl"])
        p.op("act", lambda e: e.activation(out=dl[:], in_=dl[:], func=AF.Ln, bias=ONE1[:], scale=1.0),
             reads=["s_dl", "ONE1"], writes=["s_dl"])
        p.op("dve", lambda e: e.scalar_tensor_tensor(out=dtt[:], in0=dx[:], scalar=0.0, in1=dl[:], op0=ALU.max, op1=ALU.add),
             reads=["s_dx", "s_dl"], writes=["s_dt"])
        p.op("dve", lambda e: e.tensor_scalar(out=aa[:], in0=dtt[:], scalar1=ANEG[:, 0:1], scalar2=None, op0=ALU.mult),
             reads=["s_dt", "ANEG"], writes=["s_a"])
        for i in range(2):
            p.op("dve", lambda e, i=i: e.tensor_scalar(out=abc[:, i, :], in0=ONES[:], scalar1=aa[:, i:i + 1], scalar2=None,
                                                       op0=ALU.mult), reads=["s_a", "ONES"], writes=[("s_abc", i)])
        for i in range(2):
            p.op("pe", lambda e, i=i: e.matmul(bank[0][:, 0:256], lhsT=abc[:, i, :], rhs=TRIW[:, i, :], start=(i == 0), stop=(i == 1)),
                 reads=[("s_abc", i), "TRIW"], writes=[("sbank", 0)])
        p.op("pe", lambda e: e.matmul(bank[0][:, 256:257], lhsT=TRI[:], rhs=aa[:, 0:1], start=True, stop=True),
             reads=["TRI", "s_a"], writes=[("sbank", 0)])
        p.op("pe", lambda e: e.matmul(bank[0][:, 257:258], lhsT=ONES[:], rhs=aa[:, 0:1], start=True, stop=False),
             reads=["ONES", "s_a"], writes=[("sbank", 0)])
        p.op("pe", lambda e: e.matmul(bank[0][:, 257:258], lhsT=TRI[:], rhs=aa[:, 1:2], start=False, stop=True),
             reads=["TRI", "s_a"], writes=[("sbank", 0)])
        p.op("dve", lambda e: e.tensor_copy(out=cscol[:], in_=bank[0][:, 256:258]), reads=[("sbank", 0)], writes=["s_cscol"])
        p.op("dve", lambda e: e.tensor_copy(out=cl[:], in_=bank[0][:, 255:256]), reads=[("sbank", 0)], writes=["s_cl"])
        for i in range(2):
            p.op("dve", lambda e, i=i: e.scalar_tensor_tensor(out=lt[:, i, :], in0=bank[0][:, 0:256], scalar=cscol[:, i:i + 1],
                                                              in1=NTRIW[:, i, :], op0=ALU.subtract, op1=ALU.add),
                 reads=[("sbank", 0), "s_cscol", "NTRIW"], writes=[("s_lt", i)])
            p.op("act", lambda e, i=i: e.activation(out=lt[:, i, :], in_=lt[:, i, :], func=AF.Exp),
                 reads=[("s_lt", i)], writes=[("s_lt", i)])
        p.op("act", lambda e: e.activation(out=ecs[:], in_=bank[0][:, 0:256], func=AF.Exp), reads=[("sbank", 0)], writes=["s_ecs"])
        for j in range(2):
            p.op("pe", lambda e, j=j: e.transpose(out=bank[1][:, j * 128:(j + 1) * 128], in_=XA[:, j, 64:192], identity=IDF[:]),
                 reads=["s_XA", "IDF"], writes=[("sbank", 1)])
            p.op("pe", lambda e, j=j: e.transpose(out=bank[1][:, 256 + j * 128:256 + (j + 1) * 128], in_=XA[:, j, 192:320],
                                                  identity=IDF[:]), reads=["s_XA", "IDF"], writes=[("sbank", 1)])
        p.op("act", lambda e: e.activation(out=BCT[:], in_=bank[1][:], func=AF.Copy), reads=[("sbank", 1)], writes=["s_BCT"])
        for i in range(2):
            p.op("pe", lambda e, i=i: e.matmul(bank[2][:, i * 256:(i + 1) * 256], lhsT=BCT[:, i * 128:(i + 1) * 128],
                                               rhs=BCT[:, 256:512], start=True, stop=True),
                 reads=["s_BCT"], writes=[("sbank", 2)])
        for i in range(2):
            p.op("dve", lambda e, i=i: e.tensor_tensor(out=scT[:, i, :], in0=bank[2][:, i * 256:(i + 1) * 256], in1=lt[:, i, :],
                                                       op=ALU.mult), reads=[("sbank", 2), ("s_lt", i)], writes=[("s_scT", i)])
        p.op("pool", lambda e: e.tensor_tensor(out=CsT[:], in0=BCT[:, 256:512], in1=ecs[:], op=ALU.mult),
             reads=["s_BCT", "s_ecs"], writes=["s_CsT"])
        for i in range(2):
            p.op("dve", lambda e, i=i: e.tensor_scalar(out=Xd[:, i, :], in0=XA[:, i, 0:64], scalar1=dtt[:, i:i + 1], scalar2=None,
                                                       op0=ALU.mult), reads=["s_XA", "s_dt"], writes=[("s_Xd", i)])
        for j in range(2):
            ops = [(scT[:, i, j * 128:(j + 1) * 128], Xd[:, i, :], [("s_scT", i), ("s_Xd", i)]) for i in range(j + 1)]
            ops.append((CsT[:, j * 128:(j + 1) * 128], S[:], ["s_CsT", "S"]))
            for n, (lh, rh, rd) in enumerate(ops):
                p.op("pe", lambda e, lh=lh, rh=rh, n=n, j=j, last=(n == len(ops) - 1): e.matmul(
                    bank[3][:, j * 64:(j + 1) * 64], lhsT=lh, rhs=rh, start=(n == 0), stop=last),
                    reads=rd, writes=[("sbank", 3)])
        y2 = yo[i2]
        for j in range(2):
            p.op("dve", lambda e, j=j, y2=y2: e.scalar_tensor_tensor(out=y2[:, j, :], in0=XA[:, j, 0:64], scalar=SCL[:, 2:3],
                                                                      in1=bank[3][:, j * 64:(j + 1) * 64], op0=ALU.mult, op1=ALU.add),
                 reads=["s_XA", "SCL", ("sbank", 3)], writes=[("s_yo", i2)])
        p.dma("act", y_d[c * 256:(c + 1) * 256, :].rearrange("(j p) n -> p j n", p=128), y2[:], reads=[("s_yo", i2)])
        p.op("act", lambda e: e.activation(out=dec[:], in_=cscol[:], func=AF.Exp, bias=cl[:], scale=-1.0),
             reads=["s_cscol", "s_cl"], writes=["s_dec"])
        for i in range(2):
            p.op("dve", lambda e, i=i: e.tensor_scalar(out=Xdd[:, i, :], in0=Xd[:, i, :], scalar1=dec[:, i:i + 1], scalar2=None,
                                                       op0=ALU.mult), reads=[("s_Xd", i), "s_dec"], writes=[("s_Xdd", i)])
        for i in range(2):
            p.op("pe", lambda e, i=i: e.matmul(bank[4][:, 0:64], lhsT=XA[:, i, 64:192], rhs=Xdd[:, i, :], start=(i == 0), stop=(i == 1)),
                 reads=["s_XA", ("s_Xdd", i)], writes=[("sbank", 4)])
        p.op("act", lambda e: e.activation(out=ecl[:], in_=cl[:], func=AF.Exp), reads=["s_cl"], writes=["s_ecl"])
        p.op("dve", lambda e: e.scalar_tensor_tensor(out=S[:], in0=S[:], scalar=ecl[:, 0:1], in1=bank[4][:, 0:64],
                                                     op0=ALU.mult, op1=ALU.add), reads=["S", "s_ecl", ("sbank", 4)], writes=["S"])
    p.finish()
    return nc


NQI = SEQ // 256
NQF = SEQ // 256


def build_N():
    nc = bass.Bass("TRN2", target_bir_lowering=False)
    di = lambda n, s, d=F32: nc.dram_tensor(n, list(s), d, kind="ExternalInput").ap()
    do = lambda n, s, d=F32: nc.dram_tensor(n, list(s), d, kind="ExternalOutput").ap()
    QT_d = di("QT", [64, 4, NQF, 128], BF16)
    blk_d = di("blk", [2, 16, 128, 1024], BF16)
    w1_d = di("cw1", [2, 2048, 256])
    b1_d = di("cb1", [2, 128, 2])
    pe_d = di("cpe", [2, 128, 16])
    w2_d = di("cw2", [2, 256, 64])
    KsT_d = di("KsT", [128, SEQ], BF16)
    VsA_d = di("VsA", [SEQ, 65], BF16)
    KwT_d = di("KwT", [64, SEQ + 512], BF16)
    VwA_d = di("VwA", [SEQ + 512, 65], BF16)
    gates_d = di("gates", [NQF, 128, 12])
    TC_d = di("TC", [10, 128, 512])
    TS_d = di("TS", [4, 128, 512])
    TW_d = di("TW", [5, 128, 512])
    OV_d = di("OV", [8, 128, 256])
    FB_d = di("FB", [NQF, 128, 256])
    ident_d = di("ident", [128, 128])
    o_d = do("o", [NQF, 128, 256])
    p = Prog(nc)
    SB = [p.ps(f"nS{i}", [128, 512], F32) for i in range(2)]
    OB = [p.ps(f"nO{i}", [128, 512], F32) for i in range(3)]
    IMPB = [p.ps(f"nI{i}", [128, 512], F32) for i in range(2)]
    PT = p.ps("nPT", [128, 1024], BF16)
    identb = p.sb("identb", [128, 128], BF16)
    p.dma("pool", identb[:], ident_d, writes=["identb"])
    KC = p.sb("KC", [64, 1024], BF16)
    VC = p.sb("VC", [128, 8, 65], BF16)
    p.op("dve", lambda e: e.memset(VC[:], 1.0), writes=["VC"])
    import os
    NSK = os.environ.get("NSKIP", "")
    with p.scope() as st:
        W1 = p.sb("n_W1", [128, 2, 16, 256], BF16, st)
        W2 = p.sb("n_W2", [128, 2, 2, 64], BF16, st)
        PE2 = p.sb("n_PE", [128, 2, 16], BF16, st)
        B1 = p.sb("n_B1", [128, 2, 2], F32, st)
        hid = p.sb("n_hid", [128, 2, 2, 1024], BF16, st)
        blk = [p.sb(f"n_blk{i}", [128, 1024], BF16, st) for i in range(2)]
        for j in range(2):
            p.dma("pool", W1[:, j, :, :], w1_d[j].rearrange("(lt p) n -> p lt n", p=128), writes=["n_W1"])
            p.dma("pool", W2[:, j, :, :], w2_d[j].rearrange("(ht p) n -> p ht n", p=128), writes=["n_W2"])
            p.dma("pool", PE2[:, j, :], pe_d[j], writes=["n_PE"])
            p.dma("sp", B1[:, j, :], b1_d[j], writes=["n_B1"])
        for j in range(2):
            for ht in range(2):
                for lt in range(16):
                    p.op("pe", lambda e, j=j, ht=ht, lt=lt: e.matmul(SB[0][:, 0:1], lhsT=W1[:, j, lt, ht * 128:(ht + 1) * 128],
                                                                      rhs=PE2[:, j, lt:lt + 1], start=(lt == 0), stop=(lt == 15)),
                         reads=["n_W1", "n_PE"], writes=[("nS", 0)])
                p.op("dve", lambda e, j=j, ht=ht: e.tensor_tensor(out=B1[:, j, ht:ht + 1], in0=B1[:, j, ht:ht + 1], in1=SB[0][:, 0:1],
                                                                  op=ALU.add), reads=[("nS", 0), "n_B1"], writes=["n_B1"])
            for lt in range(16):
                bb = blk[lt % 2]
                p.dma("sp" if lt % 2 == 0 else "act", bb[:], blk_d[j, lt], writes=[("n_blk", lt % 2)])
                for ht in range(2):
                    for ic in range(2):
                        bank = (SB + OB + IMPB)[ht * 2 + ic + 2]
                        p.op("pe", lambda e, j=j, ht=ht, ic=ic, lt=lt, bb=bb, bank=bank: e.matmul(
                            bank[:], lhsT=W1[:, j, lt, ht * 128:(ht + 1) * 128], rhs=bb[:, ic * 512:(ic + 1) * 512],
                            start=(lt == 0), stop=(lt == 15)), reads=["n_W1", ("n_blk", lt % 2)], writes=[("ncmp", ht, ic)])
            for ht in range(2):
                for ic in range(2):
                    bank = (SB + OB + IMPB)[ht * 2 + ic + 2]
                    p.op("act", lambda e, j=j, ht=ht, ic=ic, bank=bank: e.activation(
                        out=hid[:, j, ht, ic * 512:(ic + 1) * 512], in_=bank[:], func=AF.Gelu_apprx_tanh, bias=B1[:, j, ht:ht + 1], scale=1.0),
                        reads=[("ncmp", ht, ic), "n_B1"], writes=[("n_hid", j)])
        for ic in range(2):
            for ht in range(2):
                p.op("pe", lambda e, ic=ic, ht=ht: e.matmul(SB[0][0:64, :], lhsT=W2[:, 0, ht, :], rhs=hid[:, 0, ht, ic * 512:(ic + 1) * 512],
                                                            start=(ht == 0), stop=(ht == 1)), reads=["n_W2", ("n_hid", 0)], writes=[("nS", 0)])
            p.op("act", lambda e, ic=ic: e.activation(out=KC[:, ic * 512:(ic + 1) * 512], in_=SB[0][0:64, :], func=AF.Copy),
                 reads=[("nS", 0)], writes=["KC"])
        for it in range(8):
            for ht in range(2):
                p.op("pe", lambda e, it=it, ht=ht: e.matmul(SB[1][:, it * 64:(it + 1) * 64], lhsT=hid[:, 1, ht, it * 128:(it + 1) * 128],
                                                            rhs=W2[:, 1, ht, :], start=(ht == 0), stop=(ht == 1)),
                     reads=["n_W2", ("n_hid", 1)], writes=[("nS", 1)])
        p.op("act", lambda e: e.activation(out=VC[:, :, 0:64], in_=SB[1][:].rearrange("p (a b) -> p a b", b=64), func=AF.Copy),
             reads=[("nS", 1)], writes=["VC"])
    KsT = p.sb("KsT_sb", [128, SEQ], BF16)
    VsA = p.sb("VsA_sb", [128, SEQ // 128, 65], BF16)
    for c4 in range(4):
        p.dma("sp", KsT[:, c4 * 4096:(c4 + 1) * 4096], KsT_d[:, c4 * 4096:(c4 + 1) * 4096], writes=["KsT"])
        p.dma("act", VsA[:, c4 * 32:(c4 + 1) * 32, :], VsA_d[c4 * 4096:(c4 + 1) * 4096, :].rearrange("(n p) c -> p n c", p=128),
              writes=["VsA"])
    TC = p.sb("TC_sb", [128, 10, 512], F32)
    TS = p.sb("TS_sb", [128, 4, 512], F32)
    TW = p.sb("TW_sb", [128, 5, 512], F32)
    OV = p.sb("OV_sb", [128, 8, 256], BF16)
    p.dma("sp", TC[:], TC_d.rearrange("n p c -> p n c"), writes=["TC"])
    p.dma("sp", TS[:], TS_d.rearrange("n p c -> p n c"), writes=["TS"])
    p.dma("sp", TW[:], TW_d.rearrange("n p c -> p n c"), writes=["TW"])
    p.dma("pool", OV[:], OV_d.rearrange("n p c -> p n c"), writes=["OV"])
    Qt = [p.sb(f"n_q{i}", [128, 4, 4, 128], BF16) for i in range(2)]
    GT = [p.sb(f"n_g{i}", [128, 12], F32) for i in range(2)]
    FBt = [p.sb(f"n_fb{i}", [128, 256], F32) for i in range(2)]
    Kw = [p.sb(f"n_kw{i}", [64, 640], BF16) for i in range(2)]
    Vw = [p.sb(f"n_vw{i}", [128, 5, 65], BF16) for i in range(2)]
    sbt = [p.sb(f"n_sb{i}", [128, 512], F32) for i in range(4)]
    Et = [p.sb(f"n_E{i}", [128, 512], BF16) for i in range(4)]
    rsc = p.sb("n_rsc", [128, 4], F32)
    imp = p.sb("n_imp", [128, 256], F32)
    imp2 = p.sb("n_imp2", [128, 256], F32)
    m8 = p.sb("n_m8", [128, 8], F32)
    m8b = p.sb("n_m8b", [128, 8], F32)
    thr = p.sb("n_thr", [128, 1], F32)
    negm = p.sb("n_negm", [128, 256], BF16)
    wbr = p.sb("n_wbr", [128, 3, 4], F32)
    ot = [p.sb(f"n_o{i}", [128, 256], F32) for i in range(2)]
    cnt = [0]

    SBANKS = [(SB[0], ("nS", 0)), (SB[1], ("nS", 1)), (IMPB[0], ("nI", 0)), (IMPB[1], ("nI", 1))]

    def att_s(t, qt_ap, qkey, nb):
        b = cnt[0] % nb
        cnt[0] += 1
        (S, skey), sbb, E = SBANKS[b], sbt[b], Et[b]
        rhs = t.get("rhs", qt_ap)
        p.op("pe", lambda e: e.matmul(S[:], lhsT=t["kT"], rhs=rhs, start=True, stop=True),
             reads=t["kkeys"] + [qkey] + t.get("rkeys", []), writes=[skey])
        p.op("dve", lambda e: e.tensor_tensor(out=sbb[:], in0=S[:], in1=t["table"], op=ALU.add),
             reads=[skey] + t["tkeys"], writes=[("n_sb", b)])
        p.op("act", lambda e: e.activation(out=E[:], in_=sbb[:], func=AF.Exp), reads=[("n_sb", b)], writes=[("n_E", b)])
        return E, b

    def att_pv(t, E, b):
        acc_i = t["acc"]
        for h in range(4):
            p.op("pe", lambda e, h=h: e.matmul(OB[acc_i][:, h * 65:(h + 1) * 65], lhsT=E[:, h * 128:(h + 1) * 128], rhs=t["v"],
                                               start=False, stop=False, skip_group_check=True),
                 reads=[("n_E", b)] + t["vkeys"], writes=[("nO", acc_i)])
        kc = t.get("imp_kc")
        if kc is not None:
            for h in range(4):
                p.op("pe", lambda e, h=h: e.matmul(IMPB[h // 2][:, (h % 2) * 256:(h % 2 + 1) * 256],
                                                   lhsT=E[:, h * 128:(h + 1) * 128], rhs=OV[:, kc, :],
                                                   start=False, stop=False, skip_group_check=True),
                     reads=[("n_E", b), "OV"], writes=[("nI", h // 2)])

    def run_tiles(tiles, qt_ap, qkey, nb):
        pend = []
        for t in tiles:
            cur = att_s(t, qt_ap, qkey, nb)
            pend.append((t, cur[0], cur[1]))
            if len(pend) > nb - 1:
                att_pv(*pend.pop(0))
        for x in pend:
            att_pv(*x)

    for i in range(NQI):
        i2 = i % 2
        q, g_, fb, kw, vw = Qt[i2], GT[i2], FBt[i2], Kw[i2], Vw[i2]
        for c4 in range(4):
            p.dma("sp", q[0:64, c4, :, :], QT_d[:, :, i, :], writes=[("n_q", i2)])
        p.dma("sp", g_[:], gates_d[i], writes=[("n_g", i2)])
        p.dma("sp", fb[:], FB_d[i], writes=[("n_fb", i2)])
        p.dma("act", kw[:], KwT_d[:, 2 * i * 128:(2 * i + 5) * 128], writes=[("n_kw", i2)])
        p.dma("act", vw[:], VwA_d[2 * i * 128:(2 * i + 5) * 128, :].rearrange("(n p) c -> p n c", p=128), writes=[("n_vw", i2)])
        for a in range(3):
            p.op("dve", lambda e, a=a: e.memset(OB[a][:], 0.0), writes=[("nO", a)])
        for a in range(2):
            p.op("dve", lambda e, a=a: e.memset(IMPB[a][:], 0.0), writes=[("nI", a)])
        qap = q[0:64, 0, :, :].rearrange("p h q -> p (h q)")
        qk = ("n_q", i2)
        tiles = []
        for kc in range((2 * i + 1) // 16 + 1):
            e_ = 2 * i - 16 * kc
            tidx = e_ // 2 if e_ <= 16 else 9
            tiles.append(dict(kT=KC[:, kc * 128:(kc + 1) * 128], kkeys=["KC"], v=VC[:, kc, :], vkeys=["VC"],
                              table=TC[:, tidx, :], tkeys=["TC"], acc=0, imp_kc=kc))
        for d in range(5):
            tiles.append(dict(kT=kw[:, d * 128:(d + 1) * 128], kkeys=[("n_kw", i2)], v=vw[:, d, :], vkeys=[("n_vw", i2)],
                              table=TW[:, d, :], tkeys=["TW"], acc=2))
        cnt[0] = 0
        run_tiles(tiles, qap, qk, 2)
        sums = lambda a: OB[a][:, 0:260].rearrange("p (h c) -> p h c", c=65)[:, :, 64]
        p.op("dve", lambda e: e.tensor_scalar(out=rsc[:], in0=sums(0), scalar1=1e-30, scalar2=None, op0=ALU.max),
             reads=[("nO", 0)], writes=["n_rsc"])
        p.op("dve", lambda e: e.reciprocal(out=rsc[:], in_=rsc[:]), reads=["n_rsc"], writes=["n_rsc"])
        for h in range(4):
            p.op("dve", lambda e, h=h, fb=fb: e.scalar_tensor_tensor(
                out=imp[:], in0=IMPB[h // 2][:, (h % 2) * 256:(h % 2 + 1) * 256], scalar=rsc[:, h:h + 1],
                in1=(fb[:] if h == 0 else imp[:]), op0=ALU.mult, op1=ALU.add),
                reads=[("nI", h // 2), "n_rsc", ("n_fb", i2), "n_imp"], writes=["n_imp"])
        p.op("dve", lambda e: e.max(out=m8[:], in_=imp[:]), reads=["n_imp"], writes=["n_m8"])
        p.op("dve", lambda e: e.match_replace(out=imp2[:], in_to_replace=m8[:], in_values=imp[:], imm_value=-1e9),
             reads=["n_imp", "n_m8"], writes=["n_imp2"])
        p.op("dve", lambda e: e.max(out=m8b[:], in_=imp2[:]), reads=["n_imp2"], writes=["n_m8b"])
        p.op("dve", lambda e: e.tensor_scalar(out=thr[:], in0=m8b[:, 7:8], scalar1=-5000.0, scalar2=None, op0=ALU.max),
             reads=["n_m8b"], writes=["n_thr"])
        p.op("dve", lambda e: e.tensor_scalar(out=imp2[:], in0=imp[:], scalar1=thr[:, 0:1], scalar2=None, op0=ALU.is_ge),
             reads=["n_imp", "n_thr"], writes=["n_imp2"])
        p.op("dve", lambda e: e.tensor_scalar(out=negm[:], in0=imp2[:], scalar1=-1.0, scalar2=30000.0, op0=ALU.add, op1=ALU.mult),
             reads=["n_imp2"], writes=["n_negm"])
        for c4 in range(4):
            p.op("pe", lambda e, c4=c4: e.transpose(out=PT[64:128, c4 * 128:(c4 + 1) * 128], in_=negm[:, c4 * 64:(c4 + 1) * 64],
                                                    identity=identb[:]), reads=["n_negm", "identb"], writes=["nPT"])
        p.op("act", lambda e, q=q: e.activation(
            out=q[64:128, :, :, :], in_=PT[64:128, 0:512].rearrange("p (c q) -> p c q", c=4).unsqueeze(2).to_broadcast([64, 4, 4, 128]),
            func=AF.Copy), reads=["nPT"], writes=[("n_qm", i2)])
        tiles = []
        for kt in range(2 * i + 2):
            d3 = kt - (2 * i - 1)
            tb = TS[:, d3, :] if d3 >= 0 else TS[:, 3, :]
            tiles.append(dict(kT=KsT[:, kt * 128:(kt + 1) * 128], kkeys=["KsT"], v=VsA[:, kt, :], vkeys=["VsA"],
                              table=tb, tkeys=["TS"], acc=1, rhs=q[:, kt // 32, :, :].rearrange("p h q -> p (h q)"),
                              rkeys=[("n_qm", i2)]))
        cnt[0] = 0
        run_tiles(tiles, qap, qk, 4)
        o = ot[i2]
        for a in range(3):
            p.op("dve", lambda e, a=a: e.tensor_scalar(out=wbr[:, a, :], in0=sums(a), scalar1=1e-30, scalar2=None, op0=ALU.max),
                 reads=[("nO", a)], writes=[("n_wbr", a)])
            p.op("dve", lambda e, a=a: e.reciprocal(out=wbr[:, a, :], in_=wbr[:, a, :]), reads=[("n_wbr", a)], writes=[("n_wbr", a)])
            p.op("dve", lambda e, a=a, g_=g_: e.tensor_tensor(out=wbr[:, a, :], in0=wbr[:, a, :],
                                                               in1=g_[:].rearrange("p (h c) -> p h c", c=3)[:, :, a], op=ALU.mult),
                 reads=[("n_wbr", a), ("n_g", i2)], writes=[("n_wbr", a)])
        for h in range(4):
            for a in range(3):
                src = OB[a][:, h * 65:h * 65 + 64]
                if a == 0:
                    p.op("dve", lambda e, h=h, src=src, o=o: e.tensor_scalar(out=o[:, h * 64:(h + 1) * 64], in0=src,
                                                                            scalar1=wbr[:, 0, h:h + 1], scalar2=None, op0=ALU.mult),
                         reads=[("nO", 0), ("n_wbr", 0)], writes=[("n_o", i2)])
                else:
                    p.op("dve", lambda e, h=h, a=a, src=src, o=o: e.scalar_tensor_tensor(
                        out=o[:, h * 64:(h + 1) * 64], in0=src, scalar=wbr[:, a, h:h + 1], in1=o[:, h * 64:(h + 1) * 64],
                        op0=ALU.mult, op1=ALU.add), reads=[("nO", a), ("n_wbr", a), ("n_o", i2)], writes=[("n_o", i2)])
        p.dma("act", o_d[i], o[:], reads=[("n_o", i2)])
    p.finish()
    return nc


_CONST = {}


def _consts():
    if not _CONST:
        _CONST["ident"] = np.eye(128, dtype=np.float32)
        s = np.arange(128)
        _CONST["tri"] = (s[:, None] <= s[None, :]).astype(np.float32)
    return _CONST


_NC_CACHE = {}


def _get_nc(name, builder):
    return builder()


def run_A(inp, l, x_cur):
    cst = _consts()
    c = inp["c"]
    maps = []
    cols = np.r_[0:3072, 3072:5120]
    ada_w = np.ascontiguousarray(inp["ada_w"][l][:, cols])
    ada_b = np.ascontiguousarray(inp["ada_b"][l][cols])
    wsT = np.ascontiguousarray(inp["gmlp_ws"][l].transpose(0, 2, 1))
    bsT = np.ascontiguousarray(inp["gmlp_bs"][l].T)
    for r in range(NCORE):
        b = r // 4
        maps.append({
            "x": np.ascontiguousarray(x_cur[r * TOKC:(r + 1) * TOKC]),
            "cT": np.ascontiguousarray(c[b].reshape(8, 128).T),
            "ada_w": ada_w, "ada_b": ada_b,
            "lng": inp["ln_g"][l, 0], "lnb": inp["ln_b"][l, 0],
            "w1": inp["ffn_w1"][l, 0], "w3": inp["ffn_w3"][l, 0], "w2": inp["ffn_w2"][l, 0],
            "w_in": inp["w_in"][l],
            "glng": inp["gmlp_ln_g"][l], "glnb": inp["gmlp_ln_b"][l],
            "wsT": wsT, "bsT": bsT,
            "ident": cst["ident"], "tri": cst["tri"],
        })
    nc = build_A()
    res = run_bass_kernel_spmd(nc, maps, core_ids=list(range(NCORE)))
    return res.results


def run_C(inp, l, x1, oatt, ogm, yssm, zz):
    cst = _consts()
    c = inp["c"]
    cols = np.r_[5120:9216]
    ada_w = np.ascontiguousarray(inp["ada_w"][l][:, cols])
    ada_b = np.ascontiguousarray(inp["ada_b"][l][cols])
    maps = []
    for r in range(NCORE):
        b = r // 4
        sl = slice(r * TOKC, (r + 1) * TOKC)
        maps.append({
            "x1": np.ascontiguousarray(x1[sl]), "cT": np.ascontiguousarray(c[b].reshape(8, 128).T),
            "ada_w": ada_w, "ada_b": ada_b,
            "lng1": inp["ln_g"][l, 1], "lnb1": inp["ln_b"][l, 1], "lng2": inp["ln_g"][l, 2], "lnb2": inp["ln_b"][l, 2],
            "w1": inp["ffn_w1"][l, 1], "w3": inp["ffn_w3"][l, 1], "w2": inp["ffn_w2"][l, 1],
            "w_out": inp["w_out"][l],
            "oatt": np.ascontiguousarray(oatt[sl]), "ogm": np.ascontiguousarray(ogm[sl]),
            "yssm": np.ascontiguousarray(yssm[sl]), "zz": np.ascontiguousarray(zz[sl]),
            "normg": inp["ssm_norm_g"][l], "ident": cst["ident"],
        })
    nc = build_C()
    res = run_bass_kernel_spmd(nc, maps, core_ids=list(range(NCORE)))
    return res.results


def run_S(inp, l, xbc, dtr):
    cst = _consts()
    s_ = np.arange(128)
    l_ = np.arange(256)
    triw = np.stack([(s_[:, None] + 128 * i <= l_[None, :]).astype(np.float32) for i in range(2)])
    ntriw = ((triw - 1.0) * 30000.0).astype(np.float32)
    maps = []
    cwl, cbl = inp["ssm_conv_w"][l], inp["ssm_conv_b"][l]
    for r in range(NCORE):
        b, h = r // 4, r % 4
        g = h // 2
        cols = np.r_[64 * h:64 * h + 64, 256 + 128 * g:256 + 128 * g + 128, 512 + 128 * g:512 + 128 * g + 128]
        xcat = xbc[b][:, cols]
        xpad = np.concatenate([np.zeros((3, 320), np.float32), xcat], 0)
        xc4 = np.ascontiguousarray(np.stack([xpad[k:k + SEQ] for k in range(4)], axis=1))
        maps.append({
            "xc4": xc4, "cw": np.ascontiguousarray(cwl[:, cols]).reshape(-1), "cb": np.ascontiguousarray(cbl[cols]),
            "dtr": np.ascontiguousarray(dtr[b][:, h]),
            "scal": np.array([inp["ssm_dt_bias"][l, h], inp["ssm_a_log"][l, h], inp["ssm_d"][l, h], 0.0], np.float32),
            "tri": cst["tri"], "triw": triw, "ntriw": ntriw, "identf": cst["ident"],
        })
    nc = build_S()
    res = run_bass_kernel_spmd(nc, maps, core_ids=list(range(NCORE)))
    y = np.zeros((BATCH, SEQ, 256), np.float32)
    for r in range(NCORE):
        y[r // 4][:, 64 * (r % 4):64 * (r % 4) + 64] = res.results[r]["y"]
    return y


def _bucket(n):
    n = np.maximum(n, 0)
    nf = np.maximum(n, 1).astype(np.float32)
    large = 16 + (np.log(nf / np.float32(16)) / np.float32(np.log(8.0)) * np.float32(16)).astype(np.int32)
    large = np.minimum(large, 31)
    return np.where(n < 16, n, large)


def _table(rb_aug, dist, valid, g):
    idx = np.where(valid, _bucket(dist), 32)
    t = rb_aug[idx][:, :, 4 * g:4 * g + 4]
    return np.ascontiguousarray(t.transpose(0, 2, 1).reshape(128, 512))


def run_N(inp, l, QT_all, KcT_all, VcT_all, KsT_all, KwT_all, Vs_all, Vw_all, gates_all):
    cst = _consts()
    T = SEQ
    rb_aug = np.concatenate([inp["rel_bias"], np.full((1, 8), -30000.0, np.float32)], 0)
    k = np.arange(128)[:, None]
    q = np.arange(128)[None, :]
    allv = np.ones((128, 128), bool)
    big = np.full((128, 128), 100000)
    kg = np.arange(1024)
    cs = 16 * kg
    j = np.arange(256)
    ov = np.clip(np.minimum(cs[:, None] + 32, 64 * j[None, :] + 64) - np.maximum(cs[:, None], 64 * j[None, :]), 0, None) / 32.0
    ov[1023] = 0.0
    OV = ov.reshape(8, 128, 256).astype(np.float32)
    jj = np.arange(64)[:, None]
    tcol = np.arange(T)[None, :]
    IND = (jj == 2 * ((tcol // 128) % 32) + (tcol % 128) // 64)
    maps = []
    for r in range(NCORE):
        b, g, par = r // 4, (r // 2) % 2, r % 2
        TC = [_table(rb_aug, 128 * (2 * e2 + par) + q - 16 * k - 31, (128 * (2 * e2 + par) + q - 16 * k - 31) >= 0, g) for e2 in range(9)]
        TC.append(_table(rb_aug, big, allv, g))
        prev = _table(rb_aug, 128 + q - k, allv, g)
        diag = _table(rb_aug, q - k, (q - k) >= 0, g)
        far = _table(rb_aug, big, allv, g)
        none = _table(rb_aug, big, ~allv, g)
        TS = [prev, diag, none, far] if par == 0 else [far, prev, diag, far]
        TW = [_table(rb_aug, q + 128 * (4 - d) - k, ((q + 128 * (4 - d) - k) >= 0) & ((q + 128 * (4 - d) - k) < 512), g) for d in range(5)]
        qt = 2 * np.arange(NQF) + par
        t = (128 * qt[:, None] + np.arange(128)[None, :])
        cur = (t // 64)[:, :, None]
        jb = np.arange(256)[None, None, :]
        FB = np.where((jb == 0) | (jb == cur) | (jb == cur - 1), 1e4, np.where(jb <= cur, 0.0, -1e4)).astype(np.float32)
        tok = slice(b * T, (b + 1) * T)
        QTc = QT_all[:, tok].reshape(8, 64, 128, 128)[4 * g:4 * g + 4, :, par::2, :].transpose(1, 0, 2, 3)
        blks = []
        for src in (KcT_all, VcT_all):
            kcg = src[g * 64:(g + 1) * 64, tok]
            bl = np.zeros((16, 2, 64, 1024), kcg.dtype)
            ii = np.arange(1023)
            for lt in range(16):
                for lb in range(2):
                    bl[lt, lb, :, :1023] = kcg[:, 16 * ii + 2 * lt + lb]
            blks.append(bl.reshape(16, 128, 1024))
        KsT = np.concatenate([KsT_all[g * 64:(g + 1) * 64, tok], IND.astype(KsT_all.dtype)], 0)
        ones = np.ones((T, 1), Vs_all.dtype)
        VsA = np.concatenate([Vs_all[tok, g * 64:(g + 1) * 64], ones], 1)
        KwP = np.zeros((64, 512 + T + 128), KwT_all.dtype)
        KwP[:, 512:512 + T] = KwT_all[g * 64:(g + 1) * 64, tok]
        VwP = np.zeros((512 + T + 128, 65), Vw_all.dtype)
        VwP[512:512 + T] = np.concatenate([Vw_all[tok, g * 64:(g + 1) * 64], ones], 1)
        gt = gates_all[tok, 12 * g:12 * g + 12].reshape(128, 128, 12)[par::2]
        maps.append({
            "QT": np.ascontiguousarray(QTc), "blk": np.ascontiguousarray(np.stack(blks)),
            "cw1": inp["cmp_w1"][l], "cb1": np.ascontiguousarray(inp["cmp_b1"][l].reshape(2, 2, 128).transpose(0, 2, 1)),
            "cpe": np.ascontiguousarray(inp["cmp_pe"][l].reshape(2, 16, 2, 64).transpose(0, 2, 3, 1).reshape(2, 128, 16)),
            "cw2": inp["cmp_w2"][l],
            "KsT": np.ascontiguousarray(KsT), "VsA": np.ascontiguousarray(VsA),
            "KwT": np.ascontiguousarray(KwP[:, par * 128:par * 128 + T + 512]),
            "VwA": np.ascontiguousarray(VwP[par * 128:par * 128 + T + 512]),
            "gates": np.ascontiguousarray(gt), "TC": np.stack(TC), "TS": np.stack(TS), "TW": np.stack(TW),
            "OV": OV, "FB": FB, "ident": cst["ident"],
        })
    nc = build_N()
    res = run_bass_kernel_spmd(nc, maps, core_ids=list(range(NCORE)))
    o_att = np.zeros((BATCH, SEQ, 512), np.float32)
    for r in range(NCORE):
        b, g, par = r // 4, (r // 2) % 2, r % 2
        o = res.results[r]["o"]
        o_att[b].reshape(128, 128, 512)[par::2, :, g * 256:(g + 1) * 256] = o
    return o_att


def kernel(**inp):
    inp = {k: np.asarray(v) for k, v in inp.items()}
    x = np.ascontiguousarray(inp["x"].reshape(-1, D))
    cat = lambda res, name, ax: np.concatenate([r[name] for r in res], ax)
    for l in range(DEPTH):
        ra = run_A(inp, l, x)
        x1 = cat(ra, "x1", 0)
        tmf = cat(ra, "tmf", 0)
        o_att = run_N(inp, l, cat(ra, "QT", 1), cat(ra, "KcT", 1), cat(ra, "VcT", 1), cat(ra, "KsT", 1), cat(ra, "KwT", 1),
                      cat(ra, "Vs", 0), cat(ra, "Vw", 0), tmf[:, 1028:1052])
        ypre = run_S(inp, l, tmf[:, 256:1024].reshape(BATCH, SEQ, 768), tmf[:, 1024:1028].reshape(BATCH, SEQ, 4))
        rcz = run_C(inp, l, x1, o_att.reshape(-1, 512), cat(ra, "ogm", 0), ypre.reshape(-1, 256), np.ascontiguousarray(tmf[:, 0:256]))
        x = cat(rcz, "x3", 0)
    return x.reshape(BATCH, SEQ, D).astype(np.float32)
```

```python
import numpy as np
import ml_dtypes
from contextlib import ExitStack, contextmanager
import concourse.bass as bass
import concourse.mybir as mybir
from concourse.bass_utils import run_bass_kernel_spmd

F32 = mybir.dt.float32
BF16 = mybir.dt.bfloat16
AF = mybir.ActivationFunctionType
ALU = mybir.AluOpType
NPBF = ml_dtypes.bfloat16

D = 1024
DEPTH = 2
SEQ = 16384
BATCH = 2
DFF = 2816
NFT = DFF // 128
DIN = 2844
ALPHA = (2 * DEPTH) ** 0.25
LN_EPS = 1e-5
NCORE = 8
TOKC = BATCH * SEQ // NCORE
NTT = TOKC // 128

ENGS = ("pe", "dve", "act", "pool", "sp")


_PSUM_KEYS = ("bank", "ptb", "sbank", "nS", "nO", "nI", "ncmp")


def _is_psum_key(k):
    return k == "nPT" or (isinstance(k, tuple) and k[0] in _PSUM_KEYS)


class Prog:
    def __init__(self, nc, n_dma_slots=8):
        self.nc = nc
        self.stack = ExitStack()
        self.ops = {e: [] for e in ENGS}
        self.sem = {e: self.stack.enter_context(nc.semaphore("s_" + e)) for e in ENGS}
        self.cnt = {e: 0 for e in ENGS}
        self.seen = {e: {} for e in ENGS}
        self.last_w = {}
        self.readers = {}
        self.nslot = n_dma_slots
        self.dma_sems, self.dma_vals, self.dma_next = {}, {}, {}
        self.semobj = dict(self.sem)
        for q in ("sp", "act", "pool"):
            self.dma_sems[q] = [self.stack.enter_context(nc.semaphore(f"d_{q}{i}")) for i in range(n_dma_slots)]
            self.dma_vals[q] = [0] * n_dma_slots
            self.dma_next[q] = 0
            for i, s in enumerate(self.dma_sems[q]):
                self.semobj[(q, i)] = s
        self.n_inst = 0
        self._rr = 0

    def sb(self, name, shape, dtype, stack=None):
        return (stack or self.stack).enter_context(self.nc.sbuf_tensor(name, list(shape), dtype))

    def ps(self, name, shape, dtype=F32, stack=None):
        return (stack or self.stack).enter_context(self.nc.psum_tensor(name, list(shape), dtype))

    def _need(self, eng, tok, waits):
        semkey, val, _ = tok
        if self.seen[eng].get(semkey, 0) >= val:
            return
        self.seen[eng][semkey] = val
        waits.append((semkey, val))

    def _deps(self, eng, engid, reads, writes):
        waits = []
        for r in reads:
            t = self.last_w.get(r)
            if t is not None:
                self._need(eng, t, waits)
        for w in writes:
            t = self.last_w.get(w)
            if t is not None and (t[2] != engid or eng != "pe"):
                self._need(eng, t, waits)
            for sk, (v, eid) in self.readers.get(w, {}).items():
                if eid != engid or eng != "pe":
                    self._need(eng, (sk, v, eid), waits)
        return waits

    def _commit(self, tok, reads, writes):
        for r in reads:
            self.readers.setdefault(r, {})[tok[0]] = (tok[1], tok[2])
        for w in writes:
            self.last_w[w] = tok
            self.readers[w] = {}

    def op(self, eng, fn, reads=(), writes=()):
        ps_reads = [k for k in reads if _is_psum_key(k)]
        if ps_reads:
            writes = list(writes) + [k for k in ps_reads if k not in writes]
        waits = self._deps(eng, eng, reads, writes)
        self.cnt[eng] += 1
        tok = (eng, self.cnt[eng], eng)
        self._commit(tok, reads, writes)
        self.ops[eng].append((waits, fn, self.sem[eng], 1))
        self.n_inst += 1
        return tok

    def dma(self, q, out, in_, reads=(), writes=(), **kw):
        slot = self.dma_next[q]
        self.dma_next[q] = (slot + 1) % self.nslot
        semkey = (q, slot)
        engid = ("dma", q, slot)
        waits = self._deps(q, engid, reads, writes)
        prev = self.dma_vals[q][slot]
        if prev > 0:
            self._need(q, (semkey, prev, engid), waits)
        val = prev + 16
        self.dma_vals[q][slot] = val
        tok = (semkey, val, engid)
        self._commit(tok, reads, writes)
        self.ops[q].append((waits, lambda e: e.dma_start(out=out, in_=in_, **kw), self.dma_sems[q][slot], 16))
        self.n_inst += 1
        return tok

    def barrier(self):
        for e in ENGS:
            waits = []
            for o in ENGS:
                if o != e and self.cnt[o] > 0:
                    self._need(e, (o, self.cnt[o], o), waits)
            for q in self.dma_sems:
                for i in range(self.nslot):
                    v = self.dma_vals[q][i]
                    if v > 0:
                        self._need(e, ((q, i), v, None), waits)
            if waits:
                self.ops[e].append((waits, None, None, 0))

    @contextmanager
    def scope(self):
        st = ExitStack()
        try:
            yield st
        finally:
            self.barrier()
            st.close()

    def finish(self):
        self.barrier()
        engmap = {"pe": "tensor", "dve": "vector", "act": "scalar", "pool": "gpsimd", "sp": "sync"}
        semobj = self.semobj
        with self.nc.Block() as block:
            for e in ENGS:
                ops = self.ops[e]
                if not ops:
                    continue

                def body(engine, ops=ops):
                    for waits, fn, sem, inc in ops:
                        for sk, v in waits:
                            engine.wait_ge(semobj[sk], v)
                        if fn is not None:
                            fn(engine).then_inc(sem, inc)

                getattr(block, engmap[e])(body)
        self.stack.close()


class RowCtx:
    def __init__(self, p, nc, consts):
        self.p, self.nc = p, nc
        self.identb = p.sb("identb", [128, 128], BF16)
        p.dma("pool", self.identb[:], consts["ident"], writes=["identb"])
        self.eps = p.sb("epsc", [128, 1], F32)
        p.op("dve", lambda e: e.memset(self.eps[:], LN_EPS), writes=["epsc"])
        self.bank = [p.ps(f"bank{i}", [128, 512], F32) for i in range(6)]
        self.ptb = [p.ps(f"ptb{i}", [128, 1024], BF16) for i in range(2)]


def emit_mod(p, rc, st, cT, ada_w, ada_b, ncols, dst, posts):
    nck = ncols // 512
    with p.scope() as s2:
        ct = p.sb("mod_ct", [128, 8], F32, s2)
        sc = p.sb("mod_sc", [128, 8], F32, s2)
        scb = p.sb("mod_scb", [128, 8, 128], F32, s2)
        bb = p.sb("mod_bb", [128, ncols], F32, s2)
        wch = [p.sb(f"mod_w{i}", [128, 8, 512], F32, s2) for i in range(2)]
        p.dma("sp", ct[:], cT, writes=["mod_ct"])
        p.dma("act", bb[:], ada_b.partition_broadcast(128), writes=["mod_bb"])
        p.op("act", lambda e: e.activation(out=sc[:], in_=ct[:], func=AF.Silu), reads=["mod_ct"], writes=["mod_sc"])
        for dt in range(8):
            p.op("dve", lambda e, dt=dt: e.tensor_copy(out=scb[:, dt, :], in_=sc[:, dt:dt + 1].to_broadcast([128, 128])),
                 reads=["mod_sc"], writes=[("mod_scb", dt)])
        for ci in range(nck):
            w = wch[ci % 2]
            wk = ("mod_w", ci % 2)
            p.dma("sp" if ci % 2 == 0 else "act", w[:],
                  ada_w[:, ci * 512:(ci + 1) * 512].rearrange("(dt p) n -> p dt n", p=128), writes=[wk])
            bk = rc.bank[ci % 2]
            for dt in range(8):
                p.op("pe", lambda e, dt=dt, w=w, bk=bk: e.matmul(bk[:], lhsT=scb[:, dt, :], rhs=w[:, dt, :],
                                                                  start=(dt == 0), stop=(dt == 7)),
                     reads=[wk, ("mod_scb", dt)], writes=[("bank", ci % 2)])
            p.op("dve", lambda e, ci=ci, bk=bk: e.tensor_tensor(out=dst[:, ci * 512:(ci + 1) * 512], in0=bk[:],
                                                                 in1=bb[:, ci * 512:(ci + 1) * 512], op=ALU.add),
                 reads=[("bank", ci % 2), "mod_bb"], writes=["modtile"])
        for (c0, c1, add, mul) in posts:
            p.op("dve", lambda e, c0=c0, c1=c1, add=add, mul=mul: e.tensor_scalar(
                out=dst[:, c0:c1], in0=dst[:, c0:c1], scalar1=float(add), scalar2=float(mul), op0=ALU.add, op1=ALU.mult),
                reads=["modtile"], writes=["modtile"])


def emit_modulate_T(p, rc, xt, xkey, SC, SH, hf, hb, hT, tag, ptb_i=0):
    p.op("dve", lambda e: e.tensor_tensor(out=hf[:], in0=xt, in1=SC, op=ALU.mult),
         reads=[xkey, "modtile"], writes=[tag + "hf"])
    p.op("pool", lambda e: e.tensor_tensor(out=hb[:], in0=hf[:], in1=SH, op=ALU.add),
         reads=[tag + "hf", "modtile"], writes=[tag + "hb"])
    pt = rc.ptb[ptb_i]
    for dt in range(8):
        p.op("pe", lambda e, dt=dt: e.transpose(out=pt[:, dt * 128:(dt + 1) * 128], in_=hb[:, dt * 128:(dt + 1) * 128],
                                                 identity=rc.identb[:]),
             reads=[tag + "hb", "identb"], writes=[("ptb", ptb_i)])
    p.op("act", lambda e: e.activation(out=hT[:].rearrange("p a b -> p (a b)"), in_=pt[:], func=AF.Copy),
         reads=[("ptb", ptb_i)], writes=[tag + "hT"])


def emit_res_ln(p, rc, ybanks, xres, xkey, GF, lng, lnb, r, stats, mv, rstd, xo, tag):
    for half in range(2):
        sl = slice(half * 512, (half + 1) * 512)
        p.op("dve", lambda e, half=half, sl=sl: e.tensor_tensor(out=r[:, sl], in0=rc.bank[ybanks[half]][:], in1=GF[:, sl],
                                                                op=ALU.mult),
             reads=[("bank", ybanks[half]), "modtile"], writes=[(tag + "r", half)])
        p.op("dve", lambda e, sl=sl: e.scalar_tensor_tensor(out=r[:, sl], in0=xres[:, sl], scalar=float(ALPHA), in1=r[:, sl],
                                                            op0=ALU.mult, op1=ALU.add),
             reads=[(tag + "r", half), xkey], writes=[(tag + "r", half)])
        p.op("dve", lambda e, half=half, sl=sl: e.bn_stats(out=stats[:, half * 6:(half + 1) * 6], in_=r[:, sl]),
             reads=[(tag + "r", half)], writes=[(tag + "st", half)])
    p.op("dve", lambda e: e.bn_aggr(out=mv[:], in_=stats[:]), reads=[(tag + "st", 0), (tag + "st", 1)], writes=[tag + "mv"])
    p.op("act", lambda e: e.activation(out=rstd[:], in_=mv[:, 1:2], func=AF.Sqrt, bias=rc.eps[:], scale=1.0),
         reads=[tag + "mv", "epsc"], writes=[tag + "rstd"])
    p.op("dve", lambda e: e.reciprocal(out=rstd[:], in_=rstd[:]), reads=[tag + "rstd"], writes=[tag + "rstd"])
    p.op("dve", lambda e: e.tensor_scalar(out=r[:], in0=r[:], scalar1=mv[:, 0:1], scalar2=rstd[:, 0:1],
                                          op0=ALU.subtract, op1=ALU.mult),
         reads=[(tag + "r", 0), (tag + "r", 1), tag + "mv", tag + "rstd"], writes=[(tag + "r", 0), (tag + "r", 1)])
    p.op("pool", lambda e: e.tensor_tensor(out=r[:], in0=r[:], in1=lng[:], op=ALU.mult),
         reads=[(tag + "r", 0), (tag + "r", 1), "lngb"], writes=[(tag + "r", 0), (tag + "r", 1)])
    p.op("pool", lambda e: e.tensor_tensor(out=xo[:], in0=r[:], in1=lnb[:], op=ALU.add),
         reads=[(tag + "r", 0), (tag + "r", 1), "lngb"], writes=[tag + "xo"])


def emit_ffn(p, rc, x_in, x_out, w1, w3, w2, SC, SH, GF, lng, lnb):
    with p.scope() as st:
        W1 = p.sb("W1", [128, 8, DFF], BF16, st)
        W3 = p.sb("W3", [128, 8, DFF], BF16, st)
        W2 = p.sb("W2", [128, NFT, D], BF16, st)
        for dt in range(8):
            p.dma("pool", W1[:, dt, :], w1[dt * 128:(dt + 1) * 128, :], writes=["W1"])
            p.dma("pool", W3[:, dt, :], w3[dt * 128:(dt + 1) * 128, :], writes=["W3"])
        for f0 in range(0, NFT, 2):
            p.dma("pool", W2[:, f0:f0 + 2, :], w2[f0 * 128:(f0 + 2) * 128, :].rearrange("(f p) n -> p f n", p=128), writes=["W2"])
        xt = [p.sb(f"f_x{i}", [128, D], F32, st) for i in range(2)]
        hf = p.sb("f_hf", [128, D], F32, st)
        hb = p.sb("f_hb", [128, D], BF16, st)
        hT = p.sb("f_hT", [128, 8, 128], BF16, st)
        aT = p.sb("f_aT", [128, NFT, 128], BF16, st)
        sg = [p.sb(f"f_sg{i}", [128, 128], F32, st) for i in range(2)]
        r = p.sb("f_r", [128, D], F32, st)
        xo = [p.sb(f"f_xo{i}", [128, D], F32, st) for i in range(2)]
        stats = p.sb("f_stats", [128, 12], F32, st)
        mv = p.sb("f_mv", [128, 2], F32, st)
        rstd = p.sb("f_rstd", [128, 1], F32, st)
        for tt in range(NTT):
            x = xt[tt % 2]
            xk = ("f_x", tt % 2)
            p.dma("sp", x[:], x_in[tt * 128:(tt + 1) * 128, :], writes=[xk])
            emit_modulate_T(p, rc, x[:], xk, SC, SH, hf, hb, hT, "f_", ptb_i=tt % 2)
            for ft in range(NFT):
                b = ft % 2
                bk = rc.bank[b]
                for dt in range(8):
                    p.op("pe", lambda e, dt=dt, ft=ft, bk=bk: e.matmul(bk[:, 0:128], lhsT=W1[:, dt, ft * 128:(ft + 1) * 128],
                                                                        rhs=hT[:, dt, :], start=(dt == 0), stop=(dt == 7)),
                         reads=["W1", "f_hT"], writes=[("bank", b)])
                for dt in range(8):
                    p.op("pe", lambda e, dt=dt, ft=ft, bk=bk: e.matmul(bk[:, 128:256], lhsT=W3[:, dt, ft * 128:(ft + 1) * 128],
                                                                        rhs=hT[:, dt, :], start=(dt == 0), stop=(dt == 7)),
                         reads=["W3", "f_hT"], writes=[("bank", b)])
                p.op("act", lambda e, b=b, bk=bk: e.activation(out=sg[b][:], in_=bk[:, 0:128], func=AF.Silu),
                     reads=[("bank", b)], writes=[("f_sg", b)])
                p.op("dve", lambda e, b=b, bk=bk, ft=ft: e.tensor_tensor(out=aT[:, ft, :], in0=sg[b][:], in1=bk[:, 128:256],
                                                                          op=ALU.mult),
                     reads=[("bank", b), ("f_sg", b)], writes=[("f_aT", ft)])
            for half in range(2):
                bk = rc.bank[2 + half]
                for ft in range(NFT):
                    p.op("pe", lambda e, ft=ft, bk=bk, half=half: e.matmul(bk[:], lhsT=aT[:, ft, :],
                                                                            rhs=W2[:, ft, half * 512:(half + 1) * 512],
                                                                            start=(ft == 0), stop=(ft == NFT - 1)),
                         reads=[("f_aT", ft), "W2"], writes=[("bank", 2 + half)])
            o = xo[tt % 2]
            emit_res_ln(p, rc, (2, 3), x, xk, GF, lng, lnb, r, stats, mv, rstd, o, "f_")
            p.dma("act", x_out[tt * 128:(tt + 1) * 128, :], o[:], reads=["f_xo"], writes=[])


def build_A():
    nc = bass.Bass("TRN2", target_bir_lowering=False)
    di = lambda n, s, d=F32: nc.dram_tensor(n, list(s), d, kind="ExternalInput").ap()
    do = lambda n, s, d=F32: nc.dram_tensor(n, list(s), d, kind="ExternalOutput").ap()
    x_in = di("x", [TOKC, D])
    cT = di("cT", [128, 8])
    ada_w = di("ada_w", [D, 5120])
    ada_b = di("ada_b", [5120])
    lng_d, lnb_d = di("lng", [D]), di("lnb", [D])
    w1, w3, w2 = di("w1", [D, DFF]), di("w3", [D, DFF]), di("w2", [DFF, D])
    w_in = di("w_in", [D, DIN])
    glng_d, glnb_d = di("glng", [256]), di("glnb", [256])
    wsT_d = di("wsT", [4, 128, 128])
    bsT_d = di("bsT", [128, 4])
    consts = {"ident": di("ident", [128, 128]), "tri": di("tri", [128, 128])}
    x1_d = do("x1", [TOKC, D])
    QT_d = do("QT", [512, TOKC], BF16)
    KcT_d, VcT_d = do("KcT", [128, TOKC], BF16), do("VcT", [128, TOKC], BF16)
    KsT_d, KwT_d = do("KsT", [128, TOKC], BF16), do("KwT", [128, TOKC], BF16)
    Vs_d, Vw_d = do("Vs", [TOKC, 128], BF16), do("Vw", [TOKC, 128], BF16)
    tmf_d = do("tmf", [TOKC, 1052])
    ogm_d = do("ogm", [TOKC, 256], BF16)

    p = Prog(nc)
    rc = RowCtx(p, nc, consts)
    MOD = p.sb("MOD", [128, 5120], F32)
    lng = p.sb("lng_t", [128, D], F32)
    lnb = p.sb("lnb_t", [128, D], F32)
    p.dma("sp", lng[:], lng_d.partition_broadcast(128), writes=["lngb"])
    p.dma("sp", lnb[:], lnb_d.partition_broadcast(128), writes=["lngb"])
    emit_mod(p, rc, None, cT, ada_w, ada_b, 5120, MOD,
             [(1024, 2048, 1.0, 1.0), (2048, 3072, 1.0, 0.5), (4096, 5120, 1.0, 1.0)])
    import os
    STOP = os.environ.get("KSTOP", "")
    if STOP == "mod":
        p.finish(); return nc
    emit_ffn(p, rc, x_in, x1_d, w1, w3, w2, MOD[:, 1024:2048], MOD[:, 0:1024], MOD[:, 2048:3072], lng, lnb)
    if STOP == "ffn":
        p.finish(); return nc

    with p.scope() as st:
        WIN = p.sb("WIN", [128, 8, DIN], BF16, st)
        for dt in range(8):
            p.dma("pool", WIN[:, dt, :], w_in[dt * 128:(dt + 1) * 128, :], writes=["WIN"])
        tri = p.sb("tri_t", [128, 128], F32, st)
        p.dma("sp", tri[:], consts["tri"], writes=["tri_t"])
        wsf = p.sb("wsf", [128, 4, 128], F32, st)
        p.dma("sp", wsf[:], wsT_d.rearrange("g s t -> s g t"), writes=["wsf"])
        WS = p.sb("WS", [128, 4, 128], BF16, st)
        for g in range(4):
            p.op("dve", lambda e, g=g: e.tensor_tensor(out=WS[:, g, :], in0=wsf[:, g, :], in1=tri[:], op=ALU.mult),
                 reads=["wsf", "tri_t"], writes=["WS"])
        BS = p.sb("BS", [128, 4], F32, st)
        p.dma("sp", BS[:], bsT_d, writes=["BS"])
        glng = p.sb("glng_t", [128, 256], F32, st)
        glnb = p.sb("glnb_t", [128, 256], F32, st)
        p.dma("sp", glng[:], glng_d.partition_broadcast(128), writes=["glngb"])
        p.dma("sp", glnb[:], glnb_d.partition_broadcast(128), writes=["glngb"])
        xt = [p.sb(f"a_x{i}", [128, D], F32, st) for i in range(2)]
        hf = p.sb("a_hf", [128, D], F32, st)
        hb = p.sb("a_hb", [128, D], BF16, st)
        hT = p.sb("a_hT", [128, 8, 128], BF16, st)
        fm = [p.sb(f"a_fm{i}", [128, 8, 128], BF16, st) for i in range(2)]
        vsw = [p.sb(f"a_vsw{i}", [128, 256], BF16, st) for i in range(2)]
        gu = p.sb("a_gu", [128, 512], F32, st)
        vn = p.sb("a_vn", [128, 256], F32, st)
        vnb = p.sb("a_vnb", [128, 256], BF16, st)
        gst = p.sb("a_gst", [128, 6], F32, st)
        gmv = p.sb("a_gmv", [128, 2], F32, st)
        grs = p.sb("a_grs", [128, 1], F32, st)
        ogm = [p.sb(f"a_ogm{i}", [128, 256], BF16, st) for i in range(2)]
        ssm = [p.sb(f"a_ssm{i}", [128, 1052], F32, st) for i in range(2)]
        SC1, SH1 = MOD[:, 4096:5120], MOD[:, 3072:4096]
        fm_cols = [0, 128, 256, 384, 512, 640, 768, 1024]
        for tt in range(NTT):
            i2 = tt % 2
            tok = slice(tt * 128, (tt + 1) * 128)
            x = xt[i2]
            xk = ("a_x", i2)
            p.dma("sp", x[:], x1_d[tok, :], reads=[], writes=[xk])
            emit_modulate_T(p, rc, x[:], xk, SC1, SH1, hf, hb, hT, "a_", ptb_i=0)
            for ci, c0 in enumerate(fm_cols):
                b = ci // 4
                bk = rc.bank[b]
                for dt in range(8):
                    p.op("pe", lambda e, dt=dt, c0=c0, bk=bk, ci=ci: e.matmul(
                        bk[:, (ci % 4) * 128:(ci % 4 + 1) * 128], lhsT=WIN[:, dt, c0:c0 + 128], rhs=hT[:, dt, :],
                        start=(dt == 0), stop=(dt == 7)), reads=["WIN", "a_hT"], writes=[("bank", b)])
            f = fm[i2]
            p.op("act", lambda e, f=f: e.activation(out=f[:, 0:4, :].rearrange("p a b -> p (a b)"), in_=rc.bank[0][:],
                                                    func=AF.Identity, scale=0.125),
                 reads=[("bank", 0)], writes=[("a_fm", i2)])
            p.op("dve", lambda e, f=f: e.tensor_copy(out=f[:, 4:8, :].rearrange("p a b -> p (a b)"), in_=rc.bank[1][:]),
                 reads=[("bank", 1)], writes=[("a_fm", i2)])
            p.dma("act", QT_d[:, tok].rearrange("(c p) t -> p c t", p=128), f[:, 0:4, :], reads=[("a_fm", i2)])
            for k, dd in enumerate((KcT_d, VcT_d, KsT_d, KwT_d)):
                p.dma("act", dd[:, tok], f[:, 4 + k, :], reads=[("a_fm", i2)])
            def tm(bank_i, col_off, c0, c1):
                bk = rc.bank[bank_i]
                for dt in range(8):
                    p.op("pe", lambda e, dt=dt: e.matmul(bk[:, col_off:col_off + (c1 - c0)], lhsT=hT[:, dt, :],
                                                         rhs=WIN[:, dt, c0:c1], start=(dt == 0), stop=(dt == 7)),
                         reads=["WIN", "a_hT"], writes=[("bank", bank_i)])
            tm(2, 0, 896, 1024)
            tm(2, 128, 1152, 1304)
            tm(2, 280, 2716, 2844)
            tm(3, 0, 1304, 1816)
            tm(4, 0, 1816, 2328)
            tm(5, 0, 2328, 2840)
            vv = vsw[i2]
            p.op("dve", lambda e, vv=vv: e.tensor_copy(out=vv[:], in_=rc.bank[2][:, 0:256]),
                 reads=[("bank", 2)], writes=[("a_vsw", i2)])
            p.dma("act", Vs_d[tok, :], vv[:, 0:128], reads=[("a_vsw", i2)])
            p.dma("act", Vw_d[tok, :], vv[:, 128:256], reads=[("a_vsw", i2)])
            sm = ssm[i2]
            p.op("act", lambda e, sm=sm: e.activation(out=sm[:, 1028:1052], in_=rc.bank[2][:, 256:280], func=AF.Sigmoid),
                 reads=[("bank", 2)], writes=[("a_ssm", i2)])
            p.op("dve", lambda e, sm=sm: e.tensor_copy(out=sm[:, 1024:1028], in_=rc.bank[2][:, 280 + 124:280 + 128]),
                 reads=[("bank", 2)], writes=[("a_ssm", i2)])
            p.op("act", lambda e, sm=sm: e.activation(out=sm[:, 0:512], in_=rc.bank[4][:], func=AF.Copy),
                 reads=[("bank", 4)], writes=[("a_ssm", i2)])
            p.op("dve", lambda e, sm=sm: e.tensor_copy(out=sm[:, 512:1024], in_=rc.bank[5][:]),
                 reads=[("bank", 5)], writes=[("a_ssm", i2)])
            p.dma("sp", tmf_d[tok, :], sm[:], reads=[("a_ssm", i2)])
            p.op("act", lambda e: e.activation(out=gu[:], in_=rc.bank[3][:], func=AF.Gelu_apprx_tanh),
                 reads=[("bank", 3)], writes=["a_gu"])
            p.op("dve", lambda e: e.bn_stats(out=gst[:], in_=gu[:, 256:512]), reads=["a_gu"], writes=["a_gst"])
            p.op("dve", lambda e: e.bn_aggr(out=gmv[:], in_=gst[:]), reads=["a_gst"], writes=["a_gmv"])
            p.op("act", lambda e: e.activation(out=grs[:], in_=gmv[:, 1:2], func=AF.Sqrt, bias=rc.eps[:], scale=1.0),
                 reads=["a_gmv", "epsc"], writes=["a_grs"])
            p.op("dve", lambda e: e.reciprocal(out=grs[:], in_=grs[:]), reads=["a_grs"], writes=["a_grs"])
            p.op("dve", lambda e: e.tensor_scalar(out=vn[:], in0=gu[:, 256:512], scalar1=gmv[:, 0:1], scalar2=grs[:, 0:1],
                                                  op0=ALU.subtract, op1=ALU.mult),
                 reads=["a_gu", "a_gmv", "a_grs"], writes=["a_vn"])
            p.op("pool", lambda e: e.tensor_tensor(out=vn[:], in0=vn[:], in1=glng[:], op=ALU.mult),
                 reads=["a_vn", "glngb"], writes=["a_vn"])
            p.op("pool", lambda e: e.tensor_tensor(out=vnb[:], in0=vn[:], in1=glnb[:], op=ALU.add),
                 reads=["a_vn", "glngb"], writes=["a_vnb"])
            for g in range(4):
                p.op("pe", lambda e, g=g: e.matmul(rc.bank[3][:, g * 64:(g + 1) * 64], lhsT=WS[:, g, :],
                                                   rhs=vnb[:, g * 64:(g + 1) * 64], start=True, stop=True),
                     reads=["WS", "a_vnb"], writes=[("bank", 3)])
            og = ogm[i2]
            for g in range(4):
                p.op("dve", lambda e, g=g, og=og: e.scalar_tensor_tensor(
                    out=og[:, g * 64:(g + 1) * 64], in0=rc.bank[3][:, g * 64:(g + 1) * 64], scalar=BS[:, g:g + 1],
                    in1=gu[:, g * 64:(g + 1) * 64], op0=ALU.add, op1=ALU.mult),
                    reads=[("bank", 3), "a_gu", "BS"], writes=[("a_ogm", i2)])
            p.dma("act", ogm_d[tok, :], og[:], reads=[("a_ogm", i2)])
    p.finish()
    return nc


def build_C():
    nc = bass.Bass("TRN2", target_bir_lowering=False)
    di = lambda n, s, d=F32: nc.dram_tensor(n, list(s), d, kind="ExternalInput").ap()
    do = lambda n, s, d=F32: nc.dram_tensor(n, list(s), d, kind="ExternalOutput").ap()
    x1_d = di("x1", [TOKC, D])
    cT = di("cT", [128, 8])
    ada_w = di("ada_w", [D, 4096])
    ada_b = di("ada_b", [4096])
    lng1_d, lnb1_d = di("lng1", [D]), di("lnb1", [D])
    lng2_d, lnb2_d = di("lng2", [D]), di("lnb2", [D])
    w1, w3, w2 = di("w1", [D, DFF]), di("w3", [D, DFF]), di("w2", [DFF, D])
    w_out = di("w_out", [D, D])
    oatt_d = di("oatt", [TOKC, 512])
    ogm_d = di("ogm", [TOKC, 256], BF16)
    yssm_d = di("yssm", [TOKC, 256])
    zz_d = di("zz", [TOKC, 256])
    ng_d = di("normg", [256])
    consts = {"ident": di("ident", [128, 128])}
    x2_d = do("x2", [TOKC, D])
    x3_d = do("x3", [TOKC, D])

    p = Prog(nc)
    rc = RowCtx(p, nc, consts)
    MOD = p.sb("MOD", [128, 4096], F32)
    lng = p.sb("lng_t", [128, D], F32)
    lnb = p.sb("lnb_t", [128, D], F32)
    emit_mod(p, rc, None, cT, ada_w, ada_b, 4096, MOD,
             [(0, 1024, 1.0, 1.0), (2048, 3072, 1.0, 1.0), (3072, 4096, 1.0, 0.5)])
    p.dma("sp", lng[:], lng1_d.partition_broadcast(128), writes=["lngb"])
    p.dma("sp", lnb[:], lnb1_d.partition_broadcast(128), writes=["lngb"])
    with p.scope() as st:
        WO = p.sb("WO", [128, 8, D], BF16, st)
        for dt in range(8):
            p.dma("pool", WO[:, dt, :], w_out[dt * 128:(dt + 1) * 128, :], writes=["WO"])
        ng = p.sb("ng_t", [128, 256], F32, st)
        p.dma("sp", ng[:], ng_d.partition_broadcast(128), writes=["ng_t"])
        xt = [p.sb(f"c_x{i}", [128, D], F32, st) for i in range(2)]
        oa = [p.sb(f"c_oa{i}", [128, 512], F32, st) for i in range(2)]
        ys = [p.sb(f"c_ys{i}", [128, 512], F32, st) for i in range(2)]
        om = p.sb("c_om", [128, D], BF16, st)
        omT = p.sb("c_omT", [128, 8, 128], BF16, st)
        sz = p.sb("c_sz", [128, 256], F32, st)
        gg = p.sb("c_gg", [128, 256], F32, st)
        junk = p.sb("c_junk", [128, 128], F32, st)
        ss = p.sb("c_ss", [128, 2], F32, st)
        r = p.sb("c_r", [128, D], F32, st)
        xo = [p.sb(f"c_xo{i}", [128, D], F32, st) for i in range(2)]
        stats = p.sb("c_stats", [128, 12], F32, st)
        mv = p.sb("c_mv", [128, 2], F32, st)
        rstd = p.sb("c_rstd", [128, 1], F32, st)
        for tt in range(NTT):
            i2 = tt % 2
            tok = slice(tt * 128, (tt + 1) * 128)
            x, xk = xt[i2], ("c_x", i2)
            p.dma("sp", x[:], x1_d[tok, :], writes=[xk])
            p.dma("sp", oa[i2][:], oatt_d[tok, :], writes=[("c_oa", i2)])
            p.dma("pool", om[:, 512:768], ogm_d[tok, :], writes=["c_om_g"])
            p.dma("act", ys[i2][:, 0:256], yssm_d[tok, :], writes=[("c_ys", i2)])
            p.dma("act", ys[i2][:, 256:512], zz_d[tok, :], writes=[("c_ys", i2)])
            p.op("pool", lambda e, i2=i2: e.tensor_copy(out=om[:, 0:512], in_=oa[i2][:]), reads=[("c_oa", i2)], writes=["c_om_a"])
            p.op("act", lambda e, i2=i2: e.activation(out=sz[:], in_=ys[i2][:, 256:512], func=AF.Silu),
                 reads=[("c_ys", i2)], writes=["c_sz"])
            p.op("dve", lambda e, i2=i2: e.tensor_tensor(out=gg[:], in0=ys[i2][:, 0:256], in1=sz[:], op=ALU.mult),
                 reads=[("c_ys", i2), "c_sz"], writes=["c_gg"])
            p.op("dve", lambda e: e.tensor_tensor(out=sz[:], in0=gg[:], in1=gg[:], op=ALU.mult), reads=["c_gg"], writes=["c_sz"])
            for k in range(2):
                p.op("dve", lambda e, k=k: e.reduce_sum(out=ss[:, k:k + 1], in_=sz[:, k * 128:(k + 1) * 128],
                                                        axis=mybir.AxisListType.X), reads=["c_sz"], writes=[("c_ss", k)])
            p.op("act", lambda e: e.activation(out=ss[:], in_=ss[:], func=AF.Sqrt, bias=rc.eps[:], scale=1.0 / 128.0),
                 reads=[("c_ss", 0), ("c_ss", 1), "epsc"], writes=[("c_ss", 0), ("c_ss", 1)])
            p.op("dve", lambda e: e.reciprocal(out=ss[:], in_=ss[:]), reads=[("c_ss", 0), ("c_ss", 1)],
                 writes=[("c_ss", 0), ("c_ss", 1)])
            for k in range(2):
                p.op("dve", lambda e, k=k: e.scalar_tensor_tensor(out=om[:, 768 + k * 128:768 + (k + 1) * 128],
                                                                  in0=gg[:, k * 128:(k + 1) * 128], scalar=ss[:, k:k + 1],
                                                                  in1=ng[:, k * 128:(k + 1) * 128], op0=ALU.mult, op1=ALU.mult),
                     reads=["c_gg", ("c_ss", 0), ("c_ss", 1), "ng_t"], writes=[("c_om_s", k)])
            pt = rc.ptb[i2]
            for dt in range(8):
                p.op("pe", lambda e, dt=dt, pt=pt: e.transpose(out=pt[:, dt * 128:(dt + 1) * 128],
                                                               in_=om[:, dt * 128:(dt + 1) * 128], identity=rc.identb[:]),
                     reads=["c_om_a", "c_om_g", ("c_om_s", 0), ("c_om_s", 1), "identb"], writes=[("ptb", i2)])
            p.op("act", lambda e, pt=pt: e.activation(out=omT[:].rearrange("p a b -> p (a b)"), in_=pt[:], func=AF.Copy),
                 reads=[("ptb", i2)], writes=["c_omT"])
            for half in range(2):
                bk = rc.bank[2 + half]
                for dt in range(8):
                    p.op("pe", lambda e, dt=dt, bk=bk, half=half: e.matmul(bk[:], lhsT=omT[:, dt, :],
                                                                            rhs=WO[:, dt, half * 512:(half + 1) * 512],
                                                                            start=(dt == 0), stop=(dt == 7)),
                         reads=["c_omT", "WO"], writes=[("bank", 2 + half)])
            o = xo[i2]
            emit_res_ln(p, rc, (2, 3), x, xk, MOD[:, 0:1024], lng, lnb, r, stats, mv, rstd, o, "c_")
            p.dma("act", x2_d[tok, :], o[:], reads=["c_xo"])
    p.dma("sp", lng[:], lng2_d.partition_broadcast(128), writes=["lngb"])
    p.dma("sp", lnb[:], lnb2_d.partition_broadcast(128), writes=["lngb"])
    emit_ffn(p, rc, x2_d, x3_d, w1, w3, w2, MOD[:, 2048:3072], MOD[:, 1024:2048], MOD[:, 3072:4096], lng, lnb)
    p.finish()
    return nc


NCH = SEQ // 256


def build_S():
    nc = bass.Bass("TRN2", target_bir_lowering=False)
    di = lambda n, s, d=F32: nc.dram_tensor(n, list(s), d, kind="ExternalInput").ap()
    do = lambda n, s, d=F32: nc.dram_tensor(n, list(s), d, kind="ExternalOutput").ap()
    xc4_d = di("xc4", [SEQ, 4, 320])
    cw_d = di("cw", [4 * 320])
    cb_d = di("cb", [320])
    dtr_d = di("dtr", [SEQ])
    sc_d = di("scal", [4])
    tri_d = di("tri", [128, 128])
    triw_d = di("triw", [2, 128, 256])
    ntriw_d = di("ntriw", [2, 128, 256])
    ident_d = di("identf", [128, 128])
    y_d = do("y", [SEQ, 64])
    p = Prog(nc)
    bank = [p.ps(f"sbank{i}", [128, 512], F32) for i in range(5)]
    CW = p.sb("CW", [128, 4, 320], F32)
    CB = p.sb("CB", [128, 320], F32)
    DTR = p.sb("DTR", [128, SEQ // 128], F32)
    SCL = p.sb("SCL", [128, 4], F32)
    TRI = p.sb("TRI", [128, 128], F32)
    ONES = p.sb("ONES", [128, 128], F32)
    TRIW = p.sb("TRIW", [128, 2, 256], F32)
    NTRIW = p.sb("NTRIW", [128, 2, 256], F32)
    IDF = p.sb("IDF", [128, 128], F32)
    ANEG = p.sb("ANEG", [128, 1], F32)
    ONE1 = p.sb("ONE1", [128, 1], F32)
    S = p.sb("Sst", [128, 64], F32)
    p.dma("sp", CW[:].rearrange("p k n -> p (k n)"), cw_d.partition_broadcast(128), writes=["CW"])
    p.dma("sp", CB[:], cb_d.partition_broadcast(128), writes=["CB"])
    p.dma("sp", DTR[:], dtr_d.rearrange("(n p) -> p n", p=128), writes=["DTR"], allow_slow_non_contiguous=True)
    p.dma("sp", SCL[:], sc_d.partition_broadcast(128), writes=["SCL"])
    p.dma("act", TRI[:], tri_d, writes=["TRI"])
    p.dma("act", TRIW[:], triw_d.rearrange("i p n -> p i n"), writes=["TRIW"])
    p.dma("act", NTRIW[:], ntriw_d.rearrange("i p n -> p i n"), writes=["NTRIW"])
    p.dma("act", IDF[:], ident_d, writes=["IDF"])
    p.op("dve", lambda e: e.memset(ONES[:], 1.0), writes=["ONES"])
    p.op("dve", lambda e: e.memset(ONE1[:], 1.0), writes=["ONE1"])
    p.op("dve", lambda e: e.memset(S[:], 0.0), writes=["S"])
    p.op("act", lambda e: e.activation(out=ANEG[:], in_=SCL[:, 1:2], func=AF.Exp), reads=["SCL"], writes=["ANEG"])
    p.op("dve", lambda e: e.tensor_scalar(out=ANEG[:], in0=ANEG[:], scalar1=-1.0, scalar2=None, op0=ALU.mult),
         reads=["ANEG"], writes=["ANEG"])
    X4 = [p.sb(f"s_x4{i}", [128, 2, 4, 320], F32) for i in range(2)]
    acc = p.sb("s_acc", [128, 2, 320], F32)
    tmp = p.sb("s_tmp", [128, 2, 320], F32)
    XA = p.sb("s_XA", [128, 2, 320], F32)
    dx = p.sb("s_dx", [128, 2], F32)
    dax = p.sb("s_dax", [128, 2], F32)
    dl = p.sb("s_dl", [128, 2], F32)
    dtt = p.sb("s_dt", [128, 2], F32)
    aa = p.sb("s_a", [128, 2], F32)
    abc = p.sb("s_abc", [128, 2, 128], F32)
    cscol = p.sb("s_cscol", [128, 2], F32)
    cl = p.sb("s_cl", [128, 1], F32)
    ecl = p.sb("s_ecl", [128, 1], F32)
    dec = p.sb("s_dec", [128, 2], F32)
    lt = p.sb("s_lt", [128, 2, 256], F32)
    ecs = p.sb("s_ecs", [128, 256], F32)
    BCT = p.sb("s_BCT", [128, 512], F32)
    scT = p.sb("s_scT", [128, 2, 256], F32)
    CsT = p.sb("s_CsT", [128, 256], F32)
    Xd = p.sb("s_Xd", [128, 2, 64], F32)
    Xdd = p.sb("s_Xdd", [128, 2, 64], F32)
    yo = [p.sb(f"s_yo{i}", [128, 2, 64], F32) for i in range(2)]
    for c in range(NCH):
        i2 = c % 2
        x4 = X4[i2]
        p.dma("sp" if i2 == 0 else "act", x4[:].rearrange("p j k n -> p j (k n)"),
              xc4_d[c * 256:(c + 1) * 256].rearrange("(j p) k n -> p j (k n)", p=128), writes=[("s_x4", i2)])
        for j in range(2):
            eng = "dve" if j == 0 else "pool"
            p.op(eng, lambda e, j=j, x4=x4: e.tensor_tensor(out=acc[:, j, :], in0=x4[:, j, 0, :], in1=CW[:, 0, :], op=ALU.mult),
                 reads=[("s_x4", i2), "CW"], writes=[("s_acc", j)])
            for k in range(1, 4):
                p.op(eng, lambda e, j=j, k=k, x4=x4: e.tensor_tensor(out=tmp[:, j, :], in0=x4[:, j, k, :], in1=CW[:, k, :], op=ALU.mult),
                     reads=[("s_x4", i2), "CW"], writes=[("s_tmp", j)])
                p.op(eng, lambda e, j=j: e.tensor_tensor(out=acc[:, j, :], in0=acc[:, j, :], in1=tmp[:, j, :], op=ALU.add),
                     reads=[("s_acc", j), ("s_tmp", j)], writes=[("s_acc", j)])
            p.op(eng, lambda e, j=j: e.tensor_tensor(out=acc[:, j, :], in0=acc[:, j, :], in1=CB[:], op=ALU.add),
                 reads=[("s_acc", j), "CB"], writes=[("s_acc", j)])
        p.op("act", lambda e: e.activation(out=XA[:].rearrange("p j n -> p (j n)"), in_=acc[:].rearrange("p j n -> p (j n)"),
                                           func=AF.Silu), reads=[("s_acc", 0), ("s_acc", 1)], writes=["s_XA"])
        p.op("dve", lambda e, c=c: e.tensor_scalar(out=dx[:], in0=DTR[:, 2 * c:2 * c + 2], scalar1=SCL[:, 0:1], scalar2=None,
                                                    op0=ALU.add), reads=["DTR", "SCL"], writes=["s_dx"])
        p.op("act", lambda e: e.activation(out=dax[:], in_=dx[:], func=AF.Abs), reads=["s_dx"], writes=["s_dax"])
        p.op("act", lambda e: e.activation(out=dl[:], in_=dax[:], func=AF.Exp, scale=-1.0), reads=["s_dax"], writes=["s_dl"])
        p.op("act", lambda e: e.activation(out=dl[:], in_=dl[:], func=AF.Ln, bias=ONE1[:], scale=1.0),
             reads=["s_dl", "ONE1"], writes=["s_dl"])
        p.op("dve", lambda e: e.scalar_tensor_tensor(out=dtt[:], in0=dx[:], scalar=0.0, in1=dl[:], op0=ALU.max, op1=ALU.add),
             reads=["s_dx", "s_dl"], writes=["s_dt"])
        p.op("dve", lambda e: e.tensor_scalar(out=aa[:], in0=dtt[:], scalar1=ANEG[:, 0:1], scalar2=None, op0=ALU.mult),
             reads=["s_dt", "ANEG"], writes=["s_a"])
        for i in range(2):
            p.op("dve", lambda e, i=i: e.tensor_scalar(out=abc[:, i, :], in0=ONES[:], scalar1=aa[:, i:i + 1], scalar2=None,
                                                       op0=ALU.mult), reads=["s_a", "ONES"], writes=[("s_abc", i)])
        for i in range(2):
            p.op("pe", lambda e, i=i: e.matmul(bank[0][:, 0:256], lhsT=abc[:, i, :], rhs=TRIW[:, i, :], start=(i == 0), stop=(i == 1)),
                 reads=[("s_abc", i), "TRIW"], writes=[("sbank", 0)])
        p.op("pe", lambda e: e.matmul(bank[0][:, 256:257], lhsT=TRI[:], rhs=aa[:, 0:1], start=True, stop=True),
             reads=["TRI", "s_a"], writes=[("sbank", 0)])
        p.op("pe", lambda e: e.matmul(bank[0][:, 257:258], lhsT=ONES[:], rhs=aa[:, 0:1], start=True, stop=False),
             reads=["ONES", "s_a"], writes=[("sbank", 0)])
        p.op("pe", lambda e: e.matmul(bank[0][:, 257:258], lhsT=TRI[:], rhs=aa[:, 1:2], start=False, stop=True),
             reads=["TRI", "s_a"], writes=[("sbank", 0)])
        p.op("dve", lambda e: e.tensor_copy(out=cscol[:], in_=bank[0][:, 256:258]), reads=[("sbank", 0)], writes=["s_cscol"])
        p.op("dve", lambda e: e.tensor_copy(out=cl[:], in_=bank[0][:, 255:256]), reads=[("sbank", 0)], writes=["s_cl"])
        for i in range(2):
            p.op("dve", lambda e, i=i: e.scalar_tensor_tensor(out=lt[:, i, :], in0=bank[0][:, 0:256], scalar=cscol[:, i:i + 1],
                                                              in1=NTRIW[:, i, :], op0=ALU.subtract, op1=ALU.add),
                 reads=[("sbank", 0), "s_cscol", "NTRIW"], writes=[("s_lt", i)])
            p.op("act", lambda e, i=i: e.activation(out=lt[:, i, :], in_=lt[:, i, :], func=AF.Exp),
                 reads=[("s_lt", i)], writes=[("s_lt", i)])
        p.op("act", lambda e: e.activation(out=ecs[:], in_=bank[0][:, 0:256], func=AF.Exp), reads=[("sbank", 0)], writes=["s_ecs"])
        for j in range(2):
            p.op("pe", lambda e, j=j: e.transpose(out=bank[1][:, j * 128:(j + 1) * 128], in_=XA[:, j, 64:192], identity=IDF[:]),
                 reads=["s_XA", "IDF"], writes=[("sbank", 1)])
            p.op("pe", lambda e, j=j: e.transpose(out=bank[1][:, 256 + j * 128:256 + (j + 1) * 128], in_=XA[:, j, 192:320],
                                                  identity=IDF[:]), reads=["s_XA", "IDF"], writes=[("sbank", 1)])
        p.op("act", lambda e: e.activation(out=BCT[:], in_=bank[1][:], func=AF.Copy), reads=[("sbank", 1)], writes=["s_BCT"])
        for i in range(2):
            p.op("pe", lambda e, i=i: e.matmul(bank[2][:, i * 256:(i + 1) * 256], lhsT=BCT[:, i * 128:(i + 1) * 128],
                                               rhs=BCT[:, 256:512], start=True, stop=True),
                 reads=["s_BCT"], writes=[("sbank", 2)])
        for i in range(2):
            p.op("dve", lambda e, i=i: e.tensor_tensor(out=scT[:, i, :], in0=bank[2][:, i * 256:(i + 1) * 256], in1=lt[:, i, :],
                                                       op=ALU.mult), reads=[("sbank", 2), ("s_lt", i)], writes=[("s_scT", i)])
        p.op("pool", lambda e: e.tensor_tensor(out=CsT[:], in0=BCT[:, 256:512], in1=ecs[:], op=ALU.mult),
             reads=["s_BCT", "s_ecs"], writes=["s_CsT"])
        for i in range(2):
            p.op("dve", lambda e, i=i: e.tensor_scalar(out=Xd[:, i, :], in0=XA[:, i, 0:64], scalar1=dtt[:, i:i + 1], scalar2=None,
                                                       op0=ALU.mult), reads=["s_XA", "s_dt"], writes=[("s_Xd", i)])
        for j in range(2):
            ops = [(scT[:, i, j * 128:(j + 1) * 128], Xd[:, i, :], [("s_scT", i), ("s_Xd", i)]) for i in range(j + 1)]
            ops.append((CsT[:, j * 128:(j + 1) * 128], S[:], ["s_CsT", "S"]))
            for n, (lh, rh, rd) in enumerate(ops):
                p.op("pe", lambda e, lh=lh, rh=rh, n=n, j=j, last=(n == len(ops) - 1): e.matmul(
                    bank[3][:, j * 64:(j + 1) * 64], lhsT=lh, rhs=rh, start=(n == 0), stop=last),
                    reads=rd, writes=[("sbank", 3)])
        y2 = yo[i2]
        for j in range(2):
            p.op("dve", lambda e, j=j, y2=y2: e.scalar_tensor_tensor(out=y2[:, j, :], in0=XA[:, j, 0:64], scalar=SCL[:, 2:3],
                                                                      in1=bank[3][:, j * 64:(j + 1) * 64], op0=ALU.mult, op1=ALU.add),
                 reads=["s_XA", "SCL", ("sbank", 3)], writes=[("s_yo", i2)])
        p.dma("act", y_d[c * 256:(c + 1) * 256, :].rearrange("(j p) n -> p j n", p=128), y2[:], reads=[("s_yo", i2)])
        p.op("act", lambda e: e.activation(out=dec[:], in_=cscol[:], func=AF.Exp, bias=cl[:], scale=-1.0),
             reads=["s_cscol", "s_cl"], writes=["s_dec"])
        for i in range(2):
            p.op("dve", lambda e, i=i: e.tensor_scalar(out=Xdd[:, i, :], in0=Xd[:, i, :], scalar1=dec[:, i:i + 1], scalar2=None,
                                                       op0=ALU.mult), reads=[("s_Xd", i), "s_dec"], writes=[("s_Xdd", i)])
        for i in range(2):
            p.op("pe", lambda e, i=i: e.matmul(bank[4][:, 0:64], lhsT=XA[:, i, 64:192], rhs=Xdd[:, i, :], start=(i == 0), stop=(i == 1)),
                 reads=["s_XA", ("s_Xdd", i)], writes=[("sbank", 4)])
        p.op("act", lambda e: e.activation(out=ecl[:], in_=cl[:], func=AF.Exp), reads=["s_cl"], writes=["s_ecl"])
        p.op("dve", lambda e: e.scalar_tensor_tensor(out=S[:], in0=S[:], scalar=ecl[:, 0:1], in1=bank[4][:, 0:64],
                                                     op0=ALU.mult, op1=ALU.add), reads=["S", "s_ecl", ("sbank", 4)], writes=["S"])
    p.finish()
    return nc


NQI = SEQ // 256
NQF = SEQ // 256


def build_N():
    nc = bass.Bass("TRN2", target_bir_lowering=False)
    di = lambda n, s, d=F32: nc.dram_tensor(n, list(s), d, kind="ExternalInput").ap()
    do = lambda n, s, d=F32: nc.dram_tensor(n, list(s), d, kind="ExternalOutput").ap()
    QT_d = di("QT", [64, 4, NQF, 128], BF16)
    blk_d = di("blk", [2, 16, 128, 1024], BF16)
    w1_d = di("cw1", [2, 2048, 256])
    b1_d = di("cb1", [2, 128, 2])
    pe_d = di("cpe", [2, 128, 16])
    w2_d = di("cw2", [2, 256, 64])
    KsT_d = di("KsT", [128, SEQ], BF16)
    VsA_d = di("VsA", [SEQ, 65], BF16)
    KwT_d = di("KwT", [64, SEQ + 512], BF16)
    VwA_d = di("VwA", [SEQ + 512, 65], BF16)
    gates_d = di("gates", [NQF, 128, 12])
    TC_d = di("TC", [10, 128, 512])
    TS_d = di("TS", [4, 128, 512])
    TW_d = di("TW", [5, 128, 512])
    OV_d = di("OV", [8, 128, 256])
    FB_d = di("FB", [NQF, 128, 256])
    ident_d = di("ident", [128, 128])
    o_d = do("o", [NQF, 128, 256])
    p = Prog(nc)
    SB = [p.ps(f"nS{i}", [128, 512], F32) for i in range(2)]
    OB = [p.ps(f"nO{i}", [128, 512], F32) for i in range(3)]
    IMPB = [p.ps(f"nI{i}", [128, 512], F32) for i in range(2)]
    PT = p.ps("nPT", [128, 1024], BF16)
    identb = p.sb("identb", [128, 128], BF16)
    p.dma("pool", identb[:], ident_d, writes=["identb"])
    KC = p.sb("KC", [64, 1024], BF16)
    VC = p.sb("VC", [128, 8, 65], BF16)
    p.op("dve", lambda e: e.memset(VC[:], 1.0), writes=["VC"])
    import os
    NSK = os.environ.get("NSKIP", "")
    with p.scope() as st:
        W1 = p.sb("n_W1", [128, 2, 16, 256], BF16, st)
        W2 = p.sb("n_W2", [128, 2, 2, 64], BF16, st)
        PE2 = p.sb("n_PE", [128, 2, 16], BF16, st)
        B1 = p.sb("n_B1", [128, 2, 2], F32, st)
        hid = p.sb("n_hid", [128, 2, 2, 1024], BF16, st)
        blk = [p.sb(f"n_blk{i}", [128, 1024], BF16, st) for i in range(2)]
        for j in range(2):
            p.dma("pool", W1[:, j, :, :], w1_d[j].rearrange("(lt p) n -> p lt n", p=128), writes=["n_W1"])
            p.dma("pool", W2[:, j, :, :], w2_d[j].rearrange("(ht p) n -> p ht n", p=128), writes=["n_W2"])
            p.dma("pool", PE2[:, j, :], pe_d[j], writes=["n_PE"])
            p.dma("sp", B1[:, j, :], b1_d[j], writes=["n_B1"])
        for j in range(2):
            for ht in range(2):
                for lt in range(16):
                    p.op("pe", lambda e, j=j, ht=ht, lt=lt: e.matmul(SB[0][:, 0:1], lhsT=W1[:, j, lt, ht * 128:(ht + 1) * 128],
                                                                      rhs=PE2[:, j, lt:lt + 1], start=(lt == 0), stop=(lt == 15)),
                         reads=["n_W1", "n_PE"], writes=[("nS", 0)])
                p.op("dve", lambda e, j=j, ht=ht: e.tensor_tensor(out=B1[:, j, ht:ht + 1], in0=B1[:, j, ht:ht + 1], in1=SB[0][:, 0:1],
                                                                  op=ALU.add), reads=[("nS", 0), "n_B1"], writes=["n_B1"])
            for lt in range(16):
                bb = blk[lt % 2]
                p.dma("sp" if lt % 2 == 0 else "act", bb[:], blk_d[j, lt], writes=[("n_blk", lt % 2)])
                for ht in range(2):
                    for ic in range(2):
                        bank = (SB + OB + IMPB)[ht * 2 + ic + 2]
                        p.op("pe", lambda e, j=j, ht=ht, ic=ic, lt=lt, bb=bb, bank=bank: e.matmul(
                            bank[:], lhsT=W1[:, j, lt, ht * 128:(ht + 1) * 128], rhs=bb[:, ic * 512:(ic + 1) * 512],
                            start=(lt == 0), stop=(lt == 15)), reads=["n_W1", ("n_blk", lt % 2)], writes=[("ncmp", ht, ic)])
            for ht in range(2):
                for ic in range(2):
                    bank = (SB + OB + IMPB)[ht * 2 + ic + 2]
                    p.op("act", lambda e, j=j, ht=ht, ic=ic, bank=bank: e.activation(
                        out=hid[:, j, ht, ic * 512:(ic + 1) * 512], in_=bank[:], func=AF.Gelu_apprx_tanh, bias=B1[:, j, ht:ht + 1], scale=1.0),
                        reads=[("ncmp", ht, ic), "n_B1"], writes=[("n_hid", j)])
        for ic in range(2):
            for ht in range(2):
                p.op("pe", lambda e, ic=ic, ht=ht: e.matmul(SB[0][0:64, :], lhsT=W2[:, 0, ht, :], rhs=hid[:, 0, ht, ic * 512:(ic + 1) * 512],
                                                            start=(ht == 0), stop=(ht == 1)), reads=["n_W2", ("n_hid", 0)], writes=[("nS", 0)])
            p.op("act", lambda e, ic=ic: e.activation(out=KC[:, ic * 512:(ic + 1) * 512], in_=SB[0][0:64, :], func=AF.Copy),
                 reads=[("nS", 0)], writes=["KC"])
        for it in range(8):
            for ht in range(2):
                p.op("pe", lambda e, it=it, ht=ht: e.matmul(SB[1][:, it * 64:(it + 1) * 64], lhsT=hid[:, 1, ht, it * 128:(it + 1) * 128],
                                                            rhs=W2[:, 1, ht, :], start=(ht == 0), stop=(ht == 1)),
                     reads=["n_W2", ("n_hid", 1)], writes=[("nS", 1)])
        p.op("act", lambda e: e.activation(out=VC[:, :, 0:64], in_=SB[1][:].rearrange("p (a b) -> p a b", b=64), func=AF.Copy),
             reads=[("nS", 1)], writes=["VC"])
    KsT = p.sb("KsT_sb", [128, SEQ], BF16)
    VsA = p.sb("VsA_sb", [128, SEQ // 128, 65], BF16)
    for c4 in range(4):
        p.dma("sp", KsT[:, c4 * 4096:(c4 + 1) * 4096], KsT_d[:, c4 * 4096:(c4 + 1) * 4096], writes=["KsT"])
        p.dma("act", VsA[:, c4 * 32:(c4 + 1) * 32, :], VsA_d[c4 * 4096:(c4 + 1) * 4096, :].rearrange("(n p) c -> p n c", p=128),
              writes=["VsA"])
    TC = p.sb("TC_sb", [128, 10, 512], F32)
    TS = p.sb("TS_sb", [128, 4, 512], F32)
    TW = p.sb("TW_sb", [128, 5, 512], F32)
    OV = p.sb("OV_sb", [128, 8, 256], BF16)
    p.dma("sp", TC[:], TC_d.rearrange("n p c -> p n c"), writes=["TC"])
    p.dma("sp", TS[:], TS_d.rearrange("n p c -> p n c"), writes=["TS"])
    p.dma("sp", TW[:], TW_d.rearrange("n p c -> p n c"), writes=["TW"])
    p.dma("pool", OV[:], OV_d.rearrange("n p c -> p n c"), writes=["OV"])
    ECH = p.sb("n_ech", [128, 4], F32)
    p.op("act", lambda e: e.activation(out=ECH[:], in_=TS[:, 3, :].rearrange("p (h q) -> p h q", h=4)[:, :, 0], func=AF.Exp),
         reads=["TS"], writes=["n_ech"])
    wfar = p.sb("n_wfar", [128, 4], F32)
    ssel = p.sb("n_ssel", [128, 4], F32)
    Qt = [p.sb(f"n_q{i}", [128, 4, 4, 128], BF16) for i in range(2)]
    GT = [p.sb(f"n_g{i}", [128, 12], F32) for i in range(2)]
    FBt = [p.sb(f"n_fb{i}", [128, 256], F32) for i in range(2)]
    Kw = [p.sb(f"n_kw{i}", [64, 640], BF16) for i in range(2)]
    Vw = [p.sb(f"n_vw{i}", [128, 5, 65], BF16) for i in range(2)]
    sbt = [p.sb(f"n_sb{i}", [128, 512], F32) for i in range(4)]
    Et = [p.sb(f"n_E{i}", [128, 512], BF16) for i in range(4)]
    rsc = p.sb("n_rsc", [128, 4], F32)
    imp = p.sb("n_imp", [128, 256], F32)
    imp2 = p.sb("n_imp2", [128, 256], F32)
    m8 = p.sb("n_m8", [128, 8], F32)
    m8b = p.sb("n_m8b", [128, 8], F32)
    thr = p.sb("n_thr", [128, 1], F32)
    negm = p.sb("n_negm", [128, 256], BF16)
    wbr = p.sb("n_wbr", [128, 3, 4], F32)
    ot = [p.sb(f"n_o{i}", [128, 256], F32) for i in range(2)]
    cnt = [0]

    SBANKS = [(SB[0], ("nS", 0)), (SB[1], ("nS", 1)), (IMPB[1], ("nI", 1))]

    def att_s(t, qt_ap, qkey, nb):
        b = cnt[0] % nb
        cnt[0] += 1
        (S, skey), sbb, E = SBANKS[b], sbt[b], Et[b]
        rhs = t.get("rhs", qt_ap)
        p.op("pe", lambda e: e.matmul(S[:], lhsT=t["kT"], rhs=rhs, start=True, stop=True),
             reads=t["kkeys"] + [qkey] + t.get("rkeys", []), writes=[skey])
        if t.get("table") is None:
            p.op("act", lambda e: e.activation(out=E[:], in_=S[:], func=AF.Exp), reads=[skey], writes=[("n_E", b)])
            return E, b
        p.op("dve", lambda e: e.tensor_tensor(out=sbb[:], in0=S[:], in1=t["table"], op=ALU.add),
             reads=[skey] + t["tkeys"], writes=[("n_sb", b)])
        p.op("act", lambda e: e.activation(out=E[:], in_=sbb[:], func=AF.Exp), reads=[("n_sb", b)], writes=[("n_E", b)])
        return E, b

    def att_pv(t, E, b):
        acc_i = t["acc"]
        abank, akey = (IMPB[0], ("nI", 0)) if acc_i == "far" else (OB[acc_i], ("nO", acc_i))
        for h in range(4):
            p.op("pe", lambda e, h=h: e.matmul(abank[:, h * 65:(h + 1) * 65], lhsT=E[:, h * 128:(h + 1) * 128], rhs=t["v"],
                                               start=False, stop=False, skip_group_check=True),
                 reads=[("n_E", b)] + t["vkeys"], writes=[akey])
        kc = t.get("imp_kc")
        if kc is not None:
            for h in range(4):
                p.op("pe", lambda e, h=h: e.matmul(IMPB[h // 2][:, (h % 2) * 256:(h % 2 + 1) * 256],
                                                   lhsT=E[:, h * 128:(h + 1) * 128], rhs=OV[:, kc, :],
                                                   start=False, stop=False, skip_group_check=True),
                     reads=[("n_E", b), "OV"], writes=[("nI", h // 2)])

    def run_tiles(tiles, qt_ap, qkey, nb):
        pend = []
        for t in tiles:
            cur = att_s(t, qt_ap, qkey, nb)
            pend.append((t, cur[0], cur[1]))
            if len(pend) > nb - 1:
                att_pv(*pend.pop(0))
        for x in pend:
            att_pv(*x)

    for i in range(NQI):
        i2 = i % 2
        q, g_, fb, kw, vw = Qt[i2], GT[i2], FBt[i2], Kw[i2], Vw[i2]
        for c4 in range(4):
            p.dma("sp", q[0:64, c4, :, :], QT_d[:, :, i, :], writes=[("n_q", i2)])
        p.dma("sp", g_[:], gates_d[i], writes=[("n_g", i2)])
        p.dma("sp", fb[:], FB_d[i], writes=[("n_fb", i2)])
        p.dma("act", kw[:], KwT_d[:, 2 * i * 128:(2 * i + 5) * 128], writes=[("n_kw", i2)])
        p.dma("act", vw[:], VwA_d[2 * i * 128:(2 * i + 5) * 128, :].rearrange("(n p) c -> p n c", p=128), writes=[("n_vw", i2)])
        for a in range(3):
            p.op("dve", lambda e, a=a: e.memset(OB[a][:], 0.0), writes=[("nO", a)])
        for a in range(2):
            p.op("dve", lambda e, a=a: e.memset(IMPB[a][:], 0.0), writes=[("nI", a)])
        qap = q[0:64, 0, :, :].rearrange("p h q -> p (h q)")
        qk = ("n_q", i2)
        tiles = []
        for kc in range((2 * i + 1) // 16 + 1):
            e_ = 2 * i - 16 * kc
            tidx = e_ // 2 if e_ <= 16 else 9
            tiles.append(dict(kT=KC[:, kc * 128:(kc + 1) * 128], kkeys=["KC"], v=VC[:, kc, :], vkeys=["VC"],
                              table=TC[:, tidx, :], tkeys=["TC"], acc=0, imp_kc=kc))
        for d in range(5):
            tiles.append(dict(kT=kw[:, d * 128:(d + 1) * 128], kkeys=[("n_kw", i2)], v=vw[:, d, :], vkeys=[("n_vw", i2)],
                              table=TW[:, d, :], tkeys=["TW"], acc=2))
        cnt[0] = 0
        run_tiles(tiles, qap, qk, 2)
        sums = lambda a: OB[a][:, 0:260].rearrange("p (h c) -> p h c", c=65)[:, :, 64]
        p.op("dve", lambda e: e.tensor_scalar(out=rsc[:], in0=sums(0), scalar1=1e-30, scalar2=None, op0=ALU.max),
             reads=[("nO", 0)], writes=["n_rsc"])
        p.op("dve", lambda e: e.reciprocal(out=rsc[:], in_=rsc[:]), reads=["n_rsc"], writes=["n_rsc"])
        for h in range(4):
            p.op("dve", lambda e, h=h, fb=fb: e.scalar_tensor_tensor(
                out=imp[:], in0=IMPB[h // 2][:, (h % 2) * 256:(h % 2 + 1) * 256], scalar=rsc[:, h:h + 1],
                in1=(fb[:] if h == 0 else imp[:]), op0=ALU.mult, op1=ALU.add),
                reads=[("nI", h // 2), "n_rsc", ("n_fb", i2), "n_imp"], writes=["n_imp"])
        p.op("dve", lambda e: e.max(out=m8[:], in_=imp[:]), reads=["n_imp"], writes=["n_m8"])
        p.op("dve", lambda e: e.match_replace(out=imp2[:], in_to_replace=m8[:], in_values=imp[:], imm_value=-1e9),
             reads=["n_imp", "n_m8"], writes=["n_imp2"])
        p.op("dve", lambda e: e.max(out=m8b[:], in_=imp2[:]), reads=["n_imp2"], writes=["n_m8b"])
        p.op("dve", lambda e: e.tensor_scalar(out=thr[:], in0=m8b[:, 7:8], scalar1=-5000.0, scalar2=None, op0=ALU.max),
             reads=["n_m8b"], writes=["n_thr"])
        p.op("dve", lambda e: e.tensor_scalar(out=imp2[:], in0=imp[:], scalar1=thr[:, 0:1], scalar2=None, op0=ALU.is_ge),
             reads=["n_imp", "n_thr"], writes=["n_imp2"])
        p.op("dve", lambda e: e.tensor_scalar(out=negm[:], in0=imp2[:], scalar1=-1.0, scalar2=30000.0, op0=ALU.add, op1=ALU.mult),
             reads=["n_imp2"], writes=["n_negm"])
        for c4 in range(4):
            p.op("pe", lambda e, c4=c4: e.transpose(out=PT[64:128, c4 * 128:(c4 + 1) * 128], in_=negm[:, c4 * 64:(c4 + 1) * 64],
                                                    identity=identb[:]), reads=["n_negm", "identb"], writes=["nPT"])
        p.op("act", lambda e, q=q: e.activation(
            out=q[64:128, :, :, :], in_=PT[64:128, 0:512].rearrange("p (c q) -> p c q", c=4).unsqueeze(2).to_broadcast([64, 4, 4, 128]),
            func=AF.Copy), reads=["nPT"], writes=[("n_qm", i2)])
        tiles = []
        for kt in range(2 * i + 2):
            d3 = kt - (2 * i - 1)
            tb = TS[:, d3, :] if d3 >= 0 else None
            tiles.append(dict(kT=KsT[:, kt * 128:(kt + 1) * 128], kkeys=["KsT"], v=VsA[:, kt, :], vkeys=["VsA"],
                              table=tb, tkeys=["TS"], acc=(1 if d3 >= 0 else "far"),
                              rhs=q[:, kt // 32, :, :].rearrange("p h q -> p (h q)"), rkeys=[("n_qm", i2)]))
        p.op("dve", lambda e: e.memset(IMPB[0][:], 0.0), writes=[("nI", 0)])
        cnt[0] = 0
        run_tiles(tiles, qap, qk, 3)
        o = ot[i2]
        farsum = IMPB[0][:, 0:260].rearrange("p (h c) -> p h c", c=65)[:, :, 64]
        p.op("dve", lambda e: e.tensor_tensor(out=ssel[:], in0=farsum, in1=ECH[:], op=ALU.mult),
             reads=[("nI", 0), "n_ech"], writes=["n_ssel"])
        p.op("dve", lambda e: e.tensor_tensor(out=ssel[:], in0=sums(1), in1=ssel[:], op=ALU.add),
             reads=[("nO", 1), "n_ssel"], writes=["n_ssel"])
        for a in range(3):
            src_s = ssel[:] if a == 1 else sums(a)
            p.op("dve", lambda e, a=a, src_s=src_s: e.tensor_scalar(out=wbr[:, a, :], in0=src_s, scalar1=1e-30, scalar2=None, op0=ALU.max),
                 reads=[("nO", a), "n_ssel"], writes=[("n_wbr", a)])
            p.op("dve", lambda e, a=a: e.reciprocal(out=wbr[:, a, :], in_=wbr[:, a, :]), reads=[("n_wbr", a)], writes=[("n_wbr", a)])
            p.op("dve", lambda e, a=a, g_=g_: e.tensor_tensor(out=wbr[:, a, :], in0=wbr[:, a, :],
                                                               in1=g_[:].rearrange("p (h c) -> p h c", c=3)[:, :, a], op=ALU.mult),
                 reads=[("n_wbr", a), ("n_g", i2)], writes=[("n_wbr", a)])
        p.op("dve", lambda e: e.tensor_tensor(out=wfar[:], in0=wbr[:, 1, :], in1=ECH[:], op=ALU.mult),
             reads=[("n_wbr", 1), "n_ech"], writes=["n_wfar"])
        for h in range(4):
            for a in range(3):
                src = OB[a][:, h * 65:h * 65 + 64]
                if a == 0:
                    p.op("dve", lambda e, h=h, src=src, o=o: e.tensor_scalar(out=o[:, h * 64:(h + 1) * 64], in0=src,
                                                                            scalar1=wbr[:, 0, h:h + 1], scalar2=None, op0=ALU.mult),
                         reads=[("nO", 0), ("n_wbr", 0)], writes=[("n_o", i2)])
                else:
                    p.op("dve", lambda e, h=h, a=a, src=src, o=o: e.scalar_tensor_tensor(
                        out=o[:, h * 64:(h + 1) * 64], in0=src, scalar=wbr[:, a, h:h + 1], in1=o[:, h * 64:(h + 1) * 64],
                        op0=ALU.mult, op1=ALU.add), reads=[("nO", a), ("n_wbr", a), ("n_o", i2)], writes=[("n_o", i2)])
        for h in range(4):
            p.op("dve", lambda e, h=h, o=o: e.scalar_tensor_tensor(
                out=o[:, h * 64:(h + 1) * 64], in0=IMPB[0][:, h * 65:h * 65 + 64], scalar=wfar[:, h:h + 1],
                in1=o[:, h * 64:(h + 1) * 64], op0=ALU.mult, op1=ALU.add),
                reads=[("nI", 0), "n_wfar", ("n_o", i2)], writes=[("n_o", i2)])
        p.dma("act", o_d[i], o[:], reads=[("n_o", i2)])
    p.finish()
    return nc


_CONST = {}


def _consts():
    if not _CONST:
        _CONST["ident"] = np.eye(128, dtype=np.float32)
        s = np.arange(128)
        _CONST["tri"] = (s[:, None] <= s[None, :]).astype(np.float32)
    return _CONST


_NC_CACHE = {}


def _get_nc(name, builder):
    return builder()


def run_A(inp, l, x_cur):
    cst = _consts()
    c = inp["c"]
    maps = []
    cols = np.r_[0:3072, 3072:5120]
    ada_w = np.ascontiguousarray(inp["ada_w"][l][:, cols])
    ada_b = np.ascontiguousarray(inp["ada_b"][l][cols])
    wsT = np.ascontiguousarray(inp["gmlp_ws"][l].transpose(0, 2, 1))
    bsT = np.ascontiguousarray(inp["gmlp_bs"][l].T)
    for r in range(NCORE):
        b = r // 4
        maps.append({
            "x": np.ascontiguousarray(x_cur[r * TOKC:(r + 1) * TOKC]),
            "cT": np.ascontiguousarray(c[b].reshape(8, 128).T),
            "ada_w": ada_w, "ada_b": ada_b,
            "lng": inp["ln_g"][l, 0], "lnb": inp["ln_b"][l, 0],
            "w1": inp["ffn_w1"][l, 0], "w3": inp["ffn_w3"][l, 0], "w2": inp["ffn_w2"][l, 0],
            "w_in": inp["w_in"][l],
            "glng": inp["gmlp_ln_g"][l], "glnb": inp["gmlp_ln_b"][l],
            "wsT": wsT, "bsT": bsT,
            "ident": cst["ident"], "tri": cst["tri"],
        })
    nc = build_A()
    res = run_bass_kernel_spmd(nc, maps, core_ids=list(range(NCORE)))
    return res.results


def run_C(inp, l, x1, oatt, ogm, yssm, zz):
    cst = _consts()
    c = inp["c"]
    cols = np.r_[5120:9216]
    ada_w = np.ascontiguousarray(inp["ada_w"][l][:, cols])
    ada_b = np.ascontiguousarray(inp["ada_b"][l][cols])
    maps = []
    for r in range(NCORE):
        b = r // 4
        sl = slice(r * TOKC, (r + 1) * TOKC)
        maps.append({
            "x1": np.ascontiguousarray(x1[sl]), "cT": np.ascontiguousarray(c[b].reshape(8, 128).T),
            "ada_w": ada_w, "ada_b": ada_b,
            "lng1": inp["ln_g"][l, 1], "lnb1": inp["ln_b"][l, 1], "lng2": inp["ln_g"][l, 2], "lnb2": inp["ln_b"][l, 2],
            "w1": inp["ffn_w1"][l, 1], "w3": inp["ffn_w3"][l, 1], "w2": inp["ffn_w2"][l, 1],
            "w_out": inp["w_out"][l],
            "oatt": np.ascontiguousarray(oatt[sl]), "ogm": np.ascontiguousarray(ogm[sl]),
            "yssm": np.ascontiguousarray(yssm[sl]), "zz": np.ascontiguousarray(zz[sl]),
            "normg": inp["ssm_norm_g"][l], "ident": cst["ident"],
        })
    nc = build_C()
    res = run_bass_kernel_spmd(nc, maps, core_ids=list(range(NCORE)))
    return res.results


def run_S(inp, l, xbc, dtr):
    cst = _consts()
    s_ = np.arange(128)
    l_ = np.arange(256)
    triw = np.stack([(s_[:, None] + 128 * i <= l_[None, :]).astype(np.float32) for i in range(2)])
    ntriw = ((triw - 1.0) * 30000.0).astype(np.float32)
    maps = []
    cwl, cbl = inp["ssm_conv_w"][l], inp["ssm_conv_b"][l]
    for r in range(NCORE):
        b, h = r // 4, r % 4
        g = h // 2
        cols = np.r_[64 * h:64 * h + 64, 256 + 128 * g:256 + 128 * g + 128, 512 + 128 * g:512 + 128 * g + 128]
        xcat = xbc[b][:, cols]
        xpad = np.concatenate([np.zeros((3, 320), np.float32), xcat], 0)
        xc4 = np.ascontiguousarray(np.stack([xpad[k:k + SEQ] for k in range(4)], axis=1))
        maps.append({
            "xc4": xc4, "cw": np.ascontiguousarray(cwl[:, cols]).reshape(-1), "cb": np.ascontiguousarray(cbl[cols]),
            "dtr": np.ascontiguousarray(dtr[b][:, h]),
            "scal": np.array([inp["ssm_dt_bias"][l, h], inp["ssm_a_log"][l, h], inp["ssm_d"][l, h], 0.0], np.float32),
            "tri": cst["tri"], "triw": triw, "ntriw": ntriw, "identf": cst["ident"],
        })
    nc = build_S()
    res = run_bass_kernel_spmd(nc, maps, core_ids=list(range(NCORE)))
    y = np.zeros((BATCH, SEQ, 256), np.float32)
    for r in range(NCORE):
        y[r // 4][:, 64 * (r % 4):64 * (r % 4) + 64] = res.results[r]["y"]
    return y


def _bucket(n):
    n = np.maximum(n, 0)
    nf = np.maximum(n, 1).astype(np.float32)
    large = 16 + (np.log(nf / np.float32(16)) / np.float32(np.log(8.0)) * np.float32(16)).astype(np.int32)
    large = np.minimum(large, 31)
    return np.where(n < 16, n, large)


def _table(rb_aug, dist, valid, g):
    idx = np.where(valid, _bucket(dist), 32)
    t = rb_aug[idx][:, :, 4 * g:4 * g + 4]
    return np.ascontiguousarray(t.transpose(0, 2, 1).reshape(128, 512))


def run_N(inp, l, QT_all, KcT_all, VcT_all, KsT_all, KwT_all, Vs_all, Vw_all, gates_all):
    cst = _consts()
    T = SEQ
    rb_aug = np.concatenate([inp["rel_bias"], np.full((1, 8), -30000.0, np.float32)], 0)
    k = np.arange(128)[:, None]
    q = np.arange(128)[None, :]
    allv = np.ones((128, 128), bool)
    big = np.full((128, 128), 100000)
    kg = np.arange(1024)
    cs = 16 * kg
    j = np.arange(256)
    ov = np.clip(np.minimum(cs[:, None] + 32, 64 * j[None, :] + 64) - np.maximum(cs[:, None], 64 * j[None, :]), 0, None) / 32.0
    ov[1023] = 0.0
    OV = ov.reshape(8, 128, 256).astype(np.float32)
    jj = np.arange(64)[:, None]
    tcol = np.arange(T)[None, :]
    IND = (jj == 2 * ((tcol // 128) % 32) + (tcol % 128) // 64)
    maps = []
    for r in range(NCORE):
        b, g, par = r // 4, (r // 2) % 2, r % 2
        TC = [_table(rb_aug, 128 * (2 * e2 + par) + q - 16 * k - 31, (128 * (2 * e2 + par) + q - 16 * k - 31) >= 0, g) for e2 in range(9)]
        TC.append(_table(rb_aug, big, allv, g))
        prev = _table(rb_aug, 128 + q - k, allv, g)
        diag = _table(rb_aug, q - k, (q - k) >= 0, g)
        far = _table(rb_aug, big, allv, g)
        none = _table(rb_aug, big, ~allv, g)
        TS = [prev, diag, none, far] if par == 0 else [far, prev, diag, far]
        TW = [_table(rb_aug, q + 128 * (4 - d) - k, ((q + 128 * (4 - d) - k) >= 0) & ((q + 128 * (4 - d) - k) < 512), g) for d in range(5)]
        qt = 2 * np.arange(NQF) + par
        t = (128 * qt[:, None] + np.arange(128)[None, :])
        cur = (t // 64)[:, :, None]
        jb = np.arange(256)[None, None, :]
        FB = np.where((jb == 0) | (jb == cur) | (jb == cur - 1), 1e4, np.where(jb <= cur, 0.0, -1e4)).astype(np.float32)
        tok = slice(b * T, (b + 1) * T)
        QTc = QT_all[:, tok].reshape(8, 64, 128, 128)[4 * g:4 * g + 4, :, par::2, :].transpose(1, 0, 2, 3)
        blks = []
        for src in (KcT_all, VcT_all):
            kcg = src[g * 64:(g + 1) * 64, tok]
            bl = np.zeros((16, 2, 64, 1024), kcg.dtype)
            ii = np.arange(1023)
            for lt in range(16):
                for lb in range(2):
                    bl[lt, lb, :, :1023] = kcg[:, 16 * ii + 2 * lt + lb]
            blks.append(bl.reshape(16, 128, 1024))
        KsT = np.concatenate([KsT_all[g * 64:(g + 1) * 64, tok], IND.astype(KsT_all.dtype)], 0)
        ones = np.ones((T, 1), Vs_all.dtype)
        VsA = np.concatenate([Vs_all[tok, g * 64:(g + 1) * 64], ones], 1)
        KwP = np.zeros((64, 512 + T + 128), KwT_all.dtype)
        KwP[:, 512:512 + T] = KwT_all[g * 64:(g + 1) * 64, tok]
        VwP = np.zeros((512 + T + 128, 65), Vw_all.dtype)
        VwP[512:512 + T] = np.concatenate([Vw_all[tok, g * 64:(g + 1) * 64], ones], 1)
        gt = gates_all[tok, 12 * g:12 * g + 12].reshape(128, 128, 12)[par::2]
        maps.append({
            "QT": np.ascontiguousarray(QTc), "blk": np.ascontiguousarray(np.stack(blks)),
            "cw1": inp["cmp_w1"][l], "cb1": np.ascontiguousarray(inp["cmp_b1"][l].reshape(2, 2, 128).transpose(0, 2, 1)),
            "cpe": np.ascontiguousarray(inp["cmp_pe"][l].reshape(2, 16, 2, 64).transpose(0, 2, 3, 1).reshape(2, 128, 16)),
            "cw2": inp["cmp_w2"][l],
            "KsT": np.ascontiguousarray(KsT), "VsA": np.ascontiguousarray(VsA),
            "KwT": np.ascontiguousarray(KwP[:, par * 128:par * 128 + T + 512]),
            "VwA": np.ascontiguousarray(VwP[par * 128:par * 128 + T + 512]),
            "gates": np.ascontiguousarray(gt), "TC": np.stack(TC), "TS": np.stack(TS), "TW": np.stack(TW),
            "OV": OV, "FB": FB, "ident": cst["ident"],
        })
    nc = build_N()
    res = run_bass_kernel_spmd(nc, maps, core_ids=list(range(NCORE)))
    o_att = np.zeros((BATCH, SEQ, 512), np.float32)
    for r in range(NCORE):
        b, g, par = r // 4, (r // 2) % 2, r % 2
        o = res.results[r]["o"]
        o_att[b].reshape(128, 128, 512)[par::2, :, g * 256:(g + 1) * 256] = o
    return o_att


def kernel(**inp):
    inp = {k: np.asarray(v) for k, v in inp.items()}
    x = np.ascontiguousarray(inp["x"].reshape(-1, D))
    cat = lambda res, name, ax: np.concatenate([r[name] for r in res], ax)
    for l in range(DEPTH):
        ra = run_A(inp, l, x)
        x1 = cat(ra, "x1", 0)
        tmf = cat(ra, "tmf", 0)
        o_att = run_N(inp, l, cat(ra, "QT", 1), cat(ra, "KcT", 1), cat(ra, "VcT", 1), cat(ra, "KsT", 1), cat(ra, "KwT", 1),
                      cat(ra, "Vs", 0), cat(ra, "Vw", 0), tmf[:, 1028:1052])
        ypre = run_S(inp, l, tmf[:, 256:1024].reshape(BATCH, SEQ, 768), tmf[:, 1024:1028].reshape(BATCH, SEQ, 4))
        rcz = run_C(inp, l, x1, o_att.reshape(-1, 512), cat(ra, "ogm", 0), ypre.reshape(-1, 256), np.ascontiguousarray(tmf[:, 0:256]))
        x = cat(rcz, "x3", 0)
    return x.reshape(BATCH, SEQ, D).astype(np.float32)
```
